# Optimizing a Trainium2 kernel written in Bass

```python
import jax, jax.numpy as jnp
from jax import lax
import numpy as np

D_MODEL = 1024
BATCH = 32
SEQ = 2048
DEPTH = 1

CTX_LEN = 256
GRID_W = 64
MIX_WIDTH = D_MODEL
RWKV_WIDTH = MIX_WIDTH // 2
CONV_WIDTH = MIX_WIDTH - RWKV_WIDTH
HEAD_SIZE = 64
RWKV_HEADS = RWKV_WIDTH // HEAD_SIZE
DECAY_RANK = max(32, int(round(1.8 * D_MODEL ** 0.5 / 32)) * 32)
ICLR_RANK = max(32, int(round(1.8 * D_MODEL ** 0.5 / 32)) * 32)
GATE_RANK = max(32, int(round(0.6 * D_MODEL ** 0.8 / 32)) * 32)
CONV_KERNEL = 31
D_FF = 4 * D_MODEL
N_MOD = 6
EPS_RMS = 1e-6
EPS_LN = 1e-5
EPS_GN = 64e-5

IN_SPLITS = (RWKV_WIDTH, RWKV_WIDTH, RWKV_WIDTH, DECAY_RANK, DECAY_RANK,
             ICLR_RANK, ICLR_RANK, GATE_RANK, 2 * CONV_WIDTH)
IN_COLS = sum(IN_SPLITS)
SHIFT_COLS = sum(IN_SPLITS[:-1])
RWKV_CUTS = tuple(int(v) for v in np.cumsum(IN_SPLITS[:-1])[:-1])

kernel_name = "hymba_rwkv7_conformer_dit_block"


def rms_norm(x, g):
    xf = x.astype(jnp.float32)
    y = xf * lax.rsqrt(jnp.mean(xf * xf, axis=-1, keepdims=True) + EPS_RMS)
    return (y * g.astype(jnp.float32)).astype(x.dtype)


def modulate(h, shift, scale):
    return h * (1 + scale) + shift


def split_heads(t):
    return t.reshape(t.shape[:-1] + (RWKV_HEADS, HEAD_SIZE))


def token_shift(z, mu_prev, mu_next):
    zp = jnp.pad(z, ((0, 0), (1, 1), (0, 0)))
    mp = mu_prev.astype(jnp.float32)
    mn = mu_next.astype(jnp.float32)
    return z + mp * (zp[:, :-2] - z) + mn * (zp[:, 2:] - z)


def project_stream(h, w_in, mu_prev, mu_next):
    p = h @ w_in
    rw = token_shift(p[..., :SHIFT_COLS].astype(jnp.float32), mu_prev, mu_next)
    pieces = tuple(jnp.split(rw, RWKV_CUTS, axis=-1))
    return pieces, p[..., SHIFT_COLS:]


def wkv_scan(s0, r, decay, k, v, a_vec, b_vec, reverse):
    xs = tuple(jnp.moveaxis(t, 1, 0) for t in (r, decay, k, v, a_vec, b_vec))

    def step(s, inp):
        r_t, w_t, k_t, v_t, a_t, b_t = inp
        sa = jnp.einsum('bhvk,bhk->bhv', s, a_t)
        s = (s * w_t[:, :, None, :] + sa[..., None] * b_t[:, :, None, :]
             + v_t[..., None] * k_t[:, :, None, :])
        return s, jnp.einsum('bhvk,bhk->bhv', s, r_t)

    s_fin, ys = lax.scan(step, s0, xs, reverse=reverse)
    return s_fin, jnp.moveaxis(ys, 0, 1)


def rwkv_direction(s0, r, k, v, wd, ad, w0, w2, a0, a2, k_k, k_a, reverse):
    w_log = -jax.nn.softplus(-(w0 + jnp.tanh(wd) @ w2)) - 0.5
    decay = jnp.exp(-jnp.exp(w_log))
    iclr = jax.nn.sigmoid(a0 + ad @ a2)
    kk = split_heads(k * k_k)
    kk = kk / jnp.maximum(jnp.sqrt(jnp.sum(kk * kk, axis=-1, keepdims=True)), 1e-12)
    k_dir = split_heads(k * (1 + (iclr - 1) * k_a))
    s_fin, y = wkv_scan(s0, split_heads(r), split_heads(decay), k_dir, split_heads(v),
                        -kk, kk * split_heads(iclr), reverse)
    return s_fin, y, k_dir


def rwkv_bidir(s0_f, s0_b, pieces, decay_w0, decay_w2, iclr_a0, iclr_a2, k_k, k_a):
    r, k, v, wd_f, wd_b, ad_f, ad_b, _ = pieces
    s_f, y_f, kd_f = rwkv_direction(s0_f, r, k, v, wd_f, ad_f, decay_w0[0], decay_w2[0],
                                    iclr_a0[0], iclr_a2[0], k_k, k_a, False)
    s_b, y_b, kd_b = rwkv_direction(s0_b, r, k, v, wd_b, ad_b, decay_w0[1], decay_w2[1],
                                    iclr_a0[1], iclr_a2[1], k_k, k_a, True)
    return s_f, s_b, y_f + y_b, 0.5 * (kd_f + kd_b)


def rwkv_readout(y, k_bar, pieces, r_k, gate_w2, lnx_w, lnx_b):
    r, _, v, _, _, _, _, gd = pieces
    mu = jnp.mean(y, axis=-1, keepdims=True)
    var = jnp.mean(jnp.square(y - mu), axis=-1, keepdims=True)
    yn = ((y - mu) * lax.rsqrt(var + EPS_GN)).reshape(r.shape) * lnx_w + lnx_b
    rh = split_heads(r)
    bonus = jnp.sum(rh * k_bar * r_k, axis=-1, keepdims=True) * split_heads(v)
    g = jax.nn.sigmoid(gd) @ gate_w2
    return (yn + bonus.reshape(r.shape)) * g


def conformer_conv(pcv, n_lines, line_len, conv_w, conv_b, ln_w, ln_b):
    u = pcv[..., :CONV_WIDTH] * jax.nn.sigmoid(pcv[..., CONV_WIDTH:])
    bsz, n_tok, ch = u.shape
    lines = u.reshape(bsz * n_lines, line_len, ch)
    pad = CONV_KERNEL // 2
    y = lax.conv_general_dilated(lines, conv_w[:, None, :].astype(lines.dtype), (1,),
                                 [(pad, pad)], dimension_numbers=('NWC', 'WIO', 'NWC'),
                                 feature_group_count=ch)
    yf = (y.reshape(bsz, n_tok, ch) + conv_b).astype(jnp.float32)
    mu = jnp.mean(yf, axis=-1, keepdims=True)
    var = jnp.mean(jnp.square(yf - mu), axis=-1, keepdims=True)
    yn = (yf - mu) * lax.rsqrt(var + EPS_LN) * ln_w + ln_b
    return jax.nn.silu(yn).astype(pcv.dtype)


def sqrelu_mlp(h, w1, w2):
    return jnp.square(jax.nn.relu(h @ w1)) @ w2


def setup_inputs(seed: int = 0) -> dict:
    key = jax.random.key(seed)
    ks = jax.random.split(key, 32)
    L, D, W, CW = DEPTH, D_MODEL, RWKV_WIDTH, CONV_WIDTH
    nrm = lambda k, shape, s: jax.random.normal(k, shape, jnp.float32) * s
    gain = lambda k, shape: 1.0 + nrm(k, shape, 0.02)
    return {
        "x": nrm(ks[0], (BATCH, SEQ, D), 1.0),
        "c": nrm(ks[1], (BATCH, D), 1.0),
        "ctx": nrm(ks[2], (BATCH, CTX_LEN, D), 1.0),
        "c_ctx": nrm(ks[3], (D,), 1.0),
        "ada_w": nrm(ks[4], (L, D, N_MOD * D), 0.5 * D ** -0.5),
        "ada_b": nrm(ks[5], (L, N_MOD * D), 0.02),
        "mix_pre_g": gain(ks[6], (L, D)),
        "mix_post_g": gain(ks[7], (L, D)),
        "mlp_pre_g": gain(ks[8], (L, D)),
        "mlp_post_g": gain(ks[9], (L, D)),
        "w_in": nrm(ks[10], (L, D, IN_COLS), D ** -0.5),
        "mu_prev": jax.random.uniform(ks[11], (L, SHIFT_COLS), jnp.float32, 0.0, 0.5),
        "mu_next": jax.random.uniform(ks[12], (L, SHIFT_COLS), jnp.float32, 0.0, 0.5),
        "decay_w0": jax.random.uniform(ks[13], (L, 2, W), jnp.float32, -6.0, 1.0),
        "decay_w2": nrm(ks[14], (L, 2, DECAY_RANK, W), 0.5 * DECAY_RANK ** -0.5),
        "iclr_a0": nrm(ks[15], (L, 2, W), 0.5),
        "iclr_a2": nrm(ks[16], (L, 2, ICLR_RANK, W), 0.5 * ICLR_RANK ** -0.5),
        "k_k": 0.85 + nrm(ks[17], (L, W), 0.02),
        "k_a": 1.0 + nrm(ks[18], (L, W), 0.02),
        "r_k": nrm(ks[19], (L, RWKV_HEADS, HEAD_SIZE), 0.1),
        "gate_w2": nrm(ks[20], (L, GATE_RANK, W), GATE_RANK ** -0.5),
        "lnx_w": gain(ks[21], (L, W)),
        "lnx_b": nrm(ks[22], (L, W), 0.02),
        "conv_w": nrm(ks[23], (L, CONV_KERNEL, CW), CONV_KERNEL ** -0.5),
        "conv_b": nrm(ks[24], (L, CW), 0.02),
        "conv_ln_w": gain(ks[25], (L, CW)),
        "conv_ln_b": nrm(ks[26], (L, CW), 0.02),
        "w_out": nrm(ks[27], (L, MIX_WIDTH, D), MIX_WIDTH ** -0.5),
        "mlp_w1": nrm(ks[28], (L, D, D_FF), D ** -0.5),
        "mlp_w2": nrm(ks[29], (L, D_FF, D), D_FF ** -0.5),
    }


def reference(x, c, ctx, c_ctx, ada_w, ada_b, mix_pre_g, mix_post_g, mlp_pre_g, mlp_post_g,
              w_in, mu_prev, mu_next, decay_w0, decay_w2, iclr_a0, iclr_a2, k_k, k_a, r_k,
              gate_w2, lnx_w, lnx_b, conv_w, conv_b, conv_ln_w, conv_ln_b, w_out,
              mlp_w1, mlp_w2):
    n_rows = x.shape[1] // GRID_W
    s_zero = jnp.zeros((x.shape[0], RWKV_HEADS, HEAD_SIZE, HEAD_SIZE), jnp.float32)
    for l in range(DEPTH):
        update_ctx = l + 1 < DEPTH
        mod_x = jnp.split((jax.nn.silu(c) @ ada_w[l] + ada_b[l])[:, None, :], N_MOD, axis=-1)
        mod_c = jnp.split(jax.nn.silu(c_ctx) @ ada_w[l] + ada_b[l], N_MOD, axis=-1)
        rw_params = (decay_w0[l], decay_w2[l], iclr_a0[l], iclr_a2[l], k_k[l], k_a[l])
        ro_params = (r_k[l], gate_w2[l], lnx_w[l], lnx_b[l])
        cv_params = (conv_w[l], conv_b[l], conv_ln_w[l], conv_ln_b[l])

        hx = modulate(rms_norm(x, mix_pre_g[l]), mod_x[0], mod_x[1])
        hc = modulate(rms_norm(ctx, mix_pre_g[l]), mod_c[0], mod_c[1])
        px, cvx = project_stream(hx, w_in[l], mu_prev[l], mu_next[l])
        pc, cvc = project_stream(hc, w_in[l], mu_prev[l], mu_next[l])
        s_f_c, s_b_c, y_c, kbar_c = rwkv_bidir(s_zero, s_zero, pc, *rw_params)
        _, _, y_x, kbar_x = rwkv_bidir(s_f_c, s_b_c, px, *rw_params)
        mix_x = jnp.concatenate(
            [rwkv_readout(y_x, kbar_x, px, *ro_params).astype(x.dtype),
             conformer_conv(cvx, n_rows, GRID_W, *cv_params)], axis=-1) @ w_out[l]
        x = x + mod_x[2] * rms_norm(mix_x, mix_post_g[l])
        if update_ctx:
            mix_c = jnp.concatenate(
                [rwkv_readout(y_c, kbar_c, pc, *ro_params).astype(ctx.dtype),
                 conformer_conv(cvc, 1, ctx.shape[1], *cv_params)], axis=-1) @ w_out[l]
            ctx = ctx + mod_c[2] * rms_norm(mix_c, mix_post_g[l])

        hx = modulate(rms_norm(x, mlp_pre_g[l]), mod_x[3], mod_x[4])
        x = x + mod_x[5] * rms_norm(sqrelu_mlp(hx, mlp_w1[l], mlp_w2[l]), mlp_post_g[l])
        if update_ctx:
            hc = modulate(rms_norm(ctx, mlp_pre_g[l]), mod_c[3], mod_c[4])
            ctx = ctx + mod_c[5] * rms_norm(sqrelu_mlp(hc, mlp_w1[l], mlp_w2[l]), mlp_post_g[l])
    return x
```

```python
from contextlib import ExitStack
import os
import numpy as np
import concourse.bass as bass
import concourse.mybir as mybir
from concourse.bass_utils import run_bass_kernel_spmd

F32 = mybir.dt.float32
BF16 = mybir.dt.bfloat16
AF = mybir.ActivationFunctionType
ALU = mybir.AluOpType
AX = mybir.AxisListType

ENGS = ("pe", "act", "dve", "pool", "sp")
CH = 30000
D = 1024
DK = 8
NCORES = 8
CDEC = 0.6065306597126334


class Sched:
    def __init__(self):
        self.q = {e: [] for e in ENGS}
        self.cnt = {e: 0 for e in ENGS}
        self.seen = {e: {} for e in ENGS}
        self.last_w = {}
        self.readers = {}
        self.dma_cnt = {}
        self.dma_keys = []
        self.fence_snap = None
        self.fenced = {e: True for e in ENGS}

    def fence(self):
        snap = [(e, self.cnt[e]) for e in ENGS if self.cnt[e] > 0]
        snap += [("dma:" + k, n) for k, n in self.dma_cnt.items()]
        self.fence_snap = snap
        self.fenced = {e: False for e in ENGS}
        self.last_w = {}
        self.readers = {}

    def _deps(self, eng, reads, writes):
        deps = set()
        if not self.fenced[eng]:
            self.fenced[eng] = True
            deps |= set(self.fence_snap)
        for k in reads:
            if k in self.last_w:
                deps.add(self.last_w[k])
        for k in writes:
            if k in self.last_w:
                deps.add(self.last_w[k])
            deps |= self.readers.get(k, set())
        need = {}
        for (e, s) in deps:
            need[e] = max(need.get(e, 0), s)
        waits = []
        for e, s in need.items():
            if e == "pe" and eng == "pe":
                continue
            if self.seen[eng].get(e, 0) >= s:
                continue
            self.seen[eng][e] = s
            waits.append((e, s))
        return waits

    def _commit(self, me, reads, writes):
        for k in reads:
            self.readers.setdefault(k, set()).add(me)
        for k in writes:
            self.last_w[k] = me
            self.readers[k] = set()

    def op(self, eng, fn, reads=(), writes=()):
        waits = self._deps(eng, reads, writes)
        self.cnt[eng] += 1
        self.q[eng].append(("op", waits, fn, self.cnt[eng]))
        self._commit((eng, self.cnt[eng]), reads, writes)

    def dma(self, eng, fn, semkey, reads=(), writes=()):
        waits = self._deps(eng, reads, writes)
        if semkey not in self.dma_cnt:
            self.dma_cnt[semkey] = 0
            self.dma_keys.append(semkey)
        self.dma_cnt[semkey] += 1
        self.q[eng].append(("dma", waits, fn, semkey))
        self._commit(("dma:" + semkey, self.dma_cnt[semkey]), reads, writes)

    def emit_engine(self, eng, engobj, sems, dma_sems):
        def do_wait(e, s):
            if e.startswith("dma:"):
                engobj.wait_ge(dma_sems[e[4:]], 16 * s)
            else:
                engobj.wait_ge(sems[e][(s - 1) // CH], ((s - 1) % CH) + 1)

        for item in self.q[eng]:
            for (e, s) in item[1]:
                do_wait(e, s)
            if item[0] == "op":
                item[2](engobj).then_inc(sems[eng][(item[3] - 1) // CH], 1)
            else:
                item[2](engobj).then_inc(dma_sems[item[3]], 16)
        if eng == "sp":
            for k, n in self.dma_cnt.items():
                engobj.wait_ge(dma_sems[k], 16 * n)

    def run(self, nc, es):
        sems = {e: [es.enter_context(nc.semaphore("s_%s_%d" % (e, i)))
                    for i in range(max(1, (self.cnt[e] + CH - 1) // CH))] for e in ENGS}
        dma_sems = {k: es.enter_context(nc.semaphore("d_%d" % i)) for i, k in enumerate(self.dma_keys)}
        block = es.enter_context(nc.Block())
        block.tensor(lambda e: self.emit_engine("pe", e, sems, dma_sems))
        block.scalar(lambda e: self.emit_engine("act", e, sems, dma_sems))
        block.vector(lambda e: self.emit_engine("dve", e, sems, dma_sems))
        block.gpsimd(lambda e: self.emit_engine("pool", e, sems, dma_sems))
        block.sync(lambda e: self.emit_engine("sp", e, sems, dma_sems))


class Arena:
    def __init__(self, nc, base, limit):
        self.nc, self.base, self.limit, self.off, self.n = nc, base, limit, base, 0

    def reset(self):
        self.off = self.base

    def t(self, shape, dt):
        nb = int(np.prod(shape[1:])) * (4 if dt == F32 else 2)
        nb = (nb + 63) // 64 * 64
        assert self.off + nb <= self.limit, ("SBUF overflow", self.off, nb, self.limit)
        self.n += 1
        h = self.nc.alloc_sbuf_tensor_at("a%d" % self.n, list(shape), dt, offset=self.off)
        self.off += nb
        return h


CF = {}
CR = {}


def _layout():
    off = 0
    for nm, w in (("ada_b", 48), ("mix_pre_g", 8), ("mlp_pre_g", 8), ("mu_prev", 16), ("mu_next", 16),
                  ("w0", 8), ("a0", 8), ("k_k", 4), ("k_a", 4), ("r_k", 4), ("conv_w", 124),
                  ("conv_b", 4), ("cln_w", 4), ("cln_b", 4)):
        CF[nm] = (off, w)
        off += w
    ncf = off
    off = 0
    for nm, w in (("ada_b", 6144), ("mix_post_g", 1024), ("mlp_post_g", 1024), ("lnx_w", 512), ("lnx_b", 512)):
        CR[nm] = (off, w)
        off += w
    return ncf, off


NCF, NCR = _layout()


def build(NB, T, TC, debug=False, PHASES=9):
    TT = TC + T
    NB1 = NB + 1
    nc = bass.Bass("TRN2", target_bir_lowering=False)
    dram = lambda n, s, dt, kind: nc.dram_tensor(n, list(s), dt, kind=kind).ap()
    x_d = dram("x", [NB, T, D], F32, "ExternalInput")
    ctx_d = dram("ctx", [NB, TC, D], F32, "ExternalInput")
    cT_d = dram("cT", [128, DK, NB1], F32, "ExternalInput")
    adaw_d = dram("ada_w", [D, 6144], F32, "ExternalInput")
    win_d = dram("w_in", [D, 3072], F32, "ExternalInput")
    cfm_d = dram("cfm", [128, NCF], F32, "ExternalInput")
    crow_d = dram("crow", [128, NCR], F32, "ExternalInput")
    w2d_d = dram("decay_w2", [2, 64, 512], F32, "ExternalInput")
    a2_d = dram("iclr_a2", [2, 64, 512], F32, "ExternalInput")
    gw2_d = dram("gate_w2", [160, 512], F32, "ExternalInput")
    wout_d = dram("w_out", [D, D], F32, "ExternalInput")
    w1_d = dram("mlp_w1", [D, 4096], F32, "ExternalInput")
    w2_d = dram("mlp_w2", [4096, D], F32, "ExternalInput")
    out_d = dram("out", [NB, T, D], F32, "ExternalOutput")
    skind = "ExternalOutput" if debug else "Internal"
    pa_d = dram("pa_s", [NB, 20, 128, TT], BF16, skind)
    y_d = dram("y_s", [NB, 2, T, 512], F32, skind)
    bo_d = dram("bo_s", [NB, 2, T, 512], F32, skind)

    S = Sched()
    es = ExitStack()
    with es:
        banks = [es.enter_context(nc.psum_tensor("bank%d" % i, [128, 512], F32)) for i in range(8)]
        bk = lambda i: "bank%d" % i
        P_ = Arena(nc, 17408, 51 * 1024)
        A_ = Arena(nc, 51 * 1024, 223 * 1024)

        cfm = P_.t([128, NCF], F32)
        crow = P_.t([128, NCR - 6144], F32)
        ident = P_.t([128, 128], BF16)
        identf = P_.t([128, 128], F32)
        bones = P_.t([128, 128], F32)
        hsel = P_.t([128, 2], BF16)
        msk = P_.t([128, 2, 2, 128], BF16)
        mskN = P_.t([128, 2, 64], BF16)
        identB = P_.t([128, 64], BF16)
        onesb = P_.t([128, 1], BF16)
        rstm = P_.t([128, 512], F32)
        modfm = P_.t([128, 48, NB1], F32)
        gates = P_.t([NB1, 2, 1024], F32)
        gm = P_.t([128, 2, DK, NB1], F32)
        c0 = P_.t([128, 16], F32)
        sel = P_.t([NB1, NB1, 128], F32)
        cf = lambda nm: cfm[:, CF[nm][0]:CF[nm][0] + CF[nm][1]]
        cr = lambda nm: crow[:, CR[nm][0] - 6144:CR[nm][0] - 6144 + CR[nm][1]]

        S.dma("sp", lambda e: e.dma_start(out=cfm[:], in_=cfm_d[:, :]), "c0", writes=["cfm"])
        S.dma("sp", lambda e: e.dma_start(out=crow[:], in_=crow_d[:, 6144:NCR]), "c0", writes=["crow"])
        S.op("pool", lambda e: e.memset(identf[:], 1.0), writes=["identf"])
        S.op("pool", lambda e: e.affine_select(out=identf[:], in_=identf[:], pattern=[[-1, 128]],
                                               compare_op=ALU.is_equal, fill=0.0, base=0, channel_multiplier=1),
             reads=["identf"], writes=["identf"])
        S.op("dve", lambda e: e.tensor_copy(out=ident[:], in_=identf[:]), reads=["identf"], writes=["ident"])
        S.op("pool", lambda e: e.memset(bones[:], 0.0), writes=["bones"])
        S.op("pool", lambda e: e.memset(bones[0:64, 0:64], 1.0), reads=["bones"], writes=["bones"])
        S.op("pool", lambda e: e.memset(bones[64:128, 64:128], 1.0), reads=["bones"], writes=["bones"])
        S.op("pool", lambda e: e.memset(hsel[:], 0.0), writes=["hsel"])
        S.op("pool", lambda e: e.memset(hsel[0:64, 0:1], 1.0), reads=["hsel"], writes=["hsel"])
        S.op("pool", lambda e: e.memset(hsel[64:128, 1:2], 1.0), reads=["hsel"], writes=["hsel"])
        mtmp = P_.t([64, 64], F32)

        def mk_mask(dst_ap, sign, strict):
            S.op("pool", lambda e: e.memset(mtmp[:], 1.0), reads=["mtmp"], writes=["mtmp"])
            S.op("pool", lambda e: e.affine_select(out=mtmp[:], in_=mtmp[:], pattern=[[sign, 64]],
                                                   compare_op=ALU.is_gt if strict else ALU.is_ge,
                                                   fill=0.0, base=0, channel_multiplier=-sign),
                 reads=["mtmp"], writes=["mtmp"])
            S.op("pool", lambda e: e.tensor_copy(out=dst_ap, in_=mtmp[:]), reads=["mtmp"], writes=["msk"])
        for d in range(2):
            sg_ = 1 if d == 0 else -1
            for rr in range(2):
                mk_mask(msk[0:64, d, rr, 0:64], sg_, True)
                mk_mask(msk[0:64, d, rr, 64:128], sg_, False)
            mk_mask(mskN[0:64, d, :], -sg_, True)
        S.dma("sp", lambda e: e.dma_start(out=msk[64:128], in_=msk[0:64]), "c0", reads=["msk"], writes=["msk"])
        S.dma("sp", lambda e: e.dma_start(out=mskN[64:128], in_=mskN[0:64]), "c0", reads=["msk"], writes=["msk"])
        S.op("dve", lambda e: e.tensor_tensor(out=identB[:], in0=ident[:, 0:64], in1=ident[:, 64:128], op=ALU.add), reads=["ident"], writes=["identB"])
        S.op("pool", lambda e: e.memset(onesb[:], 1.0), writes=["onesb"])
        S.op("pool", lambda e: e.memset(rstm[:], 1.0), writes=["rstm"])
        S.op("pool", lambda e: e.memset(rstm[:].rearrange("p (c t) -> p c t", t=64)[:, :, 0:1], 0.0),
             reads=["rstm"], writes=["rstm"])
        S.op("pool", lambda e: e.memset(sel[:], 0.0), writes=["sel"])
        for b in range(NB1):
            S.op("pool", lambda e, b=b: e.memset(sel[:, b, :], 1.0), reads=["sel"], writes=["sel"])
            S.op("pool", lambda e, b=b: e.affine_select(out=sel[:, b, :], in_=sel[:, b, :], pattern=[[0, 128]],
                                                        compare_op=ALU.is_equal, fill=0.0, base=-b,
                                                        channel_multiplier=1),
                 reads=["sel"], writes=["sel"])

        A_.reset()
        cT = A_.t([128, DK, NB1], F32)
        siluT = A_.t([128, DK, NB1], F32)
        adab_row = A_.t([NB1, 6144], F32)
        aw = [A_.t([128, DK, 512], F32) for _ in range(2)]
        S.dma("sp", lambda e: e.dma_start(out=cT[:], in_=cT_d[:, :, :]), "c0", writes=["cT"])
        S.dma("sp", lambda e: e.dma_start(out=adab_row[:], in_=crow_d[0:NB1, 0:6144]), "c0", writes=["adab_row"])
        S.op("act", lambda e: e.activation(out=siluT[:], in_=cT[:], func=AF.Silu), reads=["cT"], writes=["siluT"])
        for n in range(12):
            a = aw[n % 2]
            ak = "aw%d" % (n % 2)
            S.dma("sp", lambda e, a=a, n=n: e.dma_start(
                out=a[:], in_=adaw_d[:, n * 512:(n + 1) * 512].rearrange("(k p) n -> p k n", p=128)),
                ak, writes=[ak])
            m = n // 2
            if m in (2, 5):
                for k in range(DK):
                    S.op("pe", lambda e, a=a, k=k: e.matmul(banks[0][0:NB1, :], lhsT=siluT[:, k, :], rhs=a[:, k, :],
                                                            start=(k == 0), stop=(k == DK - 1)),
                         reads=[ak, "siluT"], writes=[bk(0)])
                gi = 0 if m == 2 else 1
                S.op("dve", lambda e, n=n, gi=gi: e.tensor_tensor(
                    out=gates[:, gi, (n % 2) * 512:(n % 2) * 512 + 512], in0=banks[0][0:NB1, :],
                    in1=adab_row[:, n * 512:(n + 1) * 512], op=ALU.add),
                    reads=["adab_row"], writes=[bk(0), "gates"])
            else:
                for j in range(4):
                    for k in range(DK):
                        S.op("pe", lambda e, a=a, k=k, j=j: e.matmul(
                            banks[1][:, j * NB1:(j + 1) * NB1], lhsT=a[:, k, j * 128:(j + 1) * 128],
                            rhs=siluT[:, k, :], start=(k == 0), stop=(k == DK - 1)),
                            reads=[ak, "siluT"], writes=[bk(1)])
                S.op("dve", lambda e, n=n: e.tensor_tensor(
                    out=modfm[:, n * 4:(n + 1) * 4, :],
                    in0=banks[1][:, 0:4 * NB1].rearrange("p (j b) -> p j b", b=NB1),
                    in1=cf("ada_b")[:, n * 4:(n + 1) * 4].unsqueeze(2).to_broadcast([128, 4, NB1]), op=ALU.add),
                    reads=["cfm"], writes=[bk(1), "modfm"])
        for gi, (gn, m) in enumerate((("mix_pre_g", 1), ("mlp_pre_g", 4))):
            S.op("dve", lambda e, gi=gi, m=m: e.tensor_scalar(out=gm[:, gi], in0=modfm[:, m * 8:(m + 1) * 8, :],
                                                              scalar1=1.0, scalar2=None, op0=ALU.add),
                 reads=["modfm"], writes=["gm"])
            S.op("dve", lambda e, gi=gi, gn=gn: e.tensor_tensor(
                out=gm[:, gi], in0=gm[:, gi], in1=cf(gn).unsqueeze(2).to_broadcast([128, DK, NB1]), op=ALU.mult),
                reads=["gm", "cfm"], writes=["gm"])
        S.op("dve", lambda e: e.tensor_tensor(out=c0[:], in0=cf("mu_prev"), in1=cf("mu_next"), op=ALU.add),
             reads=["cfm"], writes=["c0"])
        S.op("dve", lambda e: e.tensor_scalar(out=c0[:], in0=c0[:], scalar1=-1.0, scalar2=1.0, op0=ALU.mult,
                                              op1=ALU.add), reads=["c0"], writes=["c0"])

        def front(xt_ap, xkey, hT_ap, hkey, gi, shm, b, tmp, pbank):
            S.op("act", lambda e: e.activation(out=tmp["sq"][:], in_=xt_ap, func=AF.Square),
                 reads=[xkey], writes=["f_sq"])
            S.op("dve", lambda e: e.reduce_sum(out=tmp["ss"][:], in_=tmp["sq"][:], axis=AX.X),
                 reads=["f_sq"], writes=["f_ss"])
            S.op("dve", lambda e: e.tensor_scalar(out=tmp["ss"][:], in0=tmp["ss"][:], scalar1=1.0 / D, scalar2=1e-6,
                                                  op0=ALU.mult, op1=ALU.add), reads=["f_ss"], writes=["f_ss"])
            S.op("act", lambda e: e.activation(out=tmp["ss"][:], in_=tmp["ss"][:], func=AF.Sqrt),
                 reads=["f_ss"], writes=["f_ss"])
            S.op("dve", lambda e: e.reciprocal(out=tmp["ss"][:], in_=tmp["ss"][:]), reads=["f_ss"], writes=["f_ss"])
            S.op("act", lambda e: e.activation(out=tmp["xn"][:], in_=xt_ap, func=AF.Copy, scale=tmp["ss"][:, 0:1]),
                 reads=[xkey, "f_ss"], writes=["f_xn"])
            pb = banks[pbank].bitcast(BF16)
            for k in range(DK):
                S.op("pe", lambda e, k=k: e.transpose(pb[:, k * 128:(k + 1) * 128], tmp["xn"][:, k * 128:(k + 1) * 128],
                                                      ident[:]), reads=["f_xn", "ident"], writes=[bk(pbank)])
            S.op("dve", lambda e: e.tensor_tensor(out=hT_ap, in0=pb[:, 0:1024].rearrange("p (k t) -> p k t", t=128),
                                                  in1=gm[:, gi, :, b:b + 1].to_broadcast([128, DK, 128]), op=ALU.mult),
                 reads=["gm"], writes=[bk(pbank), hkey])
            S.op("dve", lambda e: e.tensor_tensor(
                out=hT_ap, in0=hT_ap, in1=modfm[:, shm * 8:(shm + 1) * 8, b:b + 1].to_broadcast([128, DK, 128]),
                op=ALU.add), reads=["modfm", hkey], writes=[hkey])

        S.fence()
        A_.reset()
        winb = A_.t([128, DK, 3072], BF16)
        stg = [A_.t([128, DK, 512], F32) for _ in range(2)]
        for n in range(6):
            s_ = stg[n % 2]
            sk = "stg%d" % (n % 2)
            S.dma("sp", lambda e, s_=s_, n=n: e.dma_start(
                out=s_[:], in_=win_d[:, n * 512:(n + 1) * 512].rearrange("(k p) n -> p k n", p=128)), sk, writes=[sk])
            S.op("dve" if n % 2 == 0 else "act",
                 (lambda e, s_=s_, n=n: e.tensor_copy(out=winb[:, :, n * 512:(n + 1) * 512], in_=s_[:])) if n % 2 == 0
                 else (lambda e, s_=s_, n=n: e.activation(out=winb[:, :, n * 512:(n + 1) * 512], in_=s_[:], func=AF.Copy)),
                 reads=[sk], writes=["winb"])
        TM = max(T, TC)
        hT = A_.t([128, DK, TM + 2], BF16)
        xt = [A_.t([128, D], F32) for _ in range(2)]
        ftmp = {"sq": A_.t([128, D], F32), "ss": A_.t([128, 1], F32), "xn": A_.t([128, D], BF16)}
        etmp = [A_.t([128, 512], F32) for _ in range(2)]
        obuf = [A_.t([128, 512], BF16) for _ in range(3)]
        A_sg = [A_.t([128, 512], F32) for _ in range(4)]
        S.op("pool", lambda e: e.memset(hT[:], 0.0), writes=["hT"])
        xi = 0
        ob_i = 0
        pb_i = 0
        for b in range(NB):
            for (src, Ts, toff, bmod, tiles) in ((ctx_d, TC, 0, NB, list(range(4, 14))),
                                                 (x_d, T, TC, b, list(range(0, 24)))):
                if Ts < TM:
                    S.op("pool", lambda e, Ts=Ts: e.memset(hT[:, :, Ts + 1:Ts + 2], 0.0), reads=["hT"], writes=["hT"])
                for tt in range(Ts // 128):
                    xa = xt[xi % 2]
                    xk = "xt%d" % (xi % 2)
                    xi += 1
                    S.dma("sp", lambda e, xa=xa, src=src, b=b, tt=tt: e.dma_start(
                        out=xa[:], in_=src[b, tt * 128:(tt + 1) * 128, :]), xk, writes=[xk])
                    front(xa[:], xk, hT[:, :, 1 + tt * 128:1 + (tt + 1) * 128], "hT", 0, 0, bmod, ftmp, 7)
                w0 = 0
                while w0 < Ts:
                    n = min(510, Ts - w0)
                    sg_ready = {}
                    for j in [jj for jj in tiles if jj >= 20] + [jj for jj in tiles if jj < 20]:
                        pbk = pb_i % 4
                        pb_i += 1
                        pbt = banks[pbk]
                        for k in range(DK):
                            S.op("pe", lambda e, pbt=pbt, k=k, j=j, w0=w0, n=n: e.matmul(
                                pbt[:, 0:n + 2], lhsT=winb[:, k, j * 128:(j + 1) * 128], rhs=hT[:, k, w0:w0 + n + 2],
                                start=(k == 0), stop=(k == DK - 1)), reads=["winb", "hT"], writes=[bk(pbk)])
                        if j >= 20:
                            sgt = A_sg[j - 20]
                            S.op("act", lambda e, pbt=pbt, sgt=sgt, n=n: e.activation(
                                out=sgt[:, 0:n], in_=pbt[:, 1:n + 1], func=AF.Sigmoid),
                                writes=[bk(pbk), "sg%d" % (j - 20)])
                            continue
                        ob = obuf[ob_i % 3]
                        ok = "ob%d" % (ob_i % 3)
                        ob_i += 1
                        if j >= 16:
                            sgt = A_sg[j - 16]
                            S.op("dve", lambda e, pbt=pbt, sgt=sgt, ob=ob, n=n: e.tensor_tensor(
                                out=ob[:, 0:n], in0=pbt[:, 1:n + 1], in1=sgt[:, 0:n], op=ALU.mult),
                                reads=["sg%d" % (j - 16)], writes=[bk(pbk), ok])
                        else:
                            et = etmp[j % 2]
                            ek = "et%d" % (j % 2)
                            S.op("act", lambda e, pbt=pbt, et=et, n=n, j=j: e.activation(
                                out=et[:, 0:n], in_=pbt[:, 1:n + 1], func=AF.Copy, scale=c0[:, j:j + 1]),
                                reads=["c0"], writes=[bk(pbk), ek])
                            S.op("dve", lambda e, pbt=pbt, et=et, n=n, j=j: e.scalar_tensor_tensor(
                                out=et[:, 0:n], in0=pbt[:, 0:n], scalar=cf("mu_prev")[:, j:j + 1], in1=et[:, 0:n],
                                op0=ALU.mult, op1=ALU.add), reads=["cfm", ek], writes=[bk(pbk), ek])
                            S.op("dve", lambda e, pbt=pbt, et=et, ob=ob, n=n, j=j: e.scalar_tensor_tensor(
                                out=ob[:, 0:n], in0=pbt[:, 2:n + 2], scalar=cf("mu_next")[:, j:j + 1], in1=et[:, 0:n],
                                op0=ALU.mult, op1=ALU.add), reads=["cfm", ek], writes=[bk(pbk), ok])
                        S.dma("sp", lambda e, ob=ob, b=b, j=j, toff=toff, w0=w0, n=n: e.dma_start(
                            out=pa_d[b, j, :, toff + w0:toff + w0 + n], in_=ob[:, 0:n]), "pa_st", reads=[ok])
                    w0 += n
        if PHASES >= 2:
            S.fence()
            A_.reset()
            W = 256
            NCW = W // 64
            w2b = A_.t([64, 2, 512], BF16)
            a2b = A_.t([128, 2, 512], BF16)
            wst = A_.t([128, 2, 512], F32)
            S.dma("sp", lambda e: e.dma_start(out=wst[0:64], in_=w2d_d.rearrange("d r f -> r d f")), "pbw", writes=["wst"])
            S.op("dve", lambda e: e.tensor_copy(out=w2b[:], in_=wst[0:64]), reads=["wst"], writes=["w2b"])
            S.dma("sp", lambda e: e.dma_start(out=wst[64:128], in_=a2_d.rearrange("d r f -> r d f")), "pbw", reads=["w2b"], writes=["wst2"])
            S.op("dve", lambda e: e.tensor_copy(out=a2b[64:128], in_=wst[64:128]), reads=["wst2"], writes=["a2b"])
            rs = A_.t([128, 4, TT], BF16)
            ks = A_.t([128, 4, TT], BF16)
            vs = A_.t([128, 4, TT], BF16)
            wdad = A_.t([128, 2, TT], BF16)
            f32t = lambda: A_.t([128, 4, W], F32)
            sig, icl, Ls, XE, XI, ee, t1, t2 = [f32t() for _ in range(8)]
            SC = A_.t([128, 4, NCW], F32)
            ar = [A_.t([128, 4, NCW, 2, 64], BF16) for _ in range(4)]
            bkt = [A_.t([128, 4, NCW, 2, 64], BF16) for _ in range(4)]
            prod = [A_.t([128, 4, W], BF16) for _ in range(4)]
            eLC = [A_.t([128, 4, NCW], F32) for _ in range(4)]
            H32 = [A_.t([128, 4, 64], F32) for _ in range(2)]
            Hbf = [A_.t([128, 4, 64], BF16) for _ in range(2)]
            btk = [A_.t([128, 512], BF16) for _ in range(2)]
            vtm = [A_.t([128, 4, 64], BF16) for _ in range(2)]
            AT = [A_.t([128, 4, 2, 128], BF16) for _ in range(2)]
            Pm = [[A_.t([128, 4, 64], BF16) for _ in range(2)] for _ in range(2)]
            Qm = [[A_.t([128, 4, 64], BF16) for _ in range(2)] for _ in range(2)]
            Xm = [[A_.t([128, 4, 64], BF16) for _ in range(2)] for _ in range(2)]
            Rsb = [A_.t([128, 4, 64], BF16) for _ in range(2)]
            Usb = [A_.t([128, 4, 64], BF16) for _ in range(2)]
            ybuf = [A_.t([128, 4, 64], F32) for _ in range(2)]
            bosb = [A_.t([128, 4, 64], F32) for _ in range(2)]
            bon = [A_.t([128, 4], F32) for _ in range(2)]
            K_ = lambda nm, d: "%s%d" % (nm, d)

            def prep(b, d, w0, par):
                dp = d * 2 + par
                for i in range(4):
                    S.op("pe", lambda e, i=i: e.matmul(banks[6 + i // 2][:, (i % 2) * W:(i % 2 + 1) * W],
                                                       lhsT=w2b[:, d, i * 128:(i + 1) * 128], rhs=wdad[0:64, d, w0:w0 + W],
                                                       start=True, stop=True), reads=["w2b", "wdad"], writes=[bk(6 + i // 2)])
                for i in range(4):
                    S.op("act", lambda e, i=i: e.activation(out=sig[:, i, :], in_=banks[6 + i // 2][:, (i % 2) * W:(i % 2 + 1) * W],
                                                            func=AF.Sigmoid, bias=cf("w0")[:, d * 4 + i:d * 4 + i + 1]),
                         reads=["cfm"], writes=[bk(6 + i // 2), "sig"])
                for i in range(4):
                    S.op("pe", lambda e, i=i: e.matmul(banks[6 + i // 2][:, (i % 2) * W:(i % 2 + 1) * W],
                                                       lhsT=a2b[64:128, d, i * 128:(i + 1) * 128], rhs=wdad[64:128, d, w0:w0 + W],
                                                       start=True, stop=True), reads=["a2b", "wdad"], writes=[bk(6 + i // 2)])
                for i in range(4):
                    S.op("act", lambda e, i=i: e.activation(out=icl[:, i, :], in_=banks[6 + i // 2][:, (i % 2) * W:(i % 2 + 1) * W],
                                                            func=AF.Sigmoid, bias=cf("a0")[:, d * 4 + i:d * 4 + i + 1]),
                         reads=["cfm"], writes=[bk(6 + i // 2), "icl"])
                S.op("pool", lambda e: e.tensor_tensor(out=t1[:], in0=ks[:, :, w0:w0 + W],
                                                      in1=cf("k_k").unsqueeze(2).to_broadcast([128, 4, W]), op=ALU.mult),
                     reads=["ks", "cfm"], writes=["t1"])
                yield
                S.op("pool", lambda e: e.tensor_tensor(out=t2[:], in0=t1[:], in1=t1[:], op=ALU.mult), reads=["t1"], writes=["t2"])
                yield
                for i in range(4):
                    S.op("pe", lambda e, i=i: e.matmul(banks[6 + i // 2][:, (i % 2) * W:(i % 2 + 1) * W], lhsT=bones[:],
                                                       rhs=t2[:, i, :], start=True, stop=True),
                         reads=["bones", "t2"], writes=[bk(6 + i // 2)])
                for hb in range(2):
                    S.op("dve", lambda e, hb=hb: e.tensor_scalar(
                        out=ee[:, 2 * hb:2 * hb + 2, :], in0=banks[6 + hb][:, 0:2 * W].rearrange("p (a t) -> p a t", t=W),
                        scalar1=1e-24, scalar2=None, op0=ALU.max), reads=[], writes=[bk(6 + hb), "ee"])
                S.op("act", lambda e: e.activation(out=ee[:], in_=ee[:], func=AF.Sqrt), reads=["ee"], writes=["ee"])
                yield
                S.op("dve", lambda e: e.reciprocal(out=ee[:], in_=ee[:]), reads=["ee"], writes=["ee"])
                yield
                S.op("pool", lambda e: e.tensor_tensor(out=t1[:], in0=t1[:], in1=ee[:], op=ALU.mult), reads=["t1", "ee"], writes=["t1"])
                yield
                for i in range(4):
                    S.op("dve", lambda e, i=i: e.tensor_tensor_scan(out=Ls[:, i, :], data0=rstm[:, 0:W], data1=sig[:, i, :],
                                                                    initial=0.0, op0=ALU.mult, op1=ALU.add),
                         reads=["rstm", "sig"], writes=["Ls"])
                lsc = Ls[:].rearrange("p i (c t) -> p i c t", t=64)
                S.op("dve", lambda e: e.tensor_copy(out=SC[:], in_=lsc[:, :, :, 63]), reads=["Ls"], writes=["SC"])
                yield
                S.op("act", lambda e: e.activation(out=eLC[dp][:], in_=SC[:], func=AF.Exp, scale=-CDEC), reads=["SC"], writes=[K_("eLC", dp)])
                yield
                if d == 0:
                    xi_ = Ls
                    S.op("dve", lambda e: e.tensor_tensor(out=XE[:], in0=Ls[:], in1=sig[:], op=ALU.subtract), reads=["Ls", "sig"], writes=["XE"])
                    xik = "Ls"
                else:
                    S.op("dve", lambda e: e.tensor_tensor(
                        out=XE[:].rearrange("p i (c t) -> p i c t", t=64), in0=SC[:].unsqueeze(3).to_broadcast([128, 4, NCW, 64]),
                        in1=lsc, op=ALU.subtract), reads=["Ls", "SC"], writes=["XE"])
                    S.op("dve", lambda e: e.tensor_tensor(out=XI[:], in0=XE[:], in1=sig[:], op=ALU.add), reads=["XE", "sig"], writes=["XI"])
                    xi_ = XI
                    xik = "XI"
                v5 = lambda tns, a: tns[:].rearrange("p i (c t) -> p i c t", t=64) if a is None else tns[:, :, :, a, :]
                S.op("act", lambda e: e.activation(out=ee[:], in_=XE[:], func=AF.Exp, scale=-CDEC), reads=["XE"], writes=["ee"])
                yield
                S.op("dve", lambda e: e.scalar_tensor_tensor(out=v5(ar[dp], 0), in0=v5(t1, None), scalar=-1.0, in1=v5(ee, None),
                                                             op0=ALU.mult, op1=ALU.mult), reads=["t1", "ee"], writes=[K_("ar", dp)])
                yield
                S.op("act", lambda e: e.activation(out=ee[:], in_=xi_[:], func=AF.Exp, scale=-CDEC), reads=[xik], writes=["ee"])
                yield
                S.op("dve", lambda e: e.tensor_tensor(out=v5(ar[dp], 1), in0=rs[:, :, w0:w0 + W].rearrange("p i (c t) -> p i c t", t=64),
                                                      in1=v5(ee, None), op=ALU.mult), reads=["rs", "ee"], writes=[K_("ar", dp)])
                yield
                S.op("act", lambda e: e.activation(out=ee[:], in_=xi_[:], func=AF.Exp, scale=CDEC), reads=[xik], writes=["ee"])
                yield
                S.op("pool", lambda e: e.tensor_tensor(out=t2[:], in0=t1[:], in1=icl[:], op=ALU.mult), reads=["t1", "icl"], writes=["t2"])
                yield
                S.op("dve", lambda e: e.tensor_tensor(out=v5(bkt[dp], 0), in0=v5(t2, None), in1=v5(ee, None), op=ALU.mult),
                     reads=["t2", "ee"], writes=[K_("bkt", dp)])
                yield
                S.op("pool", lambda e: e.tensor_scalar(out=t2[:], in0=icl[:], scalar1=-1.0, scalar2=None, op0=ALU.add),
                     reads=["icl"], writes=["t2"])
                yield
                S.op("pool", lambda e: e.tensor_tensor(out=t2[:], in0=t2[:], in1=cf("k_a").unsqueeze(2).to_broadcast([128, 4, W]),
                                                      op=ALU.mult), reads=["t2", "cfm"], writes=["t2"])
                yield
                S.op("pool", lambda e: e.tensor_scalar(out=t2[:], in0=t2[:], scalar1=1.0, scalar2=None, op0=ALU.add), reads=["t2"], writes=["t2"])
                yield
                S.op("pool", lambda e: e.tensor_tensor(out=t2[:], in0=t2[:], in1=ks[:, :, w0:w0 + W], op=ALU.mult), reads=["t2", "ks"], writes=["t2"])
                yield
                S.op("dve", lambda e: e.tensor_tensor(out=v5(bkt[dp], 1), in0=v5(t2, None), in1=v5(ee, None), op=ALU.mult),
                     reads=["t2", "ee"], writes=[K_("bkt", dp)])
                yield
                S.op("pool", lambda e: e.tensor_tensor(out=t2[:], in0=t2[:], in1=rs[:, :, w0:w0 + W], op=ALU.mult),
                     reads=["t2", "rs"], writes=["t2"])
                yield
                S.op("pool", lambda e: e.tensor_tensor(out=prod[dp][:], in0=t2[:], in1=cf("r_k").unsqueeze(2).to_broadcast([128, 4, W]),
                                                       op=ALU.mult), reads=["t2", "cfm"], writes=[K_("prod", dp)])
                yield


            def chunk(b, d, w0, c, is_lat, par):
                tk0 = w0 + c * 64
                dp = d * 2 + par
                HP = [((h % 2) * 64, h // 2) for h in range(8)]
                B0, B1, B2 = 3 * d, 3 * d + 1, 3 * d + 2
                B3 = B2
                pT = banks[B0].bitcast(BF16)
                for q in range(2):
                    for (po, i) in HP:
                        S.op("pe", lambda e, q=q, po=po, i=i: e.transpose(
                            pT[po:po + 64, (q * 4 + i) * 64:(q * 4 + i + 1) * 64], bkt[dp][po:po + 64, i, c, q, :],
                            ident[po:po + 64, po:po + 64]), reads=[K_("bkt", dp), "ident"], writes=[bk(B0)])
                for (po, i) in HP:
                    S.op("pe", lambda e, po=po, i=i: e.transpose(pT[po:po + 64, 512 + i * 64:512 + (i + 1) * 64],
                                                                 vs[po:po + 64, i, tk0:tk0 + 64], ident[po:po + 64, po:po + 64]),
                         reads=["vs", "ident"], writes=[bk(B0)])
                S.op("dve", lambda e: e.tensor_copy(out=btk[d][:], in_=pT[:, 0:512]), writes=[bk(B0), K_("btk", d)])
                S.op("act", lambda e: e.activation(out=vtm[d][:].rearrange("p i v -> p (i v)"), in_=pT[:, 512:768], func=AF.Copy),
                     writes=[bk(B0), K_("vtm", d)])
                yield
                btm = btk[d][:, 0:256].rearrange("p (i k) -> p i k", k=64)
                ktm = btk[d][:, 256:512].rearrange("p (i k) -> p i k", k=64)
                for bb in range(2):
                    psA = banks[B1 + bb][:, :].rearrange("p (i r t) -> p i r t", i=2, r=2)
                    for (po, i) in HP:
                        if i // 2 != bb:
                            continue
                        rhs = ar[dp][po:po + 64, i, c, :, :].rearrange("p a t -> p (a t)")
                        for r_ in range(2):
                            S.op("pe", lambda e, r_=r_, po=po, i=i, rhs=rhs, psA=psA: e.matmul(
                                psA[po:po + 64, i % 2, r_, :], lhsT=bkt[dp][po:po + 64, i, c, r_, :], rhs=rhs, start=True, stop=True),
                                reads=[K_("bkt", dp), K_("ar", dp)], writes=[bk(B1 + bb)])
                    S.op("dve", lambda e, bb=bb, psA=psA: e.tensor_tensor(
                        out=AT[d][:, bb * 2:bb * 2 + 2], in0=psA, in1=msk[:, d:d + 1].to_broadcast([128, 2, 2, 128]), op=ALU.mult),
                        reads=["msk"], writes=[bk(B1 + bb), K_("AT", d)])
                V3 = lambda bi: banks[bi][:, 0:256].rearrange("p (i s) -> p i s", s=64)
                yield
                psN = V3(B3)
                for (po, i) in HP:
                    S.op("pe", lambda e, po=po, i=i: e.matmul(psN[po:po + 64, i, :], lhsT=ar[dp][po:po + 64, i, c, 0, :],
                                                               rhs=bkt[dp][po:po + 64, i, c, 0, :], start=True, stop=True),
                         reads=[K_("bkt", dp), K_("ar", dp)], writes=[bk(B3)])
                P, Q, X = Pm[d], Qm[d], Xm[d]
                S.op("dve", lambda e: e.tensor_tensor(out=P[0][:], in0=psN, in1=mskN[:, d:d + 1].to_broadcast([128, 4, 64]), op=ALU.mult),
                     reads=["msk"], writes=[bk(B3), K_("P0", d)])
                S.op("act", lambda e: e.activation(out=Q[0][:], in_=AT[d][:, :, 0, 0:64], func=AF.Copy),
                     reads=[K_("AT", d)], writes=[K_("Q0", d)])
                S.op("dve", lambda e: e.tensor_tensor(out=X[0][:], in0=AT[d][:, :, 0, 0:64],
                                                      in1=identB[:].unsqueeze(1).to_broadcast([128, 4, 64]), op=ALU.add),
                     reads=[K_("AT", d), "identB"], writes=[K_("X0", d)])
                cur = 0
                for j in range(1, 6):
                    nxt = 1 - cur
                    yield
                    psP, psQ, psX = V3(B1), V3(B2), V3(B3)
                    for (po, i) in HP:
                        S.op("pe", lambda e, po=po, i=i, cur=cur, psP=psP: e.matmul(
                            psP[po:po + 64, i, :], lhsT=Q[cur][po:po + 64, i, :], rhs=P[cur][po:po + 64, i, :], start=True, stop=True),
                            reads=[K_("Q%d" % cur, d), K_("P%d" % cur, d)], writes=[bk(B1)])
                    S.op("dve", lambda e, nxt=nxt, psP=psP: e.tensor_copy(out=P[nxt][:], in_=psP), writes=[bk(B1), K_("P%d" % nxt, d)])
                    if j < 5:
                        for (po, i) in HP:
                            S.op("pe", lambda e, po=po, i=i, cur=cur, psQ=psQ: e.matmul(
                                psQ[po:po + 64, i, :], lhsT=P[cur][po:po + 64, i, :], rhs=Q[cur][po:po + 64, i, :], start=True, stop=True),
                                reads=[K_("Q%d" % cur, d), K_("P%d" % cur, d)], writes=[bk(B2)])
                        S.op("act", lambda e, nxt=nxt, psQ=psQ: e.activation(out=Q[nxt][:], in_=psQ, func=AF.Copy),
                             writes=[bk(B2), K_("Q%d" % nxt, d)])
                    yield
                    for (po, i) in HP:
                        S.op("pe", lambda e, po=po, i=i, cur=cur, nxt=nxt, psX=psX: e.matmul(
                            psX[po:po + 64, i, :], lhsT=P[nxt][po:po + 64, i, :], rhs=X[cur][po:po + 64, i, :], start=True, stop=True),
                            reads=[K_("P%d" % nxt, d), K_("X%d" % cur, d)], writes=[bk(B3)])
                    S.op("dve", lambda e, cur=cur, nxt=nxt, psX=psX: e.tensor_tensor(out=X[nxt][:], in0=psX, in1=X[cur][:], op=ALU.add),
                         reads=[K_("X%d" % cur, d)], writes=[bk(B3), K_("X%d" % nxt, d)])
                    cur = nxt
                XT = X[cur]
                xk = K_("X%d" % cur, d)
                yield
                psR = V3(B0)
                for (po, i) in HP:
                    S.op("pe", lambda e, po=po, i=i: e.matmul(psR[po:po + 64, i, :], lhsT=ar[dp][po:po + 64, i, c, 0, :],
                                                               rhs=Hbf[d][po:po + 64, i, :], start=True, stop=False),
                         reads=[K_("ar", dp), K_("Hbf", d)], writes=[bk(B0)])
                    S.op("pe", lambda e, po=po, i=i: e.matmul(psR[po:po + 64, i, :], lhsT=AT[d][po:po + 64, i, 1, 0:64],
                                                               rhs=vtm[d][po:po + 64, i, :], start=False, stop=True),
                         reads=[K_("AT", d), K_("vtm", d)], writes=[bk(B0)])
                S.op("act", lambda e: e.activation(out=Rsb[d][:], in_=psR, func=AF.Copy), writes=[bk(B0), K_("Rsb", d)])
                yield
                for (po, i) in HP:
                    S.op("pe", lambda e, po=po, i=i: e.matmul(psR[po:po + 64, i, :], lhsT=XT[po:po + 64, i, :],
                                                               rhs=Rsb[d][po:po + 64, i, :], start=True, stop=True),
                         reads=[xk, K_("Rsb", d)], writes=[bk(B0)])
                S.op("dve", lambda e: e.tensor_copy(out=Usb[d][:], in_=psR), writes=[bk(B0), K_("Usb", d)])
                yield
                if is_lat:
                    psY = V3(B3)
                    for (po, i) in HP:
                        S.op("pe", lambda e, po=po, i=i: e.matmul(psY[po:po + 64, i, :], lhsT=ar[dp][po:po + 64, i, c, 1, :],
                                                                   rhs=Hbf[d][po:po + 64, i, :], start=True, stop=False),
                             reads=[K_("ar", dp), K_("Hbf", d)], writes=[bk(B3)])
                        S.op("pe", lambda e, po=po, i=i: e.matmul(psY[po:po + 64, i, :], lhsT=AT[d][po:po + 64, i, 0, 64:128],
                                                                   rhs=Usb[d][po:po + 64, i, :], start=False, stop=False),
                             reads=[K_("AT", d), K_("Usb", d)], writes=[bk(B3)])
                        S.op("pe", lambda e, po=po, i=i: e.matmul(psY[po:po + 64, i, :], lhsT=AT[d][po:po + 64, i, 1, 64:128],
                                                                   rhs=vtm[d][po:po + 64, i, :], start=False, stop=True),
                             reads=[K_("AT", d), K_("vtm", d)], writes=[bk(B3)])
                    S.op("act", lambda e: e.activation(out=ybuf[d][:], in_=psY, func=AF.Copy), writes=[bk(B3), K_("ybuf", d)])
                    tl = tk0 - TC
                    for hp_ in range(2):
                        S.dma("sp", lambda e, tl=tl, hp_=hp_: e.dma_start(
                            out=y_d[b, d, tl:tl + 64, :].rearrange("t (i hp v) -> t i hp v", hp=2, v=64)[:, :, hp_, :],
                            in_=ybuf[d][hp_ * 64:(hp_ + 1) * 64]), "y_st", reads=[K_("ybuf", d)])
                    psB = banks[B1]
                    for (po, i) in HP:
                        S.op("pe", lambda e, po=po, i=i: e.matmul(psB[po:po + 64, i:i + 1], lhsT=prod[dp][po:po + 64, i, c * 64:(c + 1) * 64],
                                                                   rhs=onesb[po:po + 64, 0:1], start=True, stop=True),
                             reads=[K_("prod", dp), "onesb"], writes=[bk(B1)])
                    S.op("dve", lambda e: e.tensor_scalar(out=bon[d][:], in0=psB[:, 0:4], scalar1=0.5, scalar2=None, op0=ALU.mult), writes=[bk(B1), K_("bon", d)])
                    S.op("dve", lambda e: e.tensor_tensor(out=bosb[d][:], in0=vtm[d][:],
                                                          in1=bon[d][:].unsqueeze(2).to_broadcast([128, 4, 64]), op=ALU.mult),
                         reads=[K_("vtm", d), K_("bon", d)], writes=[K_("bosb", d)])
                    for hp_ in range(2):
                        S.dma("sp", lambda e, tl=tl, hp_=hp_: e.dma_start(
                            out=bo_d[b, d, tl:tl + 64, :].rearrange("t (i hp v) -> t i hp v", hp=2, v=64)[:, :, hp_, :],
                            in_=bosb[d][hp_ * 64:(hp_ + 1) * 64]), "bo_st", reads=[K_("bosb", d)])
                psH = V3(B0)
                for (po, i) in HP:
                    S.op("pe", lambda e, po=po, i=i: e.matmul(psH[po:po + 64, i, :], lhsT=btm[po:po + 64, i, :],
                                                               rhs=Usb[d][po:po + 64, i, :], start=True, stop=False),
                         reads=[K_("btk", d), K_("Usb", d)], writes=[bk(B0)])
                    S.op("pe", lambda e, po=po, i=i: e.matmul(psH[po:po + 64, i, :], lhsT=ktm[po:po + 64, i, :],
                                                               rhs=vtm[d][po:po + 64, i, :], start=False, stop=True),
                         reads=[K_("btk", d), K_("vtm", d)], writes=[bk(B0)])
                S.op("dve", lambda e: e.tensor_tensor(out=H32[d][:], in0=psH, in1=H32[d][:], op=ALU.add),
                     reads=[K_("H32", d)], writes=[bk(B0), K_("H32", d)])
                S.op("dve", lambda e: e.tensor_tensor(out=H32[d][:], in0=H32[d][:],
                                                      in1=eLC[dp][:, :, c:c + 1].to_broadcast([128, 4, 64]), op=ALU.mult),
                     reads=[K_("H32", d), K_("eLC", dp)], writes=[K_("H32", d)])
                S.op("act", lambda e: e.activation(out=Hbf[d][:], in_=H32[d][:], func=AF.Copy), reads=[K_("H32", d)], writes=[K_("Hbf", d)])

            for b in range(NB):
                for (dst, j0, key) in ((rs, 0, "rs"), (ks, 4, "ks"), (vs, 8, "vs")):
                    S.dma("sp", lambda e, dst=dst, j0=j0, b=b: e.dma_start(out=dst[:], in_=pa_d[b, j0:j0 + 4].rearrange("j p t -> p j t")),
                          "pb_ld", writes=[key])
                S.dma("sp", lambda e, b=b: e.dma_start(out=wdad[:], in_=pa_d[b, 12:14].rearrange("j p t -> p j t")), "pb_ld", writes=["wdad"])
                S.op("act", lambda e: e.activation(out=wdad[0:64], in_=wdad[0:64], func=AF.Tanh), reads=["wdad"], writes=["wdad"])
                for d in range(2):
                    S.op("pool", lambda e, d=d: e.memset(H32[d][:], 0.0), writes=[K_("H32", d)])
                    S.op("pool", lambda e, d=d: e.memset(Hbf[d][:], 0.0), writes=[K_("Hbf", d)])
                nwl = T // W
                ncw_c = TC // 64
                sched = {0: [(0, ncw_c, False)] + [(TC + w * W, NCW, True) for w in range(nwl)],
                         1: [(0, ncw_c, False)] + [(TC + w * W, NCW, True) for w in reversed(range(nwl))]}
                def lockstep(gens):
                    gens = list(gens)
                    while gens:
                        for g_ in list(gens):
                            try:
                                next(g_)
                            except StopIteration:
                                gens.remove(g_)

                def chunks_of_window(wi):
                    par = wi % 2
                    for cc in range(NCW):
                        gens = []
                        for d in range(2):
                            w0_, ncw_, lat_ = sched[d][wi]
                            if cc >= ncw_:
                                continue
                            c = cc if d == 0 else ncw_ - 1 - cc
                            gens.append(chunk(b, d, w0_, c, lat_, par))
                        while gens:
                            for g_ in list(gens):
                                try:
                                    next(g_)
                                except StopIteration:
                                    gens.remove(g_)
                            yield

                nw_ = len(sched[0])
                def preps(wi):
                    for d in range(2):
                        yield from prep(b, d, sched[d][wi][0], wi % 2)

                lockstep([preps(0)])
                for wi in range(nw_):
                    gl = [chunks_of_window(wi)]
                    if wi + 1 < nw_:
                        gl.append(preps(wi + 1))
                    lockstep(gl)

        def post_norm_residual(pbs, gpost, xres, xkey, outt, okey, tmp):
            for hh in range(2):
                S.op("act", lambda e, hh=hh: e.activation(out=tmp["sq"][:, hh * 512:(hh + 1) * 512], in_=banks[pbs[hh]][:, :], func=AF.Square),
                     writes=[bk(pbs[hh]), "pn_sq"])
            S.op("dve", lambda e: e.reduce_sum(out=tmp["ss"][:], in_=tmp["sq"][:], axis=AX.X), reads=["pn_sq"], writes=["pn_ss"])
            S.op("dve", lambda e: e.tensor_scalar(out=tmp["ss"][:], in0=tmp["ss"][:], scalar1=1.0 / D, scalar2=1e-6,
                                                  op0=ALU.mult, op1=ALU.add), reads=["pn_ss"], writes=["pn_ss"])
            S.op("act", lambda e: e.activation(out=tmp["ss"][:], in_=tmp["ss"][:], func=AF.Sqrt), reads=["pn_ss"], writes=["pn_ss"])
            S.op("dve", lambda e: e.reciprocal(out=tmp["ss"][:], in_=tmp["ss"][:]), reads=["pn_ss"], writes=["pn_ss"])
            for hh in range(2):
                S.op("dve", lambda e, hh=hh: e.scalar_tensor_tensor(
                    out=tmp["sq"][:, hh * 512:(hh + 1) * 512], in0=banks[pbs[hh]][:, :], scalar=tmp["ss"][:, 0:1],
                    in1=gpost[:, hh * 512:(hh + 1) * 512], op0=ALU.mult, op1=ALU.mult),
                    reads=["pn_ss", "gpost"], writes=[bk(pbs[hh]), "pn_sq"])
            S.op("pool", lambda e: e.tensor_tensor(out=outt[:], in0=tmp["sq"][:], in1=xres, op=ALU.add), reads=["pn_sq", xkey], writes=[okey])

        def make_gpost(b, gi, rowname, gpost):
            for hh in range(2):
                S.op("pe", lambda e, hh=hh: e.matmul(banks[hh][:, :], lhsT=sel[:, b, :], rhs=gates[:, gi, hh * 512:(hh + 1) * 512],
                                                     start=True, stop=True), reads=["sel", "gates"], writes=[bk(hh)])
                S.op("dve", lambda e, hh=hh: e.tensor_tensor(out=gpost[:, hh * 512:(hh + 1) * 512], in0=banks[hh][:, :],
                                                             in1=cr(rowname)[:, hh * 512:(hh + 1) * 512], op=ALU.mult),
                     reads=["crow"], writes=[bk(hh), "gpost"])

        if PHASES >= 3:
            S.fence()
            A_.reset()
            woutb = A_.t([128, DK, D], BF16)
            gw2b = A_.t([128, 2, 512], BF16)
            stg = [A_.t([128, DK, 512], F32) for _ in range(2)]
            for n in range(2):
                S.dma("sp", lambda e, n=n: e.dma_start(out=stg[n][:], in_=wout_d[:, n * 512:(n + 1) * 512].rearrange("(k p) n -> p k n", p=128)),
                      "stgd%d" % n, writes=["stgd%d" % n])
                S.op("dve", lambda e, n=n: e.tensor_copy(out=woutb[:, :, n * 512:(n + 1) * 512], in_=stg[n][:]), reads=["stgd%d" % n], writes=["woutb"])
            gst = A_.t([128, 2, 512], F32)
            S.dma("sp", lambda e: e.dma_start(out=gst[:, 0, :], in_=gw2_d[0:128, :]), "gst", writes=["gst"])
            S.dma("sp", lambda e: e.dma_start(out=gst[0:32, 1, :], in_=gw2_d[128:160, :]), "gst", writes=["gst"])
            S.op("dve", lambda e: e.tensor_copy(out=gw2b[:, 0, :], in_=gst[:, 0, :]), reads=["gst"], writes=["gw2b"])
            S.op("dve", lambda e: e.tensor_copy(out=gw2b[0:32, 1, :], in_=gst[0:32, 1, :]), reads=["gst"], writes=["gw2b"])
            S.fence()
            A_.off -= 2 * 16384 + 4096
            onesf = A_.t([128, 128], F32)
            S.op("pool", lambda e: e.memset(onesf[:], 1.0), writes=["onesf"])
            ub = A_.t([128, 4, T], BF16)
            yc = A_.t([128, 4, T], F32)
            convo = A_.t([128, 4, T], BF16)
            gdb = A_.t([128, 2, T], BF16)
            lsq = A_.t([128, 4, 512], F32)
            lmean = A_.t([128, 512], F32)
            lrstd = A_.t([128, 512], F32)
            ltmp = A_.t([128, 512], F32)
            yin = [[A_.t([128, 512], F32) for _ in range(4)] for _ in range(2)]
            ysq = A_.t([128, 512], F32)
            st8 = A_.t([128, 4, 8], F32)
            rwb = A_.t([128, 512], BF16)
            mixT = A_.t([128, 4, 128], BF16)
            xt2 = [A_.t([128, D], F32) for _ in range(2)]
            gpost = A_.t([128, D], F32)
            pn_tmp = {"sq": A_.t([128, D], F32), "ss": A_.t([128, 1], F32)}
            cw = cf("conv_w")
            ti_ = [0]

            def _pd_batch(b):
                make_gpost(b, 0, "mix_post_g", gpost)
                S.dma("sp", lambda e, b=b: e.dma_start(out=ub[:], in_=pa_d[b, 16:20, :, TC:TT].rearrange("j p t -> p j t")), "pd_ld", writes=["ub"])
                S.dma("sp", lambda e, b=b: e.dma_start(out=gdb[:], in_=pa_d[b, 14:16, :, TC:TT].rearrange("j p t -> p j t")), "pd_ld2", writes=["gdb"])
                S.op("act", lambda e: e.activation(out=gdb[:, 0, :], in_=gdb[:, 0, :], func=AF.Sigmoid), reads=["gdb"], writes=["gdb"])
                S.op("act", lambda e: e.activation(out=gdb[0:32, 1, :], in_=gdb[0:32, 1, :], func=AF.Sigmoid), reads=["gdb"], writes=["gdb"])
                for i in range(4):
                    u4 = ub[:, i, :].rearrange("p (r t) -> p r t", t=64)
                    y4 = yc[:, i, :].rearrange("p (r t) -> p r t", t=64)
                    S.op("dve", lambda e, i=i: e.tensor_scalar(out=yc[:, i, :], in0=ub[:, i, :], scalar1=cw[:, i * 31 + 15:i * 31 + 16],
                                                               scalar2=cf("conv_b")[:, i:i + 1], op0=ALU.mult, op1=ALU.add),
                         reads=["ub", "cfm"], writes=["yc%d" % i])
                    for j in range(31):
                        o = j - 15
                        if o == 0:
                            continue
                        lo_o, hi_o = max(0, -o), 64 - max(0, o)
                        lo_i, hi_i = max(0, o), 64 - max(0, -o)
                        S.op("dve", lambda e, i=i, j=j, u4=u4, y4=y4, lo_o=lo_o, hi_o=hi_o, lo_i=lo_i, hi_i=hi_i: e.scalar_tensor_tensor(
                            out=y4[:, :, lo_o:hi_o], in0=u4[:, :, lo_i:hi_i], scalar=cw[:, i * 31 + j:i * 31 + j + 1],
                            in1=y4[:, :, lo_o:hi_o], op0=ALU.mult, op1=ALU.add), reads=["ub", "cfm", "yc%d" % i], writes=["yc%d" % i])
                for w in range(T // 512 if T >= 512 else 1):
                    wn = min(512, T)
                    ws = slice(w * 512, w * 512 + wn)
                    for i in range(4):
                        S.op("pe", lambda e, i=i, ws=ws, wn=wn: e.matmul(banks[2][:, 0:wn], lhsT=onesf[:], rhs=yc[:, i, ws], start=(i == 0), stop=(i == 3)),
                             reads=["onesf", "yc%d" % i], writes=[bk(2)])
                    S.op("act", lambda e, ws=ws, wn=wn: e.activation(out=lsq[:, :, 0:wn], in_=yc[:, :, ws], func=AF.Square),
                         reads=["yc0", "yc1", "yc2", "yc3"], writes=["lsq"])
                    for i in range(4):
                        S.op("pe", lambda e, i=i, wn=wn: e.matmul(banks[3][:, 0:wn], lhsT=onesf[:], rhs=lsq[:, i, 0:wn], start=(i == 0), stop=(i == 3)),
                             reads=["onesf", "lsq"], writes=[bk(3)])
                    S.op("dve", lambda e, wn=wn: e.tensor_scalar(out=lmean[:, 0:wn], in0=banks[2][:, 0:wn], scalar1=1.0 / 512, scalar2=None, op0=ALU.mult),
                         writes=[bk(2), "lmean"])
                    S.op("dve", lambda e, wn=wn: e.tensor_tensor(out=ltmp[:, 0:wn], in0=lmean[:, 0:wn], in1=lmean[:, 0:wn], op=ALU.mult),
                         reads=["lmean"], writes=["ltmp"])
                    S.op("dve", lambda e, wn=wn: e.scalar_tensor_tensor(out=lrstd[:, 0:wn], in0=banks[3][:, 0:wn], scalar=1.0 / 512, in1=ltmp[:, 0:wn],
                                                                        op0=ALU.mult, op1=ALU.subtract), reads=["ltmp"], writes=[bk(3), "lrstd"])
                    S.op("dve", lambda e, wn=wn: e.tensor_scalar(out=lrstd[:, 0:wn], in0=lrstd[:, 0:wn], scalar1=1e-5, scalar2=None, op0=ALU.add),
                         reads=["lrstd"], writes=["lrstd"])
                    S.op("act", lambda e, wn=wn: e.activation(out=lrstd[:, 0:wn], in_=lrstd[:, 0:wn], func=AF.Sqrt), reads=["lrstd"], writes=["lrstd"])
                    S.op("dve", lambda e, wn=wn: e.reciprocal(out=lrstd[:, 0:wn], in_=lrstd[:, 0:wn]), reads=["lrstd"], writes=["lrstd"])
                    S.op("dve", lambda e, ws=ws, wn=wn: e.tensor_tensor(out=lsq[:, :, 0:wn], in0=yc[:, :, ws],
                                                                        in1=lmean[:, 0:wn].unsqueeze(1).to_broadcast([128, 4, wn]), op=ALU.subtract),
                         reads=["yc0", "yc1", "yc2", "yc3", "lmean"], writes=["lsq"])
                    S.op("dve", lambda e, wn=wn: e.tensor_tensor(out=lsq[:, :, 0:wn], in0=lsq[:, :, 0:wn],
                                                                 in1=lrstd[:, 0:wn].unsqueeze(1).to_broadcast([128, 4, wn]), op=ALU.mult),
                         reads=["lsq", "lrstd"], writes=["lsq"])
                    for i in range(4):
                        S.op("act", lambda e, i=i, ws=ws, wn=wn: e.activation(out=convo[:, i, ws], in_=lsq[:, i, 0:wn], func=AF.Silu,
                                                                              scale=cf("cln_w")[:, i:i + 1], bias=cf("cln_b")[:, i:i + 1]),
                             reads=["lsq", "cfm"], writes=["convo"])
                for tt in range(T // 128):
                    _pd_tile(b, tt)

            def _pd_tile(b, tt):
                if True:
                    tsl = slice(tt * 128, (tt + 1) * 128)
                    ti = ti_[0]
                    yy = yin[ti % 2]
                    yk = "yin%d" % (ti % 2)
                    xa, xk2 = xt2[ti % 2], "xt2_%d" % (ti % 2)
                    ti_[0] += 1
                    for q, src in enumerate((y_d, y_d, bo_d, bo_d)):
                        S.dma("sp", lambda e, q=q, src=src, yy=yy, tsl=tsl: e.dma_start(out=yy[q][:], in_=src[b, q % 2, tsl, :]), yk, writes=[yk])
                    S.dma("sp", lambda e, xa=xa, tsl=tsl: e.dma_start(out=xa[:], in_=x_d[b, tsl, :]), xk2, writes=[xk2])
                    S.op("pool", lambda e, yy=yy: e.tensor_tensor(out=yy[0][:], in0=yy[0][:], in1=yy[1][:], op=ALU.add), reads=[yk], writes=[yk])
                    S.op("pool", lambda e, yy=yy: e.tensor_tensor(out=yy[2][:], in0=yy[2][:], in1=yy[3][:], op=ALU.add), reads=[yk], writes=[yk])
                    y3 = yy[0][:].rearrange("p (h v) -> p h v", v=64)
                    S.op("dve", lambda e, y3=y3: e.reduce_sum(out=st8[:, 0, :], in_=y3, axis=AX.X), reads=[yk], writes=["st8"])
                    S.op("act", lambda e, yy=yy: e.activation(out=ysq[:], in_=yy[0][:], func=AF.Square), reads=[yk], writes=["ysq"])
                    S.op("dve", lambda e: e.reduce_sum(out=st8[:, 1, :], in_=ysq[:].rearrange("p (h v) -> p h v", v=64), axis=AX.X),
                         reads=["ysq"], writes=["st8"])
                    S.op("dve", lambda e: e.tensor_scalar(out=st8[:, 0:2, :], in0=st8[:, 0:2, :], scalar1=1.0 / 64, scalar2=None, op0=ALU.mult),
                         reads=["st8"], writes=["st8"])
                    S.op("dve", lambda e: e.tensor_tensor(out=st8[:, 2, :], in0=st8[:, 0, :], in1=st8[:, 0, :], op=ALU.mult), reads=["st8"], writes=["st8"])
                    S.op("dve", lambda e: e.tensor_tensor(out=st8[:, 3, :], in0=st8[:, 1, :], in1=st8[:, 2, :], op=ALU.subtract), reads=["st8"], writes=["st8"])
                    S.op("dve", lambda e: e.tensor_scalar(out=st8[:, 3, :], in0=st8[:, 3, :], scalar1=64e-5, scalar2=None, op0=ALU.add),
                         reads=["st8"], writes=["st8"])
                    S.op("act", lambda e: e.activation(out=st8[:, 3, :], in_=st8[:, 3, :], func=AF.Sqrt), reads=["st8"], writes=["st8"])
                    S.op("dve", lambda e: e.reciprocal(out=st8[:, 3, :], in_=st8[:, 3, :]), reads=["st8"], writes=["st8"])
                    S.op("dve", lambda e, y3=y3: e.tensor_tensor(out=y3, in0=y3, in1=st8[:, 0, :].unsqueeze(2).to_broadcast([128, 8, 64]), op=ALU.subtract),
                         reads=[yk, "st8"], writes=[yk])
                    S.op("dve", lambda e, y3=y3: e.tensor_tensor(out=y3, in0=y3, in1=st8[:, 3, :].unsqueeze(2).to_broadcast([128, 8, 64]), op=ALU.mult),
                         reads=[yk, "st8"], writes=[yk])
                    S.op("pool", lambda e, yy=yy: e.tensor_tensor(out=yy[0][:], in0=yy[0][:], in1=cr("lnx_w"), op=ALU.mult), reads=[yk, "crow"], writes=[yk])
                    S.op("pool", lambda e, yy=yy: e.tensor_tensor(out=yy[0][:], in0=yy[0][:], in1=cr("lnx_b"), op=ALU.add), reads=[yk, "crow"], writes=[yk])
                    S.op("pool", lambda e, yy=yy: e.tensor_tensor(out=yy[0][:], in0=yy[0][:], in1=yy[2][:], op=ALU.add), reads=[yk], writes=[yk])
                    S.op("pe", lambda e, tsl=tsl: e.matmul(banks[4][:, :], lhsT=gdb[:, 0, tsl], rhs=gw2b[:, 0, :], start=True, stop=False),
                         reads=["gdb", "gw2b"], writes=[bk(4)])
                    S.op("pe", lambda e, tsl=tsl: e.matmul(banks[4][:, :], lhsT=gdb[0:32, 1, tsl], rhs=gw2b[0:32, 1, :], start=False, stop=True),
                         reads=["gdb", "gw2b"], writes=[bk(4)])
                    S.op("dve", lambda e, yy=yy: e.tensor_tensor(out=rwb[:], in0=yy[0][:], in1=banks[4][:, :], op=ALU.mult),
                         reads=[yk], writes=[bk(4), "rwb"])
                    pT = banks[5].bitcast(BF16)
                    for j in range(4):
                        S.op("pe", lambda e, j=j: e.transpose(pT[:, j * 128:(j + 1) * 128], rwb[:, j * 128:(j + 1) * 128], ident[:]),
                             reads=["rwb", "ident"], writes=[bk(5)])
                    S.op("act", lambda e: e.activation(out=mixT[:].rearrange("p j t -> p (j t)"), in_=pT[:, 0:512], func=AF.Copy),
                         writes=[bk(5), "mixT"])
                    for hh in range(2):
                        for j in range(8):
                            lhs = mixT[:, j, :] if j < 4 else convo[:, j - 4, tsl]
                            S.op("pe", lambda e, hh=hh, j=j, lhs=lhs: e.matmul(banks[6 + hh][:, :], lhsT=lhs, rhs=woutb[:, j, hh * 512:(hh + 1) * 512],
                                                                               start=(j == 0), stop=(j == 7)),
                                 reads=["mixT", "convo", "woutb"], writes=[bk(6 + hh)])
                    post_norm_residual((6, 7), gpost, xa[:], xk2, xa, xk2, pn_tmp)
                    S.dma("sp", lambda e, xa=xa, tsl=tsl: e.dma_start(out=out_d[b, tsl, :], in_=xa[:]), "x1_st", reads=[xk2])

            for b in range(NB):
                _pd_batch(b)

        if PHASES >= 4:
            S.fence()
            A_.reset()
            w1b = A_.t([128, DK, 4096], BF16)
            w2b_ = A_.t([128, 32, D], BF16)
            mark = A_.off
            stg = [A_.t([128, DK, 512], F32) for _ in range(2)]
            for n in range(8):
                S.dma("sp", lambda e, n=n: e.dma_start(out=stg[n % 2][:], in_=w1_d[:, n * 512:(n + 1) * 512].rearrange("(k p) n -> p k n", p=128)),
                      "stge%d" % (n % 2), writes=["stge%d" % (n % 2)])
                if n % 2 == 0:
                    S.op("dve", lambda e, n=n: e.tensor_copy(out=w1b[:, :, n * 512:(n + 1) * 512], in_=stg[n % 2][:]), reads=["stge%d" % (n % 2)], writes=["w1b"])
                else:
                    S.op("act", lambda e, n=n: e.activation(out=w1b[:, :, n * 512:(n + 1) * 512], in_=stg[n % 2][:], func=AF.Copy),
                         reads=["stge%d" % (n % 2)], writes=["w1b"])
            for n in range(8):
                S.dma("sp", lambda e, n=n: e.dma_start(out=stg[n % 2][:].rearrange("p k n -> p (k n)").rearrange("p (f n) -> p f n", n=D),
                                                       in_=w2_d[n * 512:(n + 1) * 512, :].rearrange("(f p) n -> p f n", p=128)),
                      "stge%d" % (n % 2), writes=["stge%d" % (n % 2)])
                src = stg[n % 2][:].rearrange("p k n -> p (k n)").rearrange("p (f n) -> p f n", n=D)
                if n % 2 == 0:
                    S.op("dve", lambda e, n=n, src=src: e.tensor_copy(out=w2b_[:, n * 4:(n + 1) * 4, :], in_=src), reads=["stge%d" % (n % 2)], writes=["w2b_"])
                else:
                    S.op("act", lambda e, n=n, src=src: e.activation(out=w2b_[:, n * 4:(n + 1) * 4, :], in_=src, func=AF.Copy),
                         reads=["stge%d" % (n % 2)], writes=["w2b_"])
            S.fence()
            A_.off = mark
            G = 256
            hT2 = A_.t([128, DK, G], BF16)
            hid = A_.t([128, 32, G], BF16)
            rl = [A_.t([128, 512], F32) for _ in range(2)]
            x1g = [A_.t([128, D], F32) for _ in range(2)]
            gpost2 = A_.t([128, D], F32)
            ftmp2 = {"sq": A_.t([128, D], F32), "ss": A_.t([128, 1], F32), "xn": A_.t([128, D], BF16)}
            pn_tmp2 = {"sq": ftmp2["sq"], "ss": A_.t([128, 1], F32)}
            def _pe_group(b, g):
                if True:
                    for tq in range(G // 128):
                        tsl = slice(g * G + tq * 128, g * G + (tq + 1) * 128)
                        S.dma("sp", lambda e, tq=tq, tsl=tsl: e.dma_start(out=x1g[tq][:], in_=out_d[b, tsl, :]), "x1g%d" % tq, writes=["x1g%d" % tq])
                        front(x1g[tq][:], "x1g%d" % tq, hT2[:, :, tq * 128:(tq + 1) * 128], "hT2", 1, 3, b, ftmp2, 2)
                    for f2 in range(16):
                        pbk = 3 + (f2 % 2)
                        for ff in range(2):
                            f = f2 * 2 + ff
                            for k in range(DK):
                                S.op("pe", lambda e, pbk=pbk, ff=ff, f=f, k=k: e.matmul(
                                    banks[pbk][:, ff * G:(ff + 1) * G], lhsT=w1b[:, k, f * 128:(f + 1) * 128], rhs=hT2[:, k, :],
                                    start=(k == 0), stop=(k == DK - 1)), reads=["w1b", "hT2"], writes=[bk(pbk)])
                        S.op("act", lambda e, pbk=pbk, f2=f2: e.activation(out=rl[f2 % 2][:], in_=banks[pbk][:, :], func=AF.Relu),
                             writes=[bk(pbk), "rl%d" % (f2 % 2)])
                        S.op("dve" if f2 % 2 == 0 else "pool", lambda e, f2=f2: e.tensor_tensor(
                            out=hid[:, f2 * 2:f2 * 2 + 2, :], in0=rl[f2 % 2][:].rearrange("p (a t) -> p a t", a=2),
                            in1=rl[f2 % 2][:].rearrange("p (a t) -> p a t", a=2), op=ALU.mult), reads=["rl%d" % (f2 % 2)], writes=["hid"])
                    for tq in range(G // 128):
                        tsl = slice(g * G + tq * 128, g * G + (tq + 1) * 128)
                        for hh in range(2):
                            for f in range(32):
                                S.op("pe", lambda e, hh=hh, f=f, tq=tq: e.matmul(banks[5 + hh][:, :], lhsT=hid[:, f, tq * 128:(tq + 1) * 128],
                                                                                 rhs=w2b_[:, f, hh * 512:(hh + 1) * 512], start=(f == 0), stop=(f == 31)),
                                     reads=["hid", "w2b_"], writes=[bk(5 + hh)])
                        post_norm_residual((5, 6), gpost2, x1g[tq][:], "x1g%d" % tq, x1g[tq], "x1g%d" % tq, pn_tmp2)
                        S.dma("sp", lambda e, tq=tq, tsl=tsl: e.dma_start(out=out_d[b, tsl, :], in_=x1g[tq][:]), "out_st", reads=["x1g%d" % tq])

            for b in range(NB):
                make_gpost(b, 1, "mlp_post_g", gpost2)
                for g in range(T // G):
                    _pe_group(b, g)
        S.run(nc, es)
    return nc


def _perm_cols():
    idx = list(range(0, 1536))
    idx += list(range(1536, 1600)) + list(range(1664, 1728))
    idx += list(range(1600, 1664)) + list(range(1728, 1792))
    idx += list(range(1792, 1952)) + [-1] * 96
    idx += list(range(1952, 2976))
    return np.array(idx)


def prep_inputs(inp, NB, ncores):
    f = lambda a: np.ascontiguousarray(np.asarray(a, dtype=np.float32))
    perm = _perm_cols()
    w_in = f(inp["w_in"])[0]
    w_in_p = np.zeros((D, 3072), np.float32)
    w_in_p[:, perm >= 0] = w_in[:, perm[perm >= 0]]

    def permvec(v):
        o = np.zeros(2048, np.float32)
        p16 = perm[:2048]
        o[p16 >= 0] = v[p16[p16 >= 0]]
        return o.reshape(16, 128).T

    fm = lambda v: np.asarray(v, np.float32).reshape(-1, 128).T
    cfm = np.zeros((128, NCF), np.float32)

    def put(nm, arr):
        o, w = CF[nm]
        assert arr.shape == (128, w), (nm, arr.shape)
        cfm[:, o:o + w] = arr
    put("ada_b", fm(inp["ada_b"][0]))
    put("mix_pre_g", fm(inp["mix_pre_g"][0]))
    put("mlp_pre_g", fm(inp["mlp_pre_g"][0]))
    put("mu_prev", permvec(f(inp["mu_prev"])[0]))
    put("mu_next", permvec(f(inp["mu_next"])[0]))
    put("w0", np.concatenate([fm(inp["decay_w0"][0, 0]), fm(inp["decay_w0"][0, 1])], 1))
    put("a0", np.concatenate([fm(inp["iclr_a0"][0, 0]), fm(inp["iclr_a0"][0, 1])], 1))
    put("k_k", fm(inp["k_k"][0]))
    put("k_a", fm(inp["k_a"][0]))
    put("r_k", fm(np.asarray(inp["r_k"][0]).reshape(-1)))
    cw = f(inp["conv_w"])[0]
    put("conv_w", cw.T.reshape(4, 128, 31).transpose(1, 0, 2).reshape(128, 124))
    put("conv_b", fm(inp["conv_b"][0]))
    put("cln_w", fm(inp["conv_ln_w"][0]))
    put("cln_b", fm(inp["conv_ln_b"][0]))
    crow = np.zeros((NCR,), np.float32)
    for nm, v in (("ada_b", inp["ada_b"][0]), ("mix_post_g", inp["mix_post_g"][0]),
                  ("mlp_post_g", inp["mlp_post_g"][0]), ("lnx_w", inp["lnx_w"][0]), ("lnx_b", inp["lnx_b"][0])):
        o, w = CR[nm]
        crow[o:o + w] = np.asarray(v, np.float32)
    crow = np.ascontiguousarray(np.broadcast_to(crow[None, :], (128, NCR)))
    x = f(inp["x"])
    ctx = f(inp["ctx"])
    c = f(inp["c"])
    cc = f(inp["c_ctx"])
    shared = {"ada_w": f(inp["ada_w"])[0], "w_in": w_in_p, "cfm": cfm, "crow": crow,
              "decay_w2": f(inp["decay_w2"])[0], "iclr_a2": f(inp["iclr_a2"])[0], "gate_w2": f(inp["gate_w2"])[0],
              "w_out": f(inp["w_out"])[0], "mlp_w1": f(inp["mlp_w1"])[0], "mlp_w2": f(inp["mlp_w2"])[0]}
    maps = []
    for i in range(ncores):
        cb = np.concatenate([c[i * NB:(i + 1) * NB], cc[None, :]], 0)
        cT = np.ascontiguousarray(cb.T.reshape(DK, 128, NB + 1).transpose(1, 0, 2))
        m = dict(shared)
        m.update({"x": np.ascontiguousarray(x[i * NB:(i + 1) * NB]), "ctx": np.ascontiguousarray(ctx[i * NB:(i + 1) * NB]),
                  "cT": cT})
        maps.append(m)
    return maps


def kernel(**inputs):
    B, T, _ = inputs["x"].shape
    TC = inputs["ctx"].shape[1]
    NB = B // NCORES
    nc = build(NB, T, TC)
    maps = prep_inputs(inputs, NB, NCORES)
    res = run_bass_kernel_spmd(nc, maps, core_ids=list(range(NCORES)))
    return np.concatenate([np.asarray(r["out"]) for r in res.results], 0).astype(np.float32)
```

```python
from contextlib import ExitStack
import os
import numpy as np
import concourse.bass as bass
import concourse.mybir as mybir
from concourse.bass_utils import run_bass_kernel_spmd

F32 = mybir.dt.float32
BF16 = mybir.dt.bfloat16
AF = mybir.ActivationFunctionType
ALU = mybir.AluOpType
AX = mybir.AxisListType

ENGS = ("pe", "act", "dve", "pool", "sp")
CH = 30000
D = 1024
DK = 8
NCORES = 8
CDEC = 0.6065306597126334


class Sched:
    def __init__(self):
        self.q = {e: [] for e in ENGS}
        self.cnt = {e: 0 for e in ENGS}
        self.seen = {e: {} for e in ENGS}
        self.last_w = {}
        self.readers = {}
        self.dma_cnt = {}
        self.dma_keys = []
        self.fence_snap = None
        self.fenced = {e: True for e in ENGS}

    def fence(self):
        snap = [(e, self.cnt[e]) for e in ENGS if self.cnt[e] > 0]
        snap += [("dma:" + k, n) for k, n in self.dma_cnt.items()]
        self.fence_snap = snap
        self.fenced = {e: False for e in ENGS}
        self.last_w = {}
        self.readers = {}

    def _deps(self, eng, reads, writes):
        deps = set()
        if not self.fenced[eng]:
            self.fenced[eng] = True
            deps |= set(self.fence_snap)
        for k in reads:
            if k in self.last_w:
                deps.add(self.last_w[k])
        for k in writes:
            if k in self.last_w:
                deps.add(self.last_w[k])
            deps |= self.readers.get(k, set())
        need = {}
        for (e, s) in deps:
            need[e] = max(need.get(e, 0), s)
        waits = []
        for e, s in need.items():
            if e == "pe" and eng == "pe":
                continue
            if self.seen[eng].get(e, 0) >= s:
                continue
            self.seen[eng][e] = s
            waits.append((e, s))
        return waits

    def _commit(self, me, reads, writes):
        for k in reads:
            self.readers.setdefault(k, set()).add(me)
        for k in writes:
            self.last_w[k] = me
            self.readers[k] = set()

    def op(self, eng, fn, reads=(), writes=()):
        waits = self._deps(eng, reads, writes)
        self.cnt[eng] += 1
        self.q[eng].append(("op", waits, fn, self.cnt[eng]))
        self._commit((eng, self.cnt[eng]), reads, writes)

    def dma(self, eng, fn, semkey, reads=(), writes=()):
        waits = self._deps(eng, reads, writes)
        if semkey not in self.dma_cnt:
            self.dma_cnt[semkey] = 0
            self.dma_keys.append(semkey)
        self.dma_cnt[semkey] += 1
        self.q[eng].append(("dma", waits, fn, semkey))
        self._commit(("dma:" + semkey, self.dma_cnt[semkey]), reads, writes)

    def emit_engine(self, eng, engobj, sems, dma_sems):
        def do_wait(e, s):
            if e.startswith("dma:"):
                engobj.wait_ge(dma_sems[e[4:]], 16 * s)
            else:
                engobj.wait_ge(sems[e][(s - 1) // CH], ((s - 1) % CH) + 1)

        for item in self.q[eng]:
            for (e, s) in item[1]:
                do_wait(e, s)
            if item[0] == "op":
                item[2](engobj).then_inc(sems[eng][(item[3] - 1) // CH], 1)
            else:
                item[2](engobj).then_inc(dma_sems[item[3]], 16)
        if eng == "sp":
            for k, n in self.dma_cnt.items():
                engobj.wait_ge(dma_sems[k], 16 * n)

    def run(self, nc, es):
        sems = {e: [es.enter_context(nc.semaphore("s_%s_%d" % (e, i)))
                    for i in range(max(1, (self.cnt[e] + CH - 1) // CH))] for e in ENGS}
        dma_sems = {k: es.enter_context(nc.semaphore("d_%d" % i)) for i, k in enumerate(self.dma_keys)}
        block = es.enter_context(nc.Block())
        block.tensor(lambda e: self.emit_engine("pe", e, sems, dma_sems))
        block.scalar(lambda e: self.emit_engine("act", e, sems, dma_sems))
        block.vector(lambda e: self.emit_engine("dve", e, sems, dma_sems))
        block.gpsimd(lambda e: self.emit_engine("pool", e, sems, dma_sems))
        block.sync(lambda e: self.emit_engine("sp", e, sems, dma_sems))


class Arena:
    def __init__(self, nc, base, limit):
        self.nc, self.base, self.limit, self.off, self.n = nc, base, limit, base, 0

    def reset(self):
        self.off = self.base

    def t(self, shape, dt):
        nb = int(np.prod(shape[1:])) * (4 if dt == F32 else 2)
        nb = (nb + 63) // 64 * 64
        assert self.off + nb <= self.limit, ("SBUF overflow", self.off, nb, self.limit)
        self.n += 1
        h = self.nc.alloc_sbuf_tensor_at("a%d" % self.n, list(shape), dt, offset=self.off)
        self.off += nb
        return h


CF = {}
CR = {}


def _layout():
    off = 0
    for nm, w in (("ada_b", 48), ("mix_pre_g", 8), ("mlp_pre_g", 8), ("mu_prev", 16), ("mu_next", 16),
                  ("w0", 8), ("a0", 8), ("k_k", 4), ("k_a", 4), ("r_k", 4), ("conv_w", 124),
                  ("conv_b", 4), ("cln_w", 4), ("cln_b", 4)):
        CF[nm] = (off, w)
        off += w
    ncf = off
    off = 0
    for nm, w in (("ada_b", 6144), ("mix_post_g", 1024), ("mlp_post_g", 1024), ("lnx_w", 512), ("lnx_b", 512)):
        CR[nm] = (off, w)
        off += w
    return ncf, off


NCF, NCR = _layout()


def build(NB, T, TC, debug=False, PHASES=9):
    TT = TC + T
    NB1 = NB + 1
    nc = bass.Bass("TRN2", target_bir_lowering=False)
    dram = lambda n, s, dt, kind: nc.dram_tensor(n, list(s), dt, kind=kind).ap()
    x_d = dram("x", [NB, T, D], F32, "ExternalInput")
    ctx_d = dram("ctx", [NB, TC, D], F32, "ExternalInput")
    cT_d = dram("cT", [128, DK, NB1], F32, "ExternalInput")
    adaw_d = dram("ada_w", [D, 6144], F32, "ExternalInput")
    win_d = dram("w_in", [D, 3072], F32, "ExternalInput")
    cfm_d = dram("cfm", [128, NCF], F32, "ExternalInput")
    crow_d = dram("crow", [128, NCR], F32, "ExternalInput")
    w2d_d = dram("decay_w2", [2, 64, 512], F32, "ExternalInput")
    a2_d = dram("iclr_a2", [2, 64, 512], F32, "ExternalInput")
    gw2_d = dram("gate_w2", [160, 512], F32, "ExternalInput")
    wout_d = dram("w_out", [D, D], F32, "ExternalInput")
    w1_d = dram("mlp_w1", [D, 4096], F32, "ExternalInput")
    w2_d = dram("mlp_w2", [4096, D], F32, "ExternalInput")
    out_d = dram("out", [NB, T, D], F32, "ExternalOutput")
    skind = "ExternalOutput" if debug else "Internal"
    pa_d = dram("pa_s", [NB, 20, 128, TT], BF16, skind)
    y_d = dram("y_s", [NB, 2, T, 512], F32, skind)
    bo_d = dram("bo_s", [NB, 2, T, 512], F32, skind)

    S = Sched()
    es = ExitStack()
    with es:
        banks = [es.enter_context(nc.psum_tensor("bank%d" % i, [128, 512], F32)) for i in range(8)]
        bk = lambda i: "bank%d" % i
        P_ = Arena(nc, 17408, 51 * 1024)
        A_ = Arena(nc, 51 * 1024, 223 * 1024)

        cfm = P_.t([128, NCF], F32)
        crow = P_.t([128, NCR - 6144], F32)
        ident = P_.t([128, 128], BF16)
        identf = P_.t([128, 128], F32)
        bones = P_.t([128, 128], F32)
        hsel = P_.t([128, 2], BF16)
        msk = P_.t([128, 2, 2, 128], BF16)
        mskN = P_.t([128, 2, 64], BF16)
        identB = P_.t([128, 64], BF16)
        onesb = P_.t([128, 1], BF16)
        rstm = P_.t([128, 512], F32)
        modfm = P_.t([128, 48, NB1], F32)
        gates = P_.t([NB1, 2, 1024], F32)
        gm = P_.t([128, 2, DK, NB1], F32)
        c0 = P_.t([128, 16], F32)
        sel = P_.t([NB1, NB1, 128], F32)
        cf = lambda nm: cfm[:, CF[nm][0]:CF[nm][0] + CF[nm][1]]
        cr = lambda nm: crow[:, CR[nm][0] - 6144:CR[nm][0] - 6144 + CR[nm][1]]

        S.dma("sp", lambda e: e.dma_start(out=cfm[:], in_=cfm_d[:, :]), "c0", writes=["cfm"])
        S.dma("sp", lambda e: e.dma_start(out=crow[:], in_=crow_d[:, 6144:NCR]), "c0", writes=["crow"])
        S.op("pool", lambda e: e.memset(identf[:], 1.0), writes=["identf"])
        S.op("pool", lambda e: e.affine_select(out=identf[:], in_=identf[:], pattern=[[-1, 128]],
                                               compare_op=ALU.is_equal, fill=0.0, base=0, channel_multiplier=1),
             reads=["identf"], writes=["identf"])
        S.op("dve", lambda e: e.tensor_copy(out=ident[:], in_=identf[:]), reads=["identf"], writes=["ident"])
        S.op("pool", lambda e: e.memset(bones[:], 0.0), writes=["bones"])
        S.op("pool", lambda e: e.memset(bones[0:64, 0:64], 1.0), reads=["bones"], writes=["bones"])
        S.op("pool", lambda e: e.memset(bones[64:128, 64:128], 1.0), reads=["bones"], writes=["bones"])
        S.op("pool", lambda e: e.memset(hsel[:], 0.0), writes=["hsel"])
        S.op("pool", lambda e: e.memset(hsel[0:64, 0:1], 1.0), reads=["hsel"], writes=["hsel"])
        S.op("pool", lambda e: e.memset(hsel[64:128, 1:2], 1.0), reads=["hsel"], writes=["hsel"])
        mtmp = P_.t([64, 64], F32)

        def mk_mask(dst_ap, sign, strict):
            S.op("pool", lambda e: e.memset(mtmp[:], 1.0), reads=["mtmp"], writes=["mtmp"])
            S.op("pool", lambda e: e.affine_select(out=mtmp[:], in_=mtmp[:], pattern=[[sign, 64]],
                                                   compare_op=ALU.is_gt if strict else ALU.is_ge,
                                                   fill=0.0, base=0, channel_multiplier=-sign),
                 reads=["mtmp"], writes=["mtmp"])
            S.op("pool", lambda e: e.tensor_copy(out=dst_ap, in_=mtmp[:]), reads=["mtmp"], writes=["msk"])
        for d in range(2):
            sg_ = 1 if d == 0 else -1
            for rr in range(2):
                mk_mask(msk[0:64, d, rr, 0:64], sg_, True)
                mk_mask(msk[0:64, d, rr, 64:128], sg_, False)
            mk_mask(mskN[0:64, d, :], -sg_, True)
        S.dma("sp", lambda e: e.dma_start(out=msk[64:128], in_=msk[0:64]), "c0", reads=["msk"], writes=["msk"])
        S.dma("sp", lambda e: e.dma_start(out=mskN[64:128], in_=mskN[0:64]), "c0", reads=["msk"], writes=["msk"])
        S.op("dve", lambda e: e.tensor_tensor(out=identB[:], in0=ident[:, 0:64], in1=ident[:, 64:128], op=ALU.add), reads=["ident"], writes=["identB"])
        S.op("pool", lambda e: e.memset(onesb[:], 1.0), writes=["onesb"])
        S.op("pool", lambda e: e.memset(rstm[:], 1.0), writes=["rstm"])
        S.op("pool", lambda e: e.memset(rstm[:].rearrange("p (c t) -> p c t", t=64)[:, :, 0:1], 0.0),
             reads=["rstm"], writes=["rstm"])
        S.op("pool", lambda e: e.memset(sel[:], 0.0), writes=["sel"])
        for b in range(NB1):
            S.op("pool", lambda e, b=b: e.memset(sel[:, b, :], 1.0), reads=["sel"], writes=["sel"])
            S.op("pool", lambda e, b=b: e.affine_select(out=sel[:, b, :], in_=sel[:, b, :], pattern=[[0, 128]],
                                                        compare_op=ALU.is_equal, fill=0.0, base=-b,
                                                        channel_multiplier=1),
                 reads=["sel"], writes=["sel"])

        A_.reset()
        cT = A_.t([128, DK, NB1], F32)
        siluT = A_.t([128, DK, NB1], F32)
        adab_row = A_.t([NB1, 6144], F32)
        aw = [A_.t([128, DK, 512], F32) for _ in range(2)]
        S.dma("sp", lambda e: e.dma_start(out=cT[:], in_=cT_d[:, :, :]), "c0", writes=["cT"])
        S.dma("sp", lambda e: e.dma_start(out=adab_row[:], in_=crow_d[0:NB1, 0:6144]), "c0", writes=["adab_row"])
        S.op("act", lambda e: e.activation(out=siluT[:], in_=cT[:], func=AF.Silu), reads=["cT"], writes=["siluT"])
        for n in range(12):
            a = aw[n % 2]
            ak = "aw%d" % (n % 2)
            S.dma("sp", lambda e, a=a, n=n: e.dma_start(
                out=a[:], in_=adaw_d[:, n * 512:(n + 1) * 512].rearrange("(k p) n -> p k n", p=128)),
                ak, writes=[ak])
            m = n // 2
            if m in (2, 5):
                for k in range(DK):
                    S.op("pe", lambda e, a=a, k=k: e.matmul(banks[0][0:NB1, :], lhsT=siluT[:, k, :], rhs=a[:, k, :],
                                                            start=(k == 0), stop=(k == DK - 1)),
                         reads=[ak, "siluT"], writes=[bk(0)])
                gi = 0 if m == 2 else 1
                S.op("dve", lambda e, n=n, gi=gi: e.tensor_tensor(
                    out=gates[:, gi, (n % 2) * 512:(n % 2) * 512 + 512], in0=banks[0][0:NB1, :],
                    in1=adab_row[:, n * 512:(n + 1) * 512], op=ALU.add),
                    reads=["adab_row"], writes=[bk(0), "gates"])
            else:
                for j in range(4):
                    for k in range(DK):
                        S.op("pe", lambda e, a=a, k=k, j=j: e.matmul(
                            banks[1][:, j * NB1:(j + 1) * NB1], lhsT=a[:, k, j * 128:(j + 1) * 128],
                            rhs=siluT[:, k, :], start=(k == 0), stop=(k == DK - 1)),
                            reads=[ak, "siluT"], writes=[bk(1)])
                S.op("dve", lambda e, n=n: e.tensor_tensor(
                    out=modfm[:, n * 4:(n + 1) * 4, :],
                    in0=banks[1][:, 0:4 * NB1].rearrange("p (j b) -> p j b", b=NB1),
                    in1=cf("ada_b")[:, n * 4:(n + 1) * 4].unsqueeze(2).to_broadcast([128, 4, NB1]), op=ALU.add),
                    reads=["cfm"], writes=[bk(1), "modfm"])
        for gi, (gn, m) in enumerate((("mix_pre_g", 1), ("mlp_pre_g", 4))):
            S.op("dve", lambda e, gi=gi, m=m: e.tensor_scalar(out=gm[:, gi], in0=modfm[:, m * 8:(m + 1) * 8, :],
                                                              scalar1=1.0, scalar2=None, op0=ALU.add),
                 reads=["modfm"], writes=["gm"])
            S.op("dve", lambda e, gi=gi, gn=gn: e.tensor_tensor(
                out=gm[:, gi], in0=gm[:, gi], in1=cf(gn).unsqueeze(2).to_broadcast([128, DK, NB1]), op=ALU.mult),
                reads=["gm", "cfm"], writes=["gm"])
        S.op("dve", lambda e: e.tensor_tensor(out=c0[:], in0=cf("mu_prev"), in1=cf("mu_next"), op=ALU.add),
             reads=["cfm"], writes=["c0"])
        S.op("dve", lambda e: e.tensor_scalar(out=c0[:], in0=c0[:], scalar1=-1.0, scalar2=1.0, op0=ALU.mult,
                                              op1=ALU.add), reads=["c0"], writes=["c0"])

        def front(xt_ap, xkey, hT_ap, hkey, gi, shm, b, tmp, pbank):
            S.op("act", lambda e: e.activation(out=tmp["sq"][:], in_=xt_ap, func=AF.Square),
                 reads=[xkey], writes=["f_sq"])
            S.op("dve", lambda e: e.reduce_sum(out=tmp["ss"][:], in_=tmp["sq"][:], axis=AX.X),
                 reads=["f_sq"], writes=["f_ss"])
            S.op("dve", lambda e: e.tensor_scalar(out=tmp["ss"][:], in0=tmp["ss"][:], scalar1=1.0 / D, scalar2=1e-6,
                                                  op0=ALU.mult, op1=ALU.add), reads=["f_ss"], writes=["f_ss"])
            S.op("act", lambda e: e.activation(out=tmp["ss"][:], in_=tmp["ss"][:], func=AF.Sqrt),
                 reads=["f_ss"], writes=["f_ss"])
            S.op("dve", lambda e: e.reciprocal(out=tmp["ss"][:], in_=tmp["ss"][:]), reads=["f_ss"], writes=["f_ss"])
            S.op("act", lambda e: e.activation(out=tmp["xn"][:], in_=xt_ap, func=AF.Copy, scale=tmp["ss"][:, 0:1]),
                 reads=[xkey, "f_ss"], writes=["f_xn"])
            pb = banks[pbank].bitcast(BF16)
            for k in range(DK):
                S.op("pe", lambda e, k=k: e.transpose(pb[:, k * 128:(k + 1) * 128], tmp["xn"][:, k * 128:(k + 1) * 128],
                                                      ident[:]), reads=["f_xn", "ident"], writes=[bk(pbank)])
            S.op("dve", lambda e: e.tensor_tensor(out=hT_ap, in0=pb[:, 0:1024].rearrange("p (k t) -> p k t", t=128),
                                                  in1=gm[:, gi, :, b:b + 1].to_broadcast([128, DK, 128]), op=ALU.mult),
                 reads=["gm"], writes=[bk(pbank), hkey])
            S.op("dve", lambda e: e.tensor_tensor(
                out=hT_ap, in0=hT_ap, in1=modfm[:, shm * 8:(shm + 1) * 8, b:b + 1].to_broadcast([128, DK, 128]),
                op=ALU.add), reads=["modfm", hkey], writes=[hkey])

        S.fence()
        A_.reset()
        winb = A_.t([128, DK, 3072], BF16)
        stg = [A_.t([128, DK, 512], F32) for _ in range(2)]
        for n in range(6):
            s_ = stg[n % 2]
            sk = "stg%d" % (n % 2)
            S.dma("sp", lambda e, s_=s_, n=n: e.dma_start(
                out=s_[:], in_=win_d[:, n * 512:(n + 1) * 512].rearrange("(k p) n -> p k n", p=128)), sk, writes=[sk])
            S.op("dve" if n % 2 == 0 else "act",
                 (lambda e, s_=s_, n=n: e.tensor_copy(out=winb[:, :, n * 512:(n + 1) * 512], in_=s_[:])) if n % 2 == 0
                 else (lambda e, s_=s_, n=n: e.activation(out=winb[:, :, n * 512:(n + 1) * 512], in_=s_[:], func=AF.Copy)),
                 reads=[sk], writes=["winb"])
        TM = max(T, TC)
        hT = A_.t([128, DK, TM + 2], BF16)
        xt = [A_.t([128, D], F32) for _ in range(2)]
        ftmp = {"sq": A_.t([128, D], F32), "ss": A_.t([128, 1], F32), "xn": A_.t([128, D], BF16)}
        etmp = [A_.t([128, 512], F32) for _ in range(2)]
        obuf = [A_.t([128, 512], BF16) for _ in range(3)]
        A_sg = [A_.t([128, 512], F32) for _ in range(4)]
        S.op("pool", lambda e: e.memset(hT[:], 0.0), writes=["hT"])
        xi = 0
        ob_i = 0
        pb_i = 0
        for b in range(NB):
            for (src, Ts, toff, bmod, tiles) in ((ctx_d, TC, 0, NB, list(range(4, 14))),
                                                 (x_d, T, TC, b, list(range(0, 24)))):
                if Ts < TM:
                    S.op("pool", lambda e, Ts=Ts: e.memset(hT[:, :, Ts + 1:Ts + 2], 0.0), reads=["hT"], writes=["hT"])
                for tt in range(Ts // 128):
                    xa = xt[xi % 2]
                    xk = "xt%d" % (xi % 2)
                    xi += 1
                    S.dma("sp", lambda e, xa=xa, src=src, b=b, tt=tt: e.dma_start(
                        out=xa[:], in_=src[b, tt * 128:(tt + 1) * 128, :]), xk, writes=[xk])
                    front(xa[:], xk, hT[:, :, 1 + tt * 128:1 + (tt + 1) * 128], "hT", 0, 0, bmod, ftmp, 7)
                w0 = 0
                while w0 < Ts:
                    n = min(510, Ts - w0)
                    sg_ready = {}
                    for j in [jj for jj in tiles if jj >= 20] + [jj for jj in tiles if jj < 20]:
                        pbk = pb_i % 4
                        pb_i += 1
                        pbt = banks[pbk]
                        for k in range(DK):
                            S.op("pe", lambda e, pbt=pbt, k=k, j=j, w0=w0, n=n: e.matmul(
                                pbt[:, 0:n + 2], lhsT=winb[:, k, j * 128:(j + 1) * 128], rhs=hT[:, k, w0:w0 + n + 2],
                                start=(k == 0), stop=(k == DK - 1)), reads=["winb", "hT"], writes=[bk(pbk)])
                        if j >= 20:
                            sgt = A_sg[j - 20]
                            S.op("act", lambda e, pbt=pbt, sgt=sgt, n=n: e.activation(
                                out=sgt[:, 0:n], in_=pbt[:, 1:n + 1], func=AF.Sigmoid),
                                writes=[bk(pbk), "sg%d" % (j - 20)])
                            continue
                        ob = obuf[ob_i % 3]
                        ok = "ob%d" % (ob_i % 3)
                        ob_i += 1
                        if j >= 16:
                            sgt = A_sg[j - 16]
                            S.op("dve", lambda e, pbt=pbt, sgt=sgt, ob=ob, n=n: e.tensor_tensor(
                                out=ob[:, 0:n], in0=pbt[:, 1:n + 1], in1=sgt[:, 0:n], op=ALU.mult),
                                reads=["sg%d" % (j - 16)], writes=[bk(pbk), ok])
                        else:
                            et = etmp[j % 2]
                            ek = "et%d" % (j % 2)
                            S.op("act", lambda e, pbt=pbt, et=et, n=n, j=j: e.activation(
                                out=et[:, 0:n], in_=pbt[:, 1:n + 1], func=AF.Copy, scale=c0[:, j:j + 1]),
                                reads=["c0"], writes=[bk(pbk), ek])
                            S.op("dve", lambda e, pbt=pbt, et=et, n=n, j=j: e.scalar_tensor_tensor(
                                out=et[:, 0:n], in0=pbt[:, 0:n], scalar=cf("mu_prev")[:, j:j + 1], in1=et[:, 0:n],
                                op0=ALU.mult, op1=ALU.add), reads=["cfm", ek], writes=[bk(pbk), ek])
                            S.op("dve", lambda e, pbt=pbt, et=et, ob=ob, n=n, j=j: e.scalar_tensor_tensor(
                                out=ob[:, 0:n], in0=pbt[:, 2:n + 2], scalar=cf("mu_next")[:, j:j + 1], in1=et[:, 0:n],
                                op0=ALU.mult, op1=ALU.add), reads=["cfm", ek], writes=[bk(pbk), ok])
                        S.dma("sp", lambda e, ob=ob, b=b, j=j, toff=toff, w0=w0, n=n: e.dma_start(
                            out=pa_d[b, j, :, toff + w0:toff + w0 + n], in_=ob[:, 0:n]), "pa_st", reads=[ok])
                    w0 += n
        if PHASES >= 2:
            S.fence()
            A_.reset()
            W = 128
            NCW = W // 64
            lwb = A_.t([128, 2, 512], BF16)
            mark_pb = A_.off
            wst = A_.t([128, 2, 512], F32)
            S.dma("sp", lambda e: e.dma_start(out=wst[0:64], in_=w2d_d.rearrange("d r f -> r d f")), "pbw", writes=["wst"])
            S.dma("sp", lambda e: e.dma_start(out=wst[64:128], in_=a2_d.rearrange("d r f -> r d f")), "pbw", writes=["wst"])
            S.op("dve", lambda e: e.tensor_copy(out=lwb[:], in_=wst[:]), reads=["wst"], writes=["lwb"])
            S.fence()
            A_.off = mark_pb
            rs = A_.t([128, 4, TT], BF16)
            ks = A_.t([128, 4, TT], BF16)
            vs = A_.t([128, 4, TT], BF16)
            wdad = A_.t([128, 2, TT], BF16)
            f32t = lambda: A_.t([128, 4, W], F32)
            TMP = [dict(sig=f32t(), Ls=f32t(), ee=f32t(), t1=f32t(), t2=f32t(), icl=A_.t([128, 4, W], BF16),
                        SC=A_.t([128, 4, NCW], F32)) for _ in range(2)]
            ar = [A_.t([128, 4, NCW, 2, 64], BF16) for _ in range(6)]
            bkt = [A_.t([128, 4, NCW, 2, 64], BF16) for _ in range(6)]
            prod = [A_.t([128, 4, W], BF16) for _ in range(6)]
            eLC = [A_.t([128, 4, NCW], F32) for _ in range(6)]
            H32 = [A_.t([128, 4, 64], F32) for _ in range(2)]
            Hbf = [A_.t([128, 4, 64], BF16) for _ in range(2)]
            NJS = 4 * NCW
            btk = [A_.t([128, 512], BF16) for _ in range(NJS)]
            vtm = [A_.t([128, 4, 64], BF16) for _ in range(NJS)]
            AT = [A_.t([128, 4, 2, 128], BF16) for _ in range(NJS)]
            XT = [A_.t([128, 4, 64], BF16) for _ in range(NJS)]
            PQm = [[A_.t([128, 2, 4, 64], BF16) for _ in range(2)] for _ in range(2 * NCW)]
            Xm = [[A_.t([128, 4, 64], BF16) for _ in range(2)] for _ in range(2 * NCW)]
            Rsb = [A_.t([128, 4, 64], BF16) for _ in range(2)]
            Usb = [A_.t([128, 4, 64], BF16) for _ in range(2)]
            ybuf = [A_.t([128, 4, 64], F32) for _ in range(2)]
            bosb = [A_.t([128, 4, 64], F32) for _ in range(2)]
            bon = [A_.t([128, 4], F32) for _ in range(2)]
            K_ = lambda nm, d: "%s%d" % (nm, d)

            def prep(b, d, w0, par):
                dp = d * 3 + par
                sig, Ls, ee, t1, t2, icl, SC = [TMP[d][k_] for k_ in ("sig", "Ls", "ee", "t1", "t2", "icl", "SC")]
                kS, kL, kE, k1, k2, kI, kC = [K_(k_, d) for k_ in ("sig", "Ls", "ee", "t1", "t2", "icl", "SC")]
                PB_ = 6 + d
                pbv = lambda i: banks[PB_][:, i * W:(i + 1) * W]
                pb4 = banks[PB_][:, 0:4 * W].rearrange("p (a t) -> p a t", t=W)
                v5 = lambda tns, a: tns[:].rearrange("p i (c t) -> p i c t", t=64) if a is None else tns[:, :, :, a, :]
                for i in range(4):
                    S.op("pe", lambda e, i=i: e.matmul(pbv(i), lhsT=lwb[0:64, d, i * 128:(i + 1) * 128], rhs=wdad[0:64, d, w0:w0 + W],
                                                       start=True, stop=True), reads=["lwb", "wdad"], writes=[bk(PB_)])
                for i in range(4):
                    S.op("act", lambda e, i=i: e.activation(out=sig[:, i, :], in_=pbv(i), func=AF.Sigmoid,
                                                            bias=cf("w0")[:, d * 4 + i:d * 4 + i + 1]), reads=["cfm"], writes=[bk(PB_), kS])
                yield
                for i in range(4):
                    S.op("pe", lambda e, i=i: e.matmul(pbv(i), lhsT=lwb[64:128, d, i * 128:(i + 1) * 128], rhs=wdad[64:128, d, w0:w0 + W],
                                                       start=True, stop=True), reads=["lwb", "wdad"], writes=[bk(PB_)])
                for i in range(4):
                    S.op("act", lambda e, i=i: e.activation(out=icl[:, i, :], in_=pbv(i), func=AF.Sigmoid,
                                                            bias=cf("a0")[:, d * 4 + i:d * 4 + i + 1]), reads=["cfm"], writes=[bk(PB_), kI])
                yield
                S.op("pool", lambda e: e.tensor_tensor(out=t1[:], in0=ks[:, :, w0:w0 + W],
                                                       in1=cf("k_k").unsqueeze(2).to_broadcast([128, 4, W]), op=ALU.mult),
                     reads=["ks", "cfm"], writes=[k1])
                S.op("pool", lambda e: e.tensor_tensor(out=t2[:], in0=t1[:], in1=t1[:], op=ALU.mult), reads=[k1], writes=[k2])
                yield
                for i in range(4):
                    S.op("pe", lambda e, i=i: e.matmul(pbv(i), lhsT=bones[:], rhs=t2[:, i, :], start=True, stop=True),
                         reads=["bones", k2], writes=[bk(PB_)])
                S.op("dve", lambda e: e.tensor_scalar(out=ee[:], in0=pb4, scalar1=1e-24, scalar2=None, op0=ALU.max),
                     writes=[bk(PB_), kE])
                yield
                S.op("act", lambda e: e.activation(out=ee[:], in_=ee[:], func=AF.Sqrt), reads=[kE], writes=[kE])
                yield
                S.op("dve", lambda e: e.reciprocal(out=ee[:], in_=ee[:]), reads=[kE], writes=[kE])
                yield
                S.op("pool", lambda e: e.tensor_tensor(out=t1[:], in0=t1[:], in1=ee[:], op=ALU.mult), reads=[k1, kE], writes=[k1])
                for i in range(4):
                    S.op("dve", lambda e, i=i: e.tensor_tensor_scan(out=Ls[:, i, :], data0=rstm[:, 0:W], data1=sig[:, i, :],
                                                                    initial=0.0, op0=ALU.mult, op1=ALU.add),
                         reads=["rstm", kS], writes=[kL])
                lsc = v5(Ls, None)
                S.op("dve", lambda e: e.tensor_copy(out=SC[:], in_=lsc[:, :, :, 63]), reads=[kL], writes=[kC])
                yield
                S.op("act", lambda e: e.activation(out=eLC[dp][:], in_=SC[:], func=AF.Exp, scale=-CDEC), reads=[kC], writes=[K_("eLC", dp)])
                if d == 0:
                    S.op("dve", lambda e: e.tensor_tensor(out=sig[:], in0=Ls[:], in1=sig[:], op=ALU.subtract), reads=[kL, kS], writes=[kS])
                    XE, kXE, XI, kXI = sig, kS, Ls, kL
                else:
                    S.op("dve", lambda e: e.tensor_tensor(out=lsc, in0=SC[:].unsqueeze(3).to_broadcast([128, 4, NCW, 64]),
                                                          in1=lsc, op=ALU.subtract), reads=[kL, kC], writes=[kL])
                    S.op("dve", lambda e: e.tensor_tensor(out=sig[:], in0=Ls[:], in1=sig[:], op=ALU.add), reads=[kL, kS], writes=[kS])
                    XE, kXE, XI, kXI = Ls, kL, sig, kS
                yield
                S.op("act", lambda e: e.activation(out=ee[:], in_=XE[:], func=AF.Exp, scale=-CDEC), reads=[kXE], writes=[kE])
                yield
                S.op("dve", lambda e: e.scalar_tensor_tensor(out=v5(ar[dp], 0), in0=v5(t1, None), scalar=-1.0, in1=v5(ee, None),
                                                             op0=ALU.mult, op1=ALU.mult), reads=[k1, kE], writes=[K_("ar", dp)])
                yield
                S.op("act", lambda e: e.activation(out=ee[:], in_=XI[:], func=AF.Exp, scale=-CDEC), reads=[kXI], writes=[kE])
                yield
                S.op("dve", lambda e: e.tensor_tensor(out=v5(ar[dp], 1), in0=rs[:, :, w0:w0 + W].rearrange("p i (c t) -> p i c t", t=64),
                                                      in1=v5(ee, None), op=ALU.mult), reads=["rs", kE], writes=[K_("ar", dp)])
                yield
                S.op("act", lambda e: e.activation(out=ee[:], in_=XI[:], func=AF.Exp, scale=CDEC), reads=[kXI], writes=[kE])
                S.op("pool", lambda e: e.tensor_tensor(out=t2[:], in0=t1[:], in1=icl[:], op=ALU.mult), reads=[k1, kI], writes=[k2])
                yield
                S.op("dve", lambda e: e.tensor_tensor(out=v5(bkt[dp], 0), in0=v5(t2, None), in1=v5(ee, None), op=ALU.mult),
                     reads=[k2, kE], writes=[K_("bkt", dp)])
                yield
                S.op("pool", lambda e: e.tensor_scalar(out=t2[:], in0=icl[:], scalar1=-1.0, scalar2=None, op0=ALU.add),
                     reads=[kI], writes=[k2])
                S.op("pool", lambda e: e.tensor_tensor(out=t2[:], in0=t2[:], in1=cf("k_a").unsqueeze(2).to_broadcast([128, 4, W]),
                                                       op=ALU.mult), reads=[k2, "cfm"], writes=[k2])
                yield
                S.op("pool", lambda e: e.tensor_scalar(out=t2[:], in0=t2[:], scalar1=1.0, scalar2=None, op0=ALU.add), reads=[k2], writes=[k2])
                S.op("pool", lambda e: e.tensor_tensor(out=t2[:], in0=t2[:], in1=ks[:, :, w0:w0 + W], op=ALU.mult), reads=[k2, "ks"], writes=[k2])
                yield
                S.op("dve", lambda e: e.tensor_tensor(out=v5(bkt[dp], 1), in0=v5(t2, None), in1=v5(ee, None), op=ALU.mult),
                     reads=[k2, kE], writes=[K_("bkt", dp)])
                yield
                S.op("pool", lambda e: e.tensor_tensor(out=t2[:], in0=t2[:], in1=rs[:, :, w0:w0 + W], op=ALU.mult),
                     reads=[k2, "rs"], writes=[k2])
                S.op("pool", lambda e: e.tensor_tensor(out=prod[dp][:], in0=t2[:], in1=cf("r_k").unsqueeze(2).to_broadcast([128, 4, W]),
                                                       op=ALU.mult), reads=[k2, "cfm"], writes=[K_("prod", dp)])
                yield

            HP = [((h % 2) * 64, h // 2) for h in range(8)]
            V3 = lambda bi: banks[bi][:, 0:256].rearrange("p (i s) -> p i s", s=64)

            def inv(b, d, w0, c, par3, js, jt, B):
                tk0 = w0 + c * 64
                dp = d * 3 + par3
                arK, bkK = K_("ar", dp), K_("bkt", dp)
                pT = banks[B].bitcast(BF16)
                for q in range(2):
                    for (po, i) in HP:
                        S.op("pe", lambda e, q=q, po=po, i=i: e.transpose(
                            pT[po:po + 64, (q * 4 + i) * 64:(q * 4 + i + 1) * 64], bkt[dp][po:po + 64, i, c, q, :],
                            ident[po:po + 64, po:po + 64]), reads=[bkK, "ident"], writes=[bk(B)])
                for (po, i) in HP:
                    S.op("pe", lambda e, po=po, i=i: e.transpose(pT[po:po + 64, 512 + i * 64:512 + (i + 1) * 64],
                                                                 vs[po:po + 64, i, tk0:tk0 + 64], ident[po:po + 64, po:po + 64]),
                         reads=["vs", "ident"], writes=[bk(B)])
                S.op("dve", lambda e: e.tensor_copy(out=btk[js][:], in_=pT[:, 0:512]), writes=[bk(B), K_("btk", js)])
                S.op("act", lambda e: e.activation(out=vtm[js][:].rearrange("p i v -> p (i v)"), in_=pT[:, 512:768], func=AF.Copy),
                     writes=[bk(B), K_("vtm", js)])
                yield
                for bb in range(2):
                    psA = banks[B][:, :].rearrange("p (i r t) -> p i r t", i=2, r=2)
                    for (po, i) in HP:
                        if i // 2 != bb:
                            continue
                        rhs = ar[dp][po:po + 64, i, c, :, :].rearrange("p a t -> p (a t)")
                        for r_ in range(2):
                            S.op("pe", lambda e, r_=r_, po=po, i=i, rhs=rhs, psA=psA: e.matmul(
                                psA[po:po + 64, i % 2, r_, :], lhsT=bkt[dp][po:po + 64, i, c, r_, :], rhs=rhs, start=True, stop=True),
                                reads=[bkK, arK], writes=[bk(B)])
                    S.op("dve", lambda e, bb=bb, psA=psA: e.tensor_tensor(
                        out=AT[js][:, bb * 2:bb * 2 + 2], in0=psA, in1=msk[:, d:d + 1].to_broadcast([128, 2, 2, 128]), op=ALU.mult),
                        reads=["msk"], writes=[bk(B), K_("AT", js)])
                    yield
                psN = V3(B)
                for (po, i) in HP:
                    S.op("pe", lambda e, po=po, i=i: e.matmul(psN[po:po + 64, i, :], lhsT=ar[dp][po:po + 64, i, c, 0, :],
                                                               rhs=bkt[dp][po:po + 64, i, c, 0, :], start=True, stop=True),
                         reads=[bkK, arK], writes=[bk(B)])
                PQ, X = PQm[jt], Xm[jt]
                S.op("dve", lambda e: e.tensor_tensor(out=PQ[0][:, 0], in0=psN, in1=mskN[:, d:d + 1].to_broadcast([128, 4, 64]), op=ALU.mult),
                     reads=["msk"], writes=[bk(B), K_("PQ0", jt)])
                S.op("act", lambda e: e.activation(out=PQ[0][:, 1], in_=AT[js][:, :, 0, 0:64], func=AF.Copy),
                     reads=[K_("AT", js)], writes=[K_("PQ0", jt)])
                S.op("dve", lambda e: e.tensor_tensor(out=X[0][:], in0=AT[js][:, :, 0, 0:64],
                                                      in1=identB[:].unsqueeze(1).to_broadcast([128, 4, 64]), op=ALU.add),
                     reads=[K_("AT", js), "identB"], writes=[K_("X0", jt)])
                yield
                cur = 0
                for j in range(1, 6):
                    nxt = 1 - cur
                    psPQ = banks[B][:, :].rearrange("p (a i s) -> p a i s", a=2, s=64)
                    na = 2 if j < 5 else 1
                    for a_ in range(na):
                        for (po, i) in HP:
                            S.op("pe", lambda e, po=po, i=i, cur=cur, a_=a_, psPQ=psPQ: e.matmul(
                                psPQ[po:po + 64, a_, i, :], lhsT=PQ[cur][po:po + 64, 1 - a_, i, :], rhs=PQ[cur][po:po + 64, a_, i, :],
                                start=True, stop=True), reads=[K_("PQ%d" % cur, jt)], writes=[bk(B)])
                    if jt % 2 == 0:
                        S.op("dve", lambda e, nxt=nxt, na=na, psPQ=psPQ: e.tensor_copy(out=PQ[nxt][:, 0:na], in_=psPQ[:, 0:na]),
                             writes=[bk(B), K_("PQ%d" % nxt, jt)])
                    else:
                        S.op("act", lambda e, nxt=nxt, na=na, psPQ=psPQ: e.activation(out=PQ[nxt][:, 0:na], in_=psPQ[:, 0:na], func=AF.Copy),
                             writes=[bk(B), K_("PQ%d" % nxt, jt)])
                    yield
                    psX = V3(B)
                    for (po, i) in HP:
                        S.op("pe", lambda e, po=po, i=i, cur=cur, nxt=nxt, psX=psX: e.matmul(
                            psX[po:po + 64, i, :], lhsT=PQ[nxt][po:po + 64, 0, i, :], rhs=X[cur][po:po + 64, i, :], start=True, stop=True),
                            reads=[K_("PQ%d" % nxt, jt), K_("X%d" % cur, jt)], writes=[bk(B)])
                    xo_, xok_ = (XT[js], K_("XT", js)) if j == 5 else (X[nxt], K_("X%d" % nxt, jt))
                    S.op("dve", lambda e, cur=cur, xo_=xo_, psX=psX: e.tensor_tensor(out=xo_[:], in0=psX, in1=X[cur][:], op=ALU.add),
                         reads=[K_("X%d" % cur, jt)], writes=[bk(B), xok_])
                    yield
                    cur = nxt

            def chain(b, d, w0, c, is_lat, par3, js, B):
                tk0 = w0 + c * 64
                dp = d * 3 + par3
                arK, bkK = K_("ar", dp), K_("bkt", dp)
                btm = btk[js][:, 0:256].rearrange("p (i k) -> p i k", k=64)
                ktm = btk[js][:, 256:512].rearrange("p (i k) -> p i k", k=64)
                ATj, vt, XTj = AT[js], vtm[js], XT[js]
                psR = V3(B)
                for (po, i) in HP:
                    S.op("pe", lambda e, po=po, i=i: e.matmul(psR[po:po + 64, i, :], lhsT=ar[dp][po:po + 64, i, c, 0, :],
                                                               rhs=Hbf[d][po:po + 64, i, :], start=True, stop=False),
                         reads=[arK, K_("Hbf", d)], writes=[bk(B)])
                    S.op("pe", lambda e, po=po, i=i: e.matmul(psR[po:po + 64, i, :], lhsT=ATj[po:po + 64, i, 1, 0:64],
                                                               rhs=vt[po:po + 64, i, :], start=False, stop=True),
                         reads=[K_("AT", js), K_("vtm", js)], writes=[bk(B)])
                S.op("act", lambda e: e.activation(out=Rsb[d][:], in_=psR, func=AF.Copy), writes=[bk(B), K_("Rsb", d)])
                yield
                for (po, i) in HP:
                    S.op("pe", lambda e, po=po, i=i: e.matmul(psR[po:po + 64, i, :], lhsT=XTj[po:po + 64, i, :],
                                                               rhs=Rsb[d][po:po + 64, i, :], start=True, stop=True),
                         reads=[K_("XT", js), K_("Rsb", d)], writes=[bk(B)])
                S.op("dve", lambda e: e.tensor_copy(out=Usb[d][:], in_=psR), writes=[bk(B), K_("Usb", d)])
                yield
                psH = V3(B)
                for (po, i) in HP:
                    S.op("pe", lambda e, po=po, i=i: e.matmul(psH[po:po + 64, i, :], lhsT=btm[po:po + 64, i, :],
                                                               rhs=Usb[d][po:po + 64, i, :], start=True, stop=False),
                         reads=[K_("btk", js), K_("Usb", d)], writes=[bk(B)])
                    S.op("pe", lambda e, po=po, i=i: e.matmul(psH[po:po + 64, i, :], lhsT=ktm[po:po + 64, i, :],
                                                               rhs=vt[po:po + 64, i, :], start=False, stop=True),
                         reads=[K_("btk", js), K_("vtm", js)], writes=[bk(B)])
                if is_lat:
                    psY = banks[B][:, 256:512].rearrange("p (i s) -> p i s", s=64)
                    for (po, i) in HP:
                        S.op("pe", lambda e, po=po, i=i: e.matmul(psY[po:po + 64, i, :], lhsT=ar[dp][po:po + 64, i, c, 1, :],
                                                                   rhs=Hbf[d][po:po + 64, i, :], start=True, stop=False),
                             reads=[arK, K_("Hbf", d)], writes=[bk(B)])
                        S.op("pe", lambda e, po=po, i=i: e.matmul(psY[po:po + 64, i, :], lhsT=ATj[po:po + 64, i, 0, 64:128],
                                                                   rhs=Usb[d][po:po + 64, i, :], start=False, stop=False),
                             reads=[K_("AT", js), K_("Usb", d)], writes=[bk(B)])
                        S.op("pe", lambda e, po=po, i=i: e.matmul(psY[po:po + 64, i, :], lhsT=ATj[po:po + 64, i, 1, 64:128],
                                                                   rhs=vt[po:po + 64, i, :], start=False, stop=True),
                             reads=[K_("AT", js), K_("vtm", js)], writes=[bk(B)])
                S.op("dve", lambda e: e.tensor_tensor(out=H32[d][:], in0=psH, in1=H32[d][:], op=ALU.add),
                     reads=[K_("H32", d)], writes=[bk(B), K_("H32", d)])
                S.op("dve", lambda e: e.tensor_tensor(out=H32[d][:], in0=H32[d][:],
                                                      in1=eLC[dp][:, :, c:c + 1].to_broadcast([128, 4, 64]), op=ALU.mult),
                     reads=[K_("H32", d), K_("eLC", dp)], writes=[K_("H32", d)])
                S.op("act", lambda e: e.activation(out=Hbf[d][:], in_=H32[d][:], func=AF.Copy), reads=[K_("H32", d)], writes=[K_("Hbf", d)])
                if is_lat:
                    S.op("act", lambda e: e.activation(out=ybuf[d][:], in_=psY, func=AF.Copy), writes=[bk(B), K_("ybuf", d)])
                    tl = tk0 - TC
                    for hp_ in range(2):
                        S.dma("sp", lambda e, tl=tl, hp_=hp_: e.dma_start(
                            out=y_d[b, d, tl:tl + 64, :].rearrange("t (i hp v) -> t i hp v", hp=2, v=64)[:, :, hp_, :],
                            in_=ybuf[d][hp_ * 64:(hp_ + 1) * 64]), "y_st", reads=[K_("ybuf", d)])
                    yield
                    psB = banks[B]
                    for (po, i) in HP:
                        S.op("pe", lambda e, po=po, i=i: e.matmul(psB[po:po + 64, i:i + 1], lhsT=prod[dp][po:po + 64, i, c * 64:(c + 1) * 64],
                                                                   rhs=onesb[po:po + 64, 0:1], start=True, stop=True),
                             reads=[K_("prod", dp), "onesb"], writes=[bk(B)])
                    S.op("dve", lambda e: e.tensor_scalar(out=bon[d][:], in0=psB[:, 0:4], scalar1=0.5, scalar2=None, op0=ALU.mult),
                         writes=[bk(B), K_("bon", d)])
                    S.op("dve", lambda e: e.tensor_tensor(out=bosb[d][:], in0=vt[:],
                                                          in1=bon[d][:].unsqueeze(2).to_broadcast([128, 4, 64]), op=ALU.mult),
                         reads=[K_("vtm", js), K_("bon", d)], writes=[K_("bosb", d)])
                    for hp_ in range(2):
                        S.dma("sp", lambda e, tl=tl, hp_=hp_: e.dma_start(
                            out=bo_d[b, d, tl:tl + 64, :].rearrange("t (i hp v) -> t i hp v", hp=2, v=64)[:, :, hp_, :],
                            in_=bosb[d][hp_ * 64:(hp_ + 1) * 64]), "bo_st", reads=[K_("bosb", d)])
                yield

            def lockstep(gens):
                gens = list(gens)
                while gens:
                    for g_ in list(gens):
                        try:
                            next(g_)
                        except StopIteration:
                            gens.remove(g_)

            def pb_batch(b):
                for (dst, j0, key) in ((rs, 0, "rs"), (ks, 4, "ks"), (vs, 8, "vs")):
                    S.dma("sp", lambda e, dst=dst, j0=j0: e.dma_start(out=dst[:], in_=pa_d[b, j0:j0 + 4].rearrange("j p t -> p j t")),
                          "pb_ld_" + key, writes=[key])
                S.dma("sp", lambda e: e.dma_start(out=wdad[:], in_=pa_d[b, 12:14].rearrange("j p t -> p j t")), "pb_ld_w", writes=["wdad"])
                S.op("act", lambda e: e.activation(out=wdad[0:64], in_=wdad[0:64], func=AF.Tanh), reads=["wdad"], writes=["wdad"])
                for d in range(2):
                    S.op("pool", lambda e, d=d: e.memset(H32[d][:], 0.0), writes=[K_("H32", d)])
                    S.op("pool", lambda e, d=d: e.memset(Hbf[d][:], 0.0), writes=[K_("Hbf", d)])
                cw_ = [(w * W, False) for w in range(TC // W)]
                lw_ = [(TC + w * W, True) for w in range(T // W)]
                sched = {0: cw_ + lw_, 1: list(reversed(cw_)) + list(reversed(lw_))}
                nw_ = len(sched[0])

                def preps(wi):
                    return [prep(b, d, sched[d][wi][0], wi % 3) for d in range(2)]

                def jobs(wi):
                    for d in range(2):
                        for cc in range(NCW):
                            c = cc if d == 0 else NCW - 1 - cc
                            yield d, cc, c, (wi % 2) * 2 * NCW + d * NCW + cc, d * NCW + cc

                def chains(wi):
                    for cc in range(NCW):
                        gens = []
                        for (d, cc_, c, js, jt) in jobs(wi):
                            if cc_ == cc:
                                gens.append(chain(b, d, sched[d][wi][0], c, sched[d][wi][1], wi % 3, js, 2 * NCW + d))
                        while gens:
                            for g_ in list(gens):
                                try:
                                    next(g_)
                                except StopIteration:
                                    gens.remove(g_)
                            yield

                lockstep(preps(0))
                for t in range(nw_ + 1):
                    gl = []
                    if t >= 1:
                        gl.append(chains(t - 1))
                    if t < nw_:
                        gl += [inv(b, d, sched[d][t][0], c, t % 3, js, jt, jt) for (d, cc, c, js, jt) in jobs(t)]
                    if t + 1 < nw_:
                        gl += preps(t + 1)
                    lockstep(gl)

            for b in range(NB):
                pb_batch(b)

        def post_norm_residual(pbs, gpost, xres, xkey, outt, okey, tmp, kp="pn"):
            for hh in range(2):
                S.op("act", lambda e, hh=hh: e.activation(out=tmp["sq"][:, hh * 512:(hh + 1) * 512], in_=banks[pbs[hh]][:, :], func=AF.Square),
                     writes=[bk(pbs[hh]), kp + "_sq"])
            S.op("dve", lambda e: e.reduce_sum(out=tmp["ss"][:], in_=tmp["sq"][:], axis=AX.X), reads=[kp + "_sq"], writes=[kp + "_ss"])
            S.op("dve", lambda e: e.tensor_scalar(out=tmp["ss"][:], in0=tmp["ss"][:], scalar1=1.0 / D, scalar2=1e-6,
                                                  op0=ALU.mult, op1=ALU.add), reads=[kp + "_ss"], writes=[kp + "_ss"])
            S.op("act", lambda e: e.activation(out=tmp["ss"][:], in_=tmp["ss"][:], func=AF.Sqrt), reads=[kp + "_ss"], writes=[kp + "_ss"])
            S.op("dve", lambda e: e.reciprocal(out=tmp["ss"][:], in_=tmp["ss"][:]), reads=[kp + "_ss"], writes=[kp + "_ss"])
            for hh in range(2):
                S.op("dve", lambda e, hh=hh: e.scalar_tensor_tensor(
                    out=tmp["sq"][:, hh * 512:(hh + 1) * 512], in0=banks[pbs[hh]][:, :], scalar=tmp["ss"][:, 0:1],
                    in1=gpost[:, hh * 512:(hh + 1) * 512], op0=ALU.mult, op1=ALU.mult),
                    reads=[kp + "_ss", "gpost"], writes=[bk(pbs[hh]), kp + "_sq"])
            S.op("pool", lambda e: e.tensor_tensor(out=outt[:], in0=tmp["sq"][:], in1=xres, op=ALU.add), reads=[kp + "_sq", xkey], writes=[okey])

        def make_gpost(b, gi, rowname, gpost):
            for hh in range(2):
                S.op("pe", lambda e, hh=hh: e.matmul(banks[hh][:, :], lhsT=sel[:, b, :], rhs=gates[:, gi, hh * 512:(hh + 1) * 512],
                                                     start=True, stop=True), reads=["sel", "gates"], writes=[bk(hh)])
                S.op("dve", lambda e, hh=hh: e.tensor_tensor(out=gpost[:, hh * 512:(hh + 1) * 512], in0=banks[hh][:, :],
                                                             in1=cr(rowname)[:, hh * 512:(hh + 1) * 512], op=ALU.mult),
                     reads=["crow"], writes=[bk(hh), "gpost"])

        if PHASES >= 3:
            S.fence()
            A_.reset()
            woutb = A_.t([128, DK, D], BF16)
            gw2b = A_.t([128, 2, 512], BF16)
            stg = [A_.t([128, DK, 512], F32) for _ in range(2)]
            for n in range(2):
                S.dma("sp", lambda e, n=n: e.dma_start(out=stg[n][:], in_=wout_d[:, n * 512:(n + 1) * 512].rearrange("(k p) n -> p k n", p=128)),
                      "stgd%d" % n, writes=["stgd%d" % n])
                S.op("dve", lambda e, n=n: e.tensor_copy(out=woutb[:, :, n * 512:(n + 1) * 512], in_=stg[n][:]), reads=["stgd%d" % n], writes=["woutb"])
            gst = A_.t([128, 2, 512], F32)
            S.dma("sp", lambda e: e.dma_start(out=gst[:, 0, :], in_=gw2_d[0:128, :]), "gst", writes=["gst"])
            S.dma("sp", lambda e: e.dma_start(out=gst[0:32, 1, :], in_=gw2_d[128:160, :]), "gst", writes=["gst"])
            S.op("dve", lambda e: e.tensor_copy(out=gw2b[:, 0, :], in_=gst[:, 0, :]), reads=["gst"], writes=["gw2b"])
            S.op("dve", lambda e: e.tensor_copy(out=gw2b[0:32, 1, :], in_=gst[0:32, 1, :]), reads=["gst"], writes=["gw2b"])
            S.fence()
            A_.off -= 2 * 16384 + 4096
            onesf = A_.t([128, 128], F32)
            S.op("pool", lambda e: e.memset(onesf[:], 1.0), writes=["onesf"])
            ub = A_.t([128, 4, T], BF16)
            yc = A_.t([128, 4, T], F32)
            convo = A_.t([128, 4, T], BF16)
            gdb = A_.t([128, 2, T], BF16)
            lsq = A_.t([128, 4, 512], F32)
            lmean = A_.t([128, 512], F32)
            lrstd = A_.t([128, 512], F32)
            ltmp = A_.t([128, 512], F32)
            yin = [[A_.t([128, 512], F32) for _ in range(4)] for _ in range(2)]
            ysq_ = [A_.t([128, 512], F32) for _ in range(2)]
            st8_ = [A_.t([128, 4, 8], F32) for _ in range(2)]
            rwb_ = [A_.t([128, 512], BF16) for _ in range(2)]
            mixT_ = [A_.t([128, 4, 128], BF16) for _ in range(2)]
            xt2 = [A_.t([128, D], F32) for _ in range(2)]
            gpost = A_.t([128, D], F32)
            pn_tmp_ = [{"sq": A_.t([128, D], F32), "ss": A_.t([128, 1], F32)} for _ in range(2)]
            cw = cf("conv_w")
            ti_ = [0]

            def _pd_batch(b):
                make_gpost(b, 0, "mix_post_g", gpost)
                S.dma("sp", lambda e, b=b: e.dma_start(out=ub[:], in_=pa_d[b, 16:20, :, TC:TT].rearrange("j p t -> p j t")), "pd_ld", writes=["ub"])
                S.dma("sp", lambda e, b=b: e.dma_start(out=gdb[:], in_=pa_d[b, 14:16, :, TC:TT].rearrange("j p t -> p j t")), "pd_ld2", writes=["gdb"])
                S.op("act", lambda e: e.activation(out=gdb[:, 0, :], in_=gdb[:, 0, :], func=AF.Sigmoid), reads=["gdb"], writes=["gdb"])
                S.op("act", lambda e: e.activation(out=gdb[0:32, 1, :], in_=gdb[0:32, 1, :], func=AF.Sigmoid), reads=["gdb"], writes=["gdb"])
                for i in range(4):
                    u4 = ub[:, i, :].rearrange("p (r t) -> p r t", t=64)
                    y4 = yc[:, i, :].rearrange("p (r t) -> p r t", t=64)
                    S.op("dve", lambda e, i=i: e.tensor_scalar(out=yc[:, i, :], in0=ub[:, i, :], scalar1=cw[:, i * 31 + 15:i * 31 + 16],
                                                               scalar2=cf("conv_b")[:, i:i + 1], op0=ALU.mult, op1=ALU.add),
                         reads=["ub", "cfm"], writes=["yc%d" % i])
                    for j in range(31):
                        o = j - 15
                        if o == 0:
                            continue
                        lo_o, hi_o = max(0, -o), 64 - max(0, o)
                        lo_i, hi_i = max(0, o), 64 - max(0, -o)
                        S.op("dve", lambda e, i=i, j=j, u4=u4, y4=y4, lo_o=lo_o, hi_o=hi_o, lo_i=lo_i, hi_i=hi_i: e.scalar_tensor_tensor(
                            out=y4[:, :, lo_o:hi_o], in0=u4[:, :, lo_i:hi_i], scalar=cw[:, i * 31 + j:i * 31 + j + 1],
                            in1=y4[:, :, lo_o:hi_o], op0=ALU.mult, op1=ALU.add), reads=["ub", "cfm", "yc%d" % i], writes=["yc%d" % i])
                for w in range(T // 512 if T >= 512 else 1):
                    wn = min(512, T)
                    ws = slice(w * 512, w * 512 + wn)
                    for i in range(4):
                        S.op("pe", lambda e, i=i, ws=ws, wn=wn: e.matmul(banks[2][:, 0:wn], lhsT=onesf[:], rhs=yc[:, i, ws], start=(i == 0), stop=(i == 3)),
                             reads=["onesf", "yc%d" % i], writes=[bk(2)])
                    S.op("act", lambda e, ws=ws, wn=wn: e.activation(out=lsq[:, :, 0:wn], in_=yc[:, :, ws], func=AF.Square),
                         reads=["yc0", "yc1", "yc2", "yc3"], writes=["lsq"])
                    for i in range(4):
                        S.op("pe", lambda e, i=i, wn=wn: e.matmul(banks[3][:, 0:wn], lhsT=onesf[:], rhs=lsq[:, i, 0:wn], start=(i == 0), stop=(i == 3)),
                             reads=["onesf", "lsq"], writes=[bk(3)])
                    S.op("dve", lambda e, wn=wn: e.tensor_scalar(out=lmean[:, 0:wn], in0=banks[2][:, 0:wn], scalar1=1.0 / 512, scalar2=None, op0=ALU.mult),
                         writes=[bk(2), "lmean"])
                    S.op("dve", lambda e, wn=wn: e.tensor_tensor(out=ltmp[:, 0:wn], in0=lmean[:, 0:wn], in1=lmean[:, 0:wn], op=ALU.mult),
                         reads=["lmean"], writes=["ltmp"])
                    S.op("dve", lambda e, wn=wn: e.scalar_tensor_tensor(out=lrstd[:, 0:wn], in0=banks[3][:, 0:wn], scalar=1.0 / 512, in1=ltmp[:, 0:wn],
                                                                        op0=ALU.mult, op1=ALU.subtract), reads=["ltmp"], writes=[bk(3), "lrstd"])
                    S.op("dve", lambda e, wn=wn: e.tensor_scalar(out=lrstd[:, 0:wn], in0=lrstd[:, 0:wn], scalar1=1e-5, scalar2=None, op0=ALU.add),
                         reads=["lrstd"], writes=["lrstd"])
                    S.op("act", lambda e, wn=wn: e.activation(out=lrstd[:, 0:wn], in_=lrstd[:, 0:wn], func=AF.Sqrt), reads=["lrstd"], writes=["lrstd"])
                    S.op("dve", lambda e, wn=wn: e.reciprocal(out=lrstd[:, 0:wn], in_=lrstd[:, 0:wn]), reads=["lrstd"], writes=["lrstd"])
                    S.op("dve", lambda e, ws=ws, wn=wn: e.tensor_tensor(out=lsq[:, :, 0:wn], in0=yc[:, :, ws],
                                                                        in1=lmean[:, 0:wn].unsqueeze(1).to_broadcast([128, 4, wn]), op=ALU.subtract),
                         reads=["yc0", "yc1", "yc2", "yc3", "lmean"], writes=["lsq"])
                    S.op("dve", lambda e, wn=wn: e.tensor_tensor(out=lsq[:, :, 0:wn], in0=lsq[:, :, 0:wn],
                                                                 in1=lrstd[:, 0:wn].unsqueeze(1).to_broadcast([128, 4, wn]), op=ALU.mult),
                         reads=["lsq", "lrstd"], writes=["lsq"])
                    for i in range(4):
                        S.op("act", lambda e, i=i, ws=ws, wn=wn: e.activation(out=convo[:, i, ws], in_=lsq[:, i, 0:wn], func=AF.Silu,
                                                                              scale=cf("cln_w")[:, i:i + 1], bias=cf("cln_b")[:, i:i + 1]),
                             reads=["lsq", "cfm"], writes=["convo"])
                for tt in range(0, T // 128, 2):
                    gens = [_pd_tile(b, tt + q_, q_) for q_ in range(2) if tt + q_ < T // 128]
                    while gens:
                        for g_ in list(gens):
                            try:
                                next(g_)
                            except StopIteration:
                                gens.remove(g_)

            def _pd_tile(b, tt, sl):
                if True:
                    tsl = slice(tt * 128, (tt + 1) * 128)
                    ti = sl
                    BG, BT, BO0, BO1 = 4 * sl, 4 * sl + 1, 4 * sl + 2, 4 * sl + 3
                    ysq, st8, rwb, mixT = ysq_[sl], st8_[sl], rwb_[sl], mixT_[sl]
                    pn_tmp = pn_tmp_[sl]
                    sk = lambda nm: "%s_%d" % (nm, sl)
                    yy = yin[ti % 2]
                    yk = "yin%d" % (ti % 2)
                    xa, xk2 = xt2[ti % 2], "xt2_%d" % (ti % 2)
                    for q, src in enumerate((y_d, y_d, bo_d, bo_d)):
                        S.dma("sp", lambda e, q=q, src=src, yy=yy, tsl=tsl: e.dma_start(out=yy[q][:], in_=src[b, q % 2, tsl, :]), yk, writes=[yk])
                    S.dma("sp", lambda e, xa=xa, tsl=tsl: e.dma_start(out=xa[:], in_=x_d[b, tsl, :]), xk2, writes=[xk2])
                    S.op("pool", lambda e, yy=yy: e.tensor_tensor(out=yy[0][:], in0=yy[0][:], in1=yy[1][:], op=ALU.add), reads=[yk], writes=[yk])
                    S.op("pool", lambda e, yy=yy: e.tensor_tensor(out=yy[2][:], in0=yy[2][:], in1=yy[3][:], op=ALU.add), reads=[yk], writes=[yk])
                    yield
                    y3 = yy[0][:].rearrange("p (h v) -> p h v", v=64)
                    S.op("dve", lambda e, y3=y3: e.reduce_sum(out=st8[:, 0, :], in_=y3, axis=AX.X), reads=[yk], writes=[sk("st8")])
                    S.op("act", lambda e, yy=yy: e.activation(out=ysq[:], in_=yy[0][:], func=AF.Square), reads=[yk], writes=[sk("ysq")])
                    S.op("dve", lambda e: e.reduce_sum(out=st8[:, 1, :], in_=ysq[:].rearrange("p (h v) -> p h v", v=64), axis=AX.X),
                         reads=[sk("ysq")], writes=[sk("st8")])
                    S.op("dve", lambda e: e.tensor_scalar(out=st8[:, 0:2, :], in0=st8[:, 0:2, :], scalar1=1.0 / 64, scalar2=None, op0=ALU.mult),
                         reads=[sk("st8")], writes=[sk("st8")])
                    yield
                    S.op("dve", lambda e: e.tensor_tensor(out=st8[:, 2, :], in0=st8[:, 0, :], in1=st8[:, 0, :], op=ALU.mult), reads=[sk("st8")], writes=[sk("st8")])
                    S.op("dve", lambda e: e.tensor_tensor(out=st8[:, 3, :], in0=st8[:, 1, :], in1=st8[:, 2, :], op=ALU.subtract), reads=[sk("st8")], writes=[sk("st8")])
                    S.op("dve", lambda e: e.tensor_scalar(out=st8[:, 3, :], in0=st8[:, 3, :], scalar1=64e-5, scalar2=None, op0=ALU.add),
                         reads=[sk("st8")], writes=[sk("st8")])
                    S.op("act", lambda e: e.activation(out=st8[:, 3, :], in_=st8[:, 3, :], func=AF.Sqrt), reads=[sk("st8")], writes=[sk("st8")])
                    S.op("dve", lambda e: e.reciprocal(out=st8[:, 3, :], in_=st8[:, 3, :]), reads=[sk("st8")], writes=[sk("st8")])
                    yield
                    S.op("dve", lambda e, y3=y3: e.tensor_tensor(out=y3, in0=y3, in1=st8[:, 0, :].unsqueeze(2).to_broadcast([128, 8, 64]), op=ALU.subtract),
                         reads=[yk, sk("st8")], writes=[yk])
                    S.op("dve", lambda e, y3=y3: e.tensor_tensor(out=y3, in0=y3, in1=st8[:, 3, :].unsqueeze(2).to_broadcast([128, 8, 64]), op=ALU.mult),
                         reads=[yk, sk("st8")], writes=[yk])
                    S.op("pool", lambda e, yy=yy: e.tensor_tensor(out=yy[0][:], in0=yy[0][:], in1=cr("lnx_w"), op=ALU.mult), reads=[yk, "crow"], writes=[yk])
                    S.op("pool", lambda e, yy=yy: e.tensor_tensor(out=yy[0][:], in0=yy[0][:], in1=cr("lnx_b"), op=ALU.add), reads=[yk, "crow"], writes=[yk])
                    S.op("pool", lambda e, yy=yy: e.tensor_tensor(out=yy[0][:], in0=yy[0][:], in1=yy[2][:], op=ALU.add), reads=[yk], writes=[yk])
                    yield
                    S.op("pe", lambda e, tsl=tsl: e.matmul(banks[BG][:, :], lhsT=gdb[:, 0, tsl], rhs=gw2b[:, 0, :], start=True, stop=False),
                         reads=["gdb", "gw2b"], writes=[bk(BG)])
                    S.op("pe", lambda e, tsl=tsl: e.matmul(banks[BG][:, :], lhsT=gdb[0:32, 1, tsl], rhs=gw2b[0:32, 1, :], start=False, stop=True),
                         reads=["gdb", "gw2b"], writes=[bk(BG)])
                    S.op("dve", lambda e, yy=yy: e.tensor_tensor(out=rwb[:], in0=yy[0][:], in1=banks[BG][:, :], op=ALU.mult),
                         reads=[yk], writes=[bk(BG), sk("rwb")])
                    yield
                    pT = banks[BT].bitcast(BF16)
                    for j in range(4):
                        S.op("pe", lambda e, j=j: e.transpose(pT[:, j * 128:(j + 1) * 128], rwb[:, j * 128:(j + 1) * 128], ident[:]),
                             reads=[sk("rwb"), "ident"], writes=[bk(BT)])
                    S.op("act", lambda e: e.activation(out=mixT[:].rearrange("p j t -> p (j t)"), in_=pT[:, 0:512], func=AF.Copy),
                         writes=[bk(BT), sk("mixT")])
                    yield
                    for hh in range(2):
                        for j in range(8):
                            lhs = mixT[:, j, :] if j < 4 else convo[:, j - 4, tsl]
                            S.op("pe", lambda e, hh=hh, j=j, lhs=lhs: e.matmul(banks[BO0 + hh][:, :], lhsT=lhs, rhs=woutb[:, j, hh * 512:(hh + 1) * 512],
                                                                               start=(j == 0), stop=(j == 7)),
                                 reads=[sk("mixT"), "convo", "woutb"], writes=[bk(BO0 + hh)])
                    yield
                    post_norm_residual((BO0, BO1), gpost, xa[:], xk2, xa, xk2, pn_tmp, sk("pn"))
                    S.dma("sp", lambda e, xa=xa, tsl=tsl: e.dma_start(out=out_d[b, tsl, :], in_=xa[:]), "x1_st", reads=[xk2])

            for b in range(NB):
                _pd_batch(b)

        if PHASES >= 4:
            S.fence()
            A_.reset()
            w1b = A_.t([128, DK, 4096], BF16)
            w2b_ = A_.t([128, 32, D], BF16)
            mark = A_.off
            stg = [A_.t([128, DK, 512], F32) for _ in range(2)]
            for n in range(8):
                S.dma("sp", lambda e, n=n: e.dma_start(out=stg[n % 2][:], in_=w1_d[:, n * 512:(n + 1) * 512].rearrange("(k p) n -> p k n", p=128)),
                      "stge%d" % (n % 2), writes=["stge%d" % (n % 2)])
                if n % 2 == 0:
                    S.op("dve", lambda e, n=n: e.tensor_copy(out=w1b[:, :, n * 512:(n + 1) * 512], in_=stg[n % 2][:]), reads=["stge%d" % (n % 2)], writes=["w1b"])
                else:
                    S.op("act", lambda e, n=n: e.activation(out=w1b[:, :, n * 512:(n + 1) * 512], in_=stg[n % 2][:], func=AF.Copy),
                         reads=["stge%d" % (n % 2)], writes=["w1b"])
            for n in range(8):
                S.dma("sp", lambda e, n=n: e.dma_start(out=stg[n % 2][:].rearrange("p k n -> p (k n)").rearrange("p (f n) -> p f n", n=D),
                                                       in_=w2_d[n * 512:(n + 1) * 512, :].rearrange("(f p) n -> p f n", p=128)),
                      "stge%d" % (n % 2), writes=["stge%d" % (n % 2)])
                src = stg[n % 2][:].rearrange("p k n -> p (k n)").rearrange("p (f n) -> p f n", n=D)
                if n % 2 == 0:
                    S.op("dve", lambda e, n=n, src=src: e.tensor_copy(out=w2b_[:, n * 4:(n + 1) * 4, :], in_=src), reads=["stge%d" % (n % 2)], writes=["w2b_"])
                else:
                    S.op("act", lambda e, n=n, src=src: e.activation(out=w2b_[:, n * 4:(n + 1) * 4, :], in_=src, func=AF.Copy),
                         reads=["stge%d" % (n % 2)], writes=["w2b_"])
            S.fence()
            A_.off = mark
            G = 256
            hT2 = A_.t([128, DK, G], BF16)
            hid = A_.t([128, 32, G], BF16)
            rl = [A_.t([128, 512], F32) for _ in range(2)]
            x1g = [A_.t([128, D], F32) for _ in range(2)]
            gpost2 = A_.t([128, D], F32)
            ftmp2 = {"sq": A_.t([128, D], F32), "ss": A_.t([128, 1], F32), "xn": A_.t([128, D], BF16)}
            pn_tmp2 = {"sq": ftmp2["sq"], "ss": A_.t([128, 1], F32)}
            def _pe_group(b, g):
                if True:
                    for tq in range(G // 128):
                        tsl = slice(g * G + tq * 128, g * G + (tq + 1) * 128)
                        S.dma("sp", lambda e, tq=tq, tsl=tsl: e.dma_start(out=x1g[tq][:], in_=out_d[b, tsl, :]), "x1g%d" % tq, writes=["x1g%d" % tq])
                        front(x1g[tq][:], "x1g%d" % tq, hT2[:, :, tq * 128:(tq + 1) * 128], "hT2", 1, 3, b, ftmp2, 2)
                    for f2 in range(16):
                        pbk = 3 + (f2 % 2)
                        for ff in range(2):
                            f = f2 * 2 + ff
                            for k in range(DK):
                                S.op("pe", lambda e, pbk=pbk, ff=ff, f=f, k=k: e.matmul(
                                    banks[pbk][:, ff * G:(ff + 1) * G], lhsT=w1b[:, k, f * 128:(f + 1) * 128], rhs=hT2[:, k, :],
                                    start=(k == 0), stop=(k == DK - 1)), reads=["w1b", "hT2"], writes=[bk(pbk)])
                        S.op("act", lambda e, pbk=pbk, f2=f2: e.activation(out=rl[f2 % 2][:], in_=banks[pbk][:, :], func=AF.Relu),
                             writes=[bk(pbk), "rl%d" % (f2 % 2)])
                        S.op("dve" if f2 % 2 == 0 else "pool", lambda e, f2=f2: e.tensor_tensor(
                            out=hid[:, f2 * 2:f2 * 2 + 2, :], in0=rl[f2 % 2][:].rearrange("p (a t) -> p a t", a=2),
                            in1=rl[f2 % 2][:].rearrange("p (a t) -> p a t", a=2), op=ALU.mult), reads=["rl%d" % (f2 % 2)], writes=["hid"])
                    for tq in range(G // 128):
                        tsl = slice(g * G + tq * 128, g * G + (tq + 1) * 128)
                        for hh in range(2):
                            for f in range(32):
                                S.op("pe", lambda e, hh=hh, f=f, tq=tq: e.matmul(banks[5 + hh][:, :], lhsT=hid[:, f, tq * 128:(tq + 1) * 128],
                                                                                 rhs=w2b_[:, f, hh * 512:(hh + 1) * 512], start=(f == 0), stop=(f == 31)),
                                     reads=["hid", "w2b_"], writes=[bk(5 + hh)])
                        post_norm_residual((5, 6), gpost2, x1g[tq][:], "x1g%d" % tq, x1g[tq], "x1g%d" % tq, pn_tmp2)
                        S.dma("sp", lambda e, tq=tq, tsl=tsl: e.dma_start(out=out_d[b, tsl, :], in_=x1g[tq][:]), "out_st", reads=["x1g%d" % tq])

            for b in range(NB):
                make_gpost(b, 1, "mlp_post_g", gpost2)
                for g in range(T // G):
                    _pe_group(b, g)
        S.run(nc, es)
    return nc


def _perm_cols():
    idx = list(range(0, 1536))
    idx += list(range(1536, 1600)) + list(range(1664, 1728))
    idx += list(range(1600, 1664)) + list(range(1728, 1792))
    idx += list(range(1792, 1952)) + [-1] * 96
    idx += list(range(1952, 2976))
    return np.array(idx)


def prep_inputs(inp, NB, ncores):
    f = lambda a: np.ascontiguousarray(np.asarray(a, dtype=np.float32))
    perm = _perm_cols()
    w_in = f(inp["w_in"])[0]
    w_in_p = np.zeros((D, 3072), np.float32)
    w_in_p[:, perm >= 0] = w_in[:, perm[perm >= 0]]

    def permvec(v):
        o = np.zeros(2048, np.float32)
        p16 = perm[:2048]
        o[p16 >= 0] = v[p16[p16 >= 0]]
        return o.reshape(16, 128).T

    fm = lambda v: np.asarray(v, np.float32).reshape(-1, 128).T
    cfm = np.zeros((128, NCF), np.float32)

    def put(nm, arr):
        o, w = CF[nm]
        assert arr.shape == (128, w), (nm, arr.shape)
        cfm[:, o:o + w] = arr
    put("ada_b", fm(inp["ada_b"][0]))
    put("mix_pre_g", fm(inp["mix_pre_g"][0]))
    put("mlp_pre_g", fm(inp["mlp_pre_g"][0]))
    put("mu_prev", permvec(f(inp["mu_prev"])[0]))
    put("mu_next", permvec(f(inp["mu_next"])[0]))
    put("w0", np.concatenate([fm(inp["decay_w0"][0, 0]), fm(inp["decay_w0"][0, 1])], 1))
    put("a0", np.concatenate([fm(inp["iclr_a0"][0, 0]), fm(inp["iclr_a0"][0, 1])], 1))
    put("k_k", fm(inp["k_k"][0]))
    put("k_a", fm(inp["k_a"][0]))
    put("r_k", fm(np.asarray(inp["r_k"][0]).reshape(-1)))
    cw = f(inp["conv_w"])[0]
    put("conv_w", cw.T.reshape(4, 128, 31).transpose(1, 0, 2).reshape(128, 124))
    put("conv_b", fm(inp["conv_b"][0]))
    put("cln_w", fm(inp["conv_ln_w"][0]))
    put("cln_b", fm(inp["conv_ln_b"][0]))
    crow = np.zeros((NCR,), np.float32)
    for nm, v in (("ada_b", inp["ada_b"][0]), ("mix_post_g", inp["mix_post_g"][0]),
                  ("mlp_post_g", inp["mlp_post_g"][0]), ("lnx_w", inp["lnx_w"][0]), ("lnx_b", inp["lnx_b"][0])):
        o, w = CR[nm]
        crow[o:o + w] = np.asarray(v, np.float32)
    crow = np.ascontiguousarray(np.broadcast_to(crow[None, :], (128, NCR)))
    x = f(inp["x"])
    ctx = f(inp["ctx"])
    c = f(inp["c"])
    cc = f(inp["c_ctx"])
    shared = {"ada_w": f(inp["ada_w"])[0], "w_in": w_in_p, "cfm": cfm, "crow": crow,
              "decay_w2": f(inp["decay_w2"])[0], "iclr_a2": f(inp["iclr_a2"])[0], "gate_w2": f(inp["gate_w2"])[0],
              "w_out": f(inp["w_out"])[0], "mlp_w1": f(inp["mlp_w1"])[0], "mlp_w2": f(inp["mlp_w2"])[0]}
    maps = []
    for i in range(ncores):
        cb = np.concatenate([c[i * NB:(i + 1) * NB], cc[None, :]], 0)
        cT = np.ascontiguousarray(cb.T.reshape(DK, 128, NB + 1).transpose(1, 0, 2))
        m = dict(shared)
        m.update({"x": np.ascontiguousarray(x[i * NB:(i + 1) * NB]), "ctx": np.ascontiguousarray(ctx[i * NB:(i + 1) * NB]),
                  "cT": cT})
        maps.append(m)
    return maps


def kernel(**inputs):
    B, T, _ = inputs["x"].shape
    TC = inputs["ctx"].shape[1]
    NB = B // NCORES
    nc = build(NB, T, TC)
    maps = prep_inputs(inputs, NB, NCORES)
    res = run_bass_kernel_spmd(nc, maps, core_ids=list(range(NCORES)))
    return np.concatenate([np.asarray(r["out"]) for r in res.results], 0).astype(np.float32)
```

```python
from contextlib import ExitStack
import os
import numpy as np
import concourse.bass as bass
import concourse.mybir as mybir
from concourse.bass_utils import run_bass_kernel_spmd

F32 = mybir.dt.float32
BF16 = mybir.dt.bfloat16
AF = mybir.ActivationFunctionType
ALU = mybir.AluOpType
AX = mybir.AxisListType

ENGS = ("pe", "act", "dve", "pool", "sp")
CH = 30000
D = 1024
DK = 8
NCORES = 8
CDEC = 0.6065306597126334


class Sched:
    def __init__(self):
        self.q = {e: [] for e in ENGS}
        self.cnt = {e: 0 for e in ENGS}
        self.seen = {e: {} for e in ENGS}
        self.last_w = {}
        self.readers = {}
        self.dma_cnt = {}
        self.dma_keys = []
        self.fence_snap = None
        self.fenced = {e: True for e in ENGS}

    def fence(self):
        snap = [(e, self.cnt[e]) for e in ENGS if self.cnt[e] > 0]
        snap += [("dma:" + k, n) for k, n in self.dma_cnt.items()]
        self.fence_snap = snap
        self.fenced = {e: False for e in ENGS}
        self.last_w = {}
        self.readers = {}

    def _deps(self, eng, reads, writes):
        deps = set()
        if not self.fenced[eng]:
            self.fenced[eng] = True
            deps |= set(self.fence_snap)
        for k in reads:
            if k in self.last_w:
                deps.add(self.last_w[k])
        for k in writes:
            if k in self.last_w:
                deps.add(self.last_w[k])
            deps |= self.readers.get(k, set())
        need = {}
        for (e, s) in deps:
            need[e] = max(need.get(e, 0), s)
        waits = []
        for e, s in need.items():
            if e == "pe" and eng == "pe":
                continue
            if self.seen[eng].get(e, 0) >= s:
                continue
            self.seen[eng][e] = s
            waits.append((e, s))
        return waits

    def _commit(self, me, reads, writes):
        for k in reads:
            self.readers.setdefault(k, set()).add(me)
        for k in writes:
            self.last_w[k] = me
            self.readers[k] = set()

    def op(self, eng, fn, reads=(), writes=()):
        waits = self._deps(eng, reads, writes)
        self.cnt[eng] += 1
        self.q[eng].append(("op", waits, fn, self.cnt[eng]))
        self._commit((eng, self.cnt[eng]), reads, writes)

    def dma(self, eng, fn, semkey, reads=(), writes=()):
        waits = self._deps(eng, reads, writes)
        if semkey not in self.dma_cnt:
            self.dma_cnt[semkey] = 0
            self.dma_keys.append(semkey)
        self.dma_cnt[semkey] += 1
        self.q[eng].append(("dma", waits, fn, semkey))
        self._commit(("dma:" + semkey, self.dma_cnt[semkey]), reads, writes)

    def emit_engine(self, eng, engobj, sems, dma_sems):
        def do_wait(e, s):
            if e.startswith("dma:"):
                engobj.wait_ge(dma_sems[e[4:]], 16 * s)
            else:
                engobj.wait_ge(sems[e][(s - 1) // CH], ((s - 1) % CH) + 1)

        for item in self.q[eng]:
            for (e, s) in item[1]:
                do_wait(e, s)
            if item[0] == "op":
                item[2](engobj).then_inc(sems[eng][(item[3] - 1) // CH], 1)
            else:
                item[2](engobj).then_inc(dma_sems[item[3]], 16)
        if eng == "sp":
            for k, n in self.dma_cnt.items():
                engobj.wait_ge(dma_sems[k], 16 * n)

    def run(self, nc, es):
        sems = {e: [es.enter_context(nc.semaphore("s_%s_%d" % (e, i)))
                    for i in range(max(1, (self.cnt[e] + CH - 1) // CH))] for e in ENGS}
        dma_sems = {k: es.enter_context(nc.semaphore("d_%d" % i)) for i, k in enumerate(self.dma_keys)}
        block = es.enter_context(nc.Block())
        block.tensor(lambda e: self.emit_engine("pe", e, sems, dma_sems))
        block.scalar(lambda e: self.emit_engine("act", e, sems, dma_sems))
        block.vector(lambda e: self.emit_engine("dve", e, sems, dma_sems))
        block.gpsimd(lambda e: self.emit_engine("pool", e, sems, dma_sems))
        block.sync(lambda e: self.emit_engine("sp", e, sems, dma_sems))


class Arena:
    def __init__(self, nc, base, limit):
        self.nc, self.base, self.limit, self.off, self.n = nc, base, limit, base, 0

    def reset(self):
        self.off = self.base

    def t(self, shape, dt):
        nb = int(np.prod(shape[1:])) * (4 if dt == F32 else 2)
        nb = (nb + 63) // 64 * 64
        assert self.off + nb <= self.limit, ("SBUF overflow", self.off, nb, self.limit)
        self.n += 1
        h = self.nc.alloc_sbuf_tensor_at("a%d" % self.n, list(shape), dt, offset=self.off)
        self.off += nb
        return h


CF = {}
CR = {}


def _layout():
    off = 0
    for nm, w in (("ada_b", 48), ("mix_pre_g", 8), ("mlp_pre_g", 8), ("mu_prev", 16), ("mu_next", 16),
                  ("w0", 8), ("a0", 8), ("k_k", 4), ("k_a", 4), ("r_k", 4), ("conv_w", 124),
                  ("conv_b", 4), ("cln_w", 4), ("cln_b", 4)):
        CF[nm] = (off, w)
        off += w
    ncf = off
    off = 0
    for nm, w in (("ada_b", 6144), ("mix_post_g", 1024), ("mlp_post_g", 1024), ("lnx_w", 512), ("lnx_b", 512)):
        CR[nm] = (off, w)
        off += w
    return ncf, off


NCF, NCR = _layout()


def build(NB, T, TC, debug=False, PHASES=9):
    TT = TC + T
    NB1 = NB + 1
    nc = bass.Bass("TRN2", target_bir_lowering=False)
    dram = lambda n, s, dt, kind: nc.dram_tensor(n, list(s), dt, kind=kind).ap()
    x_d = dram("x", [NB, T, D], F32, "ExternalInput")
    ctx_d = dram("ctx", [NB, TC, D], F32, "ExternalInput")
    cT_d = dram("cT", [128, DK, NB1], F32, "ExternalInput")
    adaw_d = dram("ada_w", [D, 6144], F32, "ExternalInput")
    win_d = dram("w_in", [D, 3072], F32, "ExternalInput")
    cfm_d = dram("cfm", [128, NCF], F32, "ExternalInput")
    crow_d = dram("crow", [128, NCR], F32, "ExternalInput")
    w2d_d = dram("decay_w2", [2, 64, 512], F32, "ExternalInput")
    a2_d = dram("iclr_a2", [2, 64, 512], F32, "ExternalInput")
    gw2_d = dram("gate_w2", [160, 512], F32, "ExternalInput")
    wout_d = dram("w_out", [D, D], F32, "ExternalInput")
    w1_d = dram("mlp_w1", [D, 4096], F32, "ExternalInput")
    w2_d = dram("mlp_w2", [4096, D], F32, "ExternalInput")
    out_d = dram("out", [NB, T, D], F32, "ExternalOutput")
    skind = "ExternalOutput" if debug else "Internal"
    pa_d = dram("pa_s", [NB, 20, 128, TT], BF16, skind)
    y_d = dram("y_s", [NB, 2, T, 512], F32, skind)
    bo_d = dram("bo_s", [NB, 2, T, 512], F32, skind)

    S = Sched()
    es = ExitStack()
    with es:
        banks = [es.enter_context(nc.psum_tensor("bank%d" % i, [128, 512], F32)) for i in range(8)]
        bk = lambda i: "bank%d" % i
        P_ = Arena(nc, 17408, 51 * 1024)
        A_ = Arena(nc, 51 * 1024, 223 * 1024)

        cfm = P_.t([128, NCF], F32)
        crow = P_.t([128, NCR - 6144], F32)
        ident = P_.t([128, 128], BF16)
        identf = P_.t([128, 128], F32)
        bones = P_.t([128, 128], F32)
        hsel = P_.t([128, 2], BF16)
        msk = P_.t([128, 2, 2, 128], BF16)
        mskN = P_.t([128, 2, 64], BF16)
        identB = P_.t([128, 64], BF16)
        onesb = P_.t([128, 1], BF16)
        rstm = P_.t([128, 512], F32)
        modfm = P_.t([128, 48, NB1], F32)
        gates = P_.t([NB1, 2, 1024], F32)
        gm = P_.t([128, 2, DK, NB1], F32)
        c0 = P_.t([128, 16], F32)
        sel = P_.t([NB1, NB1, 128], F32)
        cf = lambda nm: cfm[:, CF[nm][0]:CF[nm][0] + CF[nm][1]]
        cr = lambda nm: crow[:, CR[nm][0] - 6144:CR[nm][0] - 6144 + CR[nm][1]]

        S.dma("sp", lambda e: e.dma_start(out=cfm[:], in_=cfm_d[:, :]), "c0", writes=["cfm"])
        S.dma("sp", lambda e: e.dma_start(out=crow[:], in_=crow_d[:, 6144:NCR]), "c0", writes=["crow"])
        S.op("pool", lambda e: e.memset(identf[:], 1.0), writes=["identf"])
        S.op("pool", lambda e: e.affine_select(out=identf[:], in_=identf[:], pattern=[[-1, 128]],
                                               compare_op=ALU.is_equal, fill=0.0, base=0, channel_multiplier=1),
             reads=["identf"], writes=["identf"])
        S.op("dve", lambda e: e.tensor_copy(out=ident[:], in_=identf[:]), reads=["identf"], writes=["ident"])
        S.op("pool", lambda e: e.memset(bones[:], 0.0), writes=["bones"])
        S.op("pool", lambda e: e.memset(bones[0:64, 0:64], 1.0), reads=["bones"], writes=["bones"])
        S.op("pool", lambda e: e.memset(bones[64:128, 64:128], 1.0), reads=["bones"], writes=["bones"])
        S.op("pool", lambda e: e.memset(hsel[:], 0.0), writes=["hsel"])
        S.op("pool", lambda e: e.memset(hsel[0:64, 0:1], 1.0), reads=["hsel"], writes=["hsel"])
        S.op("pool", lambda e: e.memset(hsel[64:128, 1:2], 1.0), reads=["hsel"], writes=["hsel"])
        mtmp = P_.t([64, 64], F32)

        def mk_mask(dst_ap, sign, strict):
            S.op("pool", lambda e: e.memset(mtmp[:], 1.0), reads=["mtmp"], writes=["mtmp"])
            S.op("pool", lambda e: e.affine_select(out=mtmp[:], in_=mtmp[:], pattern=[[sign, 64]],
                                                   compare_op=ALU.is_gt if strict else ALU.is_ge,
                                                   fill=0.0, base=0, channel_multiplier=-sign),
                 reads=["mtmp"], writes=["mtmp"])
            S.op("pool", lambda e: e.tensor_copy(out=dst_ap, in_=mtmp[:]), reads=["mtmp"], writes=["msk"])
        for d in range(2):
            sg_ = 1 if d == 0 else -1
            for rr in range(2):
                mk_mask(msk[0:64, d, rr, 0:64], sg_, True)
                mk_mask(msk[0:64, d, rr, 64:128], sg_, False)
            mk_mask(mskN[0:64, d, :], -sg_, True)
        S.dma("sp", lambda e: e.dma_start(out=msk[64:128], in_=msk[0:64]), "c0", reads=["msk"], writes=["msk"])
        S.dma("sp", lambda e: e.dma_start(out=mskN[64:128], in_=mskN[0:64]), "c0", reads=["msk"], writes=["msk"])
        S.op("dve", lambda e: e.tensor_tensor(out=identB[:], in0=ident[:, 0:64], in1=ident[:, 64:128], op=ALU.add), reads=["ident"], writes=["identB"])
        S.op("pool", lambda e: e.memset(onesb[:], 1.0), writes=["onesb"])
        S.op("pool", lambda e: e.memset(rstm[:], 1.0), writes=["rstm"])
        S.op("pool", lambda e: e.memset(rstm[:].rearrange("p (c t) -> p c t", t=64)[:, :, 0:1], 0.0),
             reads=["rstm"], writes=["rstm"])
        S.op("pool", lambda e: e.memset(sel[:], 0.0), writes=["sel"])
        for b in range(NB1):
            S.op("pool", lambda e, b=b: e.memset(sel[:, b, :], 1.0), reads=["sel"], writes=["sel"])
            S.op("pool", lambda e, b=b: e.affine_select(out=sel[:, b, :], in_=sel[:, b, :], pattern=[[0, 128]],
                                                        compare_op=ALU.is_equal, fill=0.0, base=-b,
                                                        channel_multiplier=1),
                 reads=["sel"], writes=["sel"])

        A_.reset()
        cT = A_.t([128, DK, NB1], F32)
        siluT = A_.t([128, DK, NB1], F32)
        adab_row = A_.t([NB1, 6144], F32)
        aw = [A_.t([128, DK, 512], F32) for _ in range(2)]
        S.dma("sp", lambda e: e.dma_start(out=cT[:], in_=cT_d[:, :, :]), "c0", writes=["cT"])
        S.dma("sp", lambda e: e.dma_start(out=adab_row[:], in_=crow_d[0:NB1, 0:6144]), "c0", writes=["adab_row"])
        S.op("act", lambda e: e.activation(out=siluT[:], in_=cT[:], func=AF.Silu), reads=["cT"], writes=["siluT"])
        for n in range(12):
            a = aw[n % 2]
            ak = "aw%d" % (n % 2)
            S.dma("sp", lambda e, a=a, n=n: e.dma_start(
                out=a[:], in_=adaw_d[:, n * 512:(n + 1) * 512].rearrange("(k p) n -> p k n", p=128)),
                ak, writes=[ak])
            m = n // 2
            if m in (2, 5):
                for k in range(DK):
                    S.op("pe", lambda e, a=a, k=k: e.matmul(banks[0][0:NB1, :], lhsT=siluT[:, k, :], rhs=a[:, k, :],
                                                            start=(k == 0), stop=(k == DK - 1)),
                         reads=[ak, "siluT"], writes=[bk(0)])
                gi = 0 if m == 2 else 1
                S.op("dve", lambda e, n=n, gi=gi: e.tensor_tensor(
                    out=gates[:, gi, (n % 2) * 512:(n % 2) * 512 + 512], in0=banks[0][0:NB1, :],
                    in1=adab_row[:, n * 512:(n + 1) * 512], op=ALU.add),
                    reads=["adab_row"], writes=[bk(0), "gates"])
            else:
                for j in range(4):
                    for k in range(DK):
                        S.op("pe", lambda e, a=a, k=k, j=j: e.matmul(
                            banks[1][:, j * NB1:(j + 1) * NB1], lhsT=a[:, k, j * 128:(j + 1) * 128],
                            rhs=siluT[:, k, :], start=(k == 0), stop=(k == DK - 1)),
                            reads=[ak, "siluT"], writes=[bk(1)])
                S.op("dve", lambda e, n=n: e.tensor_tensor(
                    out=modfm[:, n * 4:(n + 1) * 4, :],
                    in0=banks[1][:, 0:4 * NB1].rearrange("p (j b) -> p j b", b=NB1),
                    in1=cf("ada_b")[:, n * 4:(n + 1) * 4].unsqueeze(2).to_broadcast([128, 4, NB1]), op=ALU.add),
                    reads=["cfm"], writes=[bk(1), "modfm"])
        for gi, (gn, m) in enumerate((("mix_pre_g", 1), ("mlp_pre_g", 4))):
            S.op("dve", lambda e, gi=gi, m=m: e.tensor_scalar(out=gm[:, gi], in0=modfm[:, m * 8:(m + 1) * 8, :],
                                                              scalar1=1.0, scalar2=None, op0=ALU.add),
                 reads=["modfm"], writes=["gm"])
            S.op("dve", lambda e, gi=gi, gn=gn: e.tensor_tensor(
                out=gm[:, gi], in0=gm[:, gi], in1=cf(gn).unsqueeze(2).to_broadcast([128, DK, NB1]), op=ALU.mult),
                reads=["gm", "cfm"], writes=["gm"])
        S.op("dve", lambda e: e.tensor_tensor(out=c0[:], in0=cf("mu_prev"), in1=cf("mu_next"), op=ALU.add),
             reads=["cfm"], writes=["c0"])
        S.op("dve", lambda e: e.tensor_scalar(out=c0[:], in0=c0[:], scalar1=-1.0, scalar2=1.0, op0=ALU.mult,
                                              op1=ALU.add), reads=["c0"], writes=["c0"])

        def front(xt_ap, xkey, hT_ap, hkey, gi, shm, b, tmp, pbank):
            S.op("act", lambda e: e.activation(out=tmp["sq"][:], in_=xt_ap, func=AF.Square),
                 reads=[xkey], writes=["f_sq"])
            S.op("dve", lambda e: e.reduce_sum(out=tmp["ss"][:], in_=tmp["sq"][:], axis=AX.X),
                 reads=["f_sq"], writes=["f_ss"])
            S.op("dve", lambda e: e.tensor_scalar(out=tmp["ss"][:], in0=tmp["ss"][:], scalar1=1.0 / D, scalar2=1e-6,
                                                  op0=ALU.mult, op1=ALU.add), reads=["f_ss"], writes=["f_ss"])
            S.op("act", lambda e: e.activation(out=tmp["ss"][:], in_=tmp["ss"][:], func=AF.Sqrt),
                 reads=["f_ss"], writes=["f_ss"])
            S.op("dve", lambda e: e.reciprocal(out=tmp["ss"][:], in_=tmp["ss"][:]), reads=["f_ss"], writes=["f_ss"])
            S.op("act", lambda e: e.activation(out=tmp["xn"][:], in_=xt_ap, func=AF.Copy, scale=tmp["ss"][:, 0:1]),
                 reads=[xkey, "f_ss"], writes=["f_xn"])
            pb = banks[pbank].bitcast(BF16)
            for k in range(DK):
                S.op("pe", lambda e, k=k: e.transpose(pb[:, k * 128:(k + 1) * 128], tmp["xn"][:, k * 128:(k + 1) * 128],
                                                      ident[:]), reads=["f_xn", "ident"], writes=[bk(pbank)])
            S.op("dve", lambda e: e.tensor_tensor(out=hT_ap, in0=pb[:, 0:1024].rearrange("p (k t) -> p k t", t=128),
                                                  in1=gm[:, gi, :, b:b + 1].to_broadcast([128, DK, 128]), op=ALU.mult),
                 reads=["gm"], writes=[bk(pbank), hkey])
            S.op("dve", lambda e: e.tensor_tensor(
                out=hT_ap, in0=hT_ap, in1=modfm[:, shm * 8:(shm + 1) * 8, b:b + 1].to_broadcast([128, DK, 128]),
                op=ALU.add), reads=["modfm", hkey], writes=[hkey])

        S.fence()
        A_.reset()
        winb = A_.t([128, DK, 3072], BF16)
        stg = [A_.t([128, DK, 512], F32) for _ in range(2)]
        for n in range(6):
            s_ = stg[n % 2]
            sk = "stg%d" % (n % 2)
            S.dma("sp", lambda e, s_=s_, n=n: e.dma_start(
                out=s_[:], in_=win_d[:, n * 512:(n + 1) * 512].rearrange("(k p) n -> p k n", p=128)), sk, writes=[sk])
            S.op("dve" if n % 2 == 0 else "act",
                 (lambda e, s_=s_, n=n: e.tensor_copy(out=winb[:, :, n * 512:(n + 1) * 512], in_=s_[:])) if n % 2 == 0
                 else (lambda e, s_=s_, n=n: e.activation(out=winb[:, :, n * 512:(n + 1) * 512], in_=s_[:], func=AF.Copy)),
                 reads=[sk], writes=["winb"])
        TM = max(T, TC)
        hT = A_.t([128, DK, TM + 2], BF16)
        xt = [A_.t([128, D], F32) for _ in range(2)]
        ftmp = {"sq": A_.t([128, D], F32), "ss": A_.t([128, 1], F32), "xn": A_.t([128, D], BF16)}
        etmp = [A_.t([128, 512], F32) for _ in range(2)]
        obuf = [A_.t([128, 512], BF16) for _ in range(3)]
        A_sg = [A_.t([128, 512], F32) for _ in range(4)]
        S.op("pool", lambda e: e.memset(hT[:], 0.0), writes=["hT"])
        xi = 0
        ob_i = 0
        pb_i = 0
        for b in range(NB):
            for (src, Ts, toff, bmod, tiles) in ((ctx_d, TC, 0, NB, list(range(4, 14))),
                                                 (x_d, T, TC, b, list(range(0, 24)))):
                if Ts < TM:
                    S.op("pool", lambda e, Ts=Ts: e.memset(hT[:, :, Ts + 1:Ts + 2], 0.0), reads=["hT"], writes=["hT"])
                for tt in range(Ts // 128):
                    xa = xt[xi % 2]
                    xk = "xt%d" % (xi % 2)
                    xi += 1
                    S.dma("sp", lambda e, xa=xa, src=src, b=b, tt=tt: e.dma_start(
                        out=xa[:], in_=src[b, tt * 128:(tt + 1) * 128, :]), xk, writes=[xk])
                    front(xa[:], xk, hT[:, :, 1 + tt * 128:1 + (tt + 1) * 128], "hT", 0, 0, bmod, ftmp, 7)
                w0 = 0
                while w0 < Ts:
                    n = min(510, Ts - w0)
                    sg_ready = {}
                    for j in [jj for jj in tiles if jj >= 20] + [jj for jj in tiles if jj < 20]:
                        pbk = pb_i % 4
                        pb_i += 1
                        pbt = banks[pbk]
                        for k in range(DK):
                            S.op("pe", lambda e, pbt=pbt, k=k, j=j, w0=w0, n=n: e.matmul(
                                pbt[:, 0:n + 2], lhsT=winb[:, k, j * 128:(j + 1) * 128], rhs=hT[:, k, w0:w0 + n + 2],
                                start=(k == 0), stop=(k == DK - 1)), reads=["winb", "hT"], writes=[bk(pbk)])
                        if j >= 20:
                            sgt = A_sg[j - 20]
                            S.op("act", lambda e, pbt=pbt, sgt=sgt, n=n: e.activation(
                                out=sgt[:, 0:n], in_=pbt[:, 1:n + 1], func=AF.Sigmoid),
                                writes=[bk(pbk), "sg%d" % (j - 20)])
                            continue
                        ob = obuf[ob_i % 3]
                        ok = "ob%d" % (ob_i % 3)
                        ob_i += 1
                        if j >= 16:
                            sgt = A_sg[j - 16]
                            S.op("dve", lambda e, pbt=pbt, sgt=sgt, ob=ob, n=n: e.tensor_tensor(
                                out=ob[:, 0:n], in0=pbt[:, 1:n + 1], in1=sgt[:, 0:n], op=ALU.mult),
                                reads=["sg%d" % (j - 16)], writes=[bk(pbk), ok])
                        else:
                            et = etmp[j % 2]
                            ek = "et%d" % (j % 2)
                            S.op("act", lambda e, pbt=pbt, et=et, n=n, j=j: e.activation(
                                out=et[:, 0:n], in_=pbt[:, 1:n + 1], func=AF.Copy, scale=c0[:, j:j + 1]),
                                reads=["c0"], writes=[bk(pbk), ek])
                            S.op("dve", lambda e, pbt=pbt, et=et, n=n, j=j: e.scalar_tensor_tensor(
                                out=et[:, 0:n], in0=pbt[:, 0:n], scalar=cf("mu_prev")[:, j:j + 1], in1=et[:, 0:n],
                                op0=ALU.mult, op1=ALU.add), reads=["cfm", ek], writes=[bk(pbk), ek])
                            S.op("dve", lambda e, pbt=pbt, et=et, ob=ob, n=n, j=j: e.scalar_tensor_tensor(
                                out=ob[:, 0:n], in0=pbt[:, 2:n + 2], scalar=cf("mu_next")[:, j:j + 1], in1=et[:, 0:n],
                                op0=ALU.mult, op1=ALU.add), reads=["cfm", ek], writes=[bk(pbk), ok])
                        S.dma("sp", lambda e, ob=ob, b=b, j=j, toff=toff, w0=w0, n=n: e.dma_start(
                            out=pa_d[b, j, :, toff + w0:toff + w0 + n], in_=ob[:, 0:n]), "pa_st", reads=[ok])
                    w0 += n
        if PHASES >= 2:
            S.fence()
            A_.reset()
            W = 128
            NCW = W // 64
            lwb = A_.t([128, 2, 512], BF16)
            omk = A_.t([128, 4], F32)
            rkb = A_.t([128, 4], BF16)
            mark_pb = A_.off
            wst = A_.t([128, 2, 512], F32)
            S.dma("sp", lambda e: e.dma_start(out=wst[0:64], in_=w2d_d.rearrange("d r f -> r d f")), "pbw", writes=["wst"])
            S.dma("sp", lambda e: e.dma_start(out=wst[64:128], in_=a2_d.rearrange("d r f -> r d f")), "pbw", writes=["wst"])
            S.op("dve", lambda e: e.tensor_copy(out=lwb[:], in_=wst[:]), reads=["wst"], writes=["lwb"])
            S.op("dve", lambda e: e.tensor_scalar(out=omk[:], in0=cf("k_a"), scalar1=-1.0, scalar2=1.0, op0=ALU.mult, op1=ALU.add), reads=["cfm"], writes=["omk"])
            S.op("dve", lambda e: e.tensor_copy(out=rkb[:], in_=cf("r_k")), reads=["cfm"], writes=["rkb"])
            S.fence()
            A_.off = mark_pb
            rs = A_.t([128, 4, TT], BF16)
            ks = A_.t([128, 4, TT], BF16)
            vs = A_.t([128, 4, TT], BF16)
            wdad = A_.t([128, 2, TT], BF16)
            f32t = lambda: A_.t([128, 4, W], F32)
            TMP = [dict(sig=f32t(), Ls=f32t(), ee=f32t(), t1=f32t(), t2=f32t(), icl=A_.t([128, 4, W], BF16),
                        SC=A_.t([128, 4, NCW], F32)) for _ in range(2)]
            ar = [A_.t([128, 4, NCW, 2, 64], BF16) for _ in range(6)]
            bkt = [A_.t([128, 4, NCW, 2, 64], BF16) for _ in range(6)]
            prod = [A_.t([128, 4, W], BF16) for _ in range(6)]
            eLC = [A_.t([128, 4, NCW], F32) for _ in range(6)]
            H32 = [A_.t([128, 4, 64], F32) for _ in range(2)]
            Hbf = [A_.t([128, 4, 64], BF16) for _ in range(2)]
            NJS = 4 * NCW
            btk = [A_.t([128, 512], BF16) for _ in range(NJS)]
            vtm = [A_.t([128, 4, 64], BF16) for _ in range(NJS)]
            AT = [A_.t([128, 4, 2, 128], BF16) for _ in range(NJS)]
            XT = [A_.t([128, 4, 64], BF16) for _ in range(NJS)]
            PQm = [[A_.t([128, 2, 4, 64], BF16) for _ in range(2)] for _ in range(2 * NCW)]
            Xm = [[A_.t([128, 4, 64], BF16) for _ in range(2)] for _ in range(2 * NCW)]
            Rsb = [A_.t([128, 4, 64], BF16) for _ in range(2)]
            Usb = [A_.t([128, 4, 64], BF16) for _ in range(2)]
            ybuf = [A_.t([128, 4, 64], F32) for _ in range(2)]
            bosb = [A_.t([128, 4, 64], F32) for _ in range(2)]
            bon = [A_.t([128, 4], F32) for _ in range(2)]
            K_ = lambda nm, d: "%s%d" % (nm, d)

            def prep(b, d, w0, par):
                dp = d * 3 + par
                sig, Ls, ee, t1, t2, icl, SC = [TMP[d][k_] for k_ in ("sig", "Ls", "ee", "t1", "t2", "icl", "SC")]
                kS, kL, kE, k1, k2, kI, kC = [K_(k_, d) for k_ in ("sig", "Ls", "ee", "t1", "t2", "icl", "SC")]
                PB_ = 6 + d
                pbv = lambda i: banks[PB_][:, i * W:(i + 1) * W]
                pb4 = banks[PB_][:, 0:4 * W].rearrange("p (a t) -> p a t", t=W)
                v5 = lambda tns, a: tns[:].rearrange("p i (c t) -> p i c t", t=64) if a is None else tns[:, :, :, a, :]
                for i in range(4):
                    S.op("pe", lambda e, i=i: e.matmul(pbv(i), lhsT=lwb[0:64, d, i * 128:(i + 1) * 128], rhs=wdad[0:64, d, w0:w0 + W],
                                                       start=True, stop=True), reads=["lwb", "wdad"], writes=[bk(PB_)])
                for i in range(4):
                    S.op("act", lambda e, i=i: e.activation(out=sig[:, i, :], in_=pbv(i), func=AF.Sigmoid,
                                                            bias=cf("w0")[:, d * 4 + i:d * 4 + i + 1]), reads=["cfm"], writes=[bk(PB_), kS])
                yield
                for i in range(4):
                    S.op("pe", lambda e, i=i: e.matmul(pbv(i), lhsT=lwb[64:128, d, i * 128:(i + 1) * 128], rhs=wdad[64:128, d, w0:w0 + W],
                                                       start=True, stop=True), reads=["lwb", "wdad"], writes=[bk(PB_)])
                for i in range(4):
                    S.op("act", lambda e, i=i: e.activation(out=icl[:, i, :], in_=pbv(i), func=AF.Sigmoid,
                                                            bias=cf("a0")[:, d * 4 + i:d * 4 + i + 1]), reads=["cfm"], writes=[bk(PB_), kI])
                yield
                for i in range(4):
                    S.op("dve", lambda e, i=i: e.tensor_scalar(out=t1[:, i, :], in0=ks[:, i, w0:w0 + W], scalar1=cf("k_k")[:, i:i + 1],
                                                               scalar2=None, op0=ALU.mult), reads=["ks", "cfm"], writes=[k1])
                S.op("pool", lambda e: e.tensor_tensor(out=t2[:], in0=t1[:], in1=t1[:], op=ALU.mult), reads=[k1], writes=[k2])
                yield
                for i in range(4):
                    S.op("pe", lambda e, i=i: e.matmul(pbv(i), lhsT=bones[:], rhs=t2[:, i, :], start=True, stop=True),
                         reads=["bones", k2], writes=[bk(PB_)])
                S.op("dve", lambda e: e.tensor_scalar(out=ee[:], in0=pb4, scalar1=1e-24, scalar2=None, op0=ALU.max),
                     writes=[bk(PB_), kE])
                yield
                S.op("act", lambda e: e.activation(out=ee[:], in_=ee[:], func=AF.Sqrt), reads=[kE], writes=[kE])
                yield
                S.op("dve", lambda e: e.reciprocal(out=ee[:], in_=ee[:]), reads=[kE], writes=[kE])
                yield
                S.op("pool", lambda e: e.tensor_tensor(out=t1[:], in0=t1[:], in1=ee[:], op=ALU.mult), reads=[k1, kE], writes=[k1])
                for i in range(4):
                    S.op("dve", lambda e, i=i: e.tensor_tensor_scan(out=Ls[:, i, :], data0=rstm[:, 0:W], data1=sig[:, i, :],
                                                                    initial=0.0, op0=ALU.mult, op1=ALU.add),
                         reads=["rstm", kS], writes=[kL])
                lsc = v5(Ls, None)
                S.op("dve", lambda e: e.tensor_copy(out=SC[:], in_=lsc[:, :, :, 63]), reads=[kL], writes=[kC])
                yield
                S.op("act", lambda e: e.activation(out=eLC[dp][:], in_=SC[:], func=AF.Exp, scale=-CDEC), reads=[kC], writes=[K_("eLC", dp)])
                if d == 0:
                    S.op("dve", lambda e: e.tensor_tensor(out=sig[:], in0=Ls[:], in1=sig[:], op=ALU.subtract), reads=[kL, kS], writes=[kS])
                    XE, kXE, XI, kXI = sig, kS, Ls, kL
                else:
                    S.op("dve", lambda e: e.tensor_tensor(out=lsc, in0=SC[:].unsqueeze(3).to_broadcast([128, 4, NCW, 64]),
                                                          in1=lsc, op=ALU.subtract), reads=[kL, kC], writes=[kL])
                    S.op("dve", lambda e: e.tensor_tensor(out=sig[:], in0=Ls[:], in1=sig[:], op=ALU.add), reads=[kL, kS], writes=[kS])
                    XE, kXE, XI, kXI = Ls, kL, sig, kS
                yield
                S.op("act", lambda e: e.activation(out=ee[:], in_=XE[:], func=AF.Exp, scale=-CDEC), reads=[kXE], writes=[kE])
                yield
                S.op("dve", lambda e: e.scalar_tensor_tensor(out=v5(ar[dp], 0), in0=v5(t1, None), scalar=-1.0, in1=v5(ee, None),
                                                             op0=ALU.mult, op1=ALU.mult), reads=[k1, kE], writes=[K_("ar", dp)])
                yield
                S.op("act", lambda e: e.activation(out=ee[:], in_=XI[:], func=AF.Exp, scale=-CDEC), reads=[kXI], writes=[kE])
                yield
                S.op("dve", lambda e: e.tensor_tensor(out=v5(ar[dp], 1), in0=rs[:, :, w0:w0 + W].rearrange("p i (c t) -> p i c t", t=64),
                                                      in1=v5(ee, None), op=ALU.mult), reads=["rs", kE], writes=[K_("ar", dp)])
                yield
                S.op("act", lambda e: e.activation(out=ee[:], in_=XI[:], func=AF.Exp, scale=CDEC), reads=[kXI], writes=[kE])
                S.op("pool", lambda e: e.tensor_tensor(out=t2[:], in0=t1[:], in1=icl[:], op=ALU.mult), reads=[k1, kI], writes=[k2])
                yield
                S.op("dve", lambda e: e.tensor_tensor(out=v5(bkt[dp], 0), in0=v5(t2, None), in1=v5(ee, None), op=ALU.mult),
                     reads=[k2, kE], writes=[K_("bkt", dp)])
                yield
                for i in range(4):
                    S.op("dve", lambda e, i=i: e.tensor_scalar(out=t2[:, i, :], in0=icl[:, i, :], scalar1=cf("k_a")[:, i:i + 1],
                                                               scalar2=omk[:, i:i + 1], op0=ALU.mult, op1=ALU.add),
                         reads=[kI, "cfm", "omk"], writes=[k2])
                yield
                S.op("pool", lambda e: e.tensor_tensor(out=t2[:], in0=t2[:], in1=ks[:, :, w0:w0 + W], op=ALU.mult), reads=[k2, "ks"], writes=[k2])
                yield
                S.op("dve", lambda e: e.tensor_tensor(out=v5(bkt[dp], 1), in0=v5(t2, None), in1=v5(ee, None), op=ALU.mult),
                     reads=[k2, kE], writes=[K_("bkt", dp)])
                yield
                S.op("pool", lambda e: e.tensor_tensor(out=prod[dp][:], in0=t2[:], in1=rs[:, :, w0:w0 + W], op=ALU.mult),
                     reads=[k2, "rs"], writes=[K_("prod", dp)])
                yield

            HP = [((h % 2) * 64, h // 2) for h in range(8)]
            V3 = lambda bi: banks[bi][:, 0:256].rearrange("p (i s) -> p i s", s=64)

            def inv(b, d, w0, c, par3, js, jt, B):
                tk0 = w0 + c * 64
                dp = d * 3 + par3
                arK, bkK = K_("ar", dp), K_("bkt", dp)
                pT = banks[B].bitcast(BF16)
                for q in range(2):
                    for (po, i) in HP:
                        S.op("pe", lambda e, q=q, po=po, i=i: e.transpose(
                            pT[po:po + 64, (q * 4 + i) * 64:(q * 4 + i + 1) * 64], bkt[dp][po:po + 64, i, c, q, :],
                            ident[po:po + 64, po:po + 64]), reads=[bkK, "ident"], writes=[bk(B)])
                for (po, i) in HP:
                    S.op("pe", lambda e, po=po, i=i: e.transpose(pT[po:po + 64, 512 + i * 64:512 + (i + 1) * 64],
                                                                 vs[po:po + 64, i, tk0:tk0 + 64], ident[po:po + 64, po:po + 64]),
                         reads=["vs", "ident"], writes=[bk(B)])
                S.op("dve", lambda e: e.tensor_copy(out=btk[js][:], in_=pT[:, 0:512]), writes=[bk(B), K_("btk", js)])
                S.op("act", lambda e: e.activation(out=vtm[js][:].rearrange("p i v -> p (i v)"), in_=pT[:, 512:768], func=AF.Copy),
                     writes=[bk(B), K_("vtm", js)])
                yield
                for bb in range(2):
                    psA = banks[B][:, :].rearrange("p (i r t) -> p i r t", i=2, r=2)
                    for (po, i) in HP:
                        if i // 2 != bb:
                            continue
                        rhs = ar[dp][po:po + 64, i, c, :, :].rearrange("p a t -> p (a t)")
                        for r_ in range(2):
                            S.op("pe", lambda e, r_=r_, po=po, i=i, rhs=rhs, psA=psA: e.matmul(
                                psA[po:po + 64, i % 2, r_, :], lhsT=bkt[dp][po:po + 64, i, c, r_, :], rhs=rhs, start=True, stop=True),
                                reads=[bkK, arK], writes=[bk(B)])
                    S.op("dve", lambda e, bb=bb, psA=psA: e.tensor_tensor(
                        out=AT[js][:, bb * 2:bb * 2 + 2], in0=psA, in1=msk[:, d:d + 1].to_broadcast([128, 2, 2, 128]), op=ALU.mult),
                        reads=["msk"], writes=[bk(B), K_("AT", js)])
                    yield
                psN = V3(B)
                for (po, i) in HP:
                    S.op("pe", lambda e, po=po, i=i: e.matmul(psN[po:po + 64, i, :], lhsT=ar[dp][po:po + 64, i, c, 0, :],
                                                               rhs=bkt[dp][po:po + 64, i, c, 0, :], start=True, stop=True),
                         reads=[bkK, arK], writes=[bk(B)])
                PQ, X = PQm[jt], Xm[jt]
                S.op("dve", lambda e: e.tensor_tensor(out=PQ[0][:, 0], in0=psN, in1=mskN[:, d:d + 1].to_broadcast([128, 4, 64]), op=ALU.mult),
                     reads=["msk"], writes=[bk(B), K_("PQ0", jt)])
                S.op("act", lambda e: e.activation(out=PQ[0][:, 1], in_=AT[js][:, :, 0, 0:64], func=AF.Copy),
                     reads=[K_("AT", js)], writes=[K_("PQ0", jt)])
                S.op("dve", lambda e: e.tensor_tensor(out=X[0][:], in0=AT[js][:, :, 0, 0:64],
                                                      in1=identB[:].unsqueeze(1).to_broadcast([128, 4, 64]), op=ALU.add),
                     reads=[K_("AT", js), "identB"], writes=[K_("X0", jt)])
                yield
                cur = 0
                for j in range(1, 6):
                    nxt = 1 - cur
                    psPQ = banks[B][:, :].rearrange("p (a i s) -> p a i s", a=2, s=64)
                    na = 2 if j < 5 else 1
                    for a_ in range(na):
                        for (po, i) in HP:
                            S.op("pe", lambda e, po=po, i=i, cur=cur, a_=a_, psPQ=psPQ: e.matmul(
                                psPQ[po:po + 64, a_, i, :], lhsT=PQ[cur][po:po + 64, 1 - a_, i, :], rhs=PQ[cur][po:po + 64, a_, i, :],
                                start=True, stop=True), reads=[K_("PQ%d" % cur, jt)], writes=[bk(B)])
                    if False:
                        S.op("dve", lambda e, nxt=nxt, na=na, psPQ=psPQ: e.tensor_copy(out=PQ[nxt][:, 0:na], in_=psPQ[:, 0:na]),
                             writes=[bk(B), K_("PQ%d" % nxt, jt)])
                    else:
                        S.op("act", lambda e, nxt=nxt, na=na, psPQ=psPQ: e.activation(out=PQ[nxt][:, 0:na], in_=psPQ[:, 0:na], func=AF.Copy),
                             writes=[bk(B), K_("PQ%d" % nxt, jt)])
                    yield
                    psX = V3(B)
                    for (po, i) in HP:
                        S.op("pe", lambda e, po=po, i=i, cur=cur, nxt=nxt, psX=psX: e.matmul(
                            psX[po:po + 64, i, :], lhsT=PQ[nxt][po:po + 64, 0, i, :], rhs=X[cur][po:po + 64, i, :], start=True, stop=True),
                            reads=[K_("PQ%d" % nxt, jt), K_("X%d" % cur, jt)], writes=[bk(B)])
                    xo_, xok_ = (XT[js], K_("XT", js)) if j == 5 else (X[nxt], K_("X%d" % nxt, jt))
                    S.op("dve", lambda e, cur=cur, xo_=xo_, psX=psX: e.tensor_tensor(out=xo_[:], in0=psX, in1=X[cur][:], op=ALU.add),
                         reads=[K_("X%d" % cur, jt)], writes=[bk(B), xok_])
                    yield
                    cur = nxt

            def chain(b, d, w0, c, is_lat, par3, js, B):
                tk0 = w0 + c * 64
                dp = d * 3 + par3
                arK, bkK = K_("ar", dp), K_("bkt", dp)
                btm = btk[js][:, 0:256].rearrange("p (i k) -> p i k", k=64)
                ktm = btk[js][:, 256:512].rearrange("p (i k) -> p i k", k=64)
                ATj, vt, XTj = AT[js], vtm[js], XT[js]
                psR = V3(B)
                for (po, i) in HP:
                    S.op("pe", lambda e, po=po, i=i: e.matmul(psR[po:po + 64, i, :], lhsT=ar[dp][po:po + 64, i, c, 0, :],
                                                               rhs=Hbf[d][po:po + 64, i, :], start=True, stop=False),
                         reads=[arK, K_("Hbf", d)], writes=[bk(B)])
                    S.op("pe", lambda e, po=po, i=i: e.matmul(psR[po:po + 64, i, :], lhsT=ATj[po:po + 64, i, 1, 0:64],
                                                               rhs=vt[po:po + 64, i, :], start=False, stop=True),
                         reads=[K_("AT", js), K_("vtm", js)], writes=[bk(B)])
                S.op("act", lambda e: e.activation(out=Rsb[d][:], in_=psR, func=AF.Copy), writes=[bk(B), K_("Rsb", d)])
                yield
                for (po, i) in HP:
                    S.op("pe", lambda e, po=po, i=i: e.matmul(psR[po:po + 64, i, :], lhsT=XTj[po:po + 64, i, :],
                                                               rhs=Rsb[d][po:po + 64, i, :], start=True, stop=True),
                         reads=[K_("XT", js), K_("Rsb", d)], writes=[bk(B)])
                S.op("dve", lambda e: e.tensor_copy(out=Usb[d][:], in_=psR), writes=[bk(B), K_("Usb", d)])
                yield
                psH = V3(B)
                for (po, i) in HP:
                    S.op("pe", lambda e, po=po, i=i: e.matmul(psH[po:po + 64, i, :], lhsT=btm[po:po + 64, i, :],
                                                               rhs=Usb[d][po:po + 64, i, :], start=True, stop=False),
                         reads=[K_("btk", js), K_("Usb", d)], writes=[bk(B)])
                    S.op("pe", lambda e, po=po, i=i: e.matmul(psH[po:po + 64, i, :], lhsT=ktm[po:po + 64, i, :],
                                                               rhs=vt[po:po + 64, i, :], start=False, stop=True),
                         reads=[K_("btk", js), K_("vtm", js)], writes=[bk(B)])
                if is_lat:
                    psY = banks[B][:, 256:512].rearrange("p (i s) -> p i s", s=64)
                    for (po, i) in HP:
                        S.op("pe", lambda e, po=po, i=i: e.matmul(psY[po:po + 64, i, :], lhsT=ar[dp][po:po + 64, i, c, 1, :],
                                                                   rhs=Hbf[d][po:po + 64, i, :], start=True, stop=False),
                             reads=[arK, K_("Hbf", d)], writes=[bk(B)])
                        S.op("pe", lambda e, po=po, i=i: e.matmul(psY[po:po + 64, i, :], lhsT=ATj[po:po + 64, i, 0, 64:128],
                                                                   rhs=Usb[d][po:po + 64, i, :], start=False, stop=False),
                             reads=[K_("AT", js), K_("Usb", d)], writes=[bk(B)])
                        S.op("pe", lambda e, po=po, i=i: e.matmul(psY[po:po + 64, i, :], lhsT=ATj[po:po + 64, i, 1, 64:128],
                                                                   rhs=vt[po:po + 64, i, :], start=False, stop=True),
                             reads=[K_("AT", js), K_("vtm", js)], writes=[bk(B)])
                S.op("dve", lambda e: e.tensor_tensor(out=H32[d][:], in0=psH, in1=H32[d][:], op=ALU.add),
                     reads=[K_("H32", d)], writes=[bk(B), K_("H32", d)])
                S.op("dve", lambda e: e.tensor_tensor(out=H32[d][:], in0=H32[d][:],
                                                      in1=eLC[dp][:, :, c:c + 1].to_broadcast([128, 4, 64]), op=ALU.mult),
                     reads=[K_("H32", d), K_("eLC", dp)], writes=[K_("H32", d)])
                S.op("act", lambda e: e.activation(out=Hbf[d][:], in_=H32[d][:], func=AF.Copy), reads=[K_("H32", d)], writes=[K_("Hbf", d)])
                if is_lat:
                    S.op("act", lambda e: e.activation(out=ybuf[d][:], in_=psY, func=AF.Copy), writes=[bk(B), K_("ybuf", d)])
                    tl = tk0 - TC
                    for hp_ in range(2):
                        S.dma("sp", lambda e, tl=tl, hp_=hp_: e.dma_start(
                            out=y_d[b, d, tl:tl + 64, :].rearrange("t (i hp v) -> t i hp v", hp=2, v=64)[:, :, hp_, :],
                            in_=ybuf[d][hp_ * 64:(hp_ + 1) * 64]), "y_st", reads=[K_("ybuf", d)])
                    yield
                    psB = banks[B]
                    for (po, i) in HP:
                        S.op("pe", lambda e, po=po, i=i: e.matmul(psB[po:po + 64, i:i + 1], lhsT=prod[dp][po:po + 64, i, c * 64:(c + 1) * 64],
                                                                   rhs=rkb[po:po + 64, i:i + 1], start=True, stop=True),
                             reads=[K_("prod", dp), "rkb"], writes=[bk(B)])
                    S.op("dve", lambda e: e.tensor_scalar(out=bon[d][:], in0=psB[:, 0:4], scalar1=0.5, scalar2=None, op0=ALU.mult),
                         writes=[bk(B), K_("bon", d)])
                    S.op("dve", lambda e: e.tensor_tensor(out=bosb[d][:], in0=vt[:],
                                                          in1=bon[d][:].unsqueeze(2).to_broadcast([128, 4, 64]), op=ALU.mult),
                         reads=[K_("vtm", js), K_("bon", d)], writes=[K_("bosb", d)])
                    for hp_ in range(2):
                        S.dma("sp", lambda e, tl=tl, hp_=hp_: e.dma_start(
                            out=bo_d[b, d, tl:tl + 64, :].rearrange("t (i hp v) -> t i hp v", hp=2, v=64)[:, :, hp_, :],
                            in_=bosb[d][hp_ * 64:(hp_ + 1) * 64]), "bo_st", reads=[K_("bosb", d)])
                yield

            def lockstep(gens):
                gens = list(gens)
                while gens:
                    for g_ in list(gens):
                        try:
                            next(g_)
                        except StopIteration:
                            gens.remove(g_)

            def pb_batch(b):
                for (dst, j0, key) in ((rs, 0, "rs"), (ks, 4, "ks"), (vs, 8, "vs")):
                    S.dma("sp", lambda e, dst=dst, j0=j0: e.dma_start(out=dst[:], in_=pa_d[b, j0:j0 + 4].rearrange("j p t -> p j t")),
                          "pb_ld_" + key, writes=[key])
                S.dma("sp", lambda e: e.dma_start(out=wdad[:], in_=pa_d[b, 12:14].rearrange("j p t -> p j t")), "pb_ld_w", writes=["wdad"])
                S.op("act", lambda e: e.activation(out=wdad[0:64], in_=wdad[0:64], func=AF.Tanh), reads=["wdad"], writes=["wdad"])
                for d in range(2):
                    S.op("pool", lambda e, d=d: e.memset(H32[d][:], 0.0), writes=[K_("H32", d)])
                    S.op("pool", lambda e, d=d: e.memset(Hbf[d][:], 0.0), writes=[K_("Hbf", d)])
                cw_ = [(w * W, False) for w in range(TC // W)]
                lw_ = [(TC + w * W, True) for w in range(T // W)]
                sched = {0: cw_ + lw_, 1: list(reversed(cw_)) + list(reversed(lw_))}
                nw_ = len(sched[0])

                def preps(wi):
                    return [prep(b, d, sched[d][wi][0], wi % 3) for d in range(2)]

                def jobs(wi):
                    for d in range(2):
                        for cc in range(NCW):
                            c = cc if d == 0 else NCW - 1 - cc
                            yield d, cc, c, (wi % 2) * 2 * NCW + d * NCW + cc, d * NCW + cc

                def chains(wi):
                    for cc in range(NCW):
                        gens = []
                        for (d, cc_, c, js, jt) in jobs(wi):
                            if cc_ == cc:
                                gens.append(chain(b, d, sched[d][wi][0], c, sched[d][wi][1], wi % 3, js, 2 * NCW + d))
                        while gens:
                            for g_ in list(gens):
                                try:
                                    next(g_)
                                except StopIteration:
                                    gens.remove(g_)
                            yield

                lockstep(preps(0))
                for t in range(nw_ + 1):
                    gl = []
                    if t >= 1:
                        gl.append(chains(t - 1))
                    if t < nw_:
                        gl += [inv(b, d, sched[d][t][0], c, t % 3, js, jt, jt) for (d, cc, c, js, jt) in jobs(t)]
                    if t + 1 < nw_:
                        gl += preps(t + 1)
                    lockstep(gl)

            for b in range(NB):
                pb_batch(b)

        def post_norm_residual(pbs, gpost, xres, xkey, outt, okey, tmp, kp="pn"):
            for hh in range(2):
                S.op("act", lambda e, hh=hh: e.activation(out=tmp["sq"][:, hh * 512:(hh + 1) * 512], in_=banks[pbs[hh]][:, :], func=AF.Square),
                     writes=[bk(pbs[hh]), kp + "_sq"])
            S.op("dve", lambda e: e.reduce_sum(out=tmp["ss"][:], in_=tmp["sq"][:], axis=AX.X), reads=[kp + "_sq"], writes=[kp + "_ss"])
            S.op("dve", lambda e: e.tensor_scalar(out=tmp["ss"][:], in0=tmp["ss"][:], scalar1=1.0 / D, scalar2=1e-6,
                                                  op0=ALU.mult, op1=ALU.add), reads=[kp + "_ss"], writes=[kp + "_ss"])
            S.op("act", lambda e: e.activation(out=tmp["ss"][:], in_=tmp["ss"][:], func=AF.Sqrt), reads=[kp + "_ss"], writes=[kp + "_ss"])
            S.op("dve", lambda e: e.reciprocal(out=tmp["ss"][:], in_=tmp["ss"][:]), reads=[kp + "_ss"], writes=[kp + "_ss"])
            for hh in range(2):
                S.op("dve", lambda e, hh=hh: e.scalar_tensor_tensor(
                    out=tmp["sq"][:, hh * 512:(hh + 1) * 512], in0=banks[pbs[hh]][:, :], scalar=tmp["ss"][:, 0:1],
                    in1=gpost[:, hh * 512:(hh + 1) * 512], op0=ALU.mult, op1=ALU.mult),
                    reads=[kp + "_ss", "gpost"], writes=[bk(pbs[hh]), kp + "_sq"])
            S.op("pool", lambda e: e.tensor_tensor(out=outt[:], in0=tmp["sq"][:], in1=xres, op=ALU.add), reads=[kp + "_sq", xkey], writes=[okey])

        def make_gpost(b, gi, rowname, gpost):
            for hh in range(2):
                S.op("pe", lambda e, hh=hh: e.matmul(banks[hh][:, :], lhsT=sel[:, b, :], rhs=gates[:, gi, hh * 512:(hh + 1) * 512],
                                                     start=True, stop=True), reads=["sel", "gates"], writes=[bk(hh)])
                S.op("dve", lambda e, hh=hh: e.tensor_tensor(out=gpost[:, hh * 512:(hh + 1) * 512], in0=banks[hh][:, :],
                                                             in1=cr(rowname)[:, hh * 512:(hh + 1) * 512], op=ALU.mult),
                     reads=["crow"], writes=[bk(hh), "gpost"])

        if PHASES >= 3:
            S.fence()
            A_.reset()
            woutb = A_.t([128, DK, D], BF16)
            gw2b = A_.t([128, 2, 512], BF16)
            stg = [A_.t([128, DK, 512], F32) for _ in range(2)]
            for n in range(2):
                S.dma("sp", lambda e, n=n: e.dma_start(out=stg[n][:], in_=wout_d[:, n * 512:(n + 1) * 512].rearrange("(k p) n -> p k n", p=128)),
                      "stgd%d" % n, writes=["stgd%d" % n])
                S.op("dve", lambda e, n=n: e.tensor_copy(out=woutb[:, :, n * 512:(n + 1) * 512], in_=stg[n][:]), reads=["stgd%d" % n], writes=["woutb"])
            gst = A_.t([128, 2, 512], F32)
            S.dma("sp", lambda e: e.dma_start(out=gst[:, 0, :], in_=gw2_d[0:128, :]), "gst", writes=["gst"])
            S.dma("sp", lambda e: e.dma_start(out=gst[0:32, 1, :], in_=gw2_d[128:160, :]), "gst", writes=["gst"])
            S.op("dve", lambda e: e.tensor_copy(out=gw2b[:, 0, :], in_=gst[:, 0, :]), reads=["gst"], writes=["gw2b"])
            S.op("dve", lambda e: e.tensor_copy(out=gw2b[0:32, 1, :], in_=gst[0:32, 1, :]), reads=["gst"], writes=["gw2b"])
            S.fence()
            A_.off -= 2 * 16384 + 4096
            onesf = A_.t([128, 128], F32)
            S.op("pool", lambda e: e.memset(onesf[:], 1.0), writes=["onesf"])
            ub = A_.t([128, 4, T], BF16)
            yc = A_.t([128, 4, T], F32)
            convo = A_.t([128, 4, T], BF16)
            gdb = A_.t([128, 2, T], BF16)
            lsq = A_.t([128, 4, 512], F32)
            lmean = A_.t([128, 512], F32)
            lrstd = A_.t([128, 512], F32)
            ltmp = A_.t([128, 512], F32)
            yin = [[A_.t([128, 512], F32) for _ in range(4)] for _ in range(2)]
            ysq_ = [A_.t([128, 512], F32) for _ in range(2)]
            st8_ = [A_.t([128, 4, 8], F32) for _ in range(2)]
            rwb_ = [A_.t([128, 512], BF16) for _ in range(2)]
            mixT_ = [A_.t([128, 4, 128], BF16) for _ in range(2)]
            xt2 = [A_.t([128, D], F32) for _ in range(2)]
            gpost = A_.t([128, D], F32)
            pn_tmp_ = [{"sq": A_.t([128, D], F32), "ss": A_.t([128, 1], F32)} for _ in range(2)]
            cw = cf("conv_w")
            ti_ = [0]

            def _pd_batch(b):
                make_gpost(b, 0, "mix_post_g", gpost)
                S.dma("sp", lambda e, b=b: e.dma_start(out=ub[:], in_=pa_d[b, 16:20, :, TC:TT].rearrange("j p t -> p j t")), "pd_ld", writes=["ub"])
                S.dma("sp", lambda e, b=b: e.dma_start(out=gdb[:], in_=pa_d[b, 14:16, :, TC:TT].rearrange("j p t -> p j t")), "pd_ld2", writes=["gdb"])
                S.op("act", lambda e: e.activation(out=gdb[:, 0, :], in_=gdb[:, 0, :], func=AF.Sigmoid), reads=["gdb"], writes=["gdb"])
                S.op("act", lambda e: e.activation(out=gdb[0:32, 1, :], in_=gdb[0:32, 1, :], func=AF.Sigmoid), reads=["gdb"], writes=["gdb"])
                for i in range(4):
                    u4 = ub[:, i, :].rearrange("p (r t) -> p r t", t=64)
                    y4 = yc[:, i, :].rearrange("p (r t) -> p r t", t=64)
                    S.op("dve", lambda e, i=i: e.tensor_scalar(out=yc[:, i, :], in0=ub[:, i, :], scalar1=cw[:, i * 31 + 15:i * 31 + 16],
                                                               scalar2=cf("conv_b")[:, i:i + 1], op0=ALU.mult, op1=ALU.add),
                         reads=["ub", "cfm"], writes=["yc%d" % i])
                    for j in range(31):
                        o = j - 15
                        if o == 0:
                            continue
                        lo_o, hi_o = max(0, -o), 64 - max(0, o)
                        lo_i, hi_i = max(0, o), 64 - max(0, -o)
                        S.op("dve", lambda e, i=i, j=j, u4=u4, y4=y4, lo_o=lo_o, hi_o=hi_o, lo_i=lo_i, hi_i=hi_i: e.scalar_tensor_tensor(
                            out=y4[:, :, lo_o:hi_o], in0=u4[:, :, lo_i:hi_i], scalar=cw[:, i * 31 + j:i * 31 + j + 1],
                            in1=y4[:, :, lo_o:hi_o], op0=ALU.mult, op1=ALU.add), reads=["ub", "cfm", "yc%d" % i], writes=["yc%d" % i])
                for w in range(T // 512 if T >= 512 else 1):
                    wn = min(512, T)
                    ws = slice(w * 512, w * 512 + wn)
                    for i in range(4):
                        S.op("pe", lambda e, i=i, ws=ws, wn=wn: e.matmul(banks[2][:, 0:wn], lhsT=onesf[:], rhs=yc[:, i, ws], start=(i == 0), stop=(i == 3)),
                             reads=["onesf", "yc%d" % i], writes=[bk(2)])
                    S.op("act", lambda e, ws=ws, wn=wn: e.activation(out=lsq[:, :, 0:wn], in_=yc[:, :, ws], func=AF.Square),
                         reads=["yc0", "yc1", "yc2", "yc3"], writes=["lsq"])
                    for i in range(4):
                        S.op("pe", lambda e, i=i, wn=wn: e.matmul(banks[3][:, 0:wn], lhsT=onesf[:], rhs=lsq[:, i, 0:wn], start=(i == 0), stop=(i == 3)),
                             reads=["onesf", "lsq"], writes=[bk(3)])
                    S.op("dve", lambda e, wn=wn: e.tensor_scalar(out=lmean[:, 0:wn], in0=banks[2][:, 0:wn], scalar1=1.0 / 512, scalar2=None, op0=ALU.mult),
                         writes=[bk(2), "lmean"])
                    S.op("dve", lambda e, wn=wn: e.tensor_tensor(out=ltmp[:, 0:wn], in0=lmean[:, 0:wn], in1=lmean[:, 0:wn], op=ALU.mult),
                         reads=["lmean"], writes=["ltmp"])
                    S.op("dve", lambda e, wn=wn: e.scalar_tensor_tensor(out=lrstd[:, 0:wn], in0=banks[3][:, 0:wn], scalar=1.0 / 512, in1=ltmp[:, 0:wn],
                                                                        op0=ALU.mult, op1=ALU.subtract), reads=["ltmp"], writes=[bk(3), "lrstd"])
                    S.op("dve", lambda e, wn=wn: e.tensor_scalar(out=lrstd[:, 0:wn], in0=lrstd[:, 0:wn], scalar1=1e-5, scalar2=None, op0=ALU.add),
                         reads=["lrstd"], writes=["lrstd"])
                    S.op("act", lambda e, wn=wn: e.activation(out=lrstd[:, 0:wn], in_=lrstd[:, 0:wn], func=AF.Sqrt), reads=["lrstd"], writes=["lrstd"])
                    S.op("dve", lambda e, wn=wn: e.reciprocal(out=lrstd[:, 0:wn], in_=lrstd[:, 0:wn]), reads=["lrstd"], writes=["lrstd"])
                    S.op("dve", lambda e, ws=ws, wn=wn: e.tensor_tensor(out=lsq[:, :, 0:wn], in0=yc[:, :, ws],
                                                                        in1=lmean[:, 0:wn].unsqueeze(1).to_broadcast([128, 4, wn]), op=ALU.subtract),
                         reads=["yc0", "yc1", "yc2", "yc3", "lmean"], writes=["lsq"])
                    S.op("dve", lambda e, wn=wn: e.tensor_tensor(out=lsq[:, :, 0:wn], in0=lsq[:, :, 0:wn],
                                                                 in1=lrstd[:, 0:wn].unsqueeze(1).to_broadcast([128, 4, wn]), op=ALU.mult),
                         reads=["lsq", "lrstd"], writes=["lsq"])
                    for i in range(4):
                        S.op("act", lambda e, i=i, ws=ws, wn=wn: e.activation(out=convo[:, i, ws], in_=lsq[:, i, 0:wn], func=AF.Silu,
                                                                              scale=cf("cln_w")[:, i:i + 1], bias=cf("cln_b")[:, i:i + 1]),
                             reads=["lsq", "cfm"], writes=["convo"])
                for tt in range(0, T // 128, 2):
                    gens = [_pd_tile(b, tt + q_, q_) for q_ in range(2) if tt + q_ < T // 128]
                    while gens:
                        for g_ in list(gens):
                            try:
                                next(g_)
                            except StopIteration:
                                gens.remove(g_)

            def _pd_tile(b, tt, sl):
                if True:
                    tsl = slice(tt * 128, (tt + 1) * 128)
                    ti = sl
                    BG, BT, BO0, BO1 = 4 * sl, 4 * sl + 1, 4 * sl + 2, 4 * sl + 3
                    ysq, st8, rwb, mixT = ysq_[sl], st8_[sl], rwb_[sl], mixT_[sl]
                    pn_tmp = pn_tmp_[sl]
                    sk = lambda nm: "%s_%d" % (nm, sl)
                    yy = yin[ti % 2]
                    yk = "yin%d" % (ti % 2)
                    xa, xk2 = xt2[ti % 2], "xt2_%d" % (ti % 2)
                    for q, src in enumerate((y_d, y_d, bo_d, bo_d)):
                        S.dma("sp", lambda e, q=q, src=src, yy=yy, tsl=tsl: e.dma_start(out=yy[q][:], in_=src[b, q % 2, tsl, :]), yk, writes=[yk])
                    S.dma("sp", lambda e, xa=xa, tsl=tsl: e.dma_start(out=xa[:], in_=x_d[b, tsl, :]), xk2, writes=[xk2])
                    S.op("pool", lambda e, yy=yy: e.tensor_tensor(out=yy[0][:], in0=yy[0][:], in1=yy[1][:], op=ALU.add), reads=[yk], writes=[yk])
                    S.op("pool", lambda e, yy=yy: e.tensor_tensor(out=yy[2][:], in0=yy[2][:], in1=yy[3][:], op=ALU.add), reads=[yk], writes=[yk])
                    yield
                    y3 = yy[0][:].rearrange("p (h v) -> p h v", v=64)
                    S.op("dve", lambda e, y3=y3: e.reduce_sum(out=st8[:, 0, :], in_=y3, axis=AX.X), reads=[yk], writes=[sk("st8")])
                    S.op("act", lambda e, yy=yy: e.activation(out=ysq[:], in_=yy[0][:], func=AF.Square), reads=[yk], writes=[sk("ysq")])
                    S.op("dve", lambda e: e.reduce_sum(out=st8[:, 1, :], in_=ysq[:].rearrange("p (h v) -> p h v", v=64), axis=AX.X),
                         reads=[sk("ysq")], writes=[sk("st8")])
                    S.op("dve", lambda e: e.tensor_scalar(out=st8[:, 0:2, :], in0=st8[:, 0:2, :], scalar1=1.0 / 64, scalar2=None, op0=ALU.mult),
                         reads=[sk("st8")], writes=[sk("st8")])
                    yield
                    S.op("dve", lambda e: e.tensor_tensor(out=st8[:, 2, :], in0=st8[:, 0, :], in1=st8[:, 0, :], op=ALU.mult), reads=[sk("st8")], writes=[sk("st8")])
                    S.op("dve", lambda e: e.tensor_tensor(out=st8[:, 3, :], in0=st8[:, 1, :], in1=st8[:, 2, :], op=ALU.subtract), reads=[sk("st8")], writes=[sk("st8")])
                    S.op("dve", lambda e: e.tensor_scalar(out=st8[:, 3, :], in0=st8[:, 3, :], scalar1=64e-5, scalar2=None, op0=ALU.add),
                         reads=[sk("st8")], writes=[sk("st8")])
                    S.op("act", lambda e: e.activation(out=st8[:, 3, :], in_=st8[:, 3, :], func=AF.Sqrt), reads=[sk("st8")], writes=[sk("st8")])
                    S.op("dve", lambda e: e.reciprocal(out=st8[:, 3, :], in_=st8[:, 3, :]), reads=[sk("st8")], writes=[sk("st8")])
                    yield
                    S.op("dve", lambda e, y3=y3: e.tensor_tensor(out=y3, in0=y3, in1=st8[:, 0, :].unsqueeze(2).to_broadcast([128, 8, 64]), op=ALU.subtract),
                         reads=[yk, sk("st8")], writes=[yk])
                    S.op("dve", lambda e, y3=y3: e.tensor_tensor(out=y3, in0=y3, in1=st8[:, 3, :].unsqueeze(2).to_broadcast([128, 8, 64]), op=ALU.mult),
                         reads=[yk, sk("st8")], writes=[yk])
                    S.op("pool", lambda e, yy=yy: e.tensor_tensor(out=yy[0][:], in0=yy[0][:], in1=cr("lnx_w"), op=ALU.mult), reads=[yk, "crow"], writes=[yk])
                    S.op("pool", lambda e, yy=yy: e.tensor_tensor(out=yy[0][:], in0=yy[0][:], in1=cr("lnx_b"), op=ALU.add), reads=[yk, "crow"], writes=[yk])
                    S.op("pool", lambda e, yy=yy: e.tensor_tensor(out=yy[0][:], in0=yy[0][:], in1=yy[2][:], op=ALU.add), reads=[yk], writes=[yk])
                    yield
                    S.op("pe", lambda e, tsl=tsl: e.matmul(banks[BG][:, :], lhsT=gdb[:, 0, tsl], rhs=gw2b[:, 0, :], start=True, stop=False),
                         reads=["gdb", "gw2b"], writes=[bk(BG)])
                    S.op("pe", lambda e, tsl=tsl: e.matmul(banks[BG][:, :], lhsT=gdb[0:32, 1, tsl], rhs=gw2b[0:32, 1, :], start=False, stop=True),
                         reads=["gdb", "gw2b"], writes=[bk(BG)])
                    S.op("dve", lambda e, yy=yy: e.tensor_tensor(out=rwb[:], in0=yy[0][:], in1=banks[BG][:, :], op=ALU.mult),
                         reads=[yk], writes=[bk(BG), sk("rwb")])
                    yield
                    pT = banks[BT].bitcast(BF16)
                    for j in range(4):
                        S.op("pe", lambda e, j=j: e.transpose(pT[:, j * 128:(j + 1) * 128], rwb[:, j * 128:(j + 1) * 128], ident[:]),
                             reads=[sk("rwb"), "ident"], writes=[bk(BT)])
                    S.op("act", lambda e: e.activation(out=mixT[:].rearrange("p j t -> p (j t)"), in_=pT[:, 0:512], func=AF.Copy),
                         writes=[bk(BT), sk("mixT")])
                    yield
                    for hh in range(2):
                        for j in range(8):
                            lhs = mixT[:, j, :] if j < 4 else convo[:, j - 4, tsl]
                            S.op("pe", lambda e, hh=hh, j=j, lhs=lhs: e.matmul(banks[BO0 + hh][:, :], lhsT=lhs, rhs=woutb[:, j, hh * 512:(hh + 1) * 512],
                                                                               start=(j == 0), stop=(j == 7)),
                                 reads=[sk("mixT"), "convo", "woutb"], writes=[bk(BO0 + hh)])
                    yield
                    post_norm_residual((BO0, BO1), gpost, xa[:], xk2, xa, xk2, pn_tmp, sk("pn"))
                    S.dma("sp", lambda e, xa=xa, tsl=tsl: e.dma_start(out=out_d[b, tsl, :], in_=xa[:]), "x1_st", reads=[xk2])

            for b in range(NB):
                _pd_batch(b)

        if PHASES >= 4:
            S.fence()
            A_.reset()
            w1b = A_.t([128, DK, 4096], BF16)
            w2b_ = A_.t([128, 32, D], BF16)
            mark = A_.off
            stg = [A_.t([128, DK, 512], F32) for _ in range(2)]
            for n in range(8):
                S.dma("sp", lambda e, n=n: e.dma_start(out=stg[n % 2][:], in_=w1_d[:, n * 512:(n + 1) * 512].rearrange("(k p) n -> p k n", p=128)),
                      "stge%d" % (n % 2), writes=["stge%d" % (n % 2)])
                if n % 2 == 0:
                    S.op("dve", lambda e, n=n: e.tensor_copy(out=w1b[:, :, n * 512:(n + 1) * 512], in_=stg[n % 2][:]), reads=["stge%d" % (n % 2)], writes=["w1b"])
                else:
                    S.op("act", lambda e, n=n: e.activation(out=w1b[:, :, n * 512:(n + 1) * 512], in_=stg[n % 2][:], func=AF.Copy),
                         reads=["stge%d" % (n % 2)], writes=["w1b"])
            for n in range(8):
                S.dma("sp", lambda e, n=n: e.dma_start(out=stg[n % 2][:].rearrange("p k n -> p (k n)").rearrange("p (f n) -> p f n", n=D),
                                                       in_=w2_d[n * 512:(n + 1) * 512, :].rearrange("(f p) n -> p f n", p=128)),
                      "stge%d" % (n % 2), writes=["stge%d" % (n % 2)])
                src = stg[n % 2][:].rearrange("p k n -> p (k n)").rearrange("p (f n) -> p f n", n=D)
                if n % 2 == 0:
                    S.op("dve", lambda e, n=n, src=src: e.tensor_copy(out=w2b_[:, n * 4:(n + 1) * 4, :], in_=src), reads=["stge%d" % (n % 2)], writes=["w2b_"])
                else:
                    S.op("act", lambda e, n=n, src=src: e.activation(out=w2b_[:, n * 4:(n + 1) * 4, :], in_=src, func=AF.Copy),
                         reads=["stge%d" % (n % 2)], writes=["w2b_"])
            S.fence()
            A_.off = mark
            G = 256
            hT2 = A_.t([128, DK, G], BF16)
            hid = A_.t([128, 32, G], BF16)
            rl = [A_.t([128, 512], F32) for _ in range(2)]
            x1g = [A_.t([128, D], F32) for _ in range(2)]
            gpost2 = A_.t([128, D], F32)
            ftmp2 = {"sq": A_.t([128, D], F32), "ss": A_.t([128, 1], F32), "xn": A_.t([128, D], BF16)}
            pn_tmp2 = {"sq": ftmp2["sq"], "ss": A_.t([128, 1], F32)}
            def _pe_group(b, g):
                if True:
                    for tq in range(G // 128):
                        tsl = slice(g * G + tq * 128, g * G + (tq + 1) * 128)
                        S.dma("sp", lambda e, tq=tq, tsl=tsl: e.dma_start(out=x1g[tq][:], in_=out_d[b, tsl, :]), "x1g%d" % tq, writes=["x1g%d" % tq])
                        front(x1g[tq][:], "x1g%d" % tq, hT2[:, :, tq * 128:(tq + 1) * 128], "hT2", 1, 3, b, ftmp2, 2)
                    for f2 in range(16):
                        pbk = 3 + (f2 % 2)
                        for ff in range(2):
                            f = f2 * 2 + ff
                            for k in range(DK):
                                S.op("pe", lambda e, pbk=pbk, ff=ff, f=f, k=k: e.matmul(
                                    banks[pbk][:, ff * G:(ff + 1) * G], lhsT=w1b[:, k, f * 128:(f + 1) * 128], rhs=hT2[:, k, :],
                                    start=(k == 0), stop=(k == DK - 1)), reads=["w1b", "hT2"], writes=[bk(pbk)])
                        S.op("act", lambda e, pbk=pbk, f2=f2: e.activation(out=rl[f2 % 2][:], in_=banks[pbk][:, :], func=AF.Relu),
                             writes=[bk(pbk), "rl%d" % (f2 % 2)])
                        S.op("dve" if f2 % 2 == 0 else "pool", lambda e, f2=f2: e.tensor_tensor(
                            out=hid[:, f2 * 2:f2 * 2 + 2, :], in0=rl[f2 % 2][:].rearrange("p (a t) -> p a t", a=2),
                            in1=rl[f2 % 2][:].rearrange("p (a t) -> p a t", a=2), op=ALU.mult), reads=["rl%d" % (f2 % 2)], writes=["hid"])
                    for tq in range(G // 128):
                        tsl = slice(g * G + tq * 128, g * G + (tq + 1) * 128)
                        for hh in range(2):
                            for f in range(32):
                                S.op("pe", lambda e, hh=hh, f=f, tq=tq: e.matmul(banks[5 + hh][:, :], lhsT=hid[:, f, tq * 128:(tq + 1) * 128],
                                                                                 rhs=w2b_[:, f, hh * 512:(hh + 1) * 512], start=(f == 0), stop=(f == 31)),
                                     reads=["hid", "w2b_"], writes=[bk(5 + hh)])
                        post_norm_residual((5, 6), gpost2, x1g[tq][:], "x1g%d" % tq, x1g[tq], "x1g%d" % tq, pn_tmp2)
                        S.dma("sp", lambda e, tq=tq, tsl=tsl: e.dma_start(out=out_d[b, tsl, :], in_=x1g[tq][:]), "out_st", reads=["x1g%d" % tq])

            for b in range(NB):
                make_gpost(b, 1, "mlp_post_g", gpost2)
                for g in range(T // G):
                    _pe_group(b, g)
        S.run(nc, es)
    return nc


def _perm_cols():
    idx = list(range(0, 1536))
    idx += list(range(1536, 1600)) + list(range(1664, 1728))
    idx += list(range(1600, 1664)) + list(range(1728, 1792))
    idx += list(range(1792, 1952)) + [-1] * 96
    idx += list(range(1952, 2976))
    return np.array(idx)


def prep_inputs(inp, NB, ncores):
    f = lambda a: np.ascontiguousarray(np.asarray(a, dtype=np.float32))
    perm = _perm_cols()
    w_in = f(inp["w_in"])[0]
    w_in_p = np.zeros((D, 3072), np.float32)
    w_in_p[:, perm >= 0] = w_in[:, perm[perm >= 0]]

    def permvec(v):
        o = np.zeros(2048, np.float32)
        p16 = perm[:2048]
        o[p16 >= 0] = v[p16[p16 >= 0]]
        return o.reshape(16, 128).T

    fm = lambda v: np.asarray(v, np.float32).reshape(-1, 128).T
    cfm = np.zeros((128, NCF), np.float32)

    def put(nm, arr):
        o, w = CF[nm]
        assert arr.shape == (128, w), (nm, arr.shape)
        cfm[:, o:o + w] = arr
    put("ada_b", fm(inp["ada_b"][0]))
    put("mix_pre_g", fm(inp["mix_pre_g"][0]))
    put("mlp_pre_g", fm(inp["mlp_pre_g"][0]))
    put("mu_prev", permvec(f(inp["mu_prev"])[0]))
    put("mu_next", permvec(f(inp["mu_next"])[0]))
    put("w0", np.concatenate([fm(inp["decay_w0"][0, 0]), fm(inp["decay_w0"][0, 1])], 1))
    put("a0", np.concatenate([fm(inp["iclr_a0"][0, 0]), fm(inp["iclr_a0"][0, 1])], 1))
    put("k_k", fm(inp["k_k"][0]))
    put("k_a", fm(inp["k_a"][0]))
    put("r_k", fm(np.asarray(inp["r_k"][0]).reshape(-1)))
    cw = f(inp["conv_w"])[0]
    put("conv_w", cw.T.reshape(4, 128, 31).transpose(1, 0, 2).reshape(128, 124))
    put("conv_b", fm(inp["conv_b"][0]))
    put("cln_w", fm(inp["conv_ln_w"][0]))
    put("cln_b", fm(inp["conv_ln_b"][0]))
    crow = np.zeros((NCR,), np.float32)
    for nm, v in (("ada_b", inp["ada_b"][0]), ("mix_post_g", inp["mix_post_g"][0]),
                  ("mlp_post_g", inp["mlp_post_g"][0]), ("lnx_w", inp["lnx_w"][0]), ("lnx_b", inp["lnx_b"][0])):
        o, w = CR[nm]
        crow[o:o + w] = np.asarray(v, np.float32)
    crow = np.ascontiguousarray(np.broadcast_to(crow[None, :], (128, NCR)))
    x = f(inp["x"])
    ctx = f(inp["ctx"])
    c = f(inp["c"])
    cc = f(inp["c_ctx"])
    shared = {"ada_w": f(inp["ada_w"])[0], "w_in": w_in_p, "cfm": cfm, "crow": crow,
              "decay_w2": f(inp["decay_w2"])[0], "iclr_a2": f(inp["iclr_a2"])[0], "gate_w2": f(inp["gate_w2"])[0],
              "w_out": f(inp["w_out"])[0], "mlp_w1": f(inp["mlp_w1"])[0], "mlp_w2": f(inp["mlp_w2"])[0]}
    maps = []
    for i in range(ncores):
        cb = np.concatenate([c[i * NB:(i + 1) * NB], cc[None, :]], 0)
        cT = np.ascontiguousarray(cb.T.reshape(DK, 128, NB + 1).transpose(1, 0, 2))
        m = dict(shared)
        m.update({"x": np.ascontiguousarray(x[i * NB:(i + 1) * NB]), "ctx": np.ascontiguousarray(ctx[i * NB:(i + 1) * NB]),
                  "cT": cT})
        maps.append(m)
    return maps


def kernel(**inputs):
    B, T, _ = inputs["x"].shape
    TC = inputs["ctx"].shape[1]
    NB = B // NCORES
    nc = build(NB, T, TC)
    maps = prep_inputs(inputs, NB, NCORES)
    res = run_bass_kernel_spmd(nc, maps, core_ids=list(range(NCORES)))
    return np.concatenate([np.asarray(r["out"]) for r in res.results], 0).astype(np.float32)
```

```python
from contextlib import ExitStack
import os
import numpy as np
import concourse.bass as bass
import concourse.mybir as mybir
from concourse.bass_utils import run_bass_kernel_spmd

F32 = mybir.dt.float32
BF16 = mybir.dt.bfloat16
AF = mybir.ActivationFunctionType
ALU = mybir.AluOpType
AX = mybir.AxisListType

ENGS = ("pe", "act", "dve", "pool", "sp")
CH = 30000
D = 1024
DK = 8
NCORES = 8
CDEC = 0.6065306597126334


class Sched:
    def __init__(self):
        self.q = {e: [] for e in ENGS}
        self.cnt = {e: 0 for e in ENGS}
        self.seen = {e: {} for e in ENGS}
        self.last_w = {}
        self.readers = {}
        self.dma_cnt = {}
        self.dma_keys = []
        self.fence_snap = None
        self.fenced = {e: True for e in ENGS}

    def fence(self):
        snap = [(e, self.cnt[e]) for e in ENGS if self.cnt[e] > 0]
        snap += [("dma:" + k, n) for k, n in self.dma_cnt.items()]
        self.fence_snap = snap
        self.fenced = {e: False for e in ENGS}
        self.last_w = {}
        self.readers = {}

    def _deps(self, eng, reads, writes):
        deps = set()
        if not self.fenced[eng]:
            self.fenced[eng] = True
            deps |= set(self.fence_snap)
        for k in reads:
            if k in self.last_w:
                deps.add(self.last_w[k])
        for k in writes:
            if k in self.last_w:
                deps.add(self.last_w[k])
            deps |= self.readers.get(k, set())
        need = {}
        for (e, s) in deps:
            need[e] = max(need.get(e, 0), s)
        waits = []
        for e, s in need.items():
            if e == "pe" and eng == "pe":
                continue
            if self.seen[eng].get(e, 0) >= s:
                continue
            self.seen[eng][e] = s
            waits.append((e, s))
        return waits

    def _commit(self, me, reads, writes):
        for k in reads:
            self.readers.setdefault(k, set()).add(me)
        for k in writes:
            self.last_w[k] = me
            self.readers[k] = set()

    def op(self, eng, fn, reads=(), writes=()):
        waits = self._deps(eng, reads, writes)
        self.cnt[eng] += 1
        self.q[eng].append(("op", waits, fn, self.cnt[eng]))
        self._commit((eng, self.cnt[eng]), reads, writes)

    def dma(self, eng, fn, semkey, reads=(), writes=()):
        waits = self._deps(eng, reads, writes)
        if semkey not in self.dma_cnt:
            self.dma_cnt[semkey] = 0
            self.dma_keys.append(semkey)
        self.dma_cnt[semkey] += 1
        self.q[eng].append(("dma", waits, fn, semkey))
        self._commit(("dma:" + semkey, self.dma_cnt[semkey]), reads, writes)

    def emit_engine(self, eng, engobj, sems, dma_sems):
        def do_wait(e, s):
            if e.startswith("dma:"):
                engobj.wait_ge(dma_sems[e[4:]], 16 * s)
            else:
                engobj.wait_ge(sems[e][(s - 1) // CH], ((s - 1) % CH) + 1)

        for item in self.q[eng]:
            for (e, s) in item[1]:
                do_wait(e, s)
            if item[0] == "op":
                item[2](engobj).then_inc(sems[eng][(item[3] - 1) // CH], 1)
            else:
                item[2](engobj).then_inc(dma_sems[item[3]], 16)
        if eng == "sp":
            for k, n in self.dma_cnt.items():
                engobj.wait_ge(dma_sems[k], 16 * n)

    def run(self, nc, es):
        sems = {e: [es.enter_context(nc.semaphore("s_%s_%d" % (e, i)))
                    for i in range(max(1, (self.cnt[e] + CH - 1) // CH))] for e in ENGS}
        dma_sems = {k: es.enter_context(nc.semaphore("d_%d" % i)) for i, k in enumerate(self.dma_keys)}
        block = es.enter_context(nc.Block())
        block.tensor(lambda e: self.emit_engine("pe", e, sems, dma_sems))
        block.scalar(lambda e: self.emit_engine("act", e, sems, dma_sems))
        block.vector(lambda e: self.emit_engine("dve", e, sems, dma_sems))
        block.gpsimd(lambda e: self.emit_engine("pool", e, sems, dma_sems))
        block.sync(lambda e: self.emit_engine("sp", e, sems, dma_sems))


class Arena:
    def __init__(self, nc, base, limit):
        self.nc, self.base, self.limit, self.off, self.n = nc, base, limit, base, 0

    def reset(self):
        self.off = self.base

    def t(self, shape, dt):
        nb = int(np.prod(shape[1:])) * (4 if dt == F32 else 2)
        nb = (nb + 63) // 64 * 64
        assert self.off + nb <= self.limit, ("SBUF overflow", self.off, nb, self.limit)
        self.n += 1
        h = self.nc.alloc_sbuf_tensor_at("a%d" % self.n, list(shape), dt, offset=self.off)
        self.off += nb
        return h


CF = {}
CR = {}


def _layout():
    off = 0
    for nm, w in (("ada_b", 48), ("mix_pre_g", 8), ("mlp_pre_g", 8), ("mu_prev", 16), ("mu_next", 16),
                  ("w0", 8), ("a0", 8), ("k_k", 4), ("k_a", 4), ("r_k", 4), ("conv_w", 124),
                  ("conv_b", 4), ("cln_w", 4), ("cln_b", 4)):
        CF[nm] = (off, w)
        off += w
    ncf = off
    off = 0
    for nm, w in (("ada_b", 6144), ("mix_post_g", 1024), ("mlp_post_g", 1024), ("lnx_w", 512), ("lnx_b", 512)):
        CR[nm] = (off, w)
        off += w
    return ncf, off


NCF, NCR = _layout()


def build(NB, T, TC, debug=False, PHASES=9):
    TT = TC + T
    NB1 = NB + 1
    nc = bass.Bass("TRN2", target_bir_lowering=False)
    dram = lambda n, s, dt, kind: nc.dram_tensor(n, list(s), dt, kind=kind).ap()
    x_d = dram("x", [NB, T, D], F32, "ExternalInput")
    ctx_d = dram("ctx", [NB, TC, D], F32, "ExternalInput")
    cT_d = dram("cT", [128, DK, NB1], F32, "ExternalInput")
    adaw_d = dram("ada_w", [D, 6144], F32, "ExternalInput")
    win_d = dram("w_in", [D, 3072], F32, "ExternalInput")
    cfm_d = dram("cfm", [128, NCF], F32, "ExternalInput")
    crow_d = dram("crow", [128, NCR], F32, "ExternalInput")
    w2d_d = dram("decay_w2", [2, 64, 512], F32, "ExternalInput")
    a2_d = dram("iclr_a2", [2, 64, 512], F32, "ExternalInput")
    gw2_d = dram("gate_w2", [160, 512], F32, "ExternalInput")
    wout_d = dram("w_out", [D, D], F32, "ExternalInput")
    w1_d = dram("mlp_w1", [D, 4096], F32, "ExternalInput")
    w2_d = dram("mlp_w2", [4096, D], F32, "ExternalInput")
    out_d = dram("out", [NB, T, D], F32, "ExternalOutput")
    skind = "ExternalOutput" if debug else "Internal"
    pa_d = dram("pa_s", [NB, 20, 128, TT], BF16, skind)
    y_d = dram("y_s", [NB, 2, T, 512], F32, skind)
    bo_d = dram("bo_s", [NB, 2, T, 512], F32, skind)

    S = Sched()
    es = ExitStack()
    with es:
        banks = [es.enter_context(nc.psum_tensor("bank%d" % i, [128, 512], F32)) for i in range(8)]
        bk = lambda i: "bank%d" % i
        P_ = Arena(nc, 17408, 51 * 1024)
        A_ = Arena(nc, 51 * 1024, 223 * 1024)

        cfm = P_.t([128, NCF], F32)
        crow = P_.t([128, NCR - 6144], F32)
        ident = P_.t([128, 128], BF16)
        identf = P_.t([128, 128], F32)
        bones = P_.t([128, 128], F32)
        hsel = P_.t([128, 2], BF16)
        msk = P_.t([128, 2, 2, 128], BF16)
        mskN = P_.t([128, 2, 64], BF16)
        identB = P_.t([128, 64], BF16)
        onesb = P_.t([128, 1], BF16)
        rstm = P_.t([128, 512], F32)
        modfm = P_.t([128, 48, NB1], F32)
        gates = P_.t([NB1, 2, 1024], F32)
        gm = P_.t([128, 2, DK, NB1], F32)
        c0 = P_.t([128, 16], F32)
        sel = P_.t([NB1, NB1, 128], F32)
        cf = lambda nm: cfm[:, CF[nm][0]:CF[nm][0] + CF[nm][1]]
        cr = lambda nm: crow[:, CR[nm][0] - 6144:CR[nm][0] - 6144 + CR[nm][1]]

        S.dma("sp", lambda e: e.dma_start(out=cfm[:], in_=cfm_d[:, :]), "c0", writes=["cfm"])
        S.dma("sp", lambda e: e.dma_start(out=crow[:], in_=crow_d[:, 6144:NCR]), "c0", writes=["crow"])
        S.op("pool", lambda e: e.memset(identf[:], 1.0), writes=["identf"])
        S.op("pool", lambda e: e.affine_select(out=identf[:], in_=identf[:], pattern=[[-1, 128]],
                                               compare_op=ALU.is_equal, fill=0.0, base=0, channel_multiplier=1),
             reads=["identf"], writes=["identf"])
        S.op("dve", lambda e: e.tensor_copy(out=ident[:], in_=identf[:]), reads=["identf"], writes=["ident"])
        S.op("pool", lambda e: e.memset(bones[:], 0.0), writes=["bones"])
        S.op("pool", lambda e: e.memset(bones[0:64, 0:64], 1.0), reads=["bones"], writes=["bones"])
        S.op("pool", lambda e: e.memset(bones[64:128, 64:128], 1.0), reads=["bones"], writes=["bones"])
        S.op("pool", lambda e: e.memset(hsel[:], 0.0), writes=["hsel"])
        S.op("pool", lambda e: e.memset(hsel[0:64, 0:1], 1.0), reads=["hsel"], writes=["hsel"])
        S.op("pool", lambda e: e.memset(hsel[64:128, 1:2], 1.0), reads=["hsel"], writes=["hsel"])
        mtmp = P_.t([64, 64], F32)

        def mk_mask(dst_ap, sign, strict):
            S.op("pool", lambda e: e.memset(mtmp[:], 1.0), reads=["mtmp"], writes=["mtmp"])
            S.op("pool", lambda e: e.affine_select(out=mtmp[:], in_=mtmp[:], pattern=[[sign, 64]],
                                                   compare_op=ALU.is_gt if strict else ALU.is_ge,
                                                   fill=0.0, base=0, channel_multiplier=-sign),
                 reads=["mtmp"], writes=["mtmp"])
            S.op("pool", lambda e: e.tensor_copy(out=dst_ap, in_=mtmp[:]), reads=["mtmp"], writes=["msk"])
        for d in range(2):
            sg_ = 1 if d == 0 else -1
            for rr in range(2):
                mk_mask(msk[0:64, d, rr, 0:64], sg_, True)
                mk_mask(msk[0:64, d, rr, 64:128], sg_, False)
            mk_mask(mskN[0:64, d, :], -sg_, True)
        S.dma("sp", lambda e: e.dma_start(out=msk[64:128], in_=msk[0:64]), "c0", reads=["msk"], writes=["msk"])
        S.dma("sp", lambda e: e.dma_start(out=mskN[64:128], in_=mskN[0:64]), "c0", reads=["msk"], writes=["msk"])
        S.op("dve", lambda e: e.tensor_tensor(out=identB[:], in0=ident[:, 0:64], in1=ident[:, 64:128], op=ALU.add), reads=["ident"], writes=["identB"])
        S.op("pool", lambda e: e.memset(onesb[:], 1.0), writes=["onesb"])
        S.op("pool", lambda e: e.memset(rstm[:], 1.0), writes=["rstm"])
        S.op("pool", lambda e: e.memset(rstm[:].rearrange("p (c t) -> p c t", t=64)[:, :, 0:1], 0.0),
             reads=["rstm"], writes=["rstm"])
        S.op("pool", lambda e: e.memset(sel[:], 0.0), writes=["sel"])
        for b in range(NB1):
            S.op("pool", lambda e, b=b: e.memset(sel[:, b, :], 1.0), reads=["sel"], writes=["sel"])
            S.op("pool", lambda e, b=b: e.affine_select(out=sel[:, b, :], in_=sel[:, b, :], pattern=[[0, 128]],
                                                        compare_op=ALU.is_equal, fill=0.0, base=-b,
                                                        channel_multiplier=1),
                 reads=["sel"], writes=["sel"])

        A_.reset()
        cT = A_.t([128, DK, NB1], F32)
        siluT = A_.t([128, DK, NB1], F32)
        adab_row = A_.t([NB1, 6144], F32)
        aw = [A_.t([128, DK, 512], F32) for _ in range(2)]
        S.dma("sp", lambda e: e.dma_start(out=cT[:], in_=cT_d[:, :, :]), "c0", writes=["cT"])
        S.dma("sp", lambda e: e.dma_start(out=adab_row[:], in_=crow_d[0:NB1, 0:6144]), "c0", writes=["adab_row"])
        S.op("act", lambda e: e.activation(out=siluT[:], in_=cT[:], func=AF.Silu), reads=["cT"], writes=["siluT"])
        for n in range(12):
            a = aw[n % 2]
            ak = "aw%d" % (n % 2)
            S.dma("sp", lambda e, a=a, n=n: e.dma_start(
                out=a[:], in_=adaw_d[:, n * 512:(n + 1) * 512].rearrange("(k p) n -> p k n", p=128)),
                ak, writes=[ak])
            m = n // 2
            if m in (2, 5):
                for k in range(DK):
                    S.op("pe", lambda e, a=a, k=k: e.matmul(banks[0][0:NB1, :], lhsT=siluT[:, k, :], rhs=a[:, k, :],
                                                            start=(k == 0), stop=(k == DK - 1)),
                         reads=[ak, "siluT"], writes=[bk(0)])
                gi = 0 if m == 2 else 1
                S.op("dve", lambda e, n=n, gi=gi: e.tensor_tensor(
                    out=gates[:, gi, (n % 2) * 512:(n % 2) * 512 + 512], in0=banks[0][0:NB1, :],
                    in1=adab_row[:, n * 512:(n + 1) * 512], op=ALU.add),
                    reads=["adab_row"], writes=[bk(0), "gates"])
            else:
                for j in range(4):
                    for k in range(DK):
                        S.op("pe", lambda e, a=a, k=k, j=j: e.matmul(
                            banks[1][:, j * NB1:(j + 1) * NB1], lhsT=a[:, k, j * 128:(j + 1) * 128],
                            rhs=siluT[:, k, :], start=(k == 0), stop=(k == DK - 1)),
                            reads=[ak, "siluT"], writes=[bk(1)])
                S.op("dve", lambda e, n=n: e.tensor_tensor(
                    out=modfm[:, n * 4:(n + 1) * 4, :],
                    in0=banks[1][:, 0:4 * NB1].rearrange("p (j b) -> p j b", b=NB1),
                    in1=cf("ada_b")[:, n * 4:(n + 1) * 4].unsqueeze(2).to_broadcast([128, 4, NB1]), op=ALU.add),
                    reads=["cfm"], writes=[bk(1), "modfm"])
        for gi, (gn, m) in enumerate((("mix_pre_g", 1), ("mlp_pre_g", 4))):
            S.op("dve", lambda e, gi=gi, m=m: e.tensor_scalar(out=gm[:, gi], in0=modfm[:, m * 8:(m + 1) * 8, :],
                                                              scalar1=1.0, scalar2=None, op0=ALU.add),
                 reads=["modfm"], writes=["gm"])
            S.op("dve", lambda e, gi=gi, gn=gn: e.tensor_tensor(
                out=gm[:, gi], in0=gm[:, gi], in1=cf(gn).unsqueeze(2).to_broadcast([128, DK, NB1]), op=ALU.mult),
                reads=["gm", "cfm"], writes=["gm"])
        S.op("dve", lambda e: e.tensor_tensor(out=c0[:], in0=cf("mu_prev"), in1=cf("mu_next"), op=ALU.add),
             reads=["cfm"], writes=["c0"])
        S.op("dve", lambda e: e.tensor_scalar(out=c0[:], in0=c0[:], scalar1=-1.0, scalar2=1.0, op0=ALU.mult,
                                              op1=ALU.add), reads=["c0"], writes=["c0"])

        def front(xt_ap, xkey, hT_ap, hkey, gi, shm, b, tmp, pbank):
            S.op("act", lambda e: e.activation(out=tmp["sq"][:], in_=xt_ap, func=AF.Square),
                 reads=[xkey], writes=["f_sq"])
            S.op("dve", lambda e: e.reduce_sum(out=tmp["ss"][:], in_=tmp["sq"][:], axis=AX.X),
                 reads=["f_sq"], writes=["f_ss"])
            S.op("dve", lambda e: e.tensor_scalar(out=tmp["ss"][:], in0=tmp["ss"][:], scalar1=1.0 / D, scalar2=1e-6,
                                                  op0=ALU.mult, op1=ALU.add), reads=["f_ss"], writes=["f_ss"])
            S.op("act", lambda e: e.activation(out=tmp["ss"][:], in_=tmp["ss"][:], func=AF.Sqrt),
                 reads=["f_ss"], writes=["f_ss"])
            S.op("dve", lambda e: e.reciprocal(out=tmp["ss"][:], in_=tmp["ss"][:]), reads=["f_ss"], writes=["f_ss"])
            S.op("act", lambda e: e.activation(out=tmp["xn"][:], in_=xt_ap, func=AF.Copy, scale=tmp["ss"][:, 0:1]),
                 reads=[xkey, "f_ss"], writes=["f_xn"])
            pb = banks[pbank].bitcast(BF16)
            for k in range(DK):
                S.op("pe", lambda e, k=k: e.transpose(pb[:, k * 128:(k + 1) * 128], tmp["xn"][:, k * 128:(k + 1) * 128],
                                                      ident[:]), reads=["f_xn", "ident"], writes=[bk(pbank)])
            S.op("dve", lambda e: e.tensor_tensor(out=hT_ap, in0=pb[:, 0:1024].rearrange("p (k t) -> p k t", t=128),
                                                  in1=gm[:, gi, :, b:b + 1].to_broadcast([128, DK, 128]), op=ALU.mult),
                 reads=["gm"], writes=[bk(pbank), hkey])
            S.op("dve", lambda e: e.tensor_tensor(
                out=hT_ap, in0=hT_ap, in1=modfm[:, shm * 8:(shm + 1) * 8, b:b + 1].to_broadcast([128, DK, 128]),
                op=ALU.add), reads=["modfm", hkey], writes=[hkey])

        S.fence()
        A_.reset()
        winb = A_.t([128, DK, 3072], BF16)
        stg = [A_.t([128, DK, 512], F32) for _ in range(2)]
        for n in range(6):
            s_ = stg[n % 2]
            sk = "stg%d" % (n % 2)
            S.dma("sp", lambda e, s_=s_, n=n: e.dma_start(
                out=s_[:], in_=win_d[:, n * 512:(n + 1) * 512].rearrange("(k p) n -> p k n", p=128)), sk, writes=[sk])
            S.op("dve" if n % 2 == 0 else "act",
                 (lambda e, s_=s_, n=n: e.tensor_copy(out=winb[:, :, n * 512:(n + 1) * 512], in_=s_[:])) if n % 2 == 0
                 else (lambda e, s_=s_, n=n: e.activation(out=winb[:, :, n * 512:(n + 1) * 512], in_=s_[:], func=AF.Copy)),
                 reads=[sk], writes=["winb"])
        TM = max(T, TC)
        hT = A_.t([128, DK, TM + 2], BF16)
        xt = [A_.t([128, D], F32) for _ in range(2)]
        ftmp = {"sq": A_.t([128, D], F32), "ss": A_.t([128, 1], F32), "xn": A_.t([128, D], BF16)}
        etmp = [A_.t([128, 512], F32) for _ in range(2)]
        obuf = [A_.t([128, 512], BF16) for _ in range(3)]
        A_sg = [A_.t([128, 512], F32) for _ in range(4)]
        S.op("pool", lambda e: e.memset(hT[:], 0.0), writes=["hT"])
        xi = 0
        ob_i = 0
        pb_i = 0
        for b in range(NB):
            for (src, Ts, toff, bmod, tiles) in ((ctx_d, TC, 0, NB, list(range(4, 14))),
                                                 (x_d, T, TC, b, list(range(0, 24)))):
                if Ts < TM:
                    S.op("pool", lambda e, Ts=Ts: e.memset(hT[:, :, Ts + 1:Ts + 2], 0.0), reads=["hT"], writes=["hT"])
                for tt in range(Ts // 128):
                    xa = xt[xi % 2]
                    xk = "xt%d" % (xi % 2)
                    xi += 1
                    S.dma("sp", lambda e, xa=xa, src=src, b=b, tt=tt: e.dma_start(
                        out=xa[:], in_=src[b, tt * 128:(tt + 1) * 128, :]), xk, writes=[xk])
                    front(xa[:], xk, hT[:, :, 1 + tt * 128:1 + (tt + 1) * 128], "hT", 0, 0, bmod, ftmp, 7)
                w0 = 0
                while w0 < Ts:
                    n = min(510, Ts - w0)
                    sg_ready = {}
                    for j in [jj for jj in tiles if jj >= 20] + [jj for jj in tiles if jj < 20]:
                        pbk = pb_i % 4
                        pb_i += 1
                        pbt = banks[pbk]
                        for k in range(DK):
                            S.op("pe", lambda e, pbt=pbt, k=k, j=j, w0=w0, n=n: e.matmul(
                                pbt[:, 0:n + 2], lhsT=winb[:, k, j * 128:(j + 1) * 128], rhs=hT[:, k, w0:w0 + n + 2],
                                start=(k == 0), stop=(k == DK - 1)), reads=["winb", "hT"], writes=[bk(pbk)])
                        if j >= 20:
                            sgt = A_sg[j - 20]
                            S.op("act", lambda e, pbt=pbt, sgt=sgt, n=n: e.activation(
                                out=sgt[:, 0:n], in_=pbt[:, 1:n + 1], func=AF.Sigmoid),
                                writes=[bk(pbk), "sg%d" % (j - 20)])
                            continue
                        ob = obuf[ob_i % 3]
                        ok = "ob%d" % (ob_i % 3)
                        ob_i += 1
                        if j >= 16:
                            sgt = A_sg[j - 16]
                            S.op("dve", lambda e, pbt=pbt, sgt=sgt, ob=ob, n=n: e.tensor_tensor(
                                out=ob[:, 0:n], in0=pbt[:, 1:n + 1], in1=sgt[:, 0:n], op=ALU.mult),
                                reads=["sg%d" % (j - 16)], writes=[bk(pbk), ok])
                        else:
                            et = etmp[j % 2]
                            ek = "et%d" % (j % 2)
                            S.op("act", lambda e, pbt=pbt, et=et, n=n, j=j: e.activation(
                                out=et[:, 0:n], in_=pbt[:, 1:n + 1], func=AF.Copy, scale=c0[:, j:j + 1]),
                                reads=["c0"], writes=[bk(pbk), ek])
                            S.op("dve", lambda e, pbt=pbt, et=et, n=n, j=j: e.scalar_tensor_tensor(
                                out=et[:, 0:n], in0=pbt[:, 0:n], scalar=cf("mu_prev")[:, j:j + 1], in1=et[:, 0:n],
                                op0=ALU.mult, op1=ALU.add), reads=["cfm", ek], writes=[bk(pbk), ek])
                            S.op("dve", lambda e, pbt=pbt, et=et, ob=ob, n=n, j=j: e.scalar_tensor_tensor(
                                out=ob[:, 0:n], in0=pbt[:, 2:n + 2], scalar=cf("mu_next")[:, j:j + 1], in1=et[:, 0:n],
                                op0=ALU.mult, op1=ALU.add), reads=["cfm", ek], writes=[bk(pbk), ok])
                        S.dma("sp", lambda e, ob=ob, b=b, j=j, toff=toff, w0=w0, n=n: e.dma_start(
                            out=pa_d[b, j, :, toff + w0:toff + w0 + n], in_=ob[:, 0:n]), "pa_st", reads=[ok])
                    w0 += n
        if PHASES >= 2:
            S.fence()
            A_.reset()
            W = 128
            NCW = W // 64
            lwb = A_.t([128, 2, 512], BF16)
            omk = A_.t([128, 4], F32)
            rkb = A_.t([128, 4], BF16)
            mark_pb = A_.off
            wst = A_.t([128, 2, 512], F32)
            S.dma("sp", lambda e: e.dma_start(out=wst[0:64], in_=w2d_d.rearrange("d r f -> r d f")), "pbw", writes=["wst"])
            S.dma("sp", lambda e: e.dma_start(out=wst[64:128], in_=a2_d.rearrange("d r f -> r d f")), "pbw", writes=["wst"])
            S.op("dve", lambda e: e.tensor_copy(out=lwb[:], in_=wst[:]), reads=["wst"], writes=["lwb"])
            S.op("dve", lambda e: e.tensor_scalar(out=omk[:], in0=cf("k_a"), scalar1=-1.0, scalar2=1.0, op0=ALU.mult, op1=ALU.add), reads=["cfm"], writes=["omk"])
            S.op("dve", lambda e: e.tensor_copy(out=rkb[:], in_=cf("r_k")), reads=["cfm"], writes=["rkb"])
            S.fence()
            A_.off = mark_pb
            rs = A_.t([128, 4, TT], BF16)
            ks = A_.t([128, 4, TT], BF16)
            vs = A_.t([128, 4, TT], BF16)
            wdad = A_.t([128, 2, TT], BF16)
            f32t = lambda: A_.t([128, 4, W], F32)
            TMP = [dict(sig=f32t(), Ls=f32t(), ee=f32t(), t1=f32t(), t2=f32t(), icl=A_.t([128, 4, W], BF16),
                        SC=A_.t([128, 4, NCW], F32)) for _ in range(2)]
            ar = [A_.t([128, 4, NCW, 2, 64], BF16) for _ in range(6)]
            bkt = [A_.t([128, 4, NCW, 2, 64], BF16) for _ in range(6)]
            prod = [A_.t([128, 4, W], BF16) for _ in range(6)]
            eLC = [A_.t([128, 4, NCW], F32) for _ in range(6)]
            H32 = [A_.t([128, 4, 64], F32) for _ in range(2)]
            Hbf = [A_.t([128, 4, 64], BF16) for _ in range(2)]
            NJS = 4 * NCW
            btk = [A_.t([128, 512], BF16) for _ in range(NJS)]
            vtm = [A_.t([128, 4, 64], BF16) for _ in range(NJS)]
            AT = [A_.t([128, 4, 2, 128], BF16) for _ in range(NJS)]
            XT = [A_.t([128, 4, 64], BF16) for _ in range(NJS)]
            PQm = [[A_.t([128, 2, 4, 64], BF16) for _ in range(2)] for _ in range(2 * NCW)]
            Xm = [[A_.t([128, 4, 64], BF16) for _ in range(2)] for _ in range(2 * NCW)]
            Rsb = [A_.t([128, 4, 64], BF16) for _ in range(2)]
            Usb = [A_.t([128, 4, 64], BF16) for _ in range(2)]
            ybuf = [A_.t([128, 4, 64], F32) for _ in range(2)]
            bosb = [A_.t([128, 4, 64], F32) for _ in range(2)]
            bon = [A_.t([128, 4], F32) for _ in range(2)]
            K_ = lambda nm, d: "%s%d" % (nm, d)

            def prep(b, d, w0, par):
                dp = d * 3 + par
                sig, Ls, ee, t1, t2, icl, SC = [TMP[d][k_] for k_ in ("sig", "Ls", "ee", "t1", "t2", "icl", "SC")]
                kS, kL, kE, k1, k2, kI, kC = [K_(k_, d) for k_ in ("sig", "Ls", "ee", "t1", "t2", "icl", "SC")]
                PB_ = 6 + d
                pbv = lambda i: banks[PB_][:, i * W:(i + 1) * W]
                pb4 = banks[PB_][:, 0:4 * W].rearrange("p (a t) -> p a t", t=W)
                v5 = lambda tns, a: tns[:].rearrange("p i (c t) -> p i c t", t=64) if a is None else tns[:, :, :, a, :]
                for i in range(4):
                    S.op("pe", lambda e, i=i: e.matmul(pbv(i), lhsT=lwb[0:64, d, i * 128:(i + 1) * 128], rhs=wdad[0:64, d, w0:w0 + W],
                                                       start=True, stop=True), reads=["lwb", "wdad"], writes=[bk(PB_)])
                for i in range(4):
                    S.op("act", lambda e, i=i: e.activation(out=sig[:, i, :], in_=pbv(i), func=AF.Sigmoid,
                                                            bias=cf("w0")[:, d * 4 + i:d * 4 + i + 1]), reads=["cfm"], writes=[bk(PB_), kS])
                yield
                for i in range(4):
                    S.op("pe", lambda e, i=i: e.matmul(pbv(i), lhsT=lwb[64:128, d, i * 128:(i + 1) * 128], rhs=wdad[64:128, d, w0:w0 + W],
                                                       start=True, stop=True), reads=["lwb", "wdad"], writes=[bk(PB_)])
                for i in range(4):
                    S.op("act", lambda e, i=i: e.activation(out=icl[:, i, :], in_=pbv(i), func=AF.Sigmoid,
                                                            bias=cf("a0")[:, d * 4 + i:d * 4 + i + 1]), reads=["cfm"], writes=[bk(PB_), kI])
                yield
                for i in range(4):
                    S.op("dve", lambda e, i=i: e.tensor_scalar(out=t1[:, i, :], in0=ks[:, i, w0:w0 + W], scalar1=cf("k_k")[:, i:i + 1],
                                                               scalar2=None, op0=ALU.mult), reads=["ks", "cfm"], writes=[k1])
                S.op("pool", lambda e: e.tensor_tensor(out=t2[:], in0=t1[:], in1=t1[:], op=ALU.mult), reads=[k1], writes=[k2])
                yield
                for i in range(4):
                    S.op("pe", lambda e, i=i: e.matmul(pbv(i), lhsT=bones[:], rhs=t2[:, i, :], start=True, stop=True),
                         reads=["bones", k2], writes=[bk(PB_)])
                S.op("dve", lambda e: e.tensor_scalar(out=ee[:], in0=pb4, scalar1=1e-24, scalar2=None, op0=ALU.max),
                     writes=[bk(PB_), kE])
                yield
                S.op("act", lambda e: e.activation(out=ee[:], in_=ee[:], func=AF.Sqrt), reads=[kE], writes=[kE])
                yield
                S.op("dve", lambda e: e.reciprocal(out=ee[:], in_=ee[:]), reads=[kE], writes=[kE])
                yield
                S.op("pool", lambda e: e.tensor_tensor(out=t1[:], in0=t1[:], in1=ee[:], op=ALU.mult), reads=[k1, kE], writes=[k1])
                for i in range(4):
                    S.op("dve", lambda e, i=i: e.tensor_tensor_scan(out=Ls[:, i, :], data0=rstm[:, 0:W], data1=sig[:, i, :],
                                                                    initial=0.0, op0=ALU.mult, op1=ALU.add),
                         reads=["rstm", kS], writes=[kL])
                lsc = v5(Ls, None)
                S.op("dve", lambda e: e.tensor_copy(out=SC[:], in_=lsc[:, :, :, 63]), reads=[kL], writes=[kC])
                yield
                S.op("act", lambda e: e.activation(out=eLC[dp][:], in_=SC[:], func=AF.Exp, scale=-CDEC), reads=[kC], writes=[K_("eLC", dp)])
                if d == 0:
                    S.op("dve", lambda e: e.tensor_tensor(out=sig[:], in0=Ls[:], in1=sig[:], op=ALU.subtract), reads=[kL, kS], writes=[kS])
                    XE, kXE, XI, kXI = sig, kS, Ls, kL
                else:
                    S.op("dve", lambda e: e.tensor_tensor(out=lsc, in0=SC[:].unsqueeze(3).to_broadcast([128, 4, NCW, 64]),
                                                          in1=lsc, op=ALU.subtract), reads=[kL, kC], writes=[kL])
                    S.op("dve", lambda e: e.tensor_tensor(out=sig[:], in0=Ls[:], in1=sig[:], op=ALU.add), reads=[kL, kS], writes=[kS])
                    XE, kXE, XI, kXI = Ls, kL, sig, kS
                yield
                S.op("act", lambda e: e.activation(out=ee[:], in_=XE[:], func=AF.Exp, scale=-CDEC), reads=[kXE], writes=[kE])
                yield
                S.op("dve", lambda e: e.scalar_tensor_tensor(out=v5(ar[dp], 0), in0=v5(t1, None), scalar=-1.0, in1=v5(ee, None),
                                                             op0=ALU.mult, op1=ALU.mult), reads=[k1, kE], writes=[K_("ar", dp)])
                yield
                S.op("act", lambda e: e.activation(out=ee[:], in_=XI[:], func=AF.Exp, scale=-CDEC), reads=[kXI], writes=[kE])
                yield
                S.op("pool", lambda e: e.tensor_tensor(out=v5(ar[dp], 1), in0=rs[:, :, w0:w0 + W].rearrange("p i (c t) -> p i c t", t=64),
                                                       in1=v5(ee, None), op=ALU.mult), reads=["rs", kE], writes=[K_("ar", dp)])
                yield
                S.op("act", lambda e: e.activation(out=ee[:], in_=XI[:], func=AF.Exp, scale=CDEC), reads=[kXI], writes=[kE])
                S.op("pool", lambda e: e.tensor_tensor(out=t2[:], in0=t1[:], in1=icl[:], op=ALU.mult), reads=[k1, kI], writes=[k2])
                yield
                S.op("dve", lambda e: e.tensor_tensor(out=v5(bkt[dp], 0), in0=v5(t2, None), in1=v5(ee, None), op=ALU.mult),
                     reads=[k2, kE], writes=[K_("bkt", dp)])
                yield
                for i in range(4):
                    S.op("dve", lambda e, i=i: e.tensor_scalar(out=t2[:, i, :], in0=icl[:, i, :], scalar1=cf("k_a")[:, i:i + 1],
                                                               scalar2=omk[:, i:i + 1], op0=ALU.mult, op1=ALU.add),
                         reads=[kI, "cfm", "omk"], writes=[k2])
                yield
                S.op("pool", lambda e: e.tensor_tensor(out=t2[:], in0=t2[:], in1=ks[:, :, w0:w0 + W], op=ALU.mult), reads=[k2, "ks"], writes=[k2])
                yield
                S.op("dve", lambda e: e.tensor_tensor(out=v5(bkt[dp], 1), in0=v5(t2, None), in1=v5(ee, None), op=ALU.mult),
                     reads=[k2, kE], writes=[K_("bkt", dp)])
                yield
                S.op("pool", lambda e: e.tensor_tensor(out=prod[dp][:], in0=t2[:], in1=rs[:, :, w0:w0 + W], op=ALU.mult),
                     reads=[k2, "rs"], writes=[K_("prod", dp)])
                yield

            HP = [((h % 2) * 64, h // 2) for h in range(8)]
            V3 = lambda bi: banks[bi][:, 0:256].rearrange("p (i s) -> p i s", s=64)

            def inv(b, d, w0, c, par3, js, jt, B):
                tk0 = w0 + c * 64
                dp = d * 3 + par3
                arK, bkK = K_("ar", dp), K_("bkt", dp)
                pT = banks[B].bitcast(BF16)
                for q in range(2):
                    for (po, i) in HP:
                        S.op("pe", lambda e, q=q, po=po, i=i: e.transpose(
                            pT[po:po + 64, (q * 4 + i) * 64:(q * 4 + i + 1) * 64], bkt[dp][po:po + 64, i, c, q, :],
                            ident[po:po + 64, po:po + 64]), reads=[bkK, "ident"], writes=[bk(B)])
                for (po, i) in HP:
                    S.op("pe", lambda e, po=po, i=i: e.transpose(pT[po:po + 64, 512 + i * 64:512 + (i + 1) * 64],
                                                                 vs[po:po + 64, i, tk0:tk0 + 64], ident[po:po + 64, po:po + 64]),
                         reads=["vs", "ident"], writes=[bk(B)])
                S.op("act", lambda e: e.activation(out=btk[js][:], in_=pT[:, 0:512], func=AF.Copy), writes=[bk(B), K_("btk", js)])
                S.op("act", lambda e: e.activation(out=vtm[js][:].rearrange("p i v -> p (i v)"), in_=pT[:, 512:768], func=AF.Copy),
                     writes=[bk(B), K_("vtm", js)])
                yield
                for bb in range(2):
                    psA = banks[B][:, :].rearrange("p (i r t) -> p i r t", i=2, r=2)
                    for (po, i) in HP:
                        if i // 2 != bb:
                            continue
                        rhs = ar[dp][po:po + 64, i, c, :, :].rearrange("p a t -> p (a t)")
                        for r_ in range(2):
                            S.op("pe", lambda e, r_=r_, po=po, i=i, rhs=rhs, psA=psA: e.matmul(
                                psA[po:po + 64, i % 2, r_, :], lhsT=bkt[dp][po:po + 64, i, c, r_, :], rhs=rhs, start=True, stop=True),
                                reads=[bkK, arK], writes=[bk(B)])
                    S.op("dve", lambda e, bb=bb, psA=psA: e.tensor_tensor(
                        out=AT[js][:, bb * 2:bb * 2 + 2], in0=psA, in1=msk[:, d:d + 1].to_broadcast([128, 2, 2, 128]), op=ALU.mult),
                        reads=["msk"], writes=[bk(B), K_("AT", js)])
                    yield
                psN = V3(B)
                for (po, i) in HP:
                    S.op("pe", lambda e, po=po, i=i: e.matmul(psN[po:po + 64, i, :], lhsT=ar[dp][po:po + 64, i, c, 0, :],
                                                               rhs=bkt[dp][po:po + 64, i, c, 0, :], start=True, stop=True),
                         reads=[bkK, arK], writes=[bk(B)])
                PQ, X = PQm[jt], Xm[jt]
                S.op("dve", lambda e: e.tensor_tensor(out=PQ[0][:, 0], in0=psN, in1=mskN[:, d:d + 1].to_broadcast([128, 4, 64]), op=ALU.mult),
                     reads=["msk"], writes=[bk(B), K_("PQ0", jt)])
                S.op("act", lambda e: e.activation(out=PQ[0][:, 1], in_=AT[js][:, :, 0, 0:64], func=AF.Copy),
                     reads=[K_("AT", js)], writes=[K_("PQ0", jt)])
                S.op("pool", lambda e: e.tensor_tensor(out=X[0][:], in0=AT[js][:, :, 0, 0:64],
                                                       in1=identB[:].unsqueeze(1).to_broadcast([128, 4, 64]), op=ALU.add),
                     reads=[K_("AT", js), "identB"], writes=[K_("X0", jt)])
                yield
                cur = 0
                for j in range(1, 6):
                    nxt = 1 - cur
                    psPQ = banks[B][:, :].rearrange("p (a i s) -> p a i s", a=2, s=64)
                    na = 2 if j < 5 else 1
                    for a_ in range(na):
                        for (po, i) in HP:
                            S.op("pe", lambda e, po=po, i=i, cur=cur, a_=a_, psPQ=psPQ: e.matmul(
                                psPQ[po:po + 64, a_, i, :], lhsT=PQ[cur][po:po + 64, 1 - a_, i, :], rhs=PQ[cur][po:po + 64, a_, i, :],
                                start=True, stop=True), reads=[K_("PQ%d" % cur, jt)], writes=[bk(B)])
                    if False:
                        S.op("dve", lambda e, nxt=nxt, na=na, psPQ=psPQ: e.tensor_copy(out=PQ[nxt][:, 0:na], in_=psPQ[:, 0:na]),
                             writes=[bk(B), K_("PQ%d" % nxt, jt)])
                    else:
                        S.op("act", lambda e, nxt=nxt, na=na, psPQ=psPQ: e.activation(out=PQ[nxt][:, 0:na], in_=psPQ[:, 0:na], func=AF.Copy),
                             writes=[bk(B), K_("PQ%d" % nxt, jt)])
                    yield
                    psX = V3(B)
                    for (po, i) in HP:
                        S.op("pe", lambda e, po=po, i=i, cur=cur, nxt=nxt, psX=psX: e.matmul(
                            psX[po:po + 64, i, :], lhsT=PQ[nxt][po:po + 64, 0, i, :], rhs=X[cur][po:po + 64, i, :], start=True, stop=True),
                            reads=[K_("PQ%d" % nxt, jt), K_("X%d" % cur, jt)], writes=[bk(B)])
                    xo_, xok_ = (XT[js], K_("XT", js)) if j == 5 else (X[nxt], K_("X%d" % nxt, jt))
                    S.op("dve", lambda e, cur=cur, xo_=xo_, psX=psX: e.tensor_tensor(out=xo_[:], in0=psX, in1=X[cur][:], op=ALU.add),
                         reads=[K_("X%d" % cur, jt)], writes=[bk(B), xok_])
                    yield
                    cur = nxt

            def chain(b, d, w0, c, is_lat, par3, js, B):
                tk0 = w0 + c * 64
                dp = d * 3 + par3
                arK, bkK = K_("ar", dp), K_("bkt", dp)
                btm = btk[js][:, 0:256].rearrange("p (i k) -> p i k", k=64)
                ktm = btk[js][:, 256:512].rearrange("p (i k) -> p i k", k=64)
                ATj, vt, XTj = AT[js], vtm[js], XT[js]
                psR = V3(B)
                for (po, i) in HP:
                    S.op("pe", lambda e, po=po, i=i: e.matmul(psR[po:po + 64, i, :], lhsT=ar[dp][po:po + 64, i, c, 0, :],
                                                               rhs=Hbf[d][po:po + 64, i, :], start=True, stop=False),
                         reads=[arK, K_("Hbf", d)], writes=[bk(B)])
                    S.op("pe", lambda e, po=po, i=i: e.matmul(psR[po:po + 64, i, :], lhsT=ATj[po:po + 64, i, 1, 0:64],
                                                               rhs=vt[po:po + 64, i, :], start=False, stop=True),
                         reads=[K_("AT", js), K_("vtm", js)], writes=[bk(B)])
                S.op("act", lambda e: e.activation(out=Rsb[d][:], in_=psR, func=AF.Copy), writes=[bk(B), K_("Rsb", d)])
                yield
                for (po, i) in HP:
                    S.op("pe", lambda e, po=po, i=i: e.matmul(psR[po:po + 64, i, :], lhsT=XTj[po:po + 64, i, :],
                                                               rhs=Rsb[d][po:po + 64, i, :], start=True, stop=True),
                         reads=[K_("XT", js), K_("Rsb", d)], writes=[bk(B)])
                S.op("act", lambda e: e.activation(out=Usb[d][:], in_=psR, func=AF.Copy), writes=[bk(B), K_("Usb", d)])
                yield
                psH = V3(B)
                for (po, i) in HP:
                    S.op("pe", lambda e, po=po, i=i: e.matmul(psH[po:po + 64, i, :], lhsT=btm[po:po + 64, i, :],
                                                               rhs=Usb[d][po:po + 64, i, :], start=True, stop=False),
                         reads=[K_("btk", js), K_("Usb", d)], writes=[bk(B)])
                    S.op("pe", lambda e, po=po, i=i: e.matmul(psH[po:po + 64, i, :], lhsT=ktm[po:po + 64, i, :],
                                                               rhs=vt[po:po + 64, i, :], start=False, stop=True),
                         reads=[K_("btk", js), K_("vtm", js)], writes=[bk(B)])
                if is_lat:
                    psY = banks[B][:, 256:512].rearrange("p (i s) -> p i s", s=64)
                    for (po, i) in HP:
                        S.op("pe", lambda e, po=po, i=i: e.matmul(psY[po:po + 64, i, :], lhsT=ar[dp][po:po + 64, i, c, 1, :],
                                                                   rhs=Hbf[d][po:po + 64, i, :], start=True, stop=False),
                             reads=[arK, K_("Hbf", d)], writes=[bk(B)])
                        S.op("pe", lambda e, po=po, i=i: e.matmul(psY[po:po + 64, i, :], lhsT=ATj[po:po + 64, i, 0, 64:128],
                                                                   rhs=Usb[d][po:po + 64, i, :], start=False, stop=False),
                             reads=[K_("AT", js), K_("Usb", d)], writes=[bk(B)])
                        S.op("pe", lambda e, po=po, i=i: e.matmul(psY[po:po + 64, i, :], lhsT=ATj[po:po + 64, i, 1, 64:128],
                                                                   rhs=vt[po:po + 64, i, :], start=False, stop=True),
                             reads=[K_("AT", js), K_("vtm", js)], writes=[bk(B)])
                S.op("dve", lambda e: e.tensor_tensor(out=H32[d][:], in0=psH, in1=H32[d][:], op=ALU.add),
                     reads=[K_("H32", d)], writes=[bk(B), K_("H32", d)])
                S.op("pool", lambda e: e.tensor_tensor(out=H32[d][:], in0=H32[d][:],
                                                       in1=eLC[dp][:, :, c:c + 1].to_broadcast([128, 4, 64]), op=ALU.mult),
                     reads=[K_("H32", d), K_("eLC", dp)], writes=[K_("H32", d)])
                S.op("act", lambda e: e.activation(out=Hbf[d][:], in_=H32[d][:], func=AF.Copy), reads=[K_("H32", d)], writes=[K_("Hbf", d)])
                if is_lat:
                    S.op("act", lambda e: e.activation(out=ybuf[d][:], in_=psY, func=AF.Copy), writes=[bk(B), K_("ybuf", d)])
                    tl = tk0 - TC
                    for hp_ in range(2):
                        S.dma("sp", lambda e, tl=tl, hp_=hp_: e.dma_start(
                            out=y_d[b, d, tl:tl + 64, :].rearrange("t (i hp v) -> t i hp v", hp=2, v=64)[:, :, hp_, :],
                            in_=ybuf[d][hp_ * 64:(hp_ + 1) * 64]), "y_st", reads=[K_("ybuf", d)])
                    yield
                    psB = banks[B]
                    for (po, i) in HP:
                        S.op("pe", lambda e, po=po, i=i: e.matmul(psB[po:po + 64, i:i + 1], lhsT=prod[dp][po:po + 64, i, c * 64:(c + 1) * 64],
                                                                   rhs=rkb[po:po + 64, i:i + 1], start=True, stop=True),
                             reads=[K_("prod", dp), "rkb"], writes=[bk(B)])
                    S.op("dve", lambda e: e.tensor_scalar(out=bon[d][:], in0=psB[:, 0:4], scalar1=0.5, scalar2=None, op0=ALU.mult),
                         writes=[bk(B), K_("bon", d)])
                    S.op("dve", lambda e: e.tensor_tensor(out=bosb[d][:], in0=vt[:],
                                                          in1=bon[d][:].unsqueeze(2).to_broadcast([128, 4, 64]), op=ALU.mult),
                         reads=[K_("vtm", js), K_("bon", d)], writes=[K_("bosb", d)])
                    for hp_ in range(2):
                        S.dma("sp", lambda e, tl=tl, hp_=hp_: e.dma_start(
                            out=bo_d[b, d, tl:tl + 64, :].rearrange("t (i hp v) -> t i hp v", hp=2, v=64)[:, :, hp_, :],
                            in_=bosb[d][hp_ * 64:(hp_ + 1) * 64]), "bo_st", reads=[K_("bosb", d)])
                yield

            def lockstep(gens):
                gens = list(gens)
                while gens:
                    for g_ in list(gens):
                        try:
                            next(g_)
                        except StopIteration:
                            gens.remove(g_)

            def pb_batch(b):
                for (dst, j0, key) in ((rs, 0, "rs"), (ks, 4, "ks"), (vs, 8, "vs")):
                    S.dma("sp", lambda e, dst=dst, j0=j0: e.dma_start(out=dst[:], in_=pa_d[b, j0:j0 + 4].rearrange("j p t -> p j t")),
                          "pb_ld_" + key, writes=[key])
                S.dma("sp", lambda e: e.dma_start(out=wdad[:], in_=pa_d[b, 12:14].rearrange("j p t -> p j t")), "pb_ld_w", writes=["wdad"])
                S.op("act", lambda e: e.activation(out=wdad[0:64], in_=wdad[0:64], func=AF.Tanh), reads=["wdad"], writes=["wdad"])
                for d in range(2):
                    S.op("pool", lambda e, d=d: e.memset(H32[d][:], 0.0), writes=[K_("H32", d)])
                    S.op("pool", lambda e, d=d: e.memset(Hbf[d][:], 0.0), writes=[K_("Hbf", d)])
                cw_ = [(w * W, False) for w in range(TC // W)]
                lw_ = [(TC + w * W, True) for w in range(T // W)]
                sched = {0: cw_ + lw_, 1: list(reversed(cw_)) + list(reversed(lw_))}
                nw_ = len(sched[0])

                def preps(wi):
                    return [prep(b, d, sched[d][wi][0], wi % 3) for d in range(2)]

                def jobs(wi):
                    for d in range(2):
                        for cc in range(NCW):
                            c = cc if d == 0 else NCW - 1 - cc
                            yield d, cc, c, (wi % 2) * 2 * NCW + d * NCW + cc, d * NCW + cc

                def chains(wi):
                    for cc in range(NCW):
                        gens = []
                        for (d, cc_, c, js, jt) in jobs(wi):
                            if cc_ == cc:
                                gens.append(chain(b, d, sched[d][wi][0], c, sched[d][wi][1], wi % 3, js, 2 * NCW + d))
                        while gens:
                            for g_ in list(gens):
                                try:
                                    next(g_)
                                except StopIteration:
                                    gens.remove(g_)
                            yield

                lockstep(preps(0))
                for t in range(nw_ + 1):
                    gl = []
                    if t >= 1:
                        gl.append(chains(t - 1))
                    if t < nw_:
                        gl += [inv(b, d, sched[d][t][0], c, t % 3, js, jt, jt) for (d, cc, c, js, jt) in jobs(t)]
                    if t + 1 < nw_:
                        gl += preps(t + 1)
                    lockstep(gl)

            for b in range(NB):
                pb_batch(b)

        def post_norm_residual(pbs, gpost, xres, xkey, outt, okey, tmp, kp="pn"):
            for hh in range(2):
                S.op("act", lambda e, hh=hh: e.activation(out=tmp["sq"][:, hh * 512:(hh + 1) * 512], in_=banks[pbs[hh]][:, :], func=AF.Square),
                     writes=[bk(pbs[hh]), kp + "_sq"])
            S.op("dve", lambda e: e.reduce_sum(out=tmp["ss"][:], in_=tmp["sq"][:], axis=AX.X), reads=[kp + "_sq"], writes=[kp + "_ss"])
            S.op("dve", lambda e: e.tensor_scalar(out=tmp["ss"][:], in0=tmp["ss"][:], scalar1=1.0 / D, scalar2=1e-6,
                                                  op0=ALU.mult, op1=ALU.add), reads=[kp + "_ss"], writes=[kp + "_ss"])
            S.op("act", lambda e: e.activation(out=tmp["ss"][:], in_=tmp["ss"][:], func=AF.Sqrt), reads=[kp + "_ss"], writes=[kp + "_ss"])
            S.op("dve", lambda e: e.reciprocal(out=tmp["ss"][:], in_=tmp["ss"][:]), reads=[kp + "_ss"], writes=[kp + "_ss"])
            for hh in range(2):
                S.op("dve", lambda e, hh=hh: e.scalar_tensor_tensor(
                    out=tmp["sq"][:, hh * 512:(hh + 1) * 512], in0=banks[pbs[hh]][:, :], scalar=tmp["ss"][:, 0:1],
                    in1=gpost[:, hh * 512:(hh + 1) * 512], op0=ALU.mult, op1=ALU.mult),
                    reads=[kp + "_ss", "gpost"], writes=[bk(pbs[hh]), kp + "_sq"])
            S.op("pool", lambda e: e.tensor_tensor(out=outt[:], in0=tmp["sq"][:], in1=xres, op=ALU.add), reads=[kp + "_sq", xkey], writes=[okey])

        def make_gpost(b, gi, rowname, gpost):
            for hh in range(2):
                S.op("pe", lambda e, hh=hh: e.matmul(banks[hh][:, :], lhsT=sel[:, b, :], rhs=gates[:, gi, hh * 512:(hh + 1) * 512],
                                                     start=True, stop=True), reads=["sel", "gates"], writes=[bk(hh)])
                S.op("dve", lambda e, hh=hh: e.tensor_tensor(out=gpost[:, hh * 512:(hh + 1) * 512], in0=banks[hh][:, :],
                                                             in1=cr(rowname)[:, hh * 512:(hh + 1) * 512], op=ALU.mult),
                     reads=["crow"], writes=[bk(hh), "gpost"])

        if PHASES >= 3:
            S.fence()
            A_.reset()
            woutb = A_.t([128, DK, D], BF16)
            gw2b = A_.t([128, 2, 512], BF16)
            stg = [A_.t([128, DK, 512], F32) for _ in range(2)]
            for n in range(2):
                S.dma("sp", lambda e, n=n: e.dma_start(out=stg[n][:], in_=wout_d[:, n * 512:(n + 1) * 512].rearrange("(k p) n -> p k n", p=128)),
                      "stgd%d" % n, writes=["stgd%d" % n])
                S.op("dve", lambda e, n=n: e.tensor_copy(out=woutb[:, :, n * 512:(n + 1) * 512], in_=stg[n][:]), reads=["stgd%d" % n], writes=["woutb"])
            gst = A_.t([128, 2, 512], F32)
            S.dma("sp", lambda e: e.dma_start(out=gst[:, 0, :], in_=gw2_d[0:128, :]), "gst", writes=["gst"])
            S.dma("sp", lambda e: e.dma_start(out=gst[0:32, 1, :], in_=gw2_d[128:160, :]), "gst", writes=["gst"])
            S.op("dve", lambda e: e.tensor_copy(out=gw2b[:, 0, :], in_=gst[:, 0, :]), reads=["gst"], writes=["gw2b"])
            S.op("dve", lambda e: e.tensor_copy(out=gw2b[0:32, 1, :], in_=gst[0:32, 1, :]), reads=["gst"], writes=["gw2b"])
            S.fence()
            A_.off -= 2 * 16384 + 4096
            onesf = A_.t([128, 128], F32)
            S.op("pool", lambda e: e.memset(onesf[:], 1.0), writes=["onesf"])
            ub = A_.t([128, 4, T], BF16)
            yc = A_.t([128, 4, T], F32)
            convo = A_.t([128, 4, T], BF16)
            gdb = A_.t([128, 2, T], BF16)
            lsq = A_.t([128, 4, 512], F32)
            lmean = A_.t([128, 512], F32)
            lrstd = A_.t([128, 512], F32)
            ltmp = A_.t([128, 512], F32)
            yin = [[A_.t([128, 512], F32) for _ in range(4)] for _ in range(2)]
            ysq_ = [A_.t([128, 512], F32) for _ in range(2)]
            st8_ = [A_.t([128, 4, 8], F32) for _ in range(2)]
            rwb_ = [A_.t([128, 512], BF16) for _ in range(2)]
            mixT_ = [A_.t([128, 4, 128], BF16) for _ in range(2)]
            xt2 = [A_.t([128, D], F32) for _ in range(2)]
            gpost = A_.t([128, D], F32)
            pn_tmp_ = [{"sq": A_.t([128, D], F32), "ss": A_.t([128, 1], F32)} for _ in range(2)]
            cw = cf("conv_w")
            ti_ = [0]

            def _pd_batch(b):
                make_gpost(b, 0, "mix_post_g", gpost)
                S.dma("sp", lambda e, b=b: e.dma_start(out=ub[:], in_=pa_d[b, 16:20, :, TC:TT].rearrange("j p t -> p j t")), "pd_ld", writes=["ub"])
                S.dma("sp", lambda e, b=b: e.dma_start(out=gdb[:], in_=pa_d[b, 14:16, :, TC:TT].rearrange("j p t -> p j t")), "pd_ld2", writes=["gdb"])
                S.op("act", lambda e: e.activation(out=gdb[:, 0, :], in_=gdb[:, 0, :], func=AF.Sigmoid), reads=["gdb"], writes=["gdb"])
                S.op("act", lambda e: e.activation(out=gdb[0:32, 1, :], in_=gdb[0:32, 1, :], func=AF.Sigmoid), reads=["gdb"], writes=["gdb"])
                for i in range(4):
                    u4 = ub[:, i, :].rearrange("p (r t) -> p r t", t=64)
                    y4 = yc[:, i, :].rearrange("p (r t) -> p r t", t=64)
                    S.op("dve", lambda e, i=i: e.tensor_scalar(out=yc[:, i, :], in0=ub[:, i, :], scalar1=cw[:, i * 31 + 15:i * 31 + 16],
                                                               scalar2=cf("conv_b")[:, i:i + 1], op0=ALU.mult, op1=ALU.add),
                         reads=["ub", "cfm"], writes=["yc%d" % i])
                    for j in range(31):
                        o = j - 15
                        if o == 0:
                            continue
                        lo_o, hi_o = max(0, -o), 64 - max(0, o)
                        lo_i, hi_i = max(0, o), 64 - max(0, -o)
                        S.op("dve", lambda e, i=i, j=j, u4=u4, y4=y4, lo_o=lo_o, hi_o=hi_o, lo_i=lo_i, hi_i=hi_i: e.scalar_tensor_tensor(
                            out=y4[:, :, lo_o:hi_o], in0=u4[:, :, lo_i:hi_i], scalar=cw[:, i * 31 + j:i * 31 + j + 1],
                            in1=y4[:, :, lo_o:hi_o], op0=ALU.mult, op1=ALU.add), reads=["ub", "cfm", "yc%d" % i], writes=["yc%d" % i])
                for w in range(T // 512 if T >= 512 else 1):
                    wn = min(512, T)
                    ws = slice(w * 512, w * 512 + wn)
                    for i in range(4):
                        S.op("pe", lambda e, i=i, ws=ws, wn=wn: e.matmul(banks[2][:, 0:wn], lhsT=onesf[:], rhs=yc[:, i, ws], start=(i == 0), stop=(i == 3)),
                             reads=["onesf", "yc%d" % i], writes=[bk(2)])
                    S.op("act", lambda e, ws=ws, wn=wn: e.activation(out=lsq[:, :, 0:wn], in_=yc[:, :, ws], func=AF.Square),
                         reads=["yc0", "yc1", "yc2", "yc3"], writes=["lsq"])
                    for i in range(4):
                        S.op("pe", lambda e, i=i, wn=wn: e.matmul(banks[3][:, 0:wn], lhsT=onesf[:], rhs=lsq[:, i, 0:wn], start=(i == 0), stop=(i == 3)),
                             reads=["onesf", "lsq"], writes=[bk(3)])
                    S.op("dve", lambda e, wn=wn: e.tensor_scalar(out=lmean[:, 0:wn], in0=banks[2][:, 0:wn], scalar1=1.0 / 512, scalar2=None, op0=ALU.mult),
                         writes=[bk(2), "lmean"])
                    S.op("dve", lambda e, wn=wn: e.tensor_tensor(out=ltmp[:, 0:wn], in0=lmean[:, 0:wn], in1=lmean[:, 0:wn], op=ALU.mult),
                         reads=["lmean"], writes=["ltmp"])
                    S.op("dve", lambda e, wn=wn: e.scalar_tensor_tensor(out=lrstd[:, 0:wn], in0=banks[3][:, 0:wn], scalar=1.0 / 512, in1=ltmp[:, 0:wn],
                                                                        op0=ALU.mult, op1=ALU.subtract), reads=["ltmp"], writes=[bk(3), "lrstd"])
                    S.op("dve", lambda e, wn=wn: e.tensor_scalar(out=lrstd[:, 0:wn], in0=lrstd[:, 0:wn], scalar1=1e-5, scalar2=None, op0=ALU.add),
                         reads=["lrstd"], writes=["lrstd"])
                    S.op("act", lambda e, wn=wn: e.activation(out=lrstd[:, 0:wn], in_=lrstd[:, 0:wn], func=AF.Sqrt), reads=["lrstd"], writes=["lrstd"])
                    S.op("dve", lambda e, wn=wn: e.reciprocal(out=lrstd[:, 0:wn], in_=lrstd[:, 0:wn]), reads=["lrstd"], writes=["lrstd"])
                    S.op("dve", lambda e, ws=ws, wn=wn: e.tensor_tensor(out=lsq[:, :, 0:wn], in0=yc[:, :, ws],
                                                                        in1=lmean[:, 0:wn].unsqueeze(1).to_broadcast([128, 4, wn]), op=ALU.subtract),
                         reads=["yc0", "yc1", "yc2", "yc3", "lmean"], writes=["lsq"])
                    S.op("dve", lambda e, wn=wn: e.tensor_tensor(out=lsq[:, :, 0:wn], in0=lsq[:, :, 0:wn],
                                                                 in1=lrstd[:, 0:wn].unsqueeze(1).to_broadcast([128, 4, wn]), op=ALU.mult),
                         reads=["lsq", "lrstd"], writes=["lsq"])
                    for i in range(4):
                        S.op("act", lambda e, i=i, ws=ws, wn=wn: e.activation(out=convo[:, i, ws], in_=lsq[:, i, 0:wn], func=AF.Silu,
                                                                              scale=cf("cln_w")[:, i:i + 1], bias=cf("cln_b")[:, i:i + 1]),
                             reads=["lsq", "cfm"], writes=["convo"])
                for tt in range(0, T // 128, 2):
                    gens = [_pd_tile(b, tt + q_, q_) for q_ in range(2) if tt + q_ < T // 128]
                    while gens:
                        for g_ in list(gens):
                            try:
                                next(g_)
                            except StopIteration:
                                gens.remove(g_)

            def _pd_tile(b, tt, sl):
                if True:
                    tsl = slice(tt * 128, (tt + 1) * 128)
                    ti = sl
                    BG, BT, BO0, BO1 = 4 * sl, 4 * sl + 1, 4 * sl + 2, 4 * sl + 3
                    ysq, st8, rwb, mixT = ysq_[sl], st8_[sl], rwb_[sl], mixT_[sl]
                    pn_tmp = pn_tmp_[sl]
                    sk = lambda nm: "%s_%d" % (nm, sl)
                    yy = yin[ti % 2]
                    yk = "yin%d" % (ti % 2)
                    xa, xk2 = xt2[ti % 2], "xt2_%d" % (ti % 2)
                    for q, src in enumerate((y_d, y_d, bo_d, bo_d)):
                        S.dma("sp", lambda e, q=q, src=src, yy=yy, tsl=tsl: e.dma_start(out=yy[q][:], in_=src[b, q % 2, tsl, :]), yk, writes=[yk])
                    S.dma("sp", lambda e, xa=xa, tsl=tsl: e.dma_start(out=xa[:], in_=x_d[b, tsl, :]), xk2, writes=[xk2])
                    S.op("pool", lambda e, yy=yy: e.tensor_tensor(out=yy[0][:], in0=yy[0][:], in1=yy[1][:], op=ALU.add), reads=[yk], writes=[yk])
                    S.op("pool", lambda e, yy=yy: e.tensor_tensor(out=yy[2][:], in0=yy[2][:], in1=yy[3][:], op=ALU.add), reads=[yk], writes=[yk])
                    yield
                    y3 = yy[0][:].rearrange("p (h v) -> p h v", v=64)
                    S.op("dve", lambda e, y3=y3: e.reduce_sum(out=st8[:, 0, :], in_=y3, axis=AX.X), reads=[yk], writes=[sk("st8")])
                    S.op("act", lambda e, yy=yy: e.activation(out=ysq[:], in_=yy[0][:], func=AF.Square), reads=[yk], writes=[sk("ysq")])
                    S.op("dve", lambda e: e.reduce_sum(out=st8[:, 1, :], in_=ysq[:].rearrange("p (h v) -> p h v", v=64), axis=AX.X),
                         reads=[sk("ysq")], writes=[sk("st8")])
                    S.op("dve", lambda e: e.tensor_scalar(out=st8[:, 0:2, :], in0=st8[:, 0:2, :], scalar1=1.0 / 64, scalar2=None, op0=ALU.mult),
                         reads=[sk("st8")], writes=[sk("st8")])
                    yield
                    S.op("dve", lambda e: e.tensor_tensor(out=st8[:, 2, :], in0=st8[:, 0, :], in1=st8[:, 0, :], op=ALU.mult), reads=[sk("st8")], writes=[sk("st8")])
                    S.op("dve", lambda e: e.tensor_tensor(out=st8[:, 3, :], in0=st8[:, 1, :], in1=st8[:, 2, :], op=ALU.subtract), reads=[sk("st8")], writes=[sk("st8")])
                    S.op("dve", lambda e: e.tensor_scalar(out=st8[:, 3, :], in0=st8[:, 3, :], scalar1=64e-5, scalar2=None, op0=ALU.add),
                         reads=[sk("st8")], writes=[sk("st8")])
                    S.op("act", lambda e: e.activation(out=st8[:, 3, :], in_=st8[:, 3, :], func=AF.Sqrt), reads=[sk("st8")], writes=[sk("st8")])
                    S.op("dve", lambda e: e.reciprocal(out=st8[:, 3, :], in_=st8[:, 3, :]), reads=[sk("st8")], writes=[sk("st8")])
                    yield
                    S.op("dve", lambda e, y3=y3: e.tensor_tensor(out=y3, in0=y3, in1=st8[:, 0, :].unsqueeze(2).to_broadcast([128, 8, 64]), op=ALU.subtract),
                         reads=[yk, sk("st8")], writes=[yk])
                    S.op("dve", lambda e, y3=y3: e.tensor_tensor(out=y3, in0=y3, in1=st8[:, 3, :].unsqueeze(2).to_broadcast([128, 8, 64]), op=ALU.mult),
                         reads=[yk, sk("st8")], writes=[yk])
                    S.op("pool", lambda e, yy=yy: e.tensor_tensor(out=yy[0][:], in0=yy[0][:], in1=cr("lnx_w"), op=ALU.mult), reads=[yk, "crow"], writes=[yk])
                    S.op("pool", lambda e, yy=yy: e.tensor_tensor(out=yy[0][:], in0=yy[0][:], in1=cr("lnx_b"), op=ALU.add), reads=[yk, "crow"], writes=[yk])
                    S.op("pool", lambda e, yy=yy: e.tensor_tensor(out=yy[0][:], in0=yy[0][:], in1=yy[2][:], op=ALU.add), reads=[yk], writes=[yk])
                    yield
                    S.op("pe", lambda e, tsl=tsl: e.matmul(banks[BG][:, :], lhsT=gdb[:, 0, tsl], rhs=gw2b[:, 0, :], start=True, stop=False),
                         reads=["gdb", "gw2b"], writes=[bk(BG)])
                    S.op("pe", lambda e, tsl=tsl: e.matmul(banks[BG][:, :], lhsT=gdb[0:32, 1, tsl], rhs=gw2b[0:32, 1, :], start=False, stop=True),
                         reads=["gdb", "gw2b"], writes=[bk(BG)])
                    S.op("dve", lambda e, yy=yy: e.tensor_tensor(out=rwb[:], in0=yy[0][:], in1=banks[BG][:, :], op=ALU.mult),
                         reads=[yk], writes=[bk(BG), sk("rwb")])
                    yield
                    pT = banks[BT].bitcast(BF16)
                    for j in range(4):
                        S.op("pe", lambda e, j=j: e.transpose(pT[:, j * 128:(j + 1) * 128], rwb[:, j * 128:(j + 1) * 128], ident[:]),
                             reads=[sk("rwb"), "ident"], writes=[bk(BT)])
                    S.op("act", lambda e: e.activation(out=mixT[:].rearrange("p j t -> p (j t)"), in_=pT[:, 0:512], func=AF.Copy),
                         writes=[bk(BT), sk("mixT")])
                    yield
                    for hh in range(2):
                        for j in range(8):
                            lhs = mixT[:, j, :] if j < 4 else convo[:, j - 4, tsl]
                            S.op("pe", lambda e, hh=hh, j=j, lhs=lhs: e.matmul(banks[BO0 + hh][:, :], lhsT=lhs, rhs=woutb[:, j, hh * 512:(hh + 1) * 512],
                                                                               start=(j == 0), stop=(j == 7)),
                                 reads=[sk("mixT"), "convo", "woutb"], writes=[bk(BO0 + hh)])
                    yield
                    post_norm_residual((BO0, BO1), gpost, xa[:], xk2, xa, xk2, pn_tmp, sk("pn"))
                    S.dma("sp", lambda e, xa=xa, tsl=tsl: e.dma_start(out=out_d[b, tsl, :], in_=xa[:]), "x1_st", reads=[xk2])

            for b in range(NB):
                _pd_batch(b)

        if PHASES >= 4:
            S.fence()
            A_.reset()
            w1b = A_.t([128, DK, 4096], BF16)
            w2b_ = A_.t([128, 32, D], BF16)
            mark = A_.off
            stg = [A_.t([128, DK, 512], F32) for _ in range(2)]
            for n in range(8):
                S.dma("sp", lambda e, n=n: e.dma_start(out=stg[n % 2][:], in_=w1_d[:, n * 512:(n + 1) * 512].rearrange("(k p) n -> p k n", p=128)),
                      "stge%d" % (n % 2), writes=["stge%d" % (n % 2)])
                if n % 2 == 0:
                    S.op("dve", lambda e, n=n: e.tensor_copy(out=w1b[:, :, n * 512:(n + 1) * 512], in_=stg[n % 2][:]), reads=["stge%d" % (n % 2)], writes=["w1b"])
                else:
                    S.op("act", lambda e, n=n: e.activation(out=w1b[:, :, n * 512:(n + 1) * 512], in_=stg[n % 2][:], func=AF.Copy),
                         reads=["stge%d" % (n % 2)], writes=["w1b"])
            for n in range(8):
                S.dma("sp", lambda e, n=n: e.dma_start(out=stg[n % 2][:].rearrange("p k n -> p (k n)").rearrange("p (f n) -> p f n", n=D),
                                                       in_=w2_d[n * 512:(n + 1) * 512, :].rearrange("(f p) n -> p f n", p=128)),
                      "stge%d" % (n % 2), writes=["stge%d" % (n % 2)])
                src = stg[n % 2][:].rearrange("p k n -> p (k n)").rearrange("p (f n) -> p f n", n=D)
                if n % 2 == 0:
                    S.op("dve", lambda e, n=n, src=src: e.tensor_copy(out=w2b_[:, n * 4:(n + 1) * 4, :], in_=src), reads=["stge%d" % (n % 2)], writes=["w2b_"])
                else:
                    S.op("act", lambda e, n=n, src=src: e.activation(out=w2b_[:, n * 4:(n + 1) * 4, :], in_=src, func=AF.Copy),
                         reads=["stge%d" % (n % 2)], writes=["w2b_"])
            S.fence()
            A_.off = mark
            G = 256
            hT2 = [A_.t([128, DK, G], BF16) for _ in range(2)]
            hidr = [A_.t([128, 2, G], BF16) for _ in range(4)]
            rl = [A_.t([128, 512], F32) for _ in range(2)]
            x1g = [A_.t([128, D], F32) for _ in range(4)]
            gpost2 = A_.t([128, D], F32)
            ftmp2 = {"sq": A_.t([128, D], F32), "ss": A_.t([128, 1], F32), "xn": A_.t([128, D], BF16)}
            pn_tmp2 = {"sq": ftmp2["sq"], "ss": A_.t([128, 1], F32)}
            groups = [(b, g) for b in range(NB) for g in range(T // G)]
            OB = ((5, 6), (0, 1))

            def pe_front(gi):
                b, g = groups[gi]
                xs = gi % 2
                for tq in range(2):
                    tsl = slice(g * G + tq * 128, g * G + (tq + 1) * 128)
                    xt_, xk_ = x1g[xs * 2 + tq], "x1g%d" % (xs * 2 + tq)
                    S.dma("sp", lambda e, xt_=xt_, tsl=tsl, b=b: e.dma_start(out=xt_[:], in_=out_d[b, tsl, :]), xk_, writes=[xk_])
                    front(xt_[:], xk_, hT2[xs][:, :, tq * 128:(tq + 1) * 128], "hT2_%d" % xs, 1, 3, b, ftmp2, 2)

            def pe_out_pair(gi, f2):
                xs = gi % 2
                hr, hk = hidr[f2 % 4], "hidr%d" % (f2 % 4)
                for ff in range(2):
                    f = f2 * 2 + ff
                    for tq in range(2):
                        for hh in range(2):
                            S.op("pe", lambda e, hr=hr, ff=ff, f=f, tq=tq, hh=hh: e.matmul(
                                banks[OB[tq][hh]][:, :], lhsT=hr[:, ff, tq * 128:(tq + 1) * 128], rhs=w2b_[:, f, hh * 512:(hh + 1) * 512],
                                start=(f == 0), stop=(f == 31)), reads=[hk, "w2b_"], writes=[bk(OB[tq][hh])])

            def pe_group(gi):
                b, g = groups[gi]
                xs = gi % 2
                if g == 0:
                    make_gpost(b, 1, "mlp_post_g", gpost2)
                for f2 in range(16):
                    pbk = 3 + (f2 % 2)
                    for ff in range(2):
                        f = f2 * 2 + ff
                        for k in range(DK):
                            S.op("pe", lambda e, pbk=pbk, ff=ff, f=f, k=k: e.matmul(
                                banks[pbk][:, ff * G:(ff + 1) * G], lhsT=w1b[:, k, f * 128:(f + 1) * 128], rhs=hT2[xs][:, k, :],
                                start=(k == 0), stop=(k == DK - 1)), reads=["w1b", "hT2_%d" % xs], writes=[bk(pbk)])
                    S.op("act", lambda e, pbk=pbk, f2=f2: e.activation(out=rl[f2 % 2][:], in_=banks[pbk][:, :], func=AF.Relu),
                         writes=[bk(pbk), "rl%d" % (f2 % 2)])
                    S.op("dve" if f2 % 2 == 0 else "pool", lambda e, f2=f2: e.tensor_tensor(
                        out=hidr[f2 % 4][:], in0=rl[f2 % 2][:].rearrange("p (a t) -> p a t", a=2),
                        in1=rl[f2 % 2][:].rearrange("p (a t) -> p a t", a=2), op=ALU.mult), reads=["rl%d" % (f2 % 2)], writes=["hidr%d" % (f2 % 4)])
                    if f2 >= 2:
                        pe_out_pair(gi, f2 - 2)
                    if f2 == 5 and gi + 1 < len(groups):
                        pe_front(gi + 1)
                pe_out_pair(gi, 14)
                pe_out_pair(gi, 15)
                for tq in range(2):
                    tsl = slice(g * G + tq * 128, g * G + (tq + 1) * 128)
                    xt_, xk_ = x1g[xs * 2 + tq], "x1g%d" % (xs * 2 + tq)
                    post_norm_residual(OB[tq], gpost2, xt_[:], xk_, xt_, xk_, pn_tmp2, "f")
                    S.dma("sp", lambda e, xt_=xt_, tsl=tsl, b=b: e.dma_start(out=out_d[b, tsl, :], in_=xt_[:]), "out_st", reads=[xk_])

            pe_front(0)
            for gi in range(len(groups)):
                pe_group(gi)
        S.run(nc, es)
    return nc


def _perm_cols():
    idx = list(range(0, 1536))
    idx += list(range(1536, 1600)) + list(range(1664, 1728))
    idx += list(range(1600, 1664)) + list(range(1728, 1792))
    idx += list(range(1792, 1952)) + [-1] * 96
    idx += list(range(1952, 2976))
    return np.array(idx)


def prep_inputs(inp, NB, ncores):
    f = lambda a: np.ascontiguousarray(np.asarray(a, dtype=np.float32))
    perm = _perm_cols()
    w_in = f(inp["w_in"])[0]
    w_in_p = np.zeros((D, 3072), np.float32)
    w_in_p[:, perm >= 0] = w_in[:, perm[perm >= 0]]

    def permvec(v):
        o = np.zeros(2048, np.float32)
        p16 = perm[:2048]
        o[p16 >= 0] = v[p16[p16 >= 0]]
        return o.reshape(16, 128).T

    fm = lambda v: np.asarray(v, np.float32).reshape(-1, 128).T
    cfm = np.zeros((128, NCF), np.float32)

    def put(nm, arr):
        o, w = CF[nm]
        assert arr.shape == (128, w), (nm, arr.shape)
        cfm[:, o:o + w] = arr
    put("ada_b", fm(inp["ada_b"][0]))
    put("mix_pre_g", fm(inp["mix_pre_g"][0]))
    put("mlp_pre_g", fm(inp["mlp_pre_g"][0]))
    put("mu_prev", permvec(f(inp["mu_prev"])[0]))
    put("mu_next", permvec(f(inp["mu_next"])[0]))
    put("w0", np.concatenate([fm(inp["decay_w0"][0, 0]), fm(inp["decay_w0"][0, 1])], 1))
    put("a0", np.concatenate([fm(inp["iclr_a0"][0, 0]), fm(inp["iclr_a0"][0, 1])], 1))
    put("k_k", fm(inp["k_k"][0]))
    put("k_a", fm(inp["k_a"][0]))
    put("r_k", fm(np.asarray(inp["r_k"][0]).reshape(-1)))
    cw = f(inp["conv_w"])[0]
    put("conv_w", cw.T.reshape(4, 128, 31).transpose(1, 0, 2).reshape(128, 124))
    put("conv_b", fm(inp["conv_b"][0]))
    put("cln_w", fm(inp["conv_ln_w"][0]))
    put("cln_b", fm(inp["conv_ln_b"][0]))
    crow = np.zeros((NCR,), np.float32)
    for nm, v in (("ada_b", inp["ada_b"][0]), ("mix_post_g", inp["mix_post_g"][0]),
                  ("mlp_post_g", inp["mlp_post_g"][0]), ("lnx_w", inp["lnx_w"][0]), ("lnx_b", inp["lnx_b"][0])):
        o, w = CR[nm]
        crow[o:o + w] = np.asarray(v, np.float32)
    crow = np.ascontiguousarray(np.broadcast_to(crow[None, :], (128, NCR)))
    x = f(inp["x"])
    ctx = f(inp["ctx"])
    c = f(inp["c"])
    cc = f(inp["c_ctx"])
    shared = {"ada_w": f(inp["ada_w"])[0], "w_in": w_in_p, "cfm": cfm, "crow": crow,
              "decay_w2": f(inp["decay_w2"])[0], "iclr_a2": f(inp["iclr_a2"])[0], "gate_w2": f(inp["gate_w2"])[0],
              "w_out": f(inp["w_out"])[0], "mlp_w1": f(inp["mlp_w1"])[0], "mlp_w2": f(inp["mlp_w2"])[0]}
    maps = []
    for i in range(ncores):
        cb = np.concatenate([c[i * NB:(i + 1) * NB], cc[None, :]], 0)
        cT = np.ascontiguousarray(cb.T.reshape(DK, 128, NB + 1).transpose(1, 0, 2))
        m = dict(shared)
        m.update({"x": np.ascontiguousarray(x[i * NB:(i + 1) * NB]), "ctx": np.ascontiguousarray(ctx[i * NB:(i + 1) * NB]),
                  "cT": cT})
        maps.append(m)
    return maps


def kernel(**inputs):
    B, T, _ = inputs["x"].shape
    TC = inputs["ctx"].shape[1]
    NB = B // NCORES
    nc = build(NB, T, TC)
    maps = prep_inputs(inputs, NB, NCORES)
    res = run_bass_kernel_spmd(nc, maps, core_ids=list(range(NCORES)))
    return np.concatenate([np.asarray(r["out"]) for r in res.results], 0).astype(np.float32)
```

```python
from contextlib import ExitStack
import os
import numpy as np
import concourse.bass as bass
import concourse.mybir as mybir
from concourse.bass_utils import run_bass_kernel_spmd

F32 = mybir.dt.float32
BF16 = mybir.dt.bfloat16
AF = mybir.ActivationFunctionType
ALU = mybir.AluOpType
AX = mybir.AxisListType

ENGS = ("pe", "act", "dve", "pool", "sp")
CH = 30000
D = 1024
DK = 8
NCORES = 8
CDEC = 0.6065306597126334


class Sched:
    def __init__(self):
        self.q = {e: [] for e in ENGS}
        self.cnt = {e: 0 for e in ENGS}
        self.seen = {e: {} for e in ENGS}
        self.last_w = {}
        self.readers = {}
        self.dma_cnt = {}
        self.dma_keys = []
        self.fence_snap = None
        self.fenced = {e: True for e in ENGS}

    def fence(self):
        snap = [(e, self.cnt[e]) for e in ENGS if self.cnt[e] > 0]
        snap += [("dma:" + k, n) for k, n in self.dma_cnt.items()]
        self.fence_snap = snap
        self.fenced = {e: False for e in ENGS}
        self.last_w = {}
        self.readers = {}

    def _deps(self, eng, reads, writes):
        deps = set()
        if not self.fenced[eng]:
            self.fenced[eng] = True
            deps |= set(self.fence_snap)
        for k in reads:
            if k in self.last_w:
                deps.add(self.last_w[k])
        for k in writes:
            if k in self.last_w:
                deps.add(self.last_w[k])
            deps |= self.readers.get(k, set())
        need = {}
        for (e, s) in deps:
            need[e] = max(need.get(e, 0), s)
        waits = []
        for e, s in need.items():
            if e == "pe" and eng == "pe":
                continue
            if self.seen[eng].get(e, 0) >= s:
                continue
            self.seen[eng][e] = s
            waits.append((e, s))
        return waits

    def _commit(self, me, reads, writes):
        for k in reads:
            self.readers.setdefault(k, set()).add(me)
        for k in writes:
            self.last_w[k] = me
            self.readers[k] = set()

    def op(self, eng, fn, reads=(), writes=()):
        waits = self._deps(eng, reads, writes)
        self.cnt[eng] += 1
        self.q[eng].append(("op", waits, fn, self.cnt[eng]))
        self._commit((eng, self.cnt[eng]), reads, writes)

    def dma(self, eng, fn, semkey, reads=(), writes=()):
        waits = self._deps(eng, reads, writes)
        if semkey not in self.dma_cnt:
            self.dma_cnt[semkey] = 0
            self.dma_keys.append(semkey)
        self.dma_cnt[semkey] += 1
        self.q[eng].append(("dma", waits, fn, semkey))
        self._commit(("dma:" + semkey, self.dma_cnt[semkey]), reads, writes)

    def emit_engine(self, eng, engobj, sems, dma_sems):
        def do_wait(e, s):
            if e.startswith("dma:"):
                engobj.wait_ge(dma_sems[e[4:]], 16 * s)
            else:
                engobj.wait_ge(sems[e][(s - 1) // CH], ((s - 1) % CH) + 1)

        for item in self.q[eng]:
            for (e, s) in item[1]:
                do_wait(e, s)
            if item[0] == "op":
                item[2](engobj).then_inc(sems[eng][(item[3] - 1) // CH], 1)
            else:
                item[2](engobj).then_inc(dma_sems[item[3]], 16)
        if eng == "sp":
            for k, n in self.dma_cnt.items():
                engobj.wait_ge(dma_sems[k], 16 * n)

    def run(self, nc, es):
        sems = {e: [es.enter_context(nc.semaphore("s_%s_%d" % (e, i)))
                    for i in range(max(1, (self.cnt[e] + CH - 1) // CH))] for e in ENGS}
        dma_sems = {k: es.enter_context(nc.semaphore("d_%d" % i)) for i, k in enumerate(self.dma_keys)}
        block = es.enter_context(nc.Block())
        block.tensor(lambda e: self.emit_engine("pe", e, sems, dma_sems))
        block.scalar(lambda e: self.emit_engine("act", e, sems, dma_sems))
        block.vector(lambda e: self.emit_engine("dve", e, sems, dma_sems))
        block.gpsimd(lambda e: self.emit_engine("pool", e, sems, dma_sems))
        block.sync(lambda e: self.emit_engine("sp", e, sems, dma_sems))


class Arena:
    def __init__(self, nc, base, limit):
        self.nc, self.base, self.limit, self.off, self.n = nc, base, limit, base, 0

    def reset(self):
        self.off = self.base

    def t(self, shape, dt):
        nb = int(np.prod(shape[1:])) * (4 if dt == F32 else 2)
        nb = (nb + 63) // 64 * 64
        assert self.off + nb <= self.limit, ("SBUF overflow", self.off, nb, self.limit)
        self.n += 1
        h = self.nc.alloc_sbuf_tensor_at("a%d" % self.n, list(shape), dt, offset=self.off)
        self.off += nb
        return h


CF = {}
CR = {}


def _layout():
    off = 0
    for nm, w in (("ada_b", 48), ("mix_pre_g", 8), ("mlp_pre_g", 8), ("mu_prev", 16), ("mu_next", 16),
                  ("w0", 8), ("a0", 8), ("k_k", 4), ("k_a", 4), ("r_k", 4), ("conv_w", 124),
                  ("conv_b", 4), ("cln_w", 4), ("cln_b", 4)):
        CF[nm] = (off, w)
        off += w
    ncf = off
    off = 0
    for nm, w in (("ada_b", 6144), ("mix_post_g", 1024), ("mlp_post_g", 1024), ("lnx_w", 512), ("lnx_b", 512)):
        CR[nm] = (off, w)
        off += w
    return ncf, off


NCF, NCR = _layout()


def build(NB, T, TC, debug=False, PHASES=9):
    TT = TC + T
    NB1 = NB + 1
    nc = bass.Bass("TRN2", target_bir_lowering=False)
    dram = lambda n, s, dt, kind: nc.dram_tensor(n, list(s), dt, kind=kind).ap()
    x_d = dram("x", [NB, T, D], F32, "ExternalInput")
    ctx_d = dram("ctx", [NB, TC, D], F32, "ExternalInput")
    cT_d = dram("cT", [128, DK, NB1], F32, "ExternalInput")
    adaw_d = dram("ada_w", [D, 6144], F32, "ExternalInput")
    win_d = dram("w_in", [D, 3072], F32, "ExternalInput")
    cfm_d = dram("cfm", [128, NCF], F32, "ExternalInput")
    crow_d = dram("crow", [128, NCR], F32, "ExternalInput")
    w2d_d = dram("decay_w2", [2, 64, 512], F32, "ExternalInput")
    a2_d = dram("iclr_a2", [2, 64, 512], F32, "ExternalInput")
    gw2_d = dram("gate_w2", [160, 512], F32, "ExternalInput")
    wout_d = dram("w_out", [D, D], F32, "ExternalInput")
    w1_d = dram("mlp_w1", [D, 4096], F32, "ExternalInput")
    w2_d = dram("mlp_w2", [4096, D], F32, "ExternalInput")
    out_d = dram("out", [NB, T, D], F32, "ExternalOutput")
    skind = "ExternalOutput" if debug else "Internal"
    pa_d = dram("pa_s", [NB, 20, 128, TT], BF16, skind)
    y_d = dram("y_s", [NB, 2, T, 512], F32, skind)
    bo_d = dram("bo_s", [NB, 2, T, 512], F32, skind)

    S = Sched()
    es = ExitStack()
    with es:
        banks = [es.enter_context(nc.psum_tensor("bank%d" % i, [128, 512], F32)) for i in range(8)]
        bk = lambda i: "bank%d" % i
        P_ = Arena(nc, 17408, 51 * 1024)
        A_ = Arena(nc, 51 * 1024, 223 * 1024)

        cfm = P_.t([128, NCF], F32)
        crow = P_.t([128, NCR - 6144], F32)
        ident = P_.t([128, 128], BF16)
        identf = P_.t([128, 128], F32)
        bones = P_.t([128, 128], F32)
        hsel = P_.t([128, 2], BF16)
        msk = P_.t([128, 2, 2, 128], BF16)
        mskN = P_.t([128, 2, 64], BF16)
        identB = P_.t([128, 64], BF16)
        onesb = P_.t([128, 1], BF16)
        rstm = P_.t([128, 512], F32)
        modfm = P_.t([128, 48, NB1], F32)
        gates = P_.t([NB1, 2, 1024], F32)
        gm = P_.t([128, 2, DK, NB1], F32)
        c0 = P_.t([128, 16], F32)
        sel = P_.t([NB1, NB1, 128], F32)
        cf = lambda nm: cfm[:, CF[nm][0]:CF[nm][0] + CF[nm][1]]
        cr = lambda nm: crow[:, CR[nm][0] - 6144:CR[nm][0] - 6144 + CR[nm][1]]

        S.dma("sp", lambda e: e.dma_start(out=cfm[:], in_=cfm_d[:, :]), "c0", writes=["cfm"])
        S.dma("sp", lambda e: e.dma_start(out=crow[:], in_=crow_d[:, 6144:NCR]), "c0", writes=["crow"])
        S.op("pool", lambda e: e.memset(identf[:], 1.0), writes=["identf"])
        S.op("pool", lambda e: e.affine_select(out=identf[:], in_=identf[:], pattern=[[-1, 128]],
                                               compare_op=ALU.is_equal, fill=0.0, base=0, channel_multiplier=1),
             reads=["identf"], writes=["identf"])
        S.op("dve", lambda e: e.tensor_copy(out=ident[:], in_=identf[:]), reads=["identf"], writes=["ident"])
        S.op("pool", lambda e: e.memset(bones[:], 0.0), writes=["bones"])
        S.op("pool", lambda e: e.memset(bones[0:64, 0:64], 1.0), reads=["bones"], writes=["bones"])
        S.op("pool", lambda e: e.memset(bones[64:128, 64:128], 1.0), reads=["bones"], writes=["bones"])
        S.op("pool", lambda e: e.memset(hsel[:], 0.0), writes=["hsel"])
        S.op("pool", lambda e: e.memset(hsel[0:64, 0:1], 1.0), reads=["hsel"], writes=["hsel"])
        S.op("pool", lambda e: e.memset(hsel[64:128, 1:2], 1.0), reads=["hsel"], writes=["hsel"])
        mtmp = P_.t([64, 64], F32)

        def mk_mask(dst_ap, sign, strict):
            S.op("pool", lambda e: e.memset(mtmp[:], 1.0), reads=["mtmp"], writes=["mtmp"])
            S.op("pool", lambda e: e.affine_select(out=mtmp[:], in_=mtmp[:], pattern=[[sign, 64]],
                                                   compare_op=ALU.is_gt if strict else ALU.is_ge,
                                                   fill=0.0, base=0, channel_multiplier=-sign),
                 reads=["mtmp"], writes=["mtmp"])
            S.op("pool", lambda e: e.tensor_copy(out=dst_ap, in_=mtmp[:]), reads=["mtmp"], writes=["msk"])
        for d in range(2):
            sg_ = 1 if d == 0 else -1
            for rr in range(2):
                mk_mask(msk[0:64, d, rr, 0:64], sg_, True)
                mk_mask(msk[0:64, d, rr, 64:128], sg_, False)
            mk_mask(mskN[0:64, d, :], -sg_, True)
        S.dma("sp", lambda e: e.dma_start(out=msk[64:128], in_=msk[0:64]), "c0", reads=["msk"], writes=["msk"])
        S.dma("sp", lambda e: e.dma_start(out=mskN[64:128], in_=mskN[0:64]), "c0", reads=["msk"], writes=["msk"])
        S.op("dve", lambda e: e.tensor_tensor(out=identB[:], in0=ident[:, 0:64], in1=ident[:, 64:128], op=ALU.add), reads=["ident"], writes=["identB"])
        S.op("pool", lambda e: e.memset(onesb[:], 1.0), writes=["onesb"])
        S.op("pool", lambda e: e.memset(rstm[:], 1.0), writes=["rstm"])
        S.op("pool", lambda e: e.memset(rstm[:].rearrange("p (c t) -> p c t", t=64)[:, :, 0:1], 0.0),
             reads=["rstm"], writes=["rstm"])
        S.op("pool", lambda e: e.memset(sel[:], 0.0), writes=["sel"])
        for b in range(NB1):
            S.op("pool", lambda e, b=b: e.memset(sel[:, b, :], 1.0), reads=["sel"], writes=["sel"])
            S.op("pool", lambda e, b=b: e.affine_select(out=sel[:, b, :], in_=sel[:, b, :], pattern=[[0, 128]],
                                                        compare_op=ALU.is_equal, fill=0.0, base=-b,
                                                        channel_multiplier=1),
                 reads=["sel"], writes=["sel"])

        A_.reset()
        cT = A_.t([128, DK, NB1], F32)
        siluT = A_.t([128, DK, NB1], F32)
        adab_row = A_.t([NB1, 6144], F32)
        aw = [A_.t([128, DK, 512], F32) for _ in range(4)]
        S.dma("sp", lambda e: e.dma_start(out=cT[:], in_=cT_d[:, :, :]), "c0", writes=["cT"])
        S.dma("sp", lambda e: e.dma_start(out=adab_row[:], in_=crow_d[0:NB1, 0:6144]), "c0", writes=["adab_row"])
        S.op("act", lambda e: e.activation(out=siluT[:], in_=cT[:], func=AF.Silu), reads=["cT"], writes=["siluT"])
        for n in range(12):
            a = aw[n % 4]
            ak = "aw%d" % (n % 4)
            S.dma("sp", lambda e, a=a, n=n: e.dma_start(
                out=a[:], in_=adaw_d[:, n * 512:(n + 1) * 512].rearrange("(k p) n -> p k n", p=128)),
                ak, writes=[ak])
            m = n // 2
            if m in (2, 5):
                for k in range(DK):
                    S.op("pe", lambda e, a=a, k=k: e.matmul(banks[0][0:NB1, :], lhsT=siluT[:, k, :], rhs=a[:, k, :],
                                                            start=(k == 0), stop=(k == DK - 1)),
                         reads=[ak, "siluT"], writes=[bk(0)])
                gi = 0 if m == 2 else 1
                S.op("dve", lambda e, n=n, gi=gi: e.tensor_tensor(
                    out=gates[:, gi, (n % 2) * 512:(n % 2) * 512 + 512], in0=banks[0][0:NB1, :],
                    in1=adab_row[:, n * 512:(n + 1) * 512], op=ALU.add),
                    reads=["adab_row"], writes=[bk(0), "gates"])
            else:
                for j in range(4):
                    for k in range(DK):
                        S.op("pe", lambda e, a=a, k=k, j=j: e.matmul(
                            banks[1][:, j * NB1:(j + 1) * NB1], lhsT=a[:, k, j * 128:(j + 1) * 128],
                            rhs=siluT[:, k, :], start=(k == 0), stop=(k == DK - 1)),
                            reads=[ak, "siluT"], writes=[bk(1)])
                S.op("dve", lambda e, n=n: e.tensor_tensor(
                    out=modfm[:, n * 4:(n + 1) * 4, :],
                    in0=banks[1][:, 0:4 * NB1].rearrange("p (j b) -> p j b", b=NB1),
                    in1=cf("ada_b")[:, n * 4:(n + 1) * 4].unsqueeze(2).to_broadcast([128, 4, NB1]), op=ALU.add),
                    reads=["cfm"], writes=[bk(1), "modfm"])
        for gi, (gn, m) in enumerate((("mix_pre_g", 1), ("mlp_pre_g", 4))):
            S.op("dve", lambda e, gi=gi, m=m: e.tensor_scalar(out=gm[:, gi], in0=modfm[:, m * 8:(m + 1) * 8, :],
                                                              scalar1=1.0, scalar2=None, op0=ALU.add),
                 reads=["modfm"], writes=["gm"])
            S.op("dve", lambda e, gi=gi, gn=gn: e.tensor_tensor(
                out=gm[:, gi], in0=gm[:, gi], in1=cf(gn).unsqueeze(2).to_broadcast([128, DK, NB1]), op=ALU.mult),
                reads=["gm", "cfm"], writes=["gm"])
        S.op("dve", lambda e: e.tensor_tensor(out=c0[:], in0=cf("mu_prev"), in1=cf("mu_next"), op=ALU.add),
             reads=["cfm"], writes=["c0"])
        S.op("dve", lambda e: e.tensor_scalar(out=c0[:], in0=c0[:], scalar1=-1.0, scalar2=1.0, op0=ALU.mult,
                                              op1=ALU.add), reads=["c0"], writes=["c0"])

        def front(xt_ap, xkey, hT_ap, hkey, gi, shm, b, tmp, pbank):
            S.op("act", lambda e: e.activation(out=tmp["sq"][:], in_=xt_ap, func=AF.Square),
                 reads=[xkey], writes=["f_sq"])
            S.op("dve", lambda e: e.reduce_sum(out=tmp["ss"][:], in_=tmp["sq"][:], axis=AX.X),
                 reads=["f_sq"], writes=["f_ss"])
            S.op("dve", lambda e: e.tensor_scalar(out=tmp["ss"][:], in0=tmp["ss"][:], scalar1=1.0 / D, scalar2=1e-6,
                                                  op0=ALU.mult, op1=ALU.add), reads=["f_ss"], writes=["f_ss"])
            S.op("act", lambda e: e.activation(out=tmp["ss"][:], in_=tmp["ss"][:], func=AF.Sqrt),
                 reads=["f_ss"], writes=["f_ss"])
            S.op("dve", lambda e: e.reciprocal(out=tmp["ss"][:], in_=tmp["ss"][:]), reads=["f_ss"], writes=["f_ss"])
            S.op("act", lambda e: e.activation(out=tmp["xn"][:], in_=xt_ap, func=AF.Copy, scale=tmp["ss"][:, 0:1]),
                 reads=[xkey, "f_ss"], writes=["f_xn"])
            pb = banks[pbank].bitcast(BF16)
            for k in range(DK):
                S.op("pe", lambda e, k=k: e.transpose(pb[:, k * 128:(k + 1) * 128], tmp["xn"][:, k * 128:(k + 1) * 128],
                                                      ident[:]), reads=["f_xn", "ident"], writes=[bk(pbank)])
            S.op("dve", lambda e: e.tensor_tensor(out=hT_ap, in0=pb[:, 0:1024].rearrange("p (k t) -> p k t", t=128),
                                                  in1=gm[:, gi, :, b:b + 1].to_broadcast([128, DK, 128]), op=ALU.mult),
                 reads=["gm"], writes=[bk(pbank), hkey])
            S.op("dve", lambda e: e.tensor_tensor(
                out=hT_ap, in0=hT_ap, in1=modfm[:, shm * 8:(shm + 1) * 8, b:b + 1].to_broadcast([128, DK, 128]),
                op=ALU.add), reads=["modfm", hkey], writes=[hkey])

        S.fence()
        A_.reset()
        winb = A_.t([128, DK, 3072], BF16)
        stg = [A_.t([128, DK, 512], F32) for _ in range(3)]
        for n in range(6):
            s_ = stg[n % 3]
            sk = "stg%d" % (n % 3)
            S.dma("sp", lambda e, s_=s_, n=n: e.dma_start(
                out=s_[:], in_=win_d[:, n * 512:(n + 1) * 512].rearrange("(k p) n -> p k n", p=128)), sk, writes=[sk])
            S.op("dve" if n % 2 == 0 else "act",
                 (lambda e, s_=s_, n=n: e.tensor_copy(out=winb[:, :, n * 512:(n + 1) * 512], in_=s_[:])) if n % 2 == 0
                 else (lambda e, s_=s_, n=n: e.activation(out=winb[:, :, n * 512:(n + 1) * 512], in_=s_[:], func=AF.Copy)),
                 reads=[sk], writes=["winb"])
        TM = max(T, TC)
        hT = A_.t([128, DK, TM + 2], BF16)
        xt = [A_.t([128, D], F32) for _ in range(2)]
        ftmp = {"sq": A_.t([128, D], F32), "ss": A_.t([128, 1], F32), "xn": A_.t([128, D], BF16)}
        etmp = [A_.t([128, 512], F32) for _ in range(2)]
        obuf = [A_.t([128, 512], BF16) for _ in range(3)]
        A_sg = [A_.t([128, 512], F32) for _ in range(4)]
        S.op("pool", lambda e: e.memset(hT[:], 0.0), writes=["hT"])
        xi = 0
        ob_i = 0
        pb_i = 0
        for b in range(NB):
            for (src, Ts, toff, bmod, tiles) in ((ctx_d, TC, 0, NB, list(range(4, 14))),
                                                 (x_d, T, TC, b, list(range(0, 24)))):
                if Ts < TM:
                    S.op("pool", lambda e, Ts=Ts: e.memset(hT[:, :, Ts + 1:Ts + 2], 0.0), reads=["hT"], writes=["hT"])
                for tt in range(Ts // 128):
                    xa = xt[xi % 2]
                    xk = "xt%d" % (xi % 2)
                    xi += 1
                    S.dma("sp", lambda e, xa=xa, src=src, b=b, tt=tt: e.dma_start(
                        out=xa[:], in_=src[b, tt * 128:(tt + 1) * 128, :]), xk, writes=[xk])
                    front(xa[:], xk, hT[:, :, 1 + tt * 128:1 + (tt + 1) * 128], "hT", 0, 0, bmod, ftmp, 7)
                w0 = 0
                while w0 < Ts:
                    n = min(510, Ts - w0)
                    sg_ready = {}
                    for j in [jj for jj in tiles if jj >= 20] + [jj for jj in tiles if jj < 20]:
                        pbk = pb_i % 4
                        pb_i += 1
                        pbt = banks[pbk]
                        for k in range(DK):
                            S.op("pe", lambda e, pbt=pbt, k=k, j=j, w0=w0, n=n: e.matmul(
                                pbt[:, 0:n + 2], lhsT=winb[:, k, j * 128:(j + 1) * 128], rhs=hT[:, k, w0:w0 + n + 2],
                                start=(k == 0), stop=(k == DK - 1)), reads=["winb", "hT"], writes=[bk(pbk)])
                        if j >= 20:
                            sgt = A_sg[j - 20]
                            S.op("act", lambda e, pbt=pbt, sgt=sgt, n=n: e.activation(
                                out=sgt[:, 0:n], in_=pbt[:, 1:n + 1], func=AF.Sigmoid),
                                writes=[bk(pbk), "sg%d" % (j - 20)])
                            continue
                        ob = obuf[ob_i % 3]
                        ok = "ob%d" % (ob_i % 3)
                        ob_i += 1
                        if j >= 16:
                            sgt = A_sg[j - 16]
                            S.op("dve", lambda e, pbt=pbt, sgt=sgt, ob=ob, n=n: e.tensor_tensor(
                                out=ob[:, 0:n], in0=pbt[:, 1:n + 1], in1=sgt[:, 0:n], op=ALU.mult),
                                reads=["sg%d" % (j - 16)], writes=[bk(pbk), ok])
                        else:
                            et = etmp[j % 2]
                            ek = "et%d" % (j % 2)
                            S.op("act", lambda e, pbt=pbt, et=et, n=n, j=j: e.activation(
                                out=et[:, 0:n], in_=pbt[:, 1:n + 1], func=AF.Copy, scale=c0[:, j:j + 1]),
                                reads=["c0"], writes=[bk(pbk), ek])
                            S.op("dve", lambda e, pbt=pbt, et=et, n=n, j=j: e.scalar_tensor_tensor(
                                out=et[:, 0:n], in0=pbt[:, 0:n], scalar=cf("mu_prev")[:, j:j + 1], in1=et[:, 0:n],
                                op0=ALU.mult, op1=ALU.add), reads=["cfm", ek], writes=[bk(pbk), ek])
                            S.op("dve", lambda e, pbt=pbt, et=et, ob=ob, n=n, j=j: e.scalar_tensor_tensor(
                                out=ob[:, 0:n], in0=pbt[:, 2:n + 2], scalar=cf("mu_next")[:, j:j + 1], in1=et[:, 0:n],
                                op0=ALU.mult, op1=ALU.add), reads=["cfm", ek], writes=[bk(pbk), ok])
                        S.dma("sp", lambda e, ob=ob, b=b, j=j, toff=toff, w0=w0, n=n: e.dma_start(
                            out=pa_d[b, j, :, toff + w0:toff + w0 + n], in_=ob[:, 0:n]), "pa_st", reads=[ok])
                    w0 += n
        if PHASES >= 2:
            S.fence()
            A_.reset()
            W = 128
            NCW = W // 64
            lwb = A_.t([128, 2, 512], BF16)
            omk = A_.t([128, 4], F32)
            rkb = A_.t([128, 4], BF16)
            mark_pb = A_.off
            wst = A_.t([128, 2, 512], F32)
            S.dma("sp", lambda e: e.dma_start(out=wst[0:64], in_=w2d_d.rearrange("d r f -> r d f")), "pbw", writes=["wst"])
            S.dma("sp", lambda e: e.dma_start(out=wst[64:128], in_=a2_d.rearrange("d r f -> r d f")), "pbw", writes=["wst"])
            S.op("dve", lambda e: e.tensor_copy(out=lwb[:], in_=wst[:]), reads=["wst"], writes=["lwb"])
            S.op("dve", lambda e: e.tensor_scalar(out=omk[:], in0=cf("k_a"), scalar1=-1.0, scalar2=1.0, op0=ALU.mult, op1=ALU.add), reads=["cfm"], writes=["omk"])
            S.op("dve", lambda e: e.tensor_copy(out=rkb[:], in_=cf("r_k")), reads=["cfm"], writes=["rkb"])
            S.fence()
            A_.off = mark_pb
            rs = A_.t([128, 4, TT], BF16)
            ks = A_.t([128, 4, TT], BF16)
            vs = A_.t([128, 4, TT], BF16)
            wdad = A_.t([128, 2, TT], BF16)
            f32t = lambda: A_.t([128, 4, W], F32)
            TMP = [dict(sig=f32t(), Ls=f32t(), ee=f32t(), t1=f32t(), t2=f32t(), icl=A_.t([128, 4, W], BF16),
                        SC=A_.t([128, 4, NCW], F32)) for _ in range(2)]
            ar = [A_.t([128, 4, NCW, 2, 64], BF16) for _ in range(6)]
            bkt = [A_.t([128, 4, NCW, 2, 64], BF16) for _ in range(6)]
            prod = [A_.t([128, 4, W], BF16) for _ in range(6)]
            eLC = [A_.t([128, 4, NCW], F32) for _ in range(6)]
            H32 = [A_.t([128, 4, 64], F32) for _ in range(2)]
            Hbf = [A_.t([128, 4, 64], BF16) for _ in range(2)]
            NJS = 4 * NCW
            btk = [A_.t([128, 512], BF16) for _ in range(NJS)]
            vtm = [A_.t([128, 4, 64], BF16) for _ in range(NJS)]
            AT = [A_.t([128, 4, 2, 128], BF16) for _ in range(NJS)]
            XT = [A_.t([128, 4, 64], BF16) for _ in range(NJS)]
            PQm = [[A_.t([128, 2, 4, 64], BF16) for _ in range(2)] for _ in range(2 * NCW)]
            Xm = [[A_.t([128, 4, 64], BF16) for _ in range(2)] for _ in range(2 * NCW)]
            Rsb = [A_.t([128, 4, 64], BF16) for _ in range(2)]
            Usb = [A_.t([128, 4, 64], BF16) for _ in range(2)]
            ybuf = [A_.t([128, 4, 64], F32) for _ in range(2)]
            bosb = [A_.t([128, 4, 64], F32) for _ in range(2)]
            bon = [A_.t([128, 4], F32) for _ in range(2)]
            K_ = lambda nm, d: "%s%d" % (nm, d)

            def prep(b, d, w0, par):
                dp = d * 3 + par
                sig, Ls, ee, t1, t2, icl, SC = [TMP[d][k_] for k_ in ("sig", "Ls", "ee", "t1", "t2", "icl", "SC")]
                kS, kL, kE, k1, k2, kI, kC = [K_(k_, d) for k_ in ("sig", "Ls", "ee", "t1", "t2", "icl", "SC")]
                PB_ = 6 + d
                pbv = lambda i: banks[PB_][:, i * W:(i + 1) * W]
                pb4 = banks[PB_][:, 0:4 * W].rearrange("p (a t) -> p a t", t=W)
                v5 = lambda tns, a: tns[:].rearrange("p i (c t) -> p i c t", t=64) if a is None else tns[:, :, :, a, :]
                for i in range(4):
                    S.op("pe", lambda e, i=i: e.matmul(pbv(i), lhsT=lwb[0:64, d, i * 128:(i + 1) * 128], rhs=wdad[0:64, d, w0:w0 + W],
                                                       start=True, stop=True), reads=["lwb", "wdad"], writes=[bk(PB_)])
                for i in range(4):
                    S.op("act", lambda e, i=i: e.activation(out=sig[:, i, :], in_=pbv(i), func=AF.Sigmoid,
                                                            bias=cf("w0")[:, d * 4 + i:d * 4 + i + 1]), reads=["cfm"], writes=[bk(PB_), kS])
                yield
                for i in range(4):
                    S.op("pe", lambda e, i=i: e.matmul(pbv(i), lhsT=lwb[64:128, d, i * 128:(i + 1) * 128], rhs=wdad[64:128, d, w0:w0 + W],
                                                       start=True, stop=True), reads=["lwb", "wdad"], writes=[bk(PB_)])
                for i in range(4):
                    S.op("act", lambda e, i=i: e.activation(out=icl[:, i, :], in_=pbv(i), func=AF.Sigmoid,
                                                            bias=cf("a0")[:, d * 4 + i:d * 4 + i + 1]), reads=["cfm"], writes=[bk(PB_), kI])
                yield
                for i in range(4):
                    S.op("dve", lambda e, i=i: e.tensor_scalar(out=t1[:, i, :], in0=ks[:, i, w0:w0 + W], scalar1=cf("k_k")[:, i:i + 1],
                                                               scalar2=None, op0=ALU.mult), reads=["ks", "cfm"], writes=[k1])
                S.op("pool", lambda e: e.tensor_tensor(out=t2[:], in0=t1[:], in1=t1[:], op=ALU.mult), reads=[k1], writes=[k2])
                yield
                for i in range(4):
                    S.op("pe", lambda e, i=i: e.matmul(pbv(i), lhsT=bones[:], rhs=t2[:, i, :], start=True, stop=True),
                         reads=["bones", k2], writes=[bk(PB_)])
                S.op("dve", lambda e: e.tensor_scalar(out=ee[:], in0=pb4, scalar1=1e-24, scalar2=None, op0=ALU.max),
                     writes=[bk(PB_), kE])
                yield
                S.op("act", lambda e: e.activation(out=ee[:], in_=ee[:], func=AF.Sqrt), reads=[kE], writes=[kE])
                yield
                S.op("dve", lambda e: e.reciprocal(out=ee[:], in_=ee[:]), reads=[kE], writes=[kE])
                yield
                S.op("pool", lambda e: e.tensor_tensor(out=t1[:], in0=t1[:], in1=ee[:], op=ALU.mult), reads=[k1, kE], writes=[k1])
                for i in range(4):
                    S.op("dve", lambda e, i=i: e.tensor_tensor_scan(out=Ls[:, i, :], data0=rstm[:, 0:W], data1=sig[:, i, :],
                                                                    initial=0.0, op0=ALU.mult, op1=ALU.add),
                         reads=["rstm", kS], writes=[kL])
                lsc = v5(Ls, None)
                S.op("dve", lambda e: e.tensor_copy(out=SC[:], in_=lsc[:, :, :, 63]), reads=[kL], writes=[kC])
                yield
                S.op("act", lambda e: e.activation(out=eLC[dp][:], in_=SC[:], func=AF.Exp, scale=-CDEC), reads=[kC], writes=[K_("eLC", dp)])
                if d == 0:
                    S.op("dve", lambda e: e.tensor_tensor(out=sig[:], in0=Ls[:], in1=sig[:], op=ALU.subtract), reads=[kL, kS], writes=[kS])
                    XE, kXE, XI, kXI = sig, kS, Ls, kL
                else:
                    S.op("dve", lambda e: e.tensor_tensor(out=lsc, in0=SC[:].unsqueeze(3).to_broadcast([128, 4, NCW, 64]),
                                                          in1=lsc, op=ALU.subtract), reads=[kL, kC], writes=[kL])
                    S.op("dve", lambda e: e.tensor_tensor(out=sig[:], in0=Ls[:], in1=sig[:], op=ALU.add), reads=[kL, kS], writes=[kS])
                    XE, kXE, XI, kXI = Ls, kL, sig, kS
                yield
                S.op("act", lambda e: e.activation(out=ee[:], in_=XE[:], func=AF.Exp, scale=-CDEC), reads=[kXE], writes=[kE])
                yield
                S.op("dve", lambda e: e.scalar_tensor_tensor(out=v5(ar[dp], 0), in0=v5(t1, None), scalar=-1.0, in1=v5(ee, None),
                                                             op0=ALU.mult, op1=ALU.mult), reads=[k1, kE], writes=[K_("ar", dp)])
                yield
                S.op("act", lambda e: e.activation(out=ee[:], in_=XI[:], func=AF.Exp, scale=-CDEC), reads=[kXI], writes=[kE])
                yield
                S.op("pool", lambda e: e.tensor_tensor(out=v5(ar[dp], 1), in0=rs[:, :, w0:w0 + W].rearrange("p i (c t) -> p i c t", t=64),
                                                       in1=v5(ee, None), op=ALU.mult), reads=["rs", kE], writes=[K_("ar", dp)])
                yield
                S.op("act", lambda e: e.activation(out=ee[:], in_=XI[:], func=AF.Exp, scale=CDEC), reads=[kXI], writes=[kE])
                S.op("pool", lambda e: e.tensor_tensor(out=t2[:], in0=t1[:], in1=icl[:], op=ALU.mult), reads=[k1, kI], writes=[k2])
                yield
                S.op("dve", lambda e: e.tensor_tensor(out=v5(bkt[dp], 0), in0=v5(t2, None), in1=v5(ee, None), op=ALU.mult),
                     reads=[k2, kE], writes=[K_("bkt", dp)])
                yield
                for i in range(4):
                    S.op("dve", lambda e, i=i: e.tensor_scalar(out=t2[:, i, :], in0=icl[:, i, :], scalar1=cf("k_a")[:, i:i + 1],
                                                               scalar2=omk[:, i:i + 1], op0=ALU.mult, op1=ALU.add),
                         reads=[kI, "cfm", "omk"], writes=[k2])
                yield
                S.op("pool", lambda e: e.tensor_tensor(out=t2[:], in0=t2[:], in1=ks[:, :, w0:w0 + W], op=ALU.mult), reads=[k2, "ks"], writes=[k2])
                yield
                S.op("dve", lambda e: e.tensor_tensor(out=v5(bkt[dp], 1), in0=v5(t2, None), in1=v5(ee, None), op=ALU.mult),
                     reads=[k2, kE], writes=[K_("bkt", dp)])
                yield
                S.op("pool", lambda e: e.tensor_tensor(out=prod[dp][:], in0=t2[:], in1=rs[:, :, w0:w0 + W], op=ALU.mult),
                     reads=[k2, "rs"], writes=[K_("prod", dp)])
                yield

            HP = [((h % 2) * 64, h // 2) for h in range(8)]
            V3 = lambda bi: banks[bi][:, 0:256].rearrange("p (i s) -> p i s", s=64)

            def inv(b, d, w0, c, par3, js, jt, B):
                tk0 = w0 + c * 64
                dp = d * 3 + par3
                arK, bkK = K_("ar", dp), K_("bkt", dp)
                pT = banks[B].bitcast(BF16)
                for q in range(2):
                    for (po, i) in HP:
                        S.op("pe", lambda e, q=q, po=po, i=i: e.transpose(
                            pT[po:po + 64, (q * 4 + i) * 64:(q * 4 + i + 1) * 64], bkt[dp][po:po + 64, i, c, q, :],
                            ident[po:po + 64, po:po + 64]), reads=[bkK, "ident"], writes=[bk(B)])
                for (po, i) in HP:
                    S.op("pe", lambda e, po=po, i=i: e.transpose(pT[po:po + 64, 512 + i * 64:512 + (i + 1) * 64],
                                                                 vs[po:po + 64, i, tk0:tk0 + 64], ident[po:po + 64, po:po + 64]),
                         reads=["vs", "ident"], writes=[bk(B)])
                S.op("act", lambda e: e.activation(out=btk[js][:], in_=pT[:, 0:512], func=AF.Copy), writes=[bk(B), K_("btk", js)])
                S.op("act", lambda e: e.activation(out=vtm[js][:].rearrange("p i v -> p (i v)"), in_=pT[:, 512:768], func=AF.Copy),
                     writes=[bk(B), K_("vtm", js)])
                yield
                for bb in range(2):
                    psA = banks[B][:, :].rearrange("p (i r t) -> p i r t", i=2, r=2)
                    for (po, i) in HP:
                        if i // 2 != bb:
                            continue
                        rhs = ar[dp][po:po + 64, i, c, :, :].rearrange("p a t -> p (a t)")
                        for r_ in range(2):
                            S.op("pe", lambda e, r_=r_, po=po, i=i, rhs=rhs, psA=psA: e.matmul(
                                psA[po:po + 64, i % 2, r_, :], lhsT=bkt[dp][po:po + 64, i, c, r_, :], rhs=rhs, start=True, stop=True),
                                reads=[bkK, arK], writes=[bk(B)])
                    S.op("dve", lambda e, bb=bb, psA=psA: e.tensor_tensor(
                        out=AT[js][:, bb * 2:bb * 2 + 2], in0=psA, in1=msk[:, d:d + 1].to_broadcast([128, 2, 2, 128]), op=ALU.mult),
                        reads=["msk"], writes=[bk(B), K_("AT", js)])
                    yield
                psN = V3(B)
                for (po, i) in HP:
                    S.op("pe", lambda e, po=po, i=i: e.matmul(psN[po:po + 64, i, :], lhsT=ar[dp][po:po + 64, i, c, 0, :],
                                                               rhs=bkt[dp][po:po + 64, i, c, 0, :], start=True, stop=True),
                         reads=[bkK, arK], writes=[bk(B)])
                PQ, X = PQm[jt], Xm[jt]
                S.op("dve", lambda e: e.tensor_tensor(out=PQ[0][:, 0], in0=psN, in1=mskN[:, d:d + 1].to_broadcast([128, 4, 64]), op=ALU.mult),
                     reads=["msk"], writes=[bk(B), K_("PQ0", jt)])
                S.op("act", lambda e: e.activation(out=PQ[0][:, 1], in_=AT[js][:, :, 0, 0:64], func=AF.Copy),
                     reads=[K_("AT", js)], writes=[K_("PQ0", jt)])
                S.op("pool", lambda e: e.tensor_tensor(out=X[0][:], in0=AT[js][:, :, 0, 0:64],
                                                       in1=identB[:].unsqueeze(1).to_broadcast([128, 4, 64]), op=ALU.add),
                     reads=[K_("AT", js), "identB"], writes=[K_("X0", jt)])
                yield
                cur = 0
                for j in range(1, 6):
                    nxt = 1 - cur
                    psPQ = banks[B][:, :].rearrange("p (a i s) -> p a i s", a=2, s=64)
                    na = 2 if j < 5 else 1
                    for a_ in range(na):
                        for (po, i) in HP:
                            S.op("pe", lambda e, po=po, i=i, cur=cur, a_=a_, psPQ=psPQ: e.matmul(
                                psPQ[po:po + 64, a_, i, :], lhsT=PQ[cur][po:po + 64, 1 - a_, i, :], rhs=PQ[cur][po:po + 64, a_, i, :],
                                start=True, stop=True), reads=[K_("PQ%d" % cur, jt)], writes=[bk(B)])
                    if False:
                        S.op("dve", lambda e, nxt=nxt, na=na, psPQ=psPQ: e.tensor_copy(out=PQ[nxt][:, 0:na], in_=psPQ[:, 0:na]),
                             writes=[bk(B), K_("PQ%d" % nxt, jt)])
                    else:
                        S.op("act", lambda e, nxt=nxt, na=na, psPQ=psPQ: e.activation(out=PQ[nxt][:, 0:na], in_=psPQ[:, 0:na], func=AF.Copy),
                             writes=[bk(B), K_("PQ%d" % nxt, jt)])
                    yield
                    psX = V3(B)
                    for (po, i) in HP:
                        S.op("pe", lambda e, po=po, i=i, cur=cur, nxt=nxt, psX=psX: e.matmul(
                            psX[po:po + 64, i, :], lhsT=PQ[nxt][po:po + 64, 0, i, :], rhs=X[cur][po:po + 64, i, :], start=True, stop=True),
                            reads=[K_("PQ%d" % nxt, jt), K_("X%d" % cur, jt)], writes=[bk(B)])
                    xo_, xok_ = (XT[js], K_("XT", js)) if j == 5 else (X[nxt], K_("X%d" % nxt, jt))
                    S.op("dve", lambda e, cur=cur, xo_=xo_, psX=psX: e.tensor_tensor(out=xo_[:], in0=psX, in1=X[cur][:], op=ALU.add),
                         reads=[K_("X%d" % cur, jt)], writes=[bk(B), xok_])
                    yield
                    cur = nxt

            def chain(b, d, w0, c, is_lat, par3, js, B):
                tk0 = w0 + c * 64
                dp = d * 3 + par3
                arK, bkK = K_("ar", dp), K_("bkt", dp)
                btm = btk[js][:, 0:256].rearrange("p (i k) -> p i k", k=64)
                ktm = btk[js][:, 256:512].rearrange("p (i k) -> p i k", k=64)
                ATj, vt, XTj = AT[js], vtm[js], XT[js]
                psR = V3(B)
                for (po, i) in HP:
                    S.op("pe", lambda e, po=po, i=i: e.matmul(psR[po:po + 64, i, :], lhsT=ar[dp][po:po + 64, i, c, 0, :],
                                                               rhs=Hbf[d][po:po + 64, i, :], start=True, stop=False),
                         reads=[arK, K_("Hbf", d)], writes=[bk(B)])
                    S.op("pe", lambda e, po=po, i=i: e.matmul(psR[po:po + 64, i, :], lhsT=ATj[po:po + 64, i, 1, 0:64],
                                                               rhs=vt[po:po + 64, i, :], start=False, stop=True),
                         reads=[K_("AT", js), K_("vtm", js)], writes=[bk(B)])
                S.op("act", lambda e: e.activation(out=Rsb[d][:], in_=psR, func=AF.Copy), writes=[bk(B), K_("Rsb", d)])
                yield
                for (po, i) in HP:
                    S.op("pe", lambda e, po=po, i=i: e.matmul(psR[po:po + 64, i, :], lhsT=XTj[po:po + 64, i, :],
                                                               rhs=Rsb[d][po:po + 64, i, :], start=True, stop=True),
                         reads=[K_("XT", js), K_("Rsb", d)], writes=[bk(B)])
                S.op("act", lambda e: e.activation(out=Usb[d][:], in_=psR, func=AF.Copy), writes=[bk(B), K_("Usb", d)])
                yield
                psH = V3(B)
                for (po, i) in HP:
                    S.op("pe", lambda e, po=po, i=i: e.matmul(psH[po:po + 64, i, :], lhsT=btm[po:po + 64, i, :],
                                                               rhs=Usb[d][po:po + 64, i, :], start=True, stop=False),
                         reads=[K_("btk", js), K_("Usb", d)], writes=[bk(B)])
                    S.op("pe", lambda e, po=po, i=i: e.matmul(psH[po:po + 64, i, :], lhsT=ktm[po:po + 64, i, :],
                                                               rhs=vt[po:po + 64, i, :], start=False, stop=True),
                         reads=[K_("btk", js), K_("vtm", js)], writes=[bk(B)])
                if is_lat:
                    psY = banks[B][:, 256:512].rearrange("p (i s) -> p i s", s=64)
                    for (po, i) in HP:
                        S.op("pe", lambda e, po=po, i=i: e.matmul(psY[po:po + 64, i, :], lhsT=ar[dp][po:po + 64, i, c, 1, :],
                                                                   rhs=Hbf[d][po:po + 64, i, :], start=True, stop=False),
                             reads=[arK, K_("Hbf", d)], writes=[bk(B)])
                        S.op("pe", lambda e, po=po, i=i: e.matmul(psY[po:po + 64, i, :], lhsT=ATj[po:po + 64, i, 0, 64:128],
                                                                   rhs=Usb[d][po:po + 64, i, :], start=False, stop=False),
                             reads=[K_("AT", js), K_("Usb", d)], writes=[bk(B)])
                        S.op("pe", lambda e, po=po, i=i: e.matmul(psY[po:po + 64, i, :], lhsT=ATj[po:po + 64, i, 1, 64:128],
                                                                   rhs=vt[po:po + 64, i, :], start=False, stop=True),
                             reads=[K_("AT", js), K_("vtm", js)], writes=[bk(B)])
                S.op("dve", lambda e: e.tensor_tensor(out=H32[d][:], in0=psH, in1=H32[d][:], op=ALU.add),
                     reads=[K_("H32", d)], writes=[bk(B), K_("H32", d)])
                S.op("pool", lambda e: e.tensor_tensor(out=H32[d][:], in0=H32[d][:],
                                                       in1=eLC[dp][:, :, c:c + 1].to_broadcast([128, 4, 64]), op=ALU.mult),
                     reads=[K_("H32", d), K_("eLC", dp)], writes=[K_("H32", d)])
                S.op("act", lambda e: e.activation(out=Hbf[d][:], in_=H32[d][:], func=AF.Copy), reads=[K_("H32", d)], writes=[K_("Hbf", d)])
                if is_lat:
                    S.op("act", lambda e: e.activation(out=ybuf[d][:], in_=psY, func=AF.Copy), writes=[bk(B), K_("ybuf", d)])
                    tl = tk0 - TC
                    for hp_ in range(2):
                        S.dma("sp", lambda e, tl=tl, hp_=hp_: e.dma_start(
                            out=y_d[b, d, tl:tl + 64, :].rearrange("t (i hp v) -> t i hp v", hp=2, v=64)[:, :, hp_, :],
                            in_=ybuf[d][hp_ * 64:(hp_ + 1) * 64]), "y_st", reads=[K_("ybuf", d)])
                    yield
                    psB = banks[B]
                    for (po, i) in HP:
                        S.op("pe", lambda e, po=po, i=i: e.matmul(psB[po:po + 64, i:i + 1], lhsT=prod[dp][po:po + 64, i, c * 64:(c + 1) * 64],
                                                                   rhs=rkb[po:po + 64, i:i + 1], start=True, stop=True),
                             reads=[K_("prod", dp), "rkb"], writes=[bk(B)])
                    S.op("dve", lambda e: e.tensor_scalar(out=bon[d][:], in0=psB[:, 0:4], scalar1=0.5, scalar2=None, op0=ALU.mult),
                         writes=[bk(B), K_("bon", d)])
                    S.op("dve", lambda e: e.tensor_tensor(out=bosb[d][:], in0=vt[:],
                                                          in1=bon[d][:].unsqueeze(2).to_broadcast([128, 4, 64]), op=ALU.mult),
                         reads=[K_("vtm", js), K_("bon", d)], writes=[K_("bosb", d)])
                    for hp_ in range(2):
                        S.dma("sp", lambda e, tl=tl, hp_=hp_: e.dma_start(
                            out=bo_d[b, d, tl:tl + 64, :].rearrange("t (i hp v) -> t i hp v", hp=2, v=64)[:, :, hp_, :],
                            in_=bosb[d][hp_ * 64:(hp_ + 1) * 64]), "bo_st", reads=[K_("bosb", d)])
                yield

            def lockstep(gens):
                gens = list(gens)
                while gens:
                    for g_ in list(gens):
                        try:
                            next(g_)
                        except StopIteration:
                            gens.remove(g_)

            def pb_batch(b):
                for (dst, j0, key) in ((rs, 0, "rs"), (ks, 4, "ks"), (vs, 8, "vs")):
                    S.dma("sp", lambda e, dst=dst, j0=j0: e.dma_start(out=dst[:], in_=pa_d[b, j0:j0 + 4].rearrange("j p t -> p j t")),
                          "pb_ld_" + key, writes=[key])
                S.dma("sp", lambda e: e.dma_start(out=wdad[:], in_=pa_d[b, 12:14].rearrange("j p t -> p j t")), "pb_ld_w", writes=["wdad"])
                S.op("act", lambda e: e.activation(out=wdad[0:64], in_=wdad[0:64], func=AF.Tanh), reads=["wdad"], writes=["wdad"])
                for d in range(2):
                    S.op("pool", lambda e, d=d: e.memset(H32[d][:], 0.0), writes=[K_("H32", d)])
                    S.op("pool", lambda e, d=d: e.memset(Hbf[d][:], 0.0), writes=[K_("Hbf", d)])
                cw_ = [(w * W, False) for w in range(TC // W)]
                lw_ = [(TC + w * W, True) for w in range(T // W)]
                sched = {0: cw_ + lw_, 1: list(reversed(cw_)) + list(reversed(lw_))}
                nw_ = len(sched[0])

                def preps(wi):
                    return [prep(b, d, sched[d][wi][0], wi % 3) for d in range(2)]

                def jobs(wi):
                    for d in range(2):
                        for cc in range(NCW):
                            c = cc if d == 0 else NCW - 1 - cc
                            yield d, cc, c, (wi % 2) * 2 * NCW + d * NCW + cc, d * NCW + cc

                def chains(wi):
                    for cc in range(NCW):
                        gens = []
                        for (d, cc_, c, js, jt) in jobs(wi):
                            if cc_ == cc:
                                gens.append(chain(b, d, sched[d][wi][0], c, sched[d][wi][1], wi % 3, js, 2 * NCW + d))
                        while gens:
                            for g_ in list(gens):
                                try:
                                    next(g_)
                                except StopIteration:
                                    gens.remove(g_)
                            yield

                lockstep(preps(0))
                for t in range(nw_ + 1):
                    gl = []
                    if t >= 1:
                        gl.append(chains(t - 1))
                    if t < nw_:
                        gl += [inv(b, d, sched[d][t][0], c, t % 3, js, jt, jt) for (d, cc, c, js, jt) in jobs(t)]
                    if t + 1 < nw_:
                        gl += preps(t + 1)
                    lockstep(gl)

            for b in range(NB):
                pb_batch(b)

        def post_norm_residual(pbs, gpost, xres, xkey, outt, okey, tmp, kp="pn", gk="gpost"):
            for hh in range(2):
                S.op("act", lambda e, hh=hh: e.activation(out=tmp["sq"][:, hh * 512:(hh + 1) * 512], in_=banks[pbs[hh]][:, :], func=AF.Square),
                     writes=[bk(pbs[hh]), kp + "_sq"])
            S.op("dve", lambda e: e.reduce_sum(out=tmp["ss"][:], in_=tmp["sq"][:], axis=AX.X), reads=[kp + "_sq"], writes=[kp + "_ss"])
            S.op("dve", lambda e: e.tensor_scalar(out=tmp["ss"][:], in0=tmp["ss"][:], scalar1=1.0 / D, scalar2=1e-6,
                                                  op0=ALU.mult, op1=ALU.add), reads=[kp + "_ss"], writes=[kp + "_ss"])
            S.op("act", lambda e: e.activation(out=tmp["ss"][:], in_=tmp["ss"][:], func=AF.Sqrt), reads=[kp + "_ss"], writes=[kp + "_ss"])
            S.op("dve", lambda e: e.reciprocal(out=tmp["ss"][:], in_=tmp["ss"][:]), reads=[kp + "_ss"], writes=[kp + "_ss"])
            for hh in range(2):
                S.op("dve", lambda e, hh=hh: e.scalar_tensor_tensor(
                    out=tmp["sq"][:, hh * 512:(hh + 1) * 512], in0=banks[pbs[hh]][:, :], scalar=tmp["ss"][:, 0:1],
                    in1=gpost[:, hh * 512:(hh + 1) * 512], op0=ALU.mult, op1=ALU.mult),
                    reads=[kp + "_ss", gk], writes=[bk(pbs[hh]), kp + "_sq"])
            S.op("pool", lambda e: e.tensor_tensor(out=outt[:], in0=tmp["sq"][:], in1=xres, op=ALU.add), reads=[kp + "_sq", xkey], writes=[okey])

        def make_gpost(b, gi, rowname, gpost, gb=(0, 1), gk="gpost"):
            for hh in range(2):
                S.op("pe", lambda e, hh=hh: e.matmul(banks[gb[hh]][:, :], lhsT=sel[:, b, :], rhs=gates[:, gi, hh * 512:(hh + 1) * 512],
                                                     start=True, stop=True), reads=["sel", "gates"], writes=[bk(gb[hh])])
                S.op("dve", lambda e, hh=hh: e.tensor_tensor(out=gpost[:, hh * 512:(hh + 1) * 512], in0=banks[gb[hh]][:, :],
                                                             in1=cr(rowname)[:, hh * 512:(hh + 1) * 512], op=ALU.mult),
                     reads=["crow"], writes=[bk(gb[hh]), gk])

        if PHASES >= 3:
            S.fence()
            A_.reset()
            woutb = A_.t([128, DK, D], BF16)
            gw2b = A_.t([128, 2, 512], BF16)
            stg = [A_.t([128, DK, 512], F32) for _ in range(2)]
            for n in range(2):
                S.dma("sp", lambda e, n=n: e.dma_start(out=stg[n][:], in_=wout_d[:, n * 512:(n + 1) * 512].rearrange("(k p) n -> p k n", p=128)),
                      "stgd%d" % n, writes=["stgd%d" % n])
                S.op("dve", lambda e, n=n: e.tensor_copy(out=woutb[:, :, n * 512:(n + 1) * 512], in_=stg[n][:]), reads=["stgd%d" % n], writes=["woutb"])
            gst = A_.t([128, 2, 512], F32)
            S.dma("sp", lambda e: e.dma_start(out=gst[:, 0, :], in_=gw2_d[0:128, :]), "gst", writes=["gst"])
            S.dma("sp", lambda e: e.dma_start(out=gst[0:32, 1, :], in_=gw2_d[128:160, :]), "gst", writes=["gst"])
            S.op("dve", lambda e: e.tensor_copy(out=gw2b[:, 0, :], in_=gst[:, 0, :]), reads=["gst"], writes=["gw2b"])
            S.op("dve", lambda e: e.tensor_copy(out=gw2b[0:32, 1, :], in_=gst[0:32, 1, :]), reads=["gst"], writes=["gw2b"])
            S.fence()
            A_.off -= 2 * 16384 + 4096
            onesf = A_.t([128, 128], F32)
            S.op("pool", lambda e: e.memset(onesf[:], 1.0), writes=["onesf"])
            ub = A_.t([128, 4, T], BF16)
            yc = A_.t([128, 4, T], F32)
            convo_ = [A_.t([128, 4, T], BF16) for _ in range(2)]
            gds_ = [A_.t([128, 2, 128], BF16) for _ in range(2)]
            lsq = A_.t([128, 4, 512], F32)
            lmean = A_.t([128, 512], F32)
            lrstd = A_.t([128, 512], F32)
            ltmp = A_.t([128, 512], F32)
            yin = [[A_.t([128, 512], F32) for _ in range(4)] for _ in range(2)]
            ysq_ = [A_.t([128, 512], F32) for _ in range(2)]
            st8_ = [A_.t([128, 4, 8], F32) for _ in range(2)]
            rwb_ = [A_.t([128, 512], BF16) for _ in range(2)]
            mixT_ = [A_.t([128, 4, 128], BF16) for _ in range(2)]
            xt2 = [A_.t([128, D], F32) for _ in range(2)]
            gpost_ = [A_.t([128, D], F32) for _ in range(2)]
            pn_tmp_ = [{"sq": A_.t([128, D], F32), "ss": A_.t([128, 1], F32)} for _ in range(2)]
            cw = cf("conv_w")
            ti_ = [0]

            def pd_conv(b):
                cs = b % 2
                convo, gpost = convo_[cs], gpost_[cs]
                ck, gk = "convo%d" % cs, "gpost%d" % cs
                make_gpost(b, 0, "mix_post_g", gpost, (6, 7), gk)
                yield
                S.dma("sp", lambda e, b=b: e.dma_start(out=ub[:], in_=pa_d[b, 16:20, :, TC:TT].rearrange("j p t -> p j t")), "pd_ld", writes=["ub"])
                for i in range(4):
                    u4 = ub[:, i, :].rearrange("p (r t) -> p r t", t=64)
                    y4 = yc[:, i, :].rearrange("p (r t) -> p r t", t=64)
                    S.op("dve", lambda e, i=i: e.tensor_scalar(out=yc[:, i, :], in0=ub[:, i, :], scalar1=cw[:, i * 31 + 15:i * 31 + 16],
                                                               scalar2=cf("conv_b")[:, i:i + 1], op0=ALU.mult, op1=ALU.add),
                         reads=["ub", "cfm"], writes=["yc%d" % i])
                    for j in range(31):
                        o = j - 15
                        if o == 0:
                            continue
                        lo_o, hi_o = max(0, -o), 64 - max(0, o)
                        lo_i, hi_i = max(0, o), 64 - max(0, -o)
                        S.op("dve", lambda e, i=i, j=j, u4=u4, y4=y4, lo_o=lo_o, hi_o=hi_o, lo_i=lo_i, hi_i=hi_i: e.scalar_tensor_tensor(
                            out=y4[:, :, lo_o:hi_o], in0=u4[:, :, lo_i:hi_i], scalar=cw[:, i * 31 + j:i * 31 + j + 1],
                            in1=y4[:, :, lo_o:hi_o], op0=ALU.mult, op1=ALU.add), reads=["ub", "cfm", "yc%d" % i], writes=["yc%d" % i])
                        if j % 3 == 0:
                            yield
                for w in range(T // 512 if T >= 512 else 1):
                    wn = min(512, T)
                    ws = slice(w * 512, w * 512 + wn)
                    for i in range(4):
                        S.op("pe", lambda e, i=i, ws=ws, wn=wn: e.matmul(banks[6][:, 0:wn], lhsT=onesf[:], rhs=yc[:, i, ws], start=(i == 0), stop=(i == 3)),
                             reads=["onesf", "yc%d" % i], writes=[bk(6)])
                    S.op("act", lambda e, ws=ws, wn=wn: e.activation(out=lsq[:, :, 0:wn], in_=yc[:, :, ws], func=AF.Square),
                         reads=["yc0", "yc1", "yc2", "yc3"], writes=["lsq"])
                    for i in range(4):
                        S.op("pe", lambda e, i=i, wn=wn: e.matmul(banks[7][:, 0:wn], lhsT=onesf[:], rhs=lsq[:, i, 0:wn], start=(i == 0), stop=(i == 3)),
                             reads=["onesf", "lsq"], writes=[bk(7)])
                    yield
                    S.op("dve", lambda e, wn=wn: e.tensor_scalar(out=lmean[:, 0:wn], in0=banks[6][:, 0:wn], scalar1=1.0 / 512, scalar2=None, op0=ALU.mult),
                         writes=[bk(6), "lmean"])
                    S.op("dve", lambda e, wn=wn: e.tensor_tensor(out=ltmp[:, 0:wn], in0=lmean[:, 0:wn], in1=lmean[:, 0:wn], op=ALU.mult),
                         reads=["lmean"], writes=["ltmp"])
                    S.op("dve", lambda e, wn=wn: e.scalar_tensor_tensor(out=lrstd[:, 0:wn], in0=banks[7][:, 0:wn], scalar=1.0 / 512, in1=ltmp[:, 0:wn],
                                                                        op0=ALU.mult, op1=ALU.subtract), reads=["ltmp"], writes=[bk(7), "lrstd"])
                    S.op("dve", lambda e, wn=wn: e.tensor_scalar(out=lrstd[:, 0:wn], in0=lrstd[:, 0:wn], scalar1=1e-5, scalar2=None, op0=ALU.add),
                         reads=["lrstd"], writes=["lrstd"])
                    yield
                    S.op("act", lambda e, wn=wn: e.activation(out=lrstd[:, 0:wn], in_=lrstd[:, 0:wn], func=AF.Sqrt), reads=["lrstd"], writes=["lrstd"])
                    S.op("dve", lambda e, wn=wn: e.reciprocal(out=lrstd[:, 0:wn], in_=lrstd[:, 0:wn]), reads=["lrstd"], writes=["lrstd"])
                    yield
                    S.op("dve", lambda e, ws=ws, wn=wn: e.tensor_tensor(out=lsq[:, :, 0:wn], in0=yc[:, :, ws],
                                                                        in1=lmean[:, 0:wn].unsqueeze(1).to_broadcast([128, 4, wn]), op=ALU.subtract),
                         reads=["yc0", "yc1", "yc2", "yc3", "lmean"], writes=["lsq"])
                    S.op("dve", lambda e, wn=wn: e.tensor_tensor(out=lsq[:, :, 0:wn], in0=lsq[:, :, 0:wn],
                                                                 in1=lrstd[:, 0:wn].unsqueeze(1).to_broadcast([128, 4, wn]), op=ALU.mult),
                         reads=["lsq", "lrstd"], writes=["lsq"])
                    yield
                    for i in range(4):
                        S.op("act", lambda e, i=i, ws=ws, wn=wn: e.activation(out=convo[:, i, ws], in_=lsq[:, i, 0:wn], func=AF.Silu,
                                                                              scale=cf("cln_w")[:, i:i + 1], bias=cf("cln_b")[:, i:i + 1]),
                             reads=["lsq", "cfm"], writes=[ck])
                yield

            def pd_tiles(b):
                for tt in range(0, T // 128, 2):
                    gens = [_pd_tile(b, tt + q_, q_) for q_ in range(2) if tt + q_ < T // 128]
                    while gens:
                        for g_ in list(gens):
                            try:
                                next(g_)
                            except StopIteration:
                                gens.remove(g_)
                        yield

            def _pd_tile(b, tt, sl):
                if True:
                    tsl = slice(tt * 128, (tt + 1) * 128)
                    ti = sl
                    BG, BT, BO0, BO1 = 3 * sl, 3 * sl, 3 * sl + 1, 3 * sl + 2
                    cs = b % 2
                    convo, gpost = convo_[cs], gpost_[cs]
                    ck, gk = "convo%d" % cs, "gpost%d" % cs
                    gds = gds_[sl]
                    ysq, st8, rwb, mixT = ysq_[sl], st8_[sl], rwb_[sl], mixT_[sl]
                    pn_tmp = pn_tmp_[sl]
                    sk = lambda nm: "%s_%d" % (nm, sl)
                    yy = yin[ti % 2]
                    yk = "yin%d" % (ti % 2)
                    xa, xk2 = xt2[ti % 2], "xt2_%d" % (ti % 2)
                    for q, src in enumerate((y_d, y_d, bo_d, bo_d)):
                        S.dma("sp", lambda e, q=q, src=src, yy=yy, tsl=tsl: e.dma_start(out=yy[q][:], in_=src[b, q % 2, tsl, :]), yk, writes=[yk])
                    S.dma("sp", lambda e, xa=xa, tsl=tsl: e.dma_start(out=xa[:], in_=x_d[b, tsl, :]), xk2, writes=[xk2])
                    S.dma("sp", lambda e, tsl=tsl: e.dma_start(out=gds[:], in_=pa_d[b, 14:16, :, TC + tsl.start:TC + tsl.stop].rearrange("j p t -> p j t")), sk("gds"), writes=[sk("gds")])
                    S.op("act", lambda e: e.activation(out=gds[:, 0, :], in_=gds[:, 0, :], func=AF.Sigmoid), reads=[sk("gds")], writes=[sk("gds")])
                    S.op("act", lambda e: e.activation(out=gds[0:32, 1, :], in_=gds[0:32, 1, :], func=AF.Sigmoid), reads=[sk("gds")], writes=[sk("gds")])
                    S.op("pool", lambda e, yy=yy: e.tensor_tensor(out=yy[0][:], in0=yy[0][:], in1=yy[1][:], op=ALU.add), reads=[yk], writes=[yk])
                    S.op("pool", lambda e, yy=yy: e.tensor_tensor(out=yy[2][:], in0=yy[2][:], in1=yy[3][:], op=ALU.add), reads=[yk], writes=[yk])
                    yield
                    y3 = yy[0][:].rearrange("p (h v) -> p h v", v=64)
                    S.op("dve", lambda e, y3=y3: e.reduce_sum(out=st8[:, 0, :], in_=y3, axis=AX.X), reads=[yk], writes=[sk("st8")])
                    S.op("act", lambda e, yy=yy: e.activation(out=ysq[:], in_=yy[0][:], func=AF.Square), reads=[yk], writes=[sk("ysq")])
                    S.op("dve", lambda e: e.reduce_sum(out=st8[:, 1, :], in_=ysq[:].rearrange("p (h v) -> p h v", v=64), axis=AX.X),
                         reads=[sk("ysq")], writes=[sk("st8")])
                    S.op("dve", lambda e: e.tensor_scalar(out=st8[:, 0:2, :], in0=st8[:, 0:2, :], scalar1=1.0 / 64, scalar2=None, op0=ALU.mult),
                         reads=[sk("st8")], writes=[sk("st8")])
                    yield
                    S.op("dve", lambda e: e.tensor_tensor(out=st8[:, 2, :], in0=st8[:, 0, :], in1=st8[:, 0, :], op=ALU.mult), reads=[sk("st8")], writes=[sk("st8")])
                    S.op("dve", lambda e: e.tensor_tensor(out=st8[:, 3, :], in0=st8[:, 1, :], in1=st8[:, 2, :], op=ALU.subtract), reads=[sk("st8")], writes=[sk("st8")])
                    S.op("dve", lambda e: e.tensor_scalar(out=st8[:, 3, :], in0=st8[:, 3, :], scalar1=64e-5, scalar2=None, op0=ALU.add),
                         reads=[sk("st8")], writes=[sk("st8")])
                    S.op("act", lambda e: e.activation(out=st8[:, 3, :], in_=st8[:, 3, :], func=AF.Sqrt), reads=[sk("st8")], writes=[sk("st8")])
                    S.op("dve", lambda e: e.reciprocal(out=st8[:, 3, :], in_=st8[:, 3, :]), reads=[sk("st8")], writes=[sk("st8")])
                    yield
                    S.op("dve", lambda e, y3=y3: e.tensor_tensor(out=y3, in0=y3, in1=st8[:, 0, :].unsqueeze(2).to_broadcast([128, 8, 64]), op=ALU.subtract),
                         reads=[yk, sk("st8")], writes=[yk])
                    S.op("dve", lambda e, y3=y3: e.tensor_tensor(out=y3, in0=y3, in1=st8[:, 3, :].unsqueeze(2).to_broadcast([128, 8, 64]), op=ALU.mult),
                         reads=[yk, sk("st8")], writes=[yk])
                    S.op("pool", lambda e, yy=yy: e.tensor_tensor(out=yy[0][:], in0=yy[0][:], in1=cr("lnx_w"), op=ALU.mult), reads=[yk, "crow"], writes=[yk])
                    S.op("pool", lambda e, yy=yy: e.tensor_tensor(out=yy[0][:], in0=yy[0][:], in1=cr("lnx_b"), op=ALU.add), reads=[yk, "crow"], writes=[yk])
                    S.op("pool", lambda e, yy=yy: e.tensor_tensor(out=yy[0][:], in0=yy[0][:], in1=yy[2][:], op=ALU.add), reads=[yk], writes=[yk])
                    yield
                    S.op("pe", lambda e, tsl=tsl: e.matmul(banks[BG][:, :], lhsT=gds[:, 0, :], rhs=gw2b[:, 0, :], start=True, stop=False),
                         reads=[sk("gds"), "gw2b"], writes=[bk(BG)])
                    S.op("pe", lambda e, tsl=tsl: e.matmul(banks[BG][:, :], lhsT=gds[0:32, 1, :], rhs=gw2b[0:32, 1, :], start=False, stop=True),
                         reads=[sk("gds"), "gw2b"], writes=[bk(BG)])
                    S.op("dve", lambda e, yy=yy: e.tensor_tensor(out=rwb[:], in0=yy[0][:], in1=banks[BG][:, :], op=ALU.mult),
                         reads=[yk], writes=[bk(BG), sk("rwb")])
                    yield
                    pT = banks[BT].bitcast(BF16)
                    for j in range(4):
                        S.op("pe", lambda e, j=j: e.transpose(pT[:, j * 128:(j + 1) * 128], rwb[:, j * 128:(j + 1) * 128], ident[:]),
                             reads=[sk("rwb"), "ident"], writes=[bk(BT)])
                    S.op("act", lambda e: e.activation(out=mixT[:].rearrange("p j t -> p (j t)"), in_=pT[:, 0:512], func=AF.Copy),
                         writes=[bk(BT), sk("mixT")])
                    yield
                    for hh in range(2):
                        for j in range(8):
                            lhs = mixT[:, j, :] if j < 4 else convo[:, j - 4, tsl]
                            S.op("pe", lambda e, hh=hh, j=j, lhs=lhs: e.matmul(banks[BO0 + hh][:, :], lhsT=lhs, rhs=woutb[:, j, hh * 512:(hh + 1) * 512],
                                                                               start=(j == 0), stop=(j == 7)),
                                 reads=[sk("mixT"), ck, "woutb"], writes=[bk(BO0 + hh)])
                    yield
                    post_norm_residual((BO0, BO1), gpost, xa[:], xk2, xa, xk2, pn_tmp, sk("pn"), gk)
                    S.dma("sp", lambda e, xa=xa, tsl=tsl: e.dma_start(out=out_d[b, tsl, :], in_=xa[:]), "x1_st", reads=[xk2])

            def lockstep_pd(gens):
                gens = list(gens)
                while gens:
                    for g_ in list(gens):
                        try:
                            next(g_)
                        except StopIteration:
                            gens.remove(g_)

            lockstep_pd([pd_conv(0)])
            for b in range(NB):
                lockstep_pd([pd_tiles(b)] + ([pd_conv(b + 1)] if b + 1 < NB else []))

        if PHASES >= 4:
            S.fence()
            A_.reset()
            w1b = A_.t([128, DK, 4096], BF16)
            w2b_ = A_.t([128, 32, D], BF16)
            mark = A_.off
            stg = [A_.t([128, DK, 512], F32) for _ in range(2)]
            for n in range(8):
                S.dma("sp", lambda e, n=n: e.dma_start(out=stg[n % 2][:], in_=w1_d[:, n * 512:(n + 1) * 512].rearrange("(k p) n -> p k n", p=128)),
                      "stge%d" % (n % 2), writes=["stge%d" % (n % 2)])
                if n % 2 == 0:
                    S.op("dve", lambda e, n=n: e.tensor_copy(out=w1b[:, :, n * 512:(n + 1) * 512], in_=stg[n % 2][:]), reads=["stge%d" % (n % 2)], writes=["w1b"])
                else:
                    S.op("act", lambda e, n=n: e.activation(out=w1b[:, :, n * 512:(n + 1) * 512], in_=stg[n % 2][:], func=AF.Copy),
                         reads=["stge%d" % (n % 2)], writes=["w1b"])
            for n in range(8):
                S.dma("sp", lambda e, n=n: e.dma_start(out=stg[n % 2][:].rearrange("p k n -> p (k n)").rearrange("p (f n) -> p f n", n=D),
                                                       in_=w2_d[n * 512:(n + 1) * 512, :].rearrange("(f p) n -> p f n", p=128)),
                      "stge%d" % (n % 2), writes=["stge%d" % (n % 2)])
                src = stg[n % 2][:].rearrange("p k n -> p (k n)").rearrange("p (f n) -> p f n", n=D)
                if n % 2 == 0:
                    S.op("dve", lambda e, n=n, src=src: e.tensor_copy(out=w2b_[:, n * 4:(n + 1) * 4, :], in_=src), reads=["stge%d" % (n % 2)], writes=["w2b_"])
                else:
                    S.op("act", lambda e, n=n, src=src: e.activation(out=w2b_[:, n * 4:(n + 1) * 4, :], in_=src, func=AF.Copy),
                         reads=["stge%d" % (n % 2)], writes=["w2b_"])
            S.fence()
            A_.off = mark
            G = 256
            hT2 = [A_.t([128, DK, G], BF16) for _ in range(2)]
            hidr = [A_.t([128, 2, G], BF16) for _ in range(4)]
            rl = [A_.t([128, 512], F32) for _ in range(2)]
            x1g = [A_.t([128, D], F32) for _ in range(4)]
            gpost2 = A_.t([128, D], F32)
            ftmp2 = {"sq": A_.t([128, D], F32), "ss": A_.t([128, 1], F32), "xn": A_.t([128, D], BF16)}
            pn_tmp2 = {"sq": ftmp2["sq"], "ss": A_.t([128, 1], F32)}
            groups = [(b, g) for b in range(NB) for g in range(T // G)]
            OB = ((5, 6), (0, 1))

            def pe_front(gi):
                b, g = groups[gi]
                xs = gi % 2
                for tq in range(2):
                    tsl = slice(g * G + tq * 128, g * G + (tq + 1) * 128)
                    xt_, xk_ = x1g[xs * 2 + tq], "x1g%d" % (xs * 2 + tq)
                    S.dma("sp", lambda e, xt_=xt_, tsl=tsl, b=b: e.dma_start(out=xt_[:], in_=out_d[b, tsl, :]), xk_, writes=[xk_])
                    front(xt_[:], xk_, hT2[xs][:, :, tq * 128:(tq + 1) * 128], "hT2_%d" % xs, 1, 3, b, ftmp2, 2)

            def pe_out_pair(gi, f2):
                xs = gi % 2
                hr, hk = hidr[f2 % 4], "hidr%d" % (f2 % 4)
                for ff in range(2):
                    f = f2 * 2 + ff
                    for tq in range(2):
                        for hh in range(2):
                            S.op("pe", lambda e, hr=hr, ff=ff, f=f, tq=tq, hh=hh: e.matmul(
                                banks[OB[tq][hh]][:, :], lhsT=hr[:, ff, tq * 128:(tq + 1) * 128], rhs=w2b_[:, f, hh * 512:(hh + 1) * 512],
                                start=(f == 0), stop=(f == 31)), reads=[hk, "w2b_"], writes=[bk(OB[tq][hh])])

            def pe_group(gi):
                b, g = groups[gi]
                xs = gi % 2
                if g == 0:
                    make_gpost(b, 1, "mlp_post_g", gpost2)
                for f2 in range(16):
                    pbk = 3 + (f2 % 2)
                    for ff in range(2):
                        f = f2 * 2 + ff
                        for k in range(DK):
                            S.op("pe", lambda e, pbk=pbk, ff=ff, f=f, k=k: e.matmul(
                                banks[pbk][:, ff * G:(ff + 1) * G], lhsT=w1b[:, k, f * 128:(f + 1) * 128], rhs=hT2[xs][:, k, :],
                                start=(k == 0), stop=(k == DK - 1)), reads=["w1b", "hT2_%d" % xs], writes=[bk(pbk)])
                    S.op("act", lambda e, pbk=pbk, f2=f2: e.activation(out=rl[f2 % 2][:], in_=banks[pbk][:, :], func=AF.Relu),
                         writes=[bk(pbk), "rl%d" % (f2 % 2)])
                    S.op("dve" if f2 % 2 == 0 else "pool", lambda e, f2=f2: e.tensor_tensor(
                        out=hidr[f2 % 4][:], in0=rl[f2 % 2][:].rearrange("p (a t) -> p a t", a=2),
                        in1=rl[f2 % 2][:].rearrange("p (a t) -> p a t", a=2), op=ALU.mult), reads=["rl%d" % (f2 % 2)], writes=["hidr%d" % (f2 % 4)])
                    if f2 >= 2:
                        pe_out_pair(gi, f2 - 2)
                    if f2 == 5 and gi + 1 < len(groups):
                        pe_front(gi + 1)
                pe_out_pair(gi, 14)
                pe_out_pair(gi, 15)
                for tq in range(2):
                    tsl = slice(g * G + tq * 128, g * G + (tq + 1) * 128)
                    xt_, xk_ = x1g[xs * 2 + tq], "x1g%d" % (xs * 2 + tq)
                    post_norm_residual(OB[tq], gpost2, xt_[:], xk_, xt_, xk_, pn_tmp2, "f")
                    S.dma("sp", lambda e, xt_=xt_, tsl=tsl, b=b: e.dma_start(out=out_d[b, tsl, :], in_=xt_[:]), "out_st", reads=[xk_])

            pe_front(0)
            for gi in range(len(groups)):
                pe_group(gi)
        S.run(nc, es)
    return nc


def _perm_cols():
    idx = list(range(0, 1536))
    idx += list(range(1536, 1600)) + list(range(1664, 1728))
    idx += list(range(1600, 1664)) + list(range(1728, 1792))
    idx += list(range(1792, 1952)) + [-1] * 96
    idx += list(range(1952, 2976))
    return np.array(idx)


def prep_inputs(inp, NB, ncores):
    f = lambda a: np.ascontiguousarray(np.asarray(a, dtype=np.float32))
    perm = _perm_cols()
    w_in = f(inp["w_in"])[0]
    w_in_p = np.zeros((D, 3072), np.float32)
    w_in_p[:, perm >= 0] = w_in[:, perm[perm >= 0]]

    def permvec(v):
        o = np.zeros(2048, np.float32)
        p16 = perm[:2048]
        o[p16 >= 0] = v[p16[p16 >= 0]]
        return o.reshape(16, 128).T

    fm = lambda v: np.asarray(v, np.float32).reshape(-1, 128).T
    cfm = np.zeros((128, NCF), np.float32)

    def put(nm, arr):
        o, w = CF[nm]
        assert arr.shape == (128, w), (nm, arr.shape)
        cfm[:, o:o + w] = arr
    put("ada_b", fm(inp["ada_b"][0]))
    put("mix_pre_g", fm(inp["mix_pre_g"][0]))
    put("mlp_pre_g", fm(inp["mlp_pre_g"][0]))
    put("mu_prev", permvec(f(inp["mu_prev"])[0]))
    put("mu_next", permvec(f(inp["mu_next"])[0]))
    put("w0", np.concatenate([fm(inp["decay_w0"][0, 0]), fm(inp["decay_w0"][0, 1])], 1))
    put("a0", np.concatenate([fm(inp["iclr_a0"][0, 0]), fm(inp["iclr_a0"][0, 1])], 1))
    put("k_k", fm(inp["k_k"][0]))
    put("k_a", fm(inp["k_a"][0]))
    put("r_k", fm(np.asarray(inp["r_k"][0]).reshape(-1)))
    cw = f(inp["conv_w"])[0]
    put("conv_w", cw.T.reshape(4, 128, 31).transpose(1, 0, 2).reshape(128, 124))
    put("conv_b", fm(inp["conv_b"][0]))
    put("cln_w", fm(inp["conv_ln_w"][0]))
    put("cln_b", fm(inp["conv_ln_b"][0]))
    crow = np.zeros((NCR,), np.float32)
    for nm, v in (("ada_b", inp["ada_b"][0]), ("mix_post_g", inp["mix_post_g"][0]),
                  ("mlp_post_g", inp["mlp_post_g"][0]), ("lnx_w", inp["lnx_w"][0]), ("lnx_b", inp["lnx_b"][0])):
        o, w = CR[nm]
        crow[o:o + w] = np.asarray(v, np.float32)
    crow = np.ascontiguousarray(np.broadcast_to(crow[None, :], (128, NCR)))
    x = f(inp["x"])
    ctx = f(inp["ctx"])
    c = f(inp["c"])
    cc = f(inp["c_ctx"])
    shared = {"ada_w": f(inp["ada_w"])[0], "w_in": w_in_p, "cfm": cfm, "crow": crow,
              "decay_w2": f(inp["decay_w2"])[0], "iclr_a2": f(inp["iclr_a2"])[0], "gate_w2": f(inp["gate_w2"])[0],
              "w_out": f(inp["w_out"])[0], "mlp_w1": f(inp["mlp_w1"])[0], "mlp_w2": f(inp["mlp_w2"])[0]}
    maps = []
    for i in range(ncores):
        cb = np.concatenate([c[i * NB:(i + 1) * NB], cc[None, :]], 0)
        cT = np.ascontiguousarray(cb.T.reshape(DK, 128, NB + 1).transpose(1, 0, 2))
        m = dict(shared)
        m.update({"x": np.ascontiguousarray(x[i * NB:(i + 1) * NB]), "ctx": np.ascontiguousarray(ctx[i * NB:(i + 1) * NB]),
                  "cT": cT})
        maps.append(m)
    return maps


def kernel(**inputs):
    B, T, _ = inputs["x"].shape
    TC = inputs["ctx"].shape[1]
    NB = B // NCORES
    nc = build(NB, T, TC)
    maps = prep_inputs(inputs, NB, NCORES)
    res = run_bass_kernel_spmd(nc, maps, core_ids=list(range(NCORES)))
    return np.concatenate([np.asarray(r["out"]) for r in res.results], 0).astype(np.float32)
```

```python
from contextlib import ExitStack
import os
import numpy as np
import concourse.bass as bass
import concourse.mybir as mybir
from concourse.bass_utils import run_bass_kernel_spmd

F32 = mybir.dt.float32
BF16 = mybir.dt.bfloat16
AF = mybir.ActivationFunctionType
ALU = mybir.AluOpType
AX = mybir.AxisListType

ENGS = ("pe", "act", "dve", "pool", "sp")
CH = 30000
D = 1024
DK = 8
NCORES = 8
CDEC = 0.6065306597126334


class Sched:
    def __init__(self):
        self.q = {e: [] for e in ENGS}
        self.cnt = {e: 0 for e in ENGS}
        self.seen = {e: {} for e in ENGS}
        self.last_w = {}
        self.readers = {}
        self.dma_cnt = {}
        self.dma_keys = []
        self.fence_snap = None
        self.fenced = {e: True for e in ENGS}

    def fence(self):
        snap = [(e, self.cnt[e]) for e in ENGS if self.cnt[e] > 0]
        snap += [("dma:" + k, n) for k, n in self.dma_cnt.items()]
        self.fence_snap = snap
        self.fenced = {e: False for e in ENGS}
        self.last_w = {}
        self.readers = {}

    def _deps(self, eng, reads, writes):
        deps = set()
        if not self.fenced[eng]:
            self.fenced[eng] = True
            deps |= set(self.fence_snap)
        for k in reads:
            if k in self.last_w:
                deps.add(self.last_w[k])
        for k in writes:
            if k in self.last_w:
                deps.add(self.last_w[k])
            deps |= self.readers.get(k, set())
        need = {}
        for (e, s) in deps:
            need[e] = max(need.get(e, 0), s)
        waits = []
        for e, s in need.items():
            if e == "pe" and eng == "pe":
                continue
            if self.seen[eng].get(e, 0) >= s:
                continue
            self.seen[eng][e] = s
            waits.append((e, s))
        return waits

    def _commit(self, me, reads, writes):
        for k in reads:
            self.readers.setdefault(k, set()).add(me)
        for k in writes:
            self.last_w[k] = me
            self.readers[k] = set()

    def op(self, eng, fn, reads=(), writes=()):
        waits = self._deps(eng, reads, writes)
        self.cnt[eng] += 1
        self.q[eng].append(("op", waits, fn, self.cnt[eng]))
        self._commit((eng, self.cnt[eng]), reads, writes)

    def dma(self, eng, fn, semkey, reads=(), writes=()):
        waits = self._deps(eng, reads, writes)
        if semkey not in self.dma_cnt:
            self.dma_cnt[semkey] = 0
            self.dma_keys.append(semkey)
        self.dma_cnt[semkey] += 1
        self.q[eng].append(("dma", waits, fn, semkey))
        self._commit(("dma:" + semkey, self.dma_cnt[semkey]), reads, writes)

    def emit_engine(self, eng, engobj, sems, dma_sems):
        def do_wait(e, s):
            if e.startswith("dma:"):
                engobj.wait_ge(dma_sems[e[4:]], 16 * s)
            else:
                engobj.wait_ge(sems[e][(s - 1) // CH], ((s - 1) % CH) + 1)

        for item in self.q[eng]:
            for (e, s) in item[1]:
                do_wait(e, s)
            if item[0] == "op":
                item[2](engobj).then_inc(sems[eng][(item[3] - 1) // CH], 1)
            else:
                item[2](engobj).then_inc(dma_sems[item[3]], 16)
        if eng == "sp":
            for k, n in self.dma_cnt.items():
                engobj.wait_ge(dma_sems[k], 16 * n)

    def run(self, nc, es):
        sems = {e: [es.enter_context(nc.semaphore("s_%s_%d" % (e, i)))
                    for i in range(max(1, (self.cnt[e] + CH - 1) // CH))] for e in ENGS}
        dma_sems = {k: es.enter_context(nc.semaphore("d_%d" % i)) for i, k in enumerate(self.dma_keys)}
        block = es.enter_context(nc.Block())
        block.tensor(lambda e: self.emit_engine("pe", e, sems, dma_sems))
        block.scalar(lambda e: self.emit_engine("act", e, sems, dma_sems))
        block.vector(lambda e: self.emit_engine("dve", e, sems, dma_sems))
        block.gpsimd(lambda e: self.emit_engine("pool", e, sems, dma_sems))
        block.sync(lambda e: self.emit_engine("sp", e, sems, dma_sems))


class Arena:
    def __init__(self, nc, base, limit):
        self.nc, self.base, self.limit, self.off, self.n = nc, base, limit, base, 0

    def reset(self):
        self.off = self.base

    def t(self, shape, dt):
        nb = int(np.prod(shape[1:])) * (4 if dt == F32 else 2)
        nb = (nb + 63) // 64 * 64
        assert self.off + nb <= self.limit, ("SBUF overflow", self.off, nb, self.limit)
        self.n += 1
        h = self.nc.alloc_sbuf_tensor_at("a%d" % self.n, list(shape), dt, offset=self.off)
        self.off += nb
        return h


CF = {}
CR = {}


def _layout():
    off = 0
    for nm, w in (("ada_b", 48), ("mix_pre_g", 8), ("mlp_pre_g", 8), ("mu_prev", 16), ("mu_next", 16),
                  ("w0", 8), ("a0", 8), ("k_k", 4), ("k_a", 4), ("r_k", 4), ("conv_w", 124),
                  ("conv_b", 4), ("cln_w", 4), ("cln_b", 4)):
        CF[nm] = (off, w)
        off += w
    ncf = off
    off = 0
    for nm, w in (("ada_b", 6144), ("mix_post_g", 1024), ("mlp_post_g", 1024), ("lnx_w", 512), ("lnx_b", 512)):
        CR[nm] = (off, w)
        off += w
    return ncf, off


NCF, NCR = _layout()


def build(NB, T, TC, debug=False, PHASES=9):
    TT = TC + T
    NB1 = NB + 1
    nc = bass.Bass("TRN2", target_bir_lowering=False)
    dram = lambda n, s, dt, kind: nc.dram_tensor(n, list(s), dt, kind=kind).ap()
    x_d = dram("x", [NB, T, D], F32, "ExternalInput")
    ctx_d = dram("ctx", [NB, TC, D], F32, "ExternalInput")
    cT_d = dram("cT", [128, DK, NB1], F32, "ExternalInput")
    adaw_d = dram("ada_w", [D, 6144], F32, "ExternalInput")
    win_d = dram("w_in", [D, 3072], F32, "ExternalInput")
    cfm_d = dram("cfm", [128, NCF], F32, "ExternalInput")
    crow_d = dram("crow", [128, NCR], F32, "ExternalInput")
    w2d_d = dram("decay_w2", [2, 64, 512], F32, "ExternalInput")
    a2_d = dram("iclr_a2", [2, 64, 512], F32, "ExternalInput")
    gw2_d = dram("gate_w2", [160, 512], F32, "ExternalInput")
    wout_d = dram("w_out", [D, D], F32, "ExternalInput")
    w1_d = dram("mlp_w1", [D, 4096], F32, "ExternalInput")
    w2_d = dram("mlp_w2", [4096, D], F32, "ExternalInput")
    out_d = dram("out", [NB, T, D], F32, "ExternalOutput")
    skind = "ExternalOutput" if debug else "Internal"
    pa_d = dram("pa_s", [NB, 20, 128, TT], BF16, skind)
    y_d = dram("y_s", [NB, 2, T, 512], F32, skind)
    bo_d = dram("bo_s", [NB, 2, T, 512], F32, skind)

    S = Sched()
    es = ExitStack()
    with es:
        banks = [es.enter_context(nc.psum_tensor("bank%d" % i, [128, 512], F32)) for i in range(8)]
        bk = lambda i: "bank%d" % i
        P_ = Arena(nc, 17408, 51 * 1024)
        A_ = Arena(nc, 51 * 1024, 223 * 1024)

        cfm = P_.t([128, NCF], F32)
        crow = P_.t([128, NCR - 6144], F32)
        ident = P_.t([128, 128], BF16)
        identf = P_.t([128, 128], F32)
        bones = P_.t([128, 128], F32)
        hsel = P_.t([128, 2], BF16)
        msk = P_.t([128, 2, 2, 128], BF16)
        mskN = P_.t([128, 2, 64], BF16)
        identB = P_.t([128, 64], BF16)
        onesb = P_.t([128, 1], BF16)
        rstm = P_.t([128, 512], F32)
        modfm = P_.t([128, 48, NB1], F32)
        gates = P_.t([NB1, 2, 1024], F32)
        gm = P_.t([128, 2, DK, NB1], F32)
        c0 = P_.t([128, 16], F32)
        sel = P_.t([NB1, NB1, 128], F32)
        cf = lambda nm: cfm[:, CF[nm][0]:CF[nm][0] + CF[nm][1]]
        cr = lambda nm: crow[:, CR[nm][0] - 6144:CR[nm][0] - 6144 + CR[nm][1]]

        S.dma("sp", lambda e: e.dma_start(out=cfm[:], in_=cfm_d[:, :]), "c0", writes=["cfm"])
        S.dma("sp", lambda e: e.dma_start(out=crow[:], in_=crow_d[:, 6144:NCR]), "c0", writes=["crow"])
        S.op("pool", lambda e: e.memset(identf[:], 1.0), writes=["identf"])
        S.op("pool", lambda e: e.affine_select(out=identf[:], in_=identf[:], pattern=[[-1, 128]],
                                               compare_op=ALU.is_equal, fill=0.0, base=0, channel_multiplier=1),
             reads=["identf"], writes=["identf"])
        S.op("dve", lambda e: e.tensor_copy(out=ident[:], in_=identf[:]), reads=["identf"], writes=["ident"])
        S.op("pool", lambda e: e.memset(bones[:], 0.0), writes=["bones"])
        S.op("pool", lambda e: e.memset(bones[0:64, 0:64], 1.0), reads=["bones"], writes=["bones"])
        S.op("pool", lambda e: e.memset(bones[64:128, 64:128], 1.0), reads=["bones"], writes=["bones"])
        S.op("pool", lambda e: e.memset(hsel[:], 0.0), writes=["hsel"])
        S.op("pool", lambda e: e.memset(hsel[0:64, 0:1], 1.0), reads=["hsel"], writes=["hsel"])
        S.op("pool", lambda e: e.memset(hsel[64:128, 1:2], 1.0), reads=["hsel"], writes=["hsel"])
        mtmp = P_.t([64, 64], F32)

        def mk_mask(dst_ap, sign, strict):
            S.op("pool", lambda e: e.memset(mtmp[:], 1.0), reads=["mtmp"], writes=["mtmp"])
            S.op("pool", lambda e: e.affine_select(out=mtmp[:], in_=mtmp[:], pattern=[[sign, 64]],
                                                   compare_op=ALU.is_gt if strict else ALU.is_ge,
                                                   fill=0.0, base=0, channel_multiplier=-sign),
                 reads=["mtmp"], writes=["mtmp"])
            S.op("pool", lambda e: e.tensor_copy(out=dst_ap, in_=mtmp[:]), reads=["mtmp"], writes=["msk"])
        for d in range(2):
            sg_ = 1 if d == 0 else -1
            for rr in range(2):
                mk_mask(msk[0:64, d, rr, 0:64], sg_, True)
                mk_mask(msk[0:64, d, rr, 64:128], sg_, False)
            mk_mask(mskN[0:64, d, :], -sg_, True)
        S.dma("sp", lambda e: e.dma_start(out=msk[64:128], in_=msk[0:64]), "c0", reads=["msk"], writes=["msk"])
        S.dma("sp", lambda e: e.dma_start(out=mskN[64:128], in_=mskN[0:64]), "c0", reads=["msk"], writes=["msk"])
        S.op("dve", lambda e: e.tensor_tensor(out=identB[:], in0=ident[:, 0:64], in1=ident[:, 64:128], op=ALU.add), reads=["ident"], writes=["identB"])
        S.op("pool", lambda e: e.memset(onesb[:], 1.0), writes=["onesb"])
        S.op("pool", lambda e: e.memset(rstm[:], 1.0), writes=["rstm"])
        S.op("pool", lambda e: e.memset(rstm[:].rearrange("p (c t) -> p c t", t=64)[:, :, 0:1], 0.0),
             reads=["rstm"], writes=["rstm"])
        S.op("pool", lambda e: e.memset(sel[:], 0.0), writes=["sel"])
        for b in range(NB1):
            S.op("pool", lambda e, b=b: e.memset(sel[:, b, :], 1.0), reads=["sel"], writes=["sel"])
            S.op("pool", lambda e, b=b: e.affine_select(out=sel[:, b, :], in_=sel[:, b, :], pattern=[[0, 128]],
                                                        compare_op=ALU.is_equal, fill=0.0, base=-b,
                                                        channel_multiplier=1),
                 reads=["sel"], writes=["sel"])

        A_.reset()
        cT = A_.t([128, DK, NB1], F32)
        siluT = A_.t([128, DK, NB1], F32)
        adab_row = A_.t([NB1, 6144], F32)
        aw = [A_.t([128, DK, 512], F32) for _ in range(4)]
        S.dma("sp", lambda e: e.dma_start(out=cT[:], in_=cT_d[:, :, :]), "c0", writes=["cT"])
        S.dma("sp", lambda e: e.dma_start(out=adab_row[:], in_=crow_d[0:NB1, 0:6144]), "c0", writes=["adab_row"])
        S.op("act", lambda e: e.activation(out=siluT[:], in_=cT[:], func=AF.Silu), reads=["cT"], writes=["siluT"])
        for n in range(12):
            a = aw[n % 4]
            ak = "aw%d" % (n % 4)
            S.dma("sp", lambda e, a=a, n=n: e.dma_start(
                out=a[:], in_=adaw_d[:, n * 512:(n + 1) * 512].rearrange("(k p) n -> p k n", p=128)),
                ak, writes=[ak])
            m = n // 2
            if m in (2, 5):
                for k in range(DK):
                    S.op("pe", lambda e, a=a, k=k: e.matmul(banks[0][0:NB1, :], lhsT=siluT[:, k, :], rhs=a[:, k, :],
                                                            start=(k == 0), stop=(k == DK - 1)),
                         reads=[ak, "siluT"], writes=[bk(0)])
                gi = 0 if m == 2 else 1
                S.op("dve", lambda e, n=n, gi=gi: e.tensor_tensor(
                    out=gates[:, gi, (n % 2) * 512:(n % 2) * 512 + 512], in0=banks[0][0:NB1, :],
                    in1=adab_row[:, n * 512:(n + 1) * 512], op=ALU.add),
                    reads=["adab_row"], writes=[bk(0), "gates"])
            else:
                for j in range(4):
                    for k in range(DK):
                        S.op("pe", lambda e, a=a, k=k, j=j: e.matmul(
                            banks[1][:, j * NB1:(j + 1) * NB1], lhsT=a[:, k, j * 128:(j + 1) * 128],
                            rhs=siluT[:, k, :], start=(k == 0), stop=(k == DK - 1)),
                            reads=[ak, "siluT"], writes=[bk(1)])
                S.op("dve", lambda e, n=n: e.tensor_tensor(
                    out=modfm[:, n * 4:(n + 1) * 4, :],
                    in0=banks[1][:, 0:4 * NB1].rearrange("p (j b) -> p j b", b=NB1),
                    in1=cf("ada_b")[:, n * 4:(n + 1) * 4].unsqueeze(2).to_broadcast([128, 4, NB1]), op=ALU.add),
                    reads=["cfm"], writes=[bk(1), "modfm"])
        for gi, (gn, m) in enumerate((("mix_pre_g", 1), ("mlp_pre_g", 4))):
            S.op("dve", lambda e, gi=gi, m=m: e.tensor_scalar(out=gm[:, gi], in0=modfm[:, m * 8:(m + 1) * 8, :],
                                                              scalar1=1.0, scalar2=None, op0=ALU.add),
                 reads=["modfm"], writes=["gm"])
            S.op("dve", lambda e, gi=gi, gn=gn: e.tensor_tensor(
                out=gm[:, gi], in0=gm[:, gi], in1=cf(gn).unsqueeze(2).to_broadcast([128, DK, NB1]), op=ALU.mult),
                reads=["gm", "cfm"], writes=["gm"])
        S.op("dve", lambda e: e.tensor_tensor(out=c0[:], in0=cf("mu_prev"), in1=cf("mu_next"), op=ALU.add),
             reads=["cfm"], writes=["c0"])
        S.op("dve", lambda e: e.tensor_scalar(out=c0[:], in0=c0[:], scalar1=-1.0, scalar2=1.0, op0=ALU.mult,
                                              op1=ALU.add), reads=["c0"], writes=["c0"])

        def front(xt_ap, xkey, hT_ap, hkey, gi, shm, b, tmp, pbank):
            S.op("act", lambda e: e.activation(out=tmp["sq"][:], in_=xt_ap, func=AF.Square),
                 reads=[xkey], writes=["f_sq"])
            S.op("dve", lambda e: e.reduce_sum(out=tmp["ss"][:], in_=tmp["sq"][:], axis=AX.X),
                 reads=["f_sq"], writes=["f_ss"])
            S.op("dve", lambda e: e.tensor_scalar(out=tmp["ss"][:], in0=tmp["ss"][:], scalar1=1.0 / D, scalar2=1e-6,
                                                  op0=ALU.mult, op1=ALU.add), reads=["f_ss"], writes=["f_ss"])
            S.op("act", lambda e: e.activation(out=tmp["ss"][:], in_=tmp["ss"][:], func=AF.Sqrt),
                 reads=["f_ss"], writes=["f_ss"])
            S.op("dve", lambda e: e.reciprocal(out=tmp["ss"][:], in_=tmp["ss"][:]), reads=["f_ss"], writes=["f_ss"])
            S.op("act", lambda e: e.activation(out=tmp["xn"][:], in_=xt_ap, func=AF.Copy, scale=tmp["ss"][:, 0:1]),
                 reads=[xkey, "f_ss"], writes=["f_xn"])
            pb = banks[pbank].bitcast(BF16)
            for k in range(DK):
                S.op("pe", lambda e, k=k: e.transpose(pb[:, k * 128:(k + 1) * 128], tmp["xn"][:, k * 128:(k + 1) * 128],
                                                      ident[:]), reads=["f_xn", "ident"], writes=[bk(pbank)])
            S.op("dve", lambda e: e.tensor_tensor(out=hT_ap, in0=pb[:, 0:1024].rearrange("p (k t) -> p k t", t=128),
                                                  in1=gm[:, gi, :, b:b + 1].to_broadcast([128, DK, 128]), op=ALU.mult),
                 reads=["gm"], writes=[bk(pbank), hkey])
            S.op("dve", lambda e: e.tensor_tensor(
                out=hT_ap, in0=hT_ap, in1=modfm[:, shm * 8:(shm + 1) * 8, b:b + 1].to_broadcast([128, DK, 128]),
                op=ALU.add), reads=["modfm", hkey], writes=[hkey])

        S.fence()
        A_.reset()
        winb = A_.t([128, DK, 3072], BF16)
        stg = [A_.t([128, DK, 512], F32) for _ in range(3)]
        for n in range(6):
            s_ = stg[n % 3]
            sk = "stg%d" % (n % 3)
            S.dma("sp", lambda e, s_=s_, n=n: e.dma_start(
                out=s_[:], in_=win_d[:, n * 512:(n + 1) * 512].rearrange("(k p) n -> p k n", p=128)), sk, writes=[sk])
            S.op("dve" if n % 2 == 0 else "act",
                 (lambda e, s_=s_, n=n: e.tensor_copy(out=winb[:, :, n * 512:(n + 1) * 512], in_=s_[:])) if n % 2 == 0
                 else (lambda e, s_=s_, n=n: e.activation(out=winb[:, :, n * 512:(n + 1) * 512], in_=s_[:], func=AF.Copy)),
                 reads=[sk], writes=["winb"])
        TM = max(T, TC)
        hT = A_.t([128, DK, TM + 2], BF16)
        xt = [A_.t([128, D], F32) for _ in range(2)]
        ftmp = {"sq": A_.t([128, D], F32), "ss": A_.t([128, 1], F32), "xn": A_.t([128, D], BF16)}
        etmp = [A_.t([128, 512], F32) for _ in range(2)]
        obuf = [A_.t([128, 512], BF16) for _ in range(3)]
        A_sg = [A_.t([128, 512], F32) for _ in range(4)]
        S.op("pool", lambda e: e.memset(hT[:], 0.0), writes=["hT"])
        xi = 0
        ob_i = 0
        pb_i = 0
        for b in range(NB):
            for (src, Ts, toff, bmod, tiles) in ((ctx_d, TC, 0, NB, list(range(4, 14))),
                                                 (x_d, T, TC, b, list(range(0, 24)))):
                if Ts < TM:
                    S.op("pool", lambda e, Ts=Ts: e.memset(hT[:, :, Ts + 1:Ts + 2], 0.0), reads=["hT"], writes=["hT"])
                for tt in range(Ts // 128):
                    xa = xt[xi % 2]
                    xk = "xt%d" % (xi % 2)
                    xi += 1
                    S.dma("sp", lambda e, xa=xa, src=src, b=b, tt=tt: e.dma_start(
                        out=xa[:], in_=src[b, tt * 128:(tt + 1) * 128, :]), xk, writes=[xk])
                    front(xa[:], xk, hT[:, :, 1 + tt * 128:1 + (tt + 1) * 128], "hT", 0, 0, bmod, ftmp, 7)
                w0 = 0
                while w0 < Ts:
                    n = min(510, Ts - w0)
                    sg_ready = {}
                    for j in [jj for jj in tiles if jj >= 20] + [jj for jj in tiles if jj < 20]:
                        pbk = pb_i % 4
                        pb_i += 1
                        pbt = banks[pbk]
                        for k in range(DK):
                            S.op("pe", lambda e, pbt=pbt, k=k, j=j, w0=w0, n=n: e.matmul(
                                pbt[:, 0:n + 2], lhsT=winb[:, k, j * 128:(j + 1) * 128], rhs=hT[:, k, w0:w0 + n + 2],
                                start=(k == 0), stop=(k == DK - 1)), reads=["winb", "hT"], writes=[bk(pbk)])
                        if j >= 20:
                            sgt = A_sg[j - 20]
                            S.op("act", lambda e, pbt=pbt, sgt=sgt, n=n: e.activation(
                                out=sgt[:, 0:n], in_=pbt[:, 1:n + 1], func=AF.Sigmoid),
                                writes=[bk(pbk), "sg%d" % (j - 20)])
                            continue
                        ob = obuf[ob_i % 3]
                        ok = "ob%d" % (ob_i % 3)
                        ob_i += 1
                        if j >= 16:
                            sgt = A_sg[j - 16]
                            S.op("dve", lambda e, pbt=pbt, sgt=sgt, ob=ob, n=n: e.tensor_tensor(
                                out=ob[:, 0:n], in0=pbt[:, 1:n + 1], in1=sgt[:, 0:n], op=ALU.mult),
                                reads=["sg%d" % (j - 16)], writes=[bk(pbk), ok])
                        else:
                            et = etmp[j % 2]
                            ek = "et%d" % (j % 2)
                            S.op("act", lambda e, pbt=pbt, et=et, n=n, j=j: e.activation(
                                out=et[:, 0:n], in_=pbt[:, 1:n + 1], func=AF.Copy, scale=c0[:, j:j + 1]),
                                reads=["c0"], writes=[bk(pbk), ek])
                            S.op("dve", lambda e, pbt=pbt, et=et, n=n, j=j: e.scalar_tensor_tensor(
                                out=et[:, 0:n], in0=pbt[:, 0:n], scalar=cf("mu_prev")[:, j:j + 1], in1=et[:, 0:n],
                                op0=ALU.mult, op1=ALU.add), reads=["cfm", ek], writes=[bk(pbk), ek])
                            S.op("dve", lambda e, pbt=pbt, et=et, ob=ob, n=n, j=j: e.scalar_tensor_tensor(
                                out=ob[:, 0:n], in0=pbt[:, 2:n + 2], scalar=cf("mu_next")[:, j:j + 1], in1=et[:, 0:n],
                                op0=ALU.mult, op1=ALU.add), reads=["cfm", ek], writes=[bk(pbk), ok])
                        S.dma("sp", lambda e, ob=ob, b=b, j=j, toff=toff, w0=w0, n=n: e.dma_start(
                            out=pa_d[b, j, :, toff + w0:toff + w0 + n], in_=ob[:, 0:n]), "pa_st", reads=[ok])
                    w0 += n
        if PHASES >= 2:
            S.fence()
            A_.reset()
            W = 128
            NCW = W // 64
            lwb = A_.t([128, 2, 512], BF16)
            omk = A_.t([128, 4], F32)
            rkb = A_.t([128, 4], BF16)
            mark_pb = A_.off
            wst = A_.t([128, 2, 512], F32)
            S.dma("sp", lambda e: e.dma_start(out=wst[0:64], in_=w2d_d.rearrange("d r f -> r d f")), "pbw", writes=["wst"])
            S.dma("sp", lambda e: e.dma_start(out=wst[64:128], in_=a2_d.rearrange("d r f -> r d f")), "pbw", writes=["wst"])
            S.op("dve", lambda e: e.tensor_copy(out=lwb[:], in_=wst[:]), reads=["wst"], writes=["lwb"])
            S.op("dve", lambda e: e.tensor_scalar(out=omk[:], in0=cf("k_a"), scalar1=-1.0, scalar2=1.0, op0=ALU.mult, op1=ALU.add), reads=["cfm"], writes=["omk"])
            S.op("dve", lambda e: e.tensor_copy(out=rkb[:], in_=cf("r_k")), reads=["cfm"], writes=["rkb"])
            S.fence()
            A_.off = mark_pb
            rs = A_.t([128, 4, TT], BF16)
            ks = A_.t([128, 4, TT], BF16)
            vs = A_.t([128, 4, TT], BF16)
            wdad = A_.t([128, 2, TT], BF16)
            f32t = lambda: A_.t([128, 4, W], F32)
            TMP = [dict(sig=f32t(), Ls=f32t(), ee=f32t(), t1=f32t(), t2=f32t(), icl=A_.t([128, 4, W], BF16),
                        SC=A_.t([128, 4, NCW], F32)) for _ in range(2)]
            ar = [A_.t([128, 4, NCW, 2, 64], BF16) for _ in range(6)]
            bkt = [A_.t([128, 4, NCW, 2, 64], BF16) for _ in range(6)]
            prod = [A_.t([128, 4, W], BF16) for _ in range(6)]
            eLC = [A_.t([128, 4, NCW], F32) for _ in range(6)]
            H32 = [A_.t([128, 4, 64], F32) for _ in range(2)]
            Hbf = [A_.t([128, 4, 64], BF16) for _ in range(2)]
            NJS = 4 * NCW
            btk = [A_.t([128, 512], BF16) for _ in range(NJS)]
            vtm = [A_.t([128, 4, 64], BF16) for _ in range(NJS)]
            AT = [A_.t([128, 4, 2, 128], BF16) for _ in range(NJS)]
            XT = [A_.t([128, 4, 64], BF16) for _ in range(NJS)]
            PQm = [[A_.t([128, 2, 4, 64], BF16) for _ in range(2)] for _ in range(2 * NCW)]
            Xm = [[A_.t([128, 4, 64], BF16) for _ in range(2)] for _ in range(2 * NCW)]
            Rsb = [A_.t([128, 4, 64], BF16) for _ in range(2)]
            Usb = [A_.t([128, 4, 64], BF16) for _ in range(2)]
            ybuf = [A_.t([128, 4, 64], F32) for _ in range(2)]
            bosb = [A_.t([128, 4, 64], F32) for _ in range(2)]
            bon = [A_.t([128, 4], F32) for _ in range(2)]
            K_ = lambda nm, d: "%s%d" % (nm, d)

            def prep(b, d, w0, par):
                dp = d * 3 + par
                sig, Ls, ee, t1, t2, icl, SC = [TMP[d][k_] for k_ in ("sig", "Ls", "ee", "t1", "t2", "icl", "SC")]
                kS, kL, kE, k1, k2, kI, kC = [K_(k_, d) for k_ in ("sig", "Ls", "ee", "t1", "t2", "icl", "SC")]
                PB_ = 6 + d
                pbv = lambda i: banks[PB_][:, i * W:(i + 1) * W]
                pb4 = banks[PB_][:, 0:4 * W].rearrange("p (a t) -> p a t", t=W)
                v5 = lambda tns, a: tns[:].rearrange("p i (c t) -> p i c t", t=64) if a is None else tns[:, :, :, a, :]
                for i in range(4):
                    S.op("pe", lambda e, i=i: e.matmul(pbv(i), lhsT=lwb[0:64, d, i * 128:(i + 1) * 128], rhs=wdad[0:64, d, w0:w0 + W],
                                                       start=True, stop=True), reads=["lwb", "wdad"], writes=[bk(PB_)])
                for i in range(4):
                    S.op("act", lambda e, i=i: e.activation(out=sig[:, i, :], in_=pbv(i), func=AF.Sigmoid,
                                                            bias=cf("w0")[:, d * 4 + i:d * 4 + i + 1]), reads=["cfm"], writes=[bk(PB_), kS])
                yield
                for i in range(4):
                    S.op("pe", lambda e, i=i: e.matmul(pbv(i), lhsT=lwb[64:128, d, i * 128:(i + 1) * 128], rhs=wdad[64:128, d, w0:w0 + W],
                                                       start=True, stop=True), reads=["lwb", "wdad"], writes=[bk(PB_)])
                for i in range(4):
                    S.op("act", lambda e, i=i: e.activation(out=icl[:, i, :], in_=pbv(i), func=AF.Sigmoid,
                                                            bias=cf("a0")[:, d * 4 + i:d * 4 + i + 1]), reads=["cfm"], writes=[bk(PB_), kI])
                yield
                for i in range(4):
                    S.op("dve", lambda e, i=i: e.tensor_scalar(out=t1[:, i, :], in0=ks[:, i, w0:w0 + W], scalar1=cf("k_k")[:, i:i + 1],
                                                               scalar2=None, op0=ALU.mult), reads=["ks", "cfm"], writes=[k1])
                S.op("pool", lambda e: e.tensor_tensor(out=t2[:], in0=t1[:], in1=t1[:], op=ALU.mult), reads=[k1], writes=[k2])
                yield
                for i in range(4):
                    S.op("pe", lambda e, i=i: e.matmul(pbv(i), lhsT=bones[:], rhs=t2[:, i, :], start=True, stop=True),
                         reads=["bones", k2], writes=[bk(PB_)])
                S.op("dve", lambda e: e.tensor_scalar(out=ee[:], in0=pb4, scalar1=1e-24, scalar2=None, op0=ALU.max),
                     writes=[bk(PB_), kE])
                yield
                S.op("act", lambda e: e.activation(out=ee[:], in_=ee[:], func=AF.Sqrt), reads=[kE], writes=[kE])
                yield
                S.op("dve", lambda e: e.reciprocal(out=ee[:], in_=ee[:]), reads=[kE], writes=[kE])
                yield
                S.op("pool", lambda e: e.tensor_tensor(out=t1[:], in0=t1[:], in1=ee[:], op=ALU.mult), reads=[k1, kE], writes=[k1])
                for i in range(4):
                    S.op("dve", lambda e, i=i: e.tensor_tensor_scan(out=Ls[:, i, :], data0=rstm[:, 0:W], data1=sig[:, i, :],
                                                                    initial=0.0, op0=ALU.mult, op1=ALU.add),
                         reads=["rstm", kS], writes=[kL])
                lsc = v5(Ls, None)
                S.op("dve", lambda e: e.tensor_copy(out=SC[:], in_=lsc[:, :, :, 63]), reads=[kL], writes=[kC])
                yield
                S.op("act", lambda e: e.activation(out=eLC[dp][:], in_=SC[:], func=AF.Exp, scale=-CDEC), reads=[kC], writes=[K_("eLC", dp)])
                if d == 0:
                    S.op("dve", lambda e: e.tensor_tensor(out=sig[:], in0=Ls[:], in1=sig[:], op=ALU.subtract), reads=[kL, kS], writes=[kS])
                    XE, kXE, XI, kXI = sig, kS, Ls, kL
                else:
                    S.op("dve", lambda e: e.tensor_tensor(out=lsc, in0=SC[:].unsqueeze(3).to_broadcast([128, 4, NCW, 64]),
                                                          in1=lsc, op=ALU.subtract), reads=[kL, kC], writes=[kL])
                    S.op("dve", lambda e: e.tensor_tensor(out=sig[:], in0=Ls[:], in1=sig[:], op=ALU.add), reads=[kL, kS], writes=[kS])
                    XE, kXE, XI, kXI = Ls, kL, sig, kS
                yield
                S.op("act", lambda e: e.activation(out=ee[:], in_=XE[:], func=AF.Exp, scale=-CDEC), reads=[kXE], writes=[kE])
                yield
                S.op("dve", lambda e: e.scalar_tensor_tensor(out=v5(ar[dp], 0), in0=v5(t1, None), scalar=-1.0, in1=v5(ee, None),
                                                             op0=ALU.mult, op1=ALU.mult), reads=[k1, kE], writes=[K_("ar", dp)])
                yield
                S.op("act", lambda e: e.activation(out=ee[:], in_=XI[:], func=AF.Exp, scale=-CDEC), reads=[kXI], writes=[kE])
                yield
                S.op("pool", lambda e: e.tensor_tensor(out=v5(ar[dp], 1), in0=rs[:, :, w0:w0 + W].rearrange("p i (c t) -> p i c t", t=64),
                                                       in1=v5(ee, None), op=ALU.mult), reads=["rs", kE], writes=[K_("ar", dp)])
                yield
                S.op("act", lambda e: e.activation(out=ee[:], in_=XI[:], func=AF.Exp, scale=CDEC), reads=[kXI], writes=[kE])
                S.op("pool", lambda e: e.tensor_tensor(out=t2[:], in0=t1[:], in1=icl[:], op=ALU.mult), reads=[k1, kI], writes=[k2])
                yield
                S.op("dve", lambda e: e.tensor_tensor(out=v5(bkt[dp], 0), in0=v5(t2, None), in1=v5(ee, None), op=ALU.mult),
                     reads=[k2, kE], writes=[K_("bkt", dp)])
                yield
                for i in range(4):
                    S.op("dve", lambda e, i=i: e.tensor_scalar(out=t2[:, i, :], in0=icl[:, i, :], scalar1=cf("k_a")[:, i:i + 1],
                                                               scalar2=omk[:, i:i + 1], op0=ALU.mult, op1=ALU.add),
                         reads=[kI, "cfm", "omk"], writes=[k2])
                yield
                S.op("pool", lambda e: e.tensor_tensor(out=t2[:], in0=t2[:], in1=ks[:, :, w0:w0 + W], op=ALU.mult), reads=[k2, "ks"], writes=[k2])
                yield
                S.op("dve", lambda e: e.tensor_tensor(out=v5(bkt[dp], 1), in0=v5(t2, None), in1=v5(ee, None), op=ALU.mult),
                     reads=[k2, kE], writes=[K_("bkt", dp)])
                yield
                S.op("pool", lambda e: e.tensor_tensor(out=prod[dp][:], in0=t2[:], in1=rs[:, :, w0:w0 + W], op=ALU.mult),
                     reads=[k2, "rs"], writes=[K_("prod", dp)])
                yield

            HP = [((h % 2) * 64, h // 2) for h in range(8)]
            V3 = lambda bi: banks[bi][:, 0:256].rearrange("p (i s) -> p i s", s=64)

            def inv(b, d, w0, c, par3, js, jt, B):
                tk0 = w0 + c * 64
                dp = d * 3 + par3
                arK, bkK = K_("ar", dp), K_("bkt", dp)
                pT = banks[B].bitcast(BF16)
                for q in range(2):
                    for (po, i) in HP:
                        S.op("pe", lambda e, q=q, po=po, i=i: e.transpose(
                            pT[po:po + 64, (q * 4 + i) * 64:(q * 4 + i + 1) * 64], bkt[dp][po:po + 64, i, c, q, :],
                            ident[po:po + 64, po:po + 64]), reads=[bkK, "ident"], writes=[bk(B)])
                for (po, i) in HP:
                    S.op("pe", lambda e, po=po, i=i: e.transpose(pT[po:po + 64, 512 + i * 64:512 + (i + 1) * 64],
                                                                 vs[po:po + 64, i, tk0:tk0 + 64], ident[po:po + 64, po:po + 64]),
                         reads=["vs", "ident"], writes=[bk(B)])
                S.op("act", lambda e: e.activation(out=btk[js][:], in_=pT[:, 0:512], func=AF.Copy), writes=[bk(B), K_("btk", js)])
                S.op("act", lambda e: e.activation(out=vtm[js][:].rearrange("p i v -> p (i v)"), in_=pT[:, 512:768], func=AF.Copy),
                     writes=[bk(B), K_("vtm", js)])
                yield
                for bb in range(2):
                    psA = banks[B][:, :].rearrange("p (i r t) -> p i r t", i=2, r=2)
                    for (po, i) in HP:
                        if i // 2 != bb:
                            continue
                        rhs = ar[dp][po:po + 64, i, c, :, :].rearrange("p a t -> p (a t)")
                        for r_ in range(2):
                            S.op("pe", lambda e, r_=r_, po=po, i=i, rhs=rhs, psA=psA: e.matmul(
                                psA[po:po + 64, i % 2, r_, :], lhsT=bkt[dp][po:po + 64, i, c, r_, :], rhs=rhs, start=True, stop=True),
                                reads=[bkK, arK], writes=[bk(B)])
                    S.op("dve", lambda e, bb=bb, psA=psA: e.tensor_tensor(
                        out=AT[js][:, bb * 2:bb * 2 + 2], in0=psA, in1=msk[:, d:d + 1].to_broadcast([128, 2, 2, 128]), op=ALU.mult),
                        reads=["msk"], writes=[bk(B), K_("AT", js)])
                    yield
                psN = V3(B)
                for (po, i) in HP:
                    S.op("pe", lambda e, po=po, i=i: e.matmul(psN[po:po + 64, i, :], lhsT=ar[dp][po:po + 64, i, c, 0, :],
                                                               rhs=bkt[dp][po:po + 64, i, c, 0, :], start=True, stop=True),
                         reads=[bkK, arK], writes=[bk(B)])
                PQ, X = PQm[jt], Xm[jt]
                S.op("dve", lambda e: e.tensor_tensor(out=PQ[0][:, 0], in0=psN, in1=mskN[:, d:d + 1].to_broadcast([128, 4, 64]), op=ALU.mult),
                     reads=["msk"], writes=[bk(B), K_("PQ0", jt)])
                S.op("act", lambda e: e.activation(out=PQ[0][:, 1], in_=AT[js][:, :, 0, 0:64], func=AF.Copy),
                     reads=[K_("AT", js)], writes=[K_("PQ0", jt)])
                S.op("pool", lambda e: e.tensor_tensor(out=X[0][:], in0=AT[js][:, :, 0, 0:64],
                                                       in1=identB[:].unsqueeze(1).to_broadcast([128, 4, 64]), op=ALU.add),
                     reads=[K_("AT", js), "identB"], writes=[K_("X0", jt)])
                yield
                cur = 0
                for j in range(1, 6):
                    nxt = 1 - cur
                    psPQ = banks[B][:, :].rearrange("p (a i s) -> p a i s", a=2, s=64)
                    na = 2 if j < 5 else 1
                    for a_ in range(na):
                        for (po, i) in HP:
                            S.op("pe", lambda e, po=po, i=i, cur=cur, a_=a_, psPQ=psPQ: e.matmul(
                                psPQ[po:po + 64, a_, i, :], lhsT=PQ[cur][po:po + 64, 1 - a_, i, :], rhs=PQ[cur][po:po + 64, a_, i, :],
                                start=True, stop=True), reads=[K_("PQ%d" % cur, jt)], writes=[bk(B)])
                    if False:
                        S.op("dve", lambda e, nxt=nxt, na=na, psPQ=psPQ: e.tensor_copy(out=PQ[nxt][:, 0:na], in_=psPQ[:, 0:na]),
                             writes=[bk(B), K_("PQ%d" % nxt, jt)])
                    else:
                        S.op("act", lambda e, nxt=nxt, na=na, psPQ=psPQ: e.activation(out=PQ[nxt][:, 0:na], in_=psPQ[:, 0:na], func=AF.Copy),
                             writes=[bk(B), K_("PQ%d" % nxt, jt)])
                    yield
                    psX = V3(B)
                    for (po, i) in HP:
                        S.op("pe", lambda e, po=po, i=i, cur=cur, nxt=nxt, psX=psX: e.matmul(
                            psX[po:po + 64, i, :], lhsT=PQ[nxt][po:po + 64, 0, i, :], rhs=X[cur][po:po + 64, i, :], start=True, stop=True),
                            reads=[K_("PQ%d" % nxt, jt), K_("X%d" % cur, jt)], writes=[bk(B)])
                    xo_, xok_ = (XT[js], K_("XT", js)) if j == 5 else (X[nxt], K_("X%d" % nxt, jt))
                    S.op("dve", lambda e, cur=cur, xo_=xo_, psX=psX: e.tensor_tensor(out=xo_[:], in0=psX, in1=X[cur][:], op=ALU.add),
                         reads=[K_("X%d" % cur, jt)], writes=[bk(B), xok_])
                    yield
                    cur = nxt

            def chain(b, d, w0, c, is_lat, par3, js, B):
                tk0 = w0 + c * 64
                dp = d * 3 + par3
                arK, bkK = K_("ar", dp), K_("bkt", dp)
                btm = btk[js][:, 0:256].rearrange("p (i k) -> p i k", k=64)
                ktm = btk[js][:, 256:512].rearrange("p (i k) -> p i k", k=64)
                ATj, vt, XTj = AT[js], vtm[js], XT[js]
                psR = V3(B)
                for (po, i) in HP:
                    S.op("pe", lambda e, po=po, i=i: e.matmul(psR[po:po + 64, i, :], lhsT=ar[dp][po:po + 64, i, c, 0, :],
                                                               rhs=Hbf[d][po:po + 64, i, :], start=True, stop=False),
                         reads=[arK, K_("Hbf", d)], writes=[bk(B)])
                    S.op("pe", lambda e, po=po, i=i: e.matmul(psR[po:po + 64, i, :], lhsT=ATj[po:po + 64, i, 1, 0:64],
                                                               rhs=vt[po:po + 64, i, :], start=False, stop=True),
                         reads=[K_("AT", js), K_("vtm", js)], writes=[bk(B)])
                S.op("act", lambda e: e.activation(out=Rsb[d][:], in_=psR, func=AF.Copy), writes=[bk(B), K_("Rsb", d)])
                yield
                for (po, i) in HP:
                    S.op("pe", lambda e, po=po, i=i: e.matmul(psR[po:po + 64, i, :], lhsT=XTj[po:po + 64, i, :],
                                                               rhs=Rsb[d][po:po + 64, i, :], start=True, stop=True),
                         reads=[K_("XT", js), K_("Rsb", d)], writes=[bk(B)])
                S.op("act", lambda e: e.activation(out=Usb[d][:], in_=psR, func=AF.Copy), writes=[bk(B), K_("Usb", d)])
                yield
                psH = V3(B)
                for (po, i) in HP:
                    S.op("pe", lambda e, po=po, i=i: e.matmul(psH[po:po + 64, i, :], lhsT=btm[po:po + 64, i, :],
                                                               rhs=Usb[d][po:po + 64, i, :], start=True, stop=False),
                         reads=[K_("btk", js), K_("Usb", d)], writes=[bk(B)])
                    S.op("pe", lambda e, po=po, i=i: e.matmul(psH[po:po + 64, i, :], lhsT=ktm[po:po + 64, i, :],
                                                               rhs=vt[po:po + 64, i, :], start=False, stop=True),
                         reads=[K_("btk", js), K_("vtm", js)], writes=[bk(B)])
                if is_lat:
                    psY = banks[B][:, 256:512].rearrange("p (i s) -> p i s", s=64)
                    for (po, i) in HP:
                        S.op("pe", lambda e, po=po, i=i: e.matmul(psY[po:po + 64, i, :], lhsT=ar[dp][po:po + 64, i, c, 1, :],
                                                                   rhs=Hbf[d][po:po + 64, i, :], start=True, stop=False),
                             reads=[arK, K_("Hbf", d)], writes=[bk(B)])
                        S.op("pe", lambda e, po=po, i=i: e.matmul(psY[po:po + 64, i, :], lhsT=ATj[po:po + 64, i, 0, 64:128],
                                                                   rhs=Usb[d][po:po + 64, i, :], start=False, stop=False),
                             reads=[K_("AT", js), K_("Usb", d)], writes=[bk(B)])
                        S.op("pe", lambda e, po=po, i=i: e.matmul(psY[po:po + 64, i, :], lhsT=ATj[po:po + 64, i, 1, 64:128],
                                                                   rhs=vt[po:po + 64, i, :], start=False, stop=True),
                             reads=[K_("AT", js), K_("vtm", js)], writes=[bk(B)])
                S.op("dve", lambda e: e.tensor_tensor(out=H32[d][:], in0=psH, in1=H32[d][:], op=ALU.add),
                     reads=[K_("H32", d)], writes=[bk(B), K_("H32", d)])
                S.op("pool", lambda e: e.tensor_tensor(out=H32[d][:], in0=H32[d][:],
                                                       in1=eLC[dp][:, :, c:c + 1].to_broadcast([128, 4, 64]), op=ALU.mult),
                     reads=[K_("H32", d), K_("eLC", dp)], writes=[K_("H32", d)])
                S.op("act", lambda e: e.activation(out=Hbf[d][:], in_=H32[d][:], func=AF.Copy), reads=[K_("H32", d)], writes=[K_("Hbf", d)])
                if is_lat:
                    S.op("act", lambda e: e.activation(out=ybuf[d][:], in_=psY, func=AF.Copy), writes=[bk(B), K_("ybuf", d)])
                    tl = tk0 - TC
                    for hp_ in range(2):
                        S.dma("sp", lambda e, tl=tl, hp_=hp_: e.dma_start(
                            out=y_d[b, d, tl:tl + 64, :].rearrange("t (i hp v) -> t i hp v", hp=2, v=64)[:, :, hp_, :],
                            in_=ybuf[d][hp_ * 64:(hp_ + 1) * 64]), "y_st", reads=[K_("ybuf", d)])
                    yield
                    psB = banks[B]
                    for (po, i) in HP:
                        S.op("pe", lambda e, po=po, i=i: e.matmul(psB[po:po + 64, i:i + 1], lhsT=prod[dp][po:po + 64, i, c * 64:(c + 1) * 64],
                                                                   rhs=rkb[po:po + 64, i:i + 1], start=True, stop=True),
                             reads=[K_("prod", dp), "rkb"], writes=[bk(B)])
                    S.op("dve", lambda e: e.tensor_scalar(out=bon[d][:], in0=psB[:, 0:4], scalar1=0.5, scalar2=None, op0=ALU.mult),
                         writes=[bk(B), K_("bon", d)])
                    S.op("dve", lambda e: e.tensor_tensor(out=bosb[d][:], in0=vt[:],
                                                          in1=bon[d][:].unsqueeze(2).to_broadcast([128, 4, 64]), op=ALU.mult),
                         reads=[K_("vtm", js), K_("bon", d)], writes=[K_("bosb", d)])
                    for hp_ in range(2):
                        S.dma("sp", lambda e, tl=tl, hp_=hp_: e.dma_start(
                            out=bo_d[b, d, tl:tl + 64, :].rearrange("t (i hp v) -> t i hp v", hp=2, v=64)[:, :, hp_, :],
                            in_=bosb[d][hp_ * 64:(hp_ + 1) * 64]), "bo_st", reads=[K_("bosb", d)])
                yield

            def lockstep(gens):
                gens = list(gens)
                while gens:
                    for g_ in list(gens):
                        try:
                            next(g_)
                        except StopIteration:
                            gens.remove(g_)

            def pb_batch(b):
                S.dma("sp", lambda e: e.dma_start(out=wdad[:], in_=pa_d[b, 12:14].rearrange("j p t -> p j t")), "pb_ld_w", writes=["wdad"])
                for (dst, j0, key) in ((ks, 4, "ks"), (rs, 0, "rs"), (vs, 8, "vs")):
                    S.dma("sp", lambda e, dst=dst, j0=j0: e.dma_start(out=dst[:], in_=pa_d[b, j0:j0 + 4].rearrange("j p t -> p j t")),
                          "pb_ld_" + key, writes=[key])
                S.op("act", lambda e: e.activation(out=wdad[0:64], in_=wdad[0:64], func=AF.Tanh), reads=["wdad"], writes=["wdad"])
                for d in range(2):
                    S.op("pool", lambda e, d=d: e.memset(H32[d][:], 0.0), writes=[K_("H32", d)])
                    S.op("pool", lambda e, d=d: e.memset(Hbf[d][:], 0.0), writes=[K_("Hbf", d)])
                cw_ = [(w * W, False) for w in range(TC // W)]
                lw_ = [(TC + w * W, True) for w in range(T // W)]
                sched = {0: cw_ + lw_, 1: list(reversed(cw_)) + list(reversed(lw_))}
                nw_ = len(sched[0])

                def preps(wi):
                    return [prep(b, d, sched[d][wi][0], wi % 3) for d in range(2)]

                def jobs(wi):
                    for d in range(2):
                        for cc in range(NCW):
                            c = cc if d == 0 else NCW - 1 - cc
                            yield d, cc, c, (wi % 2) * 2 * NCW + d * NCW + cc, d * NCW + cc

                def chains(wi):
                    for cc in range(NCW):
                        gens = []
                        for (d, cc_, c, js, jt) in jobs(wi):
                            if cc_ == cc:
                                gens.append(chain(b, d, sched[d][wi][0], c, sched[d][wi][1], wi % 3, js, 2 * NCW + d))
                        while gens:
                            for g_ in list(gens):
                                try:
                                    next(g_)
                                except StopIteration:
                                    gens.remove(g_)
                            yield

                lockstep(preps(0))
                for t in range(nw_ + 1):
                    gl = []
                    if t >= 1:
                        gl.append(chains(t - 1))
                    if t < nw_:
                        gl += [inv(b, d, sched[d][t][0], c, t % 3, js, jt, jt) for (d, cc, c, js, jt) in jobs(t)]
                    if t + 1 < nw_:
                        gl += preps(t + 1)
                    lockstep(gl)

            for b in range(NB):
                pb_batch(b)

        def post_norm_residual(pbs, gpost, xres, xkey, outt, okey, tmp, kp="pn", gk="gpost"):
            for hh in range(2):
                S.op("act", lambda e, hh=hh: e.activation(out=tmp["sq"][:, hh * 512:(hh + 1) * 512], in_=banks[pbs[hh]][:, :], func=AF.Square),
                     writes=[bk(pbs[hh]), kp + "_sq"])
            S.op("dve", lambda e: e.reduce_sum(out=tmp["ss"][:], in_=tmp["sq"][:], axis=AX.X), reads=[kp + "_sq"], writes=[kp + "_ss"])
            S.op("dve", lambda e: e.tensor_scalar(out=tmp["ss"][:], in0=tmp["ss"][:], scalar1=1.0 / D, scalar2=1e-6,
                                                  op0=ALU.mult, op1=ALU.add), reads=[kp + "_ss"], writes=[kp + "_ss"])
            S.op("act", lambda e: e.activation(out=tmp["ss"][:], in_=tmp["ss"][:], func=AF.Sqrt), reads=[kp + "_ss"], writes=[kp + "_ss"])
            S.op("dve", lambda e: e.reciprocal(out=tmp["ss"][:], in_=tmp["ss"][:]), reads=[kp + "_ss"], writes=[kp + "_ss"])
            for hh in range(2):
                S.op("dve", lambda e, hh=hh: e.scalar_tensor_tensor(
                    out=tmp["sq"][:, hh * 512:(hh + 1) * 512], in0=banks[pbs[hh]][:, :], scalar=tmp["ss"][:, 0:1],
                    in1=gpost[:, hh * 512:(hh + 1) * 512], op0=ALU.mult, op1=ALU.mult),
                    reads=[kp + "_ss", gk], writes=[bk(pbs[hh]), kp + "_sq"])
            S.op("pool", lambda e: e.tensor_tensor(out=outt[:], in0=tmp["sq"][:], in1=xres, op=ALU.add), reads=[kp + "_sq", xkey], writes=[okey])

        def make_gpost(b, gi, rowname, gpost, gb=(0, 1), gk="gpost"):
            for hh in range(2):
                S.op("pe", lambda e, hh=hh: e.matmul(banks[gb[hh]][:, :], lhsT=sel[:, b, :], rhs=gates[:, gi, hh * 512:(hh + 1) * 512],
                                                     start=True, stop=True), reads=["sel", "gates"], writes=[bk(gb[hh])])
                S.op("dve", lambda e, hh=hh: e.tensor_tensor(out=gpost[:, hh * 512:(hh + 1) * 512], in0=banks[gb[hh]][:, :],
                                                             in1=cr(rowname)[:, hh * 512:(hh + 1) * 512], op=ALU.mult),
                     reads=["crow"], writes=[bk(gb[hh]), gk])

        if PHASES >= 3:
            S.fence()
            A_.reset()
            woutb = A_.t([128, DK, D], BF16)
            gw2b = A_.t([128, 2, 512], BF16)
            stg = [A_.t([128, DK, 512], F32) for _ in range(2)]
            for n in range(2):
                S.dma("sp", lambda e, n=n: e.dma_start(out=stg[n][:], in_=wout_d[:, n * 512:(n + 1) * 512].rearrange("(k p) n -> p k n", p=128)),
                      "stgd%d" % n, writes=["stgd%d" % n])
                S.op("dve", lambda e, n=n: e.tensor_copy(out=woutb[:, :, n * 512:(n + 1) * 512], in_=stg[n][:]), reads=["stgd%d" % n], writes=["woutb"])
            gst = A_.t([128, 2, 512], F32)
            S.dma("sp", lambda e: e.dma_start(out=gst[:, 0, :], in_=gw2_d[0:128, :]), "gst", writes=["gst"])
            S.dma("sp", lambda e: e.dma_start(out=gst[0:32, 1, :], in_=gw2_d[128:160, :]), "gst", writes=["gst"])
            S.op("dve", lambda e: e.tensor_copy(out=gw2b[:, 0, :], in_=gst[:, 0, :]), reads=["gst"], writes=["gw2b"])
            S.op("dve", lambda e: e.tensor_copy(out=gw2b[0:32, 1, :], in_=gst[0:32, 1, :]), reads=["gst"], writes=["gw2b"])
            S.fence()
            A_.off -= 2 * 16384 + 4096
            onesf = A_.t([128, 128], F32)
            S.op("pool", lambda e: e.memset(onesf[:], 1.0), writes=["onesf"])
            ub = A_.t([128, 4, T], BF16)
            yc = A_.t([128, 4, T], F32)
            convo_ = [A_.t([128, 4, T], BF16) for _ in range(2)]
            gds_ = [A_.t([128, 2, 128], BF16) for _ in range(2)]
            lsq = A_.t([128, 4, 512], F32)
            lmean = A_.t([128, 512], F32)
            lrstd = A_.t([128, 512], F32)
            ltmp = A_.t([128, 512], F32)
            yin = [[A_.t([128, 512], F32) for _ in range(4)] for _ in range(2)]
            ysq_ = [A_.t([128, 512], F32) for _ in range(2)]
            st8_ = [A_.t([128, 4, 8], F32) for _ in range(2)]
            rwb_ = [A_.t([128, 512], BF16) for _ in range(2)]
            mixT_ = [A_.t([128, 4, 128], BF16) for _ in range(2)]
            xt2 = [A_.t([128, D], F32) for _ in range(2)]
            gpost_ = [A_.t([128, D], F32) for _ in range(2)]
            pn_tmp_ = [{"sq": A_.t([128, D], F32), "ss": A_.t([128, 1], F32)} for _ in range(2)]
            cw = cf("conv_w")
            ti_ = [0]

            def pd_conv(b):
                cs = b % 2
                convo, gpost = convo_[cs], gpost_[cs]
                ck, gk = "convo%d" % cs, "gpost%d" % cs
                make_gpost(b, 0, "mix_post_g", gpost, (6, 7), gk)
                yield
                S.dma("sp", lambda e, b=b: e.dma_start(out=ub[:], in_=pa_d[b, 16:20, :, TC:TT].rearrange("j p t -> p j t")), "pd_ld", writes=["ub"])
                for i in range(4):
                    u4 = ub[:, i, :].rearrange("p (r t) -> p r t", t=64)
                    y4 = yc[:, i, :].rearrange("p (r t) -> p r t", t=64)
                    S.op("dve", lambda e, i=i: e.tensor_scalar(out=yc[:, i, :], in0=ub[:, i, :], scalar1=cw[:, i * 31 + 15:i * 31 + 16],
                                                               scalar2=cf("conv_b")[:, i:i + 1], op0=ALU.mult, op1=ALU.add),
                         reads=["ub", "cfm"], writes=["yc%d" % i])
                    for j in range(31):
                        o = j - 15
                        if o == 0:
                            continue
                        lo_o, hi_o = max(0, -o), 64 - max(0, o)
                        lo_i, hi_i = max(0, o), 64 - max(0, -o)
                        S.op("dve", lambda e, i=i, j=j, u4=u4, y4=y4, lo_o=lo_o, hi_o=hi_o, lo_i=lo_i, hi_i=hi_i: e.scalar_tensor_tensor(
                            out=y4[:, :, lo_o:hi_o], in0=u4[:, :, lo_i:hi_i], scalar=cw[:, i * 31 + j:i * 31 + j + 1],
                            in1=y4[:, :, lo_o:hi_o], op0=ALU.mult, op1=ALU.add), reads=["ub", "cfm", "yc%d" % i], writes=["yc%d" % i])
                        if j % 3 == 0:
                            yield
                for w in range(T // 512 if T >= 512 else 1):
                    wn = min(512, T)
                    ws = slice(w * 512, w * 512 + wn)
                    for i in range(4):
                        S.op("pe", lambda e, i=i, ws=ws, wn=wn: e.matmul(banks[6][:, 0:wn], lhsT=onesf[:], rhs=yc[:, i, ws], start=(i == 0), stop=(i == 3)),
                             reads=["onesf", "yc%d" % i], writes=[bk(6)])
                    S.op("act", lambda e, ws=ws, wn=wn: e.activation(out=lsq[:, :, 0:wn], in_=yc[:, :, ws], func=AF.Square),
                         reads=["yc0", "yc1", "yc2", "yc3"], writes=["lsq"])
                    for i in range(4):
                        S.op("pe", lambda e, i=i, wn=wn: e.matmul(banks[7][:, 0:wn], lhsT=onesf[:], rhs=lsq[:, i, 0:wn], start=(i == 0), stop=(i == 3)),
                             reads=["onesf", "lsq"], writes=[bk(7)])
                    yield
                    S.op("dve", lambda e, wn=wn: e.tensor_scalar(out=lmean[:, 0:wn], in0=banks[6][:, 0:wn], scalar1=1.0 / 512, scalar2=None, op0=ALU.mult),
                         writes=[bk(6), "lmean"])
                    S.op("dve", lambda e, wn=wn: e.tensor_tensor(out=ltmp[:, 0:wn], in0=lmean[:, 0:wn], in1=lmean[:, 0:wn], op=ALU.mult),
                         reads=["lmean"], writes=["ltmp"])
                    S.op("dve", lambda e, wn=wn: e.scalar_tensor_tensor(out=lrstd[:, 0:wn], in0=banks[7][:, 0:wn], scalar=1.0 / 512, in1=ltmp[:, 0:wn],
                                                                        op0=ALU.mult, op1=ALU.subtract), reads=["ltmp"], writes=[bk(7), "lrstd"])
                    S.op("dve", lambda e, wn=wn: e.tensor_scalar(out=lrstd[:, 0:wn], in0=lrstd[:, 0:wn], scalar1=1e-5, scalar2=None, op0=ALU.add),
                         reads=["lrstd"], writes=["lrstd"])
                    yield
                    S.op("act", lambda e, wn=wn: e.activation(out=lrstd[:, 0:wn], in_=lrstd[:, 0:wn], func=AF.Sqrt), reads=["lrstd"], writes=["lrstd"])
                    S.op("dve", lambda e, wn=wn: e.reciprocal(out=lrstd[:, 0:wn], in_=lrstd[:, 0:wn]), reads=["lrstd"], writes=["lrstd"])
                    yield
                    S.op("dve", lambda e, ws=ws, wn=wn: e.tensor_tensor(out=lsq[:, :, 0:wn], in0=yc[:, :, ws],
                                                                        in1=lmean[:, 0:wn].unsqueeze(1).to_broadcast([128, 4, wn]), op=ALU.subtract),
                         reads=["yc0", "yc1", "yc2", "yc3", "lmean"], writes=["lsq"])
                    S.op("dve", lambda e, wn=wn: e.tensor_tensor(out=lsq[:, :, 0:wn], in0=lsq[:, :, 0:wn],
                                                                 in1=lrstd[:, 0:wn].unsqueeze(1).to_broadcast([128, 4, wn]), op=ALU.mult),
                         reads=["lsq", "lrstd"], writes=["lsq"])
                    yield
                    for i in range(4):
                        S.op("act", lambda e, i=i, ws=ws, wn=wn: e.activation(out=convo[:, i, ws], in_=lsq[:, i, 0:wn], func=AF.Silu,
                                                                              scale=cf("cln_w")[:, i:i + 1], bias=cf("cln_b")[:, i:i + 1]),
                             reads=["lsq", "cfm"], writes=[ck])
                yield

            def pd_tiles(b):
                for tt in range(0, T // 128, 2):
                    gens = [_pd_tile(b, tt + q_, q_) for q_ in range(2) if tt + q_ < T // 128]
                    while gens:
                        for g_ in list(gens):
                            try:
                                next(g_)
                            except StopIteration:
                                gens.remove(g_)
                        yield

            def _pd_tile(b, tt, sl):
                if True:
                    tsl = slice(tt * 128, (tt + 1) * 128)
                    ti = sl
                    BG, BT, BO0, BO1 = 3 * sl, 3 * sl, 3 * sl + 1, 3 * sl + 2
                    cs = b % 2
                    convo, gpost = convo_[cs], gpost_[cs]
                    ck, gk = "convo%d" % cs, "gpost%d" % cs
                    gds = gds_[sl]
                    ysq, st8, rwb, mixT = ysq_[sl], st8_[sl], rwb_[sl], mixT_[sl]
                    pn_tmp = pn_tmp_[sl]
                    sk = lambda nm: "%s_%d" % (nm, sl)
                    yy = yin[ti % 2]
                    yk = "yin%d" % (ti % 2)
                    xa, xk2 = xt2[ti % 2], "xt2_%d" % (ti % 2)
                    for q, src in enumerate((y_d, y_d, bo_d, bo_d)):
                        S.dma("sp", lambda e, q=q, src=src, yy=yy, tsl=tsl: e.dma_start(out=yy[q][:], in_=src[b, q % 2, tsl, :]), yk, writes=[yk])
                    S.dma("sp", lambda e, xa=xa, tsl=tsl: e.dma_start(out=xa[:], in_=x_d[b, tsl, :]), xk2, writes=[xk2])
                    S.dma("sp", lambda e, tsl=tsl: e.dma_start(out=gds[:], in_=pa_d[b, 14:16, :, TC + tsl.start:TC + tsl.stop].rearrange("j p t -> p j t")), sk("gds"), writes=[sk("gds")])
                    S.op("act", lambda e: e.activation(out=gds[:, 0, :], in_=gds[:, 0, :], func=AF.Sigmoid), reads=[sk("gds")], writes=[sk("gds")])
                    S.op("act", lambda e: e.activation(out=gds[0:32, 1, :], in_=gds[0:32, 1, :], func=AF.Sigmoid), reads=[sk("gds")], writes=[sk("gds")])
                    S.op("pool", lambda e, yy=yy: e.tensor_tensor(out=yy[0][:], in0=yy[0][:], in1=yy[1][:], op=ALU.add), reads=[yk], writes=[yk])
                    S.op("pool", lambda e, yy=yy: e.tensor_tensor(out=yy[2][:], in0=yy[2][:], in1=yy[3][:], op=ALU.add), reads=[yk], writes=[yk])
                    yield
                    y3 = yy[0][:].rearrange("p (h v) -> p h v", v=64)
                    S.op("dve", lambda e, y3=y3: e.reduce_sum(out=st8[:, 0, :], in_=y3, axis=AX.X), reads=[yk], writes=[sk("st8")])
                    S.op("act", lambda e, yy=yy: e.activation(out=ysq[:], in_=yy[0][:], func=AF.Square), reads=[yk], writes=[sk("ysq")])
                    S.op("dve", lambda e: e.reduce_sum(out=st8[:, 1, :], in_=ysq[:].rearrange("p (h v) -> p h v", v=64), axis=AX.X),
                         reads=[sk("ysq")], writes=[sk("st8")])
                    S.op("dve", lambda e: e.tensor_scalar(out=st8[:, 0:2, :], in0=st8[:, 0:2, :], scalar1=1.0 / 64, scalar2=None, op0=ALU.mult),
                         reads=[sk("st8")], writes=[sk("st8")])
                    yield
                    S.op("dve", lambda e: e.tensor_tensor(out=st8[:, 2, :], in0=st8[:, 0, :], in1=st8[:, 0, :], op=ALU.mult), reads=[sk("st8")], writes=[sk("st8")])
                    S.op("dve", lambda e: e.tensor_tensor(out=st8[:, 3, :], in0=st8[:, 1, :], in1=st8[:, 2, :], op=ALU.subtract), reads=[sk("st8")], writes=[sk("st8")])
                    S.op("dve", lambda e: e.tensor_scalar(out=st8[:, 3, :], in0=st8[:, 3, :], scalar1=64e-5, scalar2=None, op0=ALU.add),
                         reads=[sk("st8")], writes=[sk("st8")])
                    S.op("act", lambda e: e.activation(out=st8[:, 3, :], in_=st8[:, 3, :], func=AF.Sqrt), reads=[sk("st8")], writes=[sk("st8")])
                    S.op("dve", lambda e: e.reciprocal(out=st8[:, 3, :], in_=st8[:, 3, :]), reads=[sk("st8")], writes=[sk("st8")])
                    yield
                    S.op("dve", lambda e, y3=y3: e.tensor_tensor(out=y3, in0=y3, in1=st8[:, 0, :].unsqueeze(2).to_broadcast([128, 8, 64]), op=ALU.subtract),
                         reads=[yk, sk("st8")], writes=[yk])
                    S.op("dve", lambda e, y3=y3: e.tensor_tensor(out=y3, in0=y3, in1=st8[:, 3, :].unsqueeze(2).to_broadcast([128, 8, 64]), op=ALU.mult),
                         reads=[yk, sk("st8")], writes=[yk])
                    S.op("pool", lambda e, yy=yy: e.tensor_tensor(out=yy[0][:], in0=yy[0][:], in1=cr("lnx_w"), op=ALU.mult), reads=[yk, "crow"], writes=[yk])
                    S.op("pool", lambda e, yy=yy: e.tensor_tensor(out=yy[0][:], in0=yy[0][:], in1=cr("lnx_b"), op=ALU.add), reads=[yk, "crow"], writes=[yk])
                    S.op("pool", lambda e, yy=yy: e.tensor_tensor(out=yy[0][:], in0=yy[0][:], in1=yy[2][:], op=ALU.add), reads=[yk], writes=[yk])
                    yield
                    S.op("pe", lambda e, tsl=tsl: e.matmul(banks[BG][:, :], lhsT=gds[:, 0, :], rhs=gw2b[:, 0, :], start=True, stop=False),
                         reads=[sk("gds"), "gw2b"], writes=[bk(BG)])
                    S.op("pe", lambda e, tsl=tsl: e.matmul(banks[BG][:, :], lhsT=gds[0:32, 1, :], rhs=gw2b[0:32, 1, :], start=False, stop=True),
                         reads=[sk("gds"), "gw2b"], writes=[bk(BG)])
                    S.op("dve", lambda e, yy=yy: e.tensor_tensor(out=rwb[:], in0=yy[0][:], in1=banks[BG][:, :], op=ALU.mult),
                         reads=[yk], writes=[bk(BG), sk("rwb")])
                    yield
                    pT = banks[BT].bitcast(BF16)
                    for j in range(4):
                        S.op("pe", lambda e, j=j: e.transpose(pT[:, j * 128:(j + 1) * 128], rwb[:, j * 128:(j + 1) * 128], ident[:]),
                             reads=[sk("rwb"), "ident"], writes=[bk(BT)])
                    S.op("act", lambda e: e.activation(out=mixT[:].rearrange("p j t -> p (j t)"), in_=pT[:, 0:512], func=AF.Copy),
                         writes=[bk(BT), sk("mixT")])
                    yield
                    for hh in range(2):
                        for j in range(8):
                            lhs = mixT[:, j, :] if j < 4 else convo[:, j - 4, tsl]
                            S.op("pe", lambda e, hh=hh, j=j, lhs=lhs: e.matmul(banks[BO0 + hh][:, :], lhsT=lhs, rhs=woutb[:, j, hh * 512:(hh + 1) * 512],
                                                                               start=(j == 0), stop=(j == 7)),
                                 reads=[sk("mixT"), ck, "woutb"], writes=[bk(BO0 + hh)])
                    yield
                    post_norm_residual((BO0, BO1), gpost, xa[:], xk2, xa, xk2, pn_tmp, sk("pn"), gk)
                    S.dma("sp", lambda e, xa=xa, tsl=tsl: e.dma_start(out=out_d[b, tsl, :], in_=xa[:]), "x1_st", reads=[xk2])

            def lockstep_pd(gens):
                gens = list(gens)
                while gens:
                    for g_ in list(gens):
                        try:
                            next(g_)
                        except StopIteration:
                            gens.remove(g_)

            lockstep_pd([pd_conv(0)])
            for b in range(NB):
                lockstep_pd([pd_tiles(b)] + ([pd_conv(b + 1)] if b + 1 < NB else []))

        if PHASES >= 4:
            S.fence()
            A_.reset()
            w1b = A_.t([128, DK, 4096], BF16)
            w2b_ = A_.t([128, 32, D], BF16)
            mark = A_.off
            stg_e = [A_.t([128, DK, 256], F32) for _ in range(4)]
            for n in range(16):
                sb_, sk_ = stg_e[n % 4], "stge%d" % (n % 4)
                S.dma("sp", lambda e, n=n, sb_=sb_: e.dma_start(out=sb_[:], in_=w1_d[:, n * 256:(n + 1) * 256].rearrange("(k p) n -> p k n", p=128)),
                      sk_, writes=[sk_])
                if n % 2 == 0:
                    S.op("dve", lambda e, n=n, sb_=sb_: e.tensor_copy(out=w1b[:, :, n * 256:(n + 1) * 256], in_=sb_[:]), reads=[sk_], writes=["w1b"])
                else:
                    S.op("act", lambda e, n=n, sb_=sb_: e.activation(out=w1b[:, :, n * 256:(n + 1) * 256], in_=sb_[:], func=AF.Copy),
                         reads=[sk_], writes=["w1b"])
            for n in range(16):
                sb_, sk_ = stg_e[n % 4], "stge%d" % (n % 4)
                src = sb_[:].rearrange("p k n -> p (k n)").rearrange("p (f n) -> p f n", n=D)
                S.dma("sp", lambda e, n=n, src=src: e.dma_start(out=src, in_=w2_d[n * 256:(n + 1) * 256, :].rearrange("(f p) n -> p f n", p=128)),
                      sk_, writes=[sk_])
                if n % 2 == 0:
                    S.op("dve", lambda e, n=n, src=src: e.tensor_copy(out=w2b_[:, n * 2:(n + 1) * 2, :], in_=src), reads=[sk_], writes=["w2b_"])
                else:
                    S.op("act", lambda e, n=n, src=src: e.activation(out=w2b_[:, n * 2:(n + 1) * 2, :], in_=src, func=AF.Copy),
                         reads=[sk_], writes=["w2b_"])
            S.fence()
            A_.off = mark
            G = 256
            hT2 = [A_.t([128, DK, G], BF16) for _ in range(2)]
            hidr = [A_.t([128, 2, G], BF16) for _ in range(4)]
            rl = [A_.t([128, 512], F32) for _ in range(2)]
            x1g = [A_.t([128, D], F32) for _ in range(4)]
            gpost2 = A_.t([128, D], F32)
            ftmp2 = {"sq": A_.t([128, D], F32), "ss": A_.t([128, 1], F32), "xn": A_.t([128, D], BF16)}
            pn_tmp2 = {"sq": ftmp2["sq"], "ss": A_.t([128, 1], F32)}
            groups = [(b, g) for b in range(NB) for g in range(T // G)]
            OB = ((5, 6), (0, 1))

            def pe_front(gi):
                b, g = groups[gi]
                xs = gi % 2
                for tq in range(2):
                    tsl = slice(g * G + tq * 128, g * G + (tq + 1) * 128)
                    xt_, xk_ = x1g[xs * 2 + tq], "x1g%d" % (xs * 2 + tq)
                    S.dma("sp", lambda e, xt_=xt_, tsl=tsl, b=b: e.dma_start(out=xt_[:], in_=out_d[b, tsl, :]), xk_, writes=[xk_])
                    front(xt_[:], xk_, hT2[xs][:, :, tq * 128:(tq + 1) * 128], "hT2_%d" % xs, 1, 3, b, ftmp2, 2)

            def pe_out_pair(gi, f2):
                xs = gi % 2
                hr, hk = hidr[f2 % 4], "hidr%d" % (f2 % 4)
                for ff in range(2):
                    f = f2 * 2 + ff
                    for tq in range(2):
                        for hh in range(2):
                            S.op("pe", lambda e, hr=hr, ff=ff, f=f, tq=tq, hh=hh: e.matmul(
                                banks[OB[tq][hh]][:, :], lhsT=hr[:, ff, tq * 128:(tq + 1) * 128], rhs=w2b_[:, f, hh * 512:(hh + 1) * 512],
                                start=(f == 0), stop=(f == 31)), reads=[hk, "w2b_"], writes=[bk(OB[tq][hh])])

            def pe_group(gi):
                b, g = groups[gi]
                xs = gi % 2
                if g == 0:
                    make_gpost(b, 1, "mlp_post_g", gpost2)
                for f2 in range(16):
                    pbk = 3 + (f2 % 2)
                    for ff in range(2):
                        f = f2 * 2 + ff
                        for k in range(DK):
                            S.op("pe", lambda e, pbk=pbk, ff=ff, f=f, k=k: e.matmul(
                                banks[pbk][:, ff * G:(ff + 1) * G], lhsT=w1b[:, k, f * 128:(f + 1) * 128], rhs=hT2[xs][:, k, :],
                                start=(k == 0), stop=(k == DK - 1)), reads=["w1b", "hT2_%d" % xs], writes=[bk(pbk)])
                    S.op("act", lambda e, pbk=pbk, f2=f2: e.activation(out=rl[f2 % 2][:], in_=banks[pbk][:, :], func=AF.Relu),
                         writes=[bk(pbk), "rl%d" % (f2 % 2)])
                    S.op("dve" if f2 % 2 == 0 else "pool", lambda e, f2=f2: e.tensor_tensor(
                        out=hidr[f2 % 4][:], in0=rl[f2 % 2][:].rearrange("p (a t) -> p a t", a=2),
                        in1=rl[f2 % 2][:].rearrange("p (a t) -> p a t", a=2), op=ALU.mult), reads=["rl%d" % (f2 % 2)], writes=["hidr%d" % (f2 % 4)])
                    if f2 >= 2:
                        pe_out_pair(gi, f2 - 2)
                    if f2 == 5 and gi + 1 < len(groups):
                        pe_front(gi + 1)
                pe_out_pair(gi, 14)
                pe_out_pair(gi, 15)
                for tq in range(2):
                    tsl = slice(g * G + tq * 128, g * G + (tq + 1) * 128)
                    xt_, xk_ = x1g[xs * 2 + tq], "x1g%d" % (xs * 2 + tq)
                    post_norm_residual(OB[tq], gpost2, xt_[:], xk_, xt_, xk_, pn_tmp2, "f")
                    S.dma("sp", lambda e, xt_=xt_, tsl=tsl, b=b: e.dma_start(out=out_d[b, tsl, :], in_=xt_[:]), "out_st", reads=[xk_])

            pe_front(0)
            for gi in range(len(groups)):
                pe_group(gi)
        S.run(nc, es)
    return nc


def _perm_cols():
    idx = list(range(0, 1536))
    idx += list(range(1536, 1600)) + list(range(1664, 1728))
    idx += list(range(1600, 1664)) + list(range(1728, 1792))
    idx += list(range(1792, 1952)) + [-1] * 96
    idx += list(range(1952, 2976))
    return np.array(idx)


def prep_inputs(inp, NB, ncores):
    f = lambda a: np.ascontiguousarray(np.asarray(a, dtype=np.float32))
    perm = _perm_cols()
    w_in = f(inp["w_in"])[0]
    w_in_p = np.zeros((D, 3072), np.float32)
    w_in_p[:, perm >= 0] = w_in[:, perm[perm >= 0]]

    def permvec(v):
        o = np.zeros(2048, np.float32)
        p16 = perm[:2048]
        o[p16 >= 0] = v[p16[p16 >= 0]]
        return o.reshape(16, 128).T

    fm = lambda v: np.asarray(v, np.float32).reshape(-1, 128).T
    cfm = np.zeros((128, NCF), np.float32)

    def put(nm, arr):
        o, w = CF[nm]
        assert arr.shape == (128, w), (nm, arr.shape)
        cfm[:, o:o + w] = arr
    put("ada_b", fm(inp["ada_b"][0]))
    put("mix_pre_g", fm(inp["mix_pre_g"][0]))
    put("mlp_pre_g", fm(inp["mlp_pre_g"][0]))
    put("mu_prev", permvec(f(inp["mu_prev"])[0]))
    put("mu_next", permvec(f(inp["mu_next"])[0]))
    put("w0", np.concatenate([fm(inp["decay_w0"][0, 0]), fm(inp["decay_w0"][0, 1])], 1))
    put("a0", np.concatenate([fm(inp["iclr_a0"][0, 0]), fm(inp["iclr_a0"][0, 1])], 1))
    put("k_k", fm(inp["k_k"][0]))
    put("k_a", fm(inp["k_a"][0]))
    put("r_k", fm(np.asarray(inp["r_k"][0]).reshape(-1)))
    cw = f(inp["conv_w"])[0]
    put("conv_w", cw.T.reshape(4, 128, 31).transpose(1, 0, 2).reshape(128, 124))
    put("conv_b", fm(inp["conv_b"][0]))
    put("cln_w", fm(inp["conv_ln_w"][0]))
    put("cln_b", fm(inp["conv_ln_b"][0]))
    crow = np.zeros((NCR,), np.float32)
    for nm, v in (("ada_b", inp["ada_b"][0]), ("mix_post_g", inp["mix_post_g"][0]),
                  ("mlp_post_g", inp["mlp_post_g"][0]), ("lnx_w", inp["lnx_w"][0]), ("lnx_b", inp["lnx_b"][0])):
        o, w = CR[nm]
        crow[o:o + w] = np.asarray(v, np.float32)
    crow = np.ascontiguousarray(np.broadcast_to(crow[None, :], (128, NCR)))
    x = f(inp["x"])
    ctx = f(inp["ctx"])
    c = f(inp["c"])
    cc = f(inp["c_ctx"])
    shared = {"ada_w": f(inp["ada_w"])[0], "w_in": w_in_p, "cfm": cfm, "crow": crow,
              "decay_w2": f(inp["decay_w2"])[0], "iclr_a2": f(inp["iclr_a2"])[0], "gate_w2": f(inp["gate_w2"])[0],
              "w_out": f(inp["w_out"])[0], "mlp_w1": f(inp["mlp_w1"])[0], "mlp_w2": f(inp["mlp_w2"])[0]}
    maps = []
    for i in range(ncores):
        cb = np.concatenate([c[i * NB:(i + 1) * NB], cc[None, :]], 0)
        cT = np.ascontiguousarray(cb.T.reshape(DK, 128, NB + 1).transpose(1, 0, 2))
        m = dict(shared)
        m.update({"x": np.ascontiguousarray(x[i * NB:(i + 1) * NB]), "ctx": np.ascontiguousarray(ctx[i * NB:(i + 1) * NB]),
                  "cT": cT})
        maps.append(m)
    return maps


def kernel(**inputs):
    B, T, _ = inputs["x"].shape
    TC = inputs["ctx"].shape[1]
    NB = B // NCORES
    nc = build(NB, T, TC)
    maps = prep_inputs(inputs, NB, NCORES)
    res = run_bass_kernel_spmd(nc, maps, core_ids=list(range(NCORES)))
    return np.concatenate([np.asarray(r["out"]) for r in res.results], 0).astype(np.float32)
```
